# Optimizing a Trainium2 kernel written in Bass

```python
import math
import jax, jax.numpy as jnp
from jax import lax
import numpy as np

D_MODEL = 1024
BATCH = 8
SEQ = 2048
DEPTH = 4
DEC_BATCH = 128
DEC_SEQ = 4
PAST_LEN = 16384
PAGE_SIZE = 128

D_A = D_MODEL // 2
D_B = D_MODEL // 2
HEAD_B = 64
H_B = D_B // HEAD_B
LORA_W = 64
LORA_A = 64
LORA_G = 128
D_BP = 3 * D_B + LORA_W + LORA_A + LORA_G
D_IN_AB = 2 * D_A + D_BP
SPLIT_B = (D_B, 2 * D_B, 3 * D_B, 3 * D_B + LORA_W, 3 * D_B + LORA_W + LORA_A)
CONV_A_WIDTH = 31
D_C = D_MODEL
C_GROUP = 16
G_C = D_C // C_GROUP
P_C = 64
D_FF = ((8 * D_MODEL // 3 + 127) // 128) * 128
FFN_CONV_WIDTH = 3
N_AB = (DEPTH + 1) // 2
N_C = DEPTH // 2
ALPHA = (2 * DEPTH) ** 0.25
BETA = (8 * DEPTH) ** -0.25
LN_EPS = 1e-5
GN_EPS = HEAD_B * 1e-5

kernel_name = 'hybrid_conformer_rwkv7_s5_convffn_step'


def layer_norm(x, g, b, eps=LN_EPS):
    xf = x.astype(jnp.float32)
    xc = xf - jnp.mean(xf, -1, keepdims=True)
    var = jnp.mean(xc * xc, -1, keepdims=True)
    return (xc * lax.rsqrt(var + eps) * g.astype(jnp.float32) + b.astype(jnp.float32)).astype(x.dtype)


def causal_depthwise(ext, w, b):
    y = lax.conv_general_dilated(ext, w[:, None, :].astype(ext.dtype), (1,), 'VALID',
                                 dimension_numbers=('NWC', 'WIO', 'NWC'),
                                 feature_group_count=ext.shape[-1])
    return y + b.astype(ext.dtype)


def wkv_recurrence(r, k, v, decay, kk, a, s0):
    def step(s, inp):
        r_t, k_t, v_t, d_t, kk_t, a_t = inp
        sa = jnp.einsum('bhvk,bhk->bhv', s, -kk_t)
        s = (s * d_t[:, :, None, :] + sa[..., None] * (kk_t * a_t)[:, :, None, :]
             + v_t[..., None] * k_t[:, :, None, :])
        return s, jnp.einsum('bhvk,bhk->bhv', s, r_t)
    xs = tuple(jnp.swapaxes(t, 0, 1) for t in (r, k, v, decay, kk, a))
    s, y = lax.scan(step, s0.astype(jnp.float32), xs)
    return jnp.swapaxes(y, 0, 1), s


def mixer_ab(x, conv_st, shift_st, wkv_st, w_in, conv_w, conv_b, ln_g, ln_b, mu, w0, w2, a0, a2, g2,
             k_k, k_a, r_k, lnx_g, lnx_b, w_out):
    bsz, t_len, _ = x.shape
    f32 = jnp.float32
    p = x @ w_in
    pa, pb = p[..., :2 * D_A], p[..., 2 * D_A:]
    u = pa[..., :D_A] * jax.nn.sigmoid(pa[..., D_A:])
    ext = jnp.concatenate([conv_st.astype(u.dtype), u], axis=1)
    ya = jax.nn.silu(layer_norm(causal_depthwise(ext, conv_w, conv_b), ln_g, ln_b)).astype(f32)
    new_conv = ext[:, -(CONV_A_WIDTH - 1):]
    prev = jnp.concatenate([shift_st[:, None].astype(pb.dtype), pb[:, :-1]], axis=1)
    q = (pb + (prev - pb) * mu).astype(f32)
    new_shift = pb[:, -1]
    r, k, v, wl, al, gl = jnp.split(q, SPLIT_B, axis=-1)
    w = -jax.nn.softplus(-(w0 + jnp.tanh(wl) @ w2)) - 0.5
    a = jax.nn.sigmoid(a0 + al @ a2)
    g = jax.nn.sigmoid(gl) @ g2
    heads = lambda t: t.reshape(bsz, t_len, H_B, HEAD_B)
    kk = heads(k * k_k)
    kk = kk * lax.rsqrt(jnp.maximum(jnp.sum(kk * kk, -1, keepdims=True), 1e-24))
    k = k * (1.0 + (a - 1.0) * k_a)
    decay = jnp.exp(-jnp.exp(w))
    r, k, v, decay, a = (heads(t) for t in (r, k, v, decay, a))
    y, new_wkv = wkv_recurrence(r, k, v, decay, kk, a, wkv_st)
    yc = y - jnp.mean(y, -1, keepdims=True)
    y = yc * lax.rsqrt(jnp.mean(yc * yc, -1, keepdims=True) + GN_EPS)
    y = y.reshape(bsz, t_len, D_B) * lnx_g + lnx_b
    y = y + (jnp.sum(r * k * r_k, -1, keepdims=True) * v).reshape(bsz, t_len, D_B)
    yb = y * g
    out = jnp.concatenate([ya, yb], axis=-1) @ w_out
    return out.astype(x.dtype), new_conv, new_shift, new_wkv


def mixer_c(x, h_re, h_im, w_in, lam_re, lam_im, log_dt, b_re, b_im, c_re, c_im, d_skip, w_out):
    bsz, t_len, _ = x.shape
    f32 = jnp.float32
    u = (x @ w_in).astype(f32)
    ug = u.reshape(bsz, t_len, G_C, C_GROUP)
    lr, li = lam_re.astype(f32), lam_im.astype(f32)
    dt = jnp.exp(log_dt.astype(f32))[:, None]
    mag = jnp.exp(lr * dt)
    ab_re, ab_im = mag * jnp.cos(li * dt), mag * jnp.sin(li * dt)
    den = lr * lr + li * li
    nr, ni = ab_re - 1.0, ab_im
    f_re, f_im = (nr * lr + ni * li) / den, (ni * lr - nr * li) / den
    br, bi = b_re.astype(f32), b_im.astype(f32)
    bb_re = f_re[..., None] * br - f_im[..., None] * bi
    bb_im = f_re[..., None] * bi + f_im[..., None] * br
    bu_re = jnp.einsum('gpc,btgc->tbgp', bb_re, ug)
    bu_im = jnp.einsum('gpc,btgc->tbgp', bb_im, ug)
    h_re, h_im = h_re.astype(f32), h_im.astype(f32)
    bu_re = bu_re.at[0].add(ab_re * h_re - ab_im * h_im)
    bu_im = bu_im.at[0].add(ab_re * h_im + ab_im * h_re)
    a_re = jnp.broadcast_to(ab_re, (t_len,) + ab_re.shape)
    a_im = jnp.broadcast_to(ab_im, (t_len,) + ab_im.shape)

    def combine(e1, e2):
        a1r, a1i, b1r, b1i = e1
        a2r, a2i, b2r, b2i = e2
        ar = a2r * a1r - a2i * a1i
        ai = a2r * a1i + a2i * a1r
        xr = a2r[:, None] * b1r - a2i[:, None] * b1i + b2r
        xi = a2r[:, None] * b1i + a2i[:, None] * b1r + b2i
        return ar, ai, xr, xi

    _, _, xs_re, xs_im = lax.associative_scan(combine, (a_re, a_im, bu_re, bu_im), axis=0)
    y = (jnp.einsum('gcp,tbgp->btgc', c_re.astype(f32), xs_re)
         - jnp.einsum('gcp,tbgp->btgc', c_im.astype(f32), xs_im))
    y = y.reshape(bsz, t_len, D_C) + d_skip * u
    o = jax.nn.gelu(y) @ w_out
    out = o[..., :D_MODEL] * jax.nn.sigmoid(o[..., D_MODEL:])
    return out.astype(x.dtype), xs_re[-1], xs_im[-1]


def conv_ffn(x, st, w_up, conv_w, conv_b, w_down):
    hg = x @ w_up
    h, g = hg[..., :D_FF], hg[..., D_FF:]
    ext = jnp.concatenate([st.astype(h.dtype), h], axis=1)
    c = causal_depthwise(ext, conv_w, conv_b)
    y = (jax.nn.silu(c) * g) @ w_down
    return y.astype(x.dtype), ext[:, -(FFN_CONV_WIDTH - 1):]


def trunk(x, st_conv_a, st_shift, st_wkv, st_re, st_im, st_ffn, ab_params, c_params, ffn_params, norms):
    ln1_g, ln1_b, ln2_g, ln2_b = norms
    n_conv, n_shift, n_wkv, n_re, n_im, n_ffn = [], [], [], [], [], []
    for i in range(DEPTH):
        j = i // 2
        if i % 2 == 0:
            out, nc, ns, nw = mixer_ab(x, st_conv_a[:, j], st_shift[:, j], st_wkv[:, j],
                                       *(p[j] for p in ab_params))
            n_conv.append(nc); n_shift.append(ns); n_wkv.append(nw)
        else:
            out, nr, ni = mixer_c(x, st_re[:, j], st_im[:, j], *(p[j] for p in c_params))
            n_re.append(nr); n_im.append(ni)
        x = layer_norm(ALPHA * x + out, ln1_g[i], ln1_b[i])
        f, nf = conv_ffn(x, st_ffn[:, i], *(p[i] for p in ffn_params))
        n_ffn.append(nf)
        x = layer_norm(ALPHA * x + f, ln2_g[i], ln2_b[i])
    return (x, jnp.stack(n_conv, 1), jnp.stack(n_shift, 1), jnp.stack(n_wkv, 1),
            jnp.stack(n_re, 1), jnp.stack(n_im, 1), jnp.stack(n_ffn, 1))


def setup_inputs(seed: int = 0) -> dict:
    key = jax.random.key(seed)
    keys = iter(jax.random.split(key, 64))
    nrm = lambda shape, scale: scale * jax.random.normal(next(keys), shape, jnp.float32)
    uni = lambda shape, lo, hi: jax.random.uniform(next(keys), shape, jnp.float32, lo, hi)
    d = D_MODEL
    return {
        'x_prompt': nrm((BATCH, SEQ, d), 1.0),
        'x_sample': nrm((DEC_BATCH, DEC_SEQ, d), 1.0),
        'state_conv_a': nrm((DEC_BATCH, N_AB, CONV_A_WIDTH - 1, D_A), 0.5),
        'state_shift_b': nrm((DEC_BATCH, N_AB, D_BP), 1.0),
        'state_wkv_b': nrm((DEC_BATCH, N_AB, H_B, HEAD_B, HEAD_B), 0.3),
        'state_ssm_re': nrm((DEC_BATCH, N_C, G_C, P_C), 0.3),
        'state_ssm_im': nrm((DEC_BATCH, N_C, G_C, P_C), 0.3),
        'state_conv_ffn': nrm((DEC_BATCH, DEPTH, FFN_CONV_WIDTH - 1, D_FF), 1.0),
        'w_in_ab': nrm((N_AB, d, D_IN_AB), d ** -0.5),
        'conv_a_w': nrm((N_AB, CONV_A_WIDTH, D_A), CONV_A_WIDTH ** -0.5),
        'conv_a_b': nrm((N_AB, D_A), 0.02),
        'ln_a_g': 1.0 + nrm((N_AB, D_A), 0.02),
        'ln_a_b': nrm((N_AB, D_A), 0.02),
        'mu_b': uni((N_AB, D_BP), 0.0, 1.0),
        'w0_b': uni((N_AB, D_B), -6.0, 1.0),
        'w2_b': nrm((N_AB, LORA_W, D_B), 0.5 * LORA_W ** -0.5),
        'a0_b': nrm((N_AB, D_B), 0.1),
        'a2_b': nrm((N_AB, LORA_A, D_B), LORA_A ** -0.5),
        'g2_b': nrm((N_AB, LORA_G, D_B), LORA_G ** -0.5),
        'k_k_b': 0.85 + nrm((N_AB, D_B), 0.02),
        'k_a_b': 1.0 + nrm((N_AB, D_B), 0.02),
        'r_k_b': nrm((N_AB, H_B, HEAD_B), 0.1),
        'ln_x_g': 1.0 + nrm((N_AB, D_B), 0.02),
        'ln_x_b': nrm((N_AB, D_B), 0.02),
        'w_out_ab': nrm((N_AB, D_A + D_B, d), BETA * (D_A + D_B) ** -0.5),
        'w_in_c': nrm((N_C, d, D_C), d ** -0.5),
        'lam_re': -0.5 + nrm((N_C, G_C, P_C), 0.01),
        'lam_im': math.pi * jnp.arange(P_C, dtype=jnp.float32) + nrm((N_C, G_C, P_C), 0.01),
        'log_dt': uni((N_C, G_C), math.log(1e-3), math.log(1e-1)),
        'b_re': nrm((N_C, G_C, P_C, C_GROUP), (2 * C_GROUP) ** -0.5),
        'b_im': nrm((N_C, G_C, P_C, C_GROUP), (2 * C_GROUP) ** -0.5),
        'c_re': nrm((N_C, G_C, C_GROUP, P_C), P_C ** -0.5),
        'c_im': nrm((N_C, G_C, C_GROUP, P_C), P_C ** -0.5),
        'd_skip': nrm((N_C, D_C), 1.0),
        'w_out_c': nrm((N_C, D_C, 2 * d), BETA * D_C ** -0.5),
        'w_up': nrm((DEPTH, d, 2 * D_FF), d ** -0.5),
        'conv_f_w': nrm((DEPTH, FFN_CONV_WIDTH, D_FF), FFN_CONV_WIDTH ** -0.5),
        'conv_f_b': nrm((DEPTH, D_FF), 0.02),
        'w_down': nrm((DEPTH, D_FF, d), BETA * D_FF ** -0.5),
        'ln1_g': 1.0 + nrm((DEPTH, d), 0.02),
        'ln1_b': nrm((DEPTH, d), 0.02),
        'ln2_g': 1.0 + nrm((DEPTH, d), 0.02),
        'ln2_b': nrm((DEPTH, d), 0.02),
    }


def reference(x_prompt, x_sample, state_conv_a, state_shift_b, state_wkv_b, state_ssm_re, state_ssm_im,
              state_conv_ffn, w_in_ab, conv_a_w, conv_a_b, ln_a_g, ln_a_b, mu_b, w0_b, w2_b, a0_b, a2_b,
              g2_b, k_k_b, k_a_b, r_k_b, ln_x_g, ln_x_b, w_out_ab, w_in_c, lam_re, lam_im, log_dt,
              b_re, b_im, c_re, c_im, d_skip, w_out_c, w_up, conv_f_w, conv_f_b, w_down,
              ln1_g, ln1_b, ln2_g, ln2_b):
    ab_params = (w_in_ab, conv_a_w, conv_a_b, ln_a_g, ln_a_b, mu_b, w0_b, w2_b, a0_b, a2_b, g2_b,
                 k_k_b, k_a_b, r_k_b, ln_x_g, ln_x_b, w_out_ab)
    c_params = (w_in_c, lam_re, lam_im, log_dt, b_re, b_im, c_re, c_im, d_skip, w_out_c)
    ffn_params = (w_up, conv_f_w, conv_f_b, w_down)
    norms = (ln1_g, ln1_b, ln2_g, ln2_b)
    bp = x_prompt.shape[0]
    zeros_like_state = lambda s: jnp.zeros((bp,) + s.shape[1:], s.dtype)
    (y_prompt, conv_a_p, shift_b_p, wkv_b_p, ssm_re_p, ssm_im_p, conv_ffn_p) = trunk(
        x_prompt, zeros_like_state(state_conv_a), zeros_like_state(state_shift_b),
        zeros_like_state(state_wkv_b), zeros_like_state(state_ssm_re), zeros_like_state(state_ssm_im),
        zeros_like_state(state_conv_ffn), ab_params, c_params, ffn_params, norms)
    (y_sample, conv_a_s, shift_b_s, wkv_b_s, ssm_re_s, ssm_im_s, conv_ffn_s) = trunk(
        x_sample, state_conv_a, state_shift_b, state_wkv_b, state_ssm_re, state_ssm_im, state_conv_ffn,
        ab_params, c_params, ffn_params, norms)
    return (y_prompt, y_sample, conv_a_p, conv_a_s, shift_b_p, shift_b_s, wkv_b_p, wkv_b_s,
            ssm_re_p, ssm_re_s, ssm_im_p, ssm_im_s, conv_ffn_p, conv_ffn_s)
```

```python
import numpy as np
from contextlib import ExitStack
import concourse.bass as bass
import concourse.mybir as mybir
from concourse.bass_utils import run_bass_kernel_spmd

F32 = mybir.dt.float32
BF16 = mybir.dt.bfloat16
AF = mybir.ActivationFunctionType
ALU = mybir.AluOpType
AX = mybir.AxisListType

D = 1024
SEQ = 2048
NSB = 16
TS = 4
NTOK = SEQ + NSB * TS
DEPTH = 4
DFF = 2816
NFF = 22
DBP = 1792
ALPHA = (2 * DEPTH) ** 0.25
LN_EPS = 1e-5
GN_EPS = 64 * 1e-5
NCORES = 8

C_ID = 0
C_MSU = 128
C_MUI = 192
C_MSL = 256
C_BONE = 320
C_BIND = 448
C_CM = 450
C_ONE = 962
C_W = 1090


def make_consts():
    c = np.zeros((128, C_W), np.float32)
    c[:, C_ID:C_ID + 128] = np.eye(128)
    i = np.arange(64)
    c[:64, C_MSU:C_MSU + 64] = (i[:, None] < i[None, :])
    c[:64, C_MUI:C_MUI + 64] = (i[:, None] <= i[None, :])
    c[:64, C_MSL:C_MSL + 64] = (i[:, None] > i[None, :])
    c[:64, C_BONE:C_BONE + 64] = 1.0
    c[64:, C_BONE + 64:C_BONE + 128] = 1.0
    c[:64, C_BIND] = 1.0
    c[64:, C_BIND + 1] = 1.0
    for q in range(4):
        m = np.zeros((128, 128), np.float32)
        m[32 * q:32 * q + 16, 0:64] = 1.0
        m[32 * q + 16:32 * q + 32, 64:128] = 1.0
        c[:, C_CM + 128 * q:C_CM + 128 * (q + 1)] = m
    c[:, C_ONE:C_ONE + 128] = 1.0
    return c


class Buf:
    __slots__ = ("name", "w", "r", "excl")

    def __init__(self, name, excl=False):
        self.name = name
        self.w = None
        self.r = {}
        self.excl = excl


class Sched:
    EPOCH = 30000

    def __init__(self, nc, es):
        self.nc = nc
        self.es = es
        self.engs = {"pe": nc.tensor, "dve": nc.vector, "act": nc.scalar, "pool": nc.gpsimd, "sp": nc.sync}
        self.cnt = {k: 0 for k in self.engs}
        self.esem = {}
        self.waited = {k: {} for k in self.engs}
        self.dq = {"sp": [es.enter_context(nc.semaphore(f"dsp{i}")) for i in range(24)],
                   "pool": [es.enter_context(nc.semaphore(f"dpl{i}")) for i in range(8)]}
        self.dcnt = {q: [0] * len(v) for q, v in self.dq.items()}
        self.dnext = {q: 0 for q in self.dq}
        self.out_tokens = []
        self.skip_same = {"pe": True, "dve": False, "act": False, "pool": False, "sp": True}
        self.ninst = 0
        self.rec = None

    def begin_rec(self):
        self.rec = []

    def end_rec(self):
        r, self.rec = self.rec, None
        return r

    @staticmethod
    def merge(a, b):
        i = j = 0
        while i < len(a) or j < len(b):
            if j >= len(b) or (i < len(a) and i * max(len(b), 1) <= j * max(len(a), 1)):
                f, args = a[i]
                i += 1
            else:
                f, args = b[j]
                j += 1
            f(*args)

    def _semh(self, key):
        if key[0] == "e":
            if key not in self.esem:
                self.esem[key] = self.es.enter_context(self.nc.semaphore(f"e_{key[1]}_{key[2]}"))
            return self.esem[key]
        return self.dq[key[1]][key[2]]

    def _deps(self, k, r, w):
        deps = {}

        def add(t):
            if t is None:
                return
            key, val = t
            if deps.get(key, 0) < val:
                deps[key] = val
        for b in r:
            add(b.w)
            if b.excl:
                for t in b.r.values():
                    add(t)
        for b in w:
            add(b.w)
            for t in b.r.values():
                add(t)
        return deps

    def _wait(self, k, deps):
        eng = self.engs[k]
        for key, val in deps.items():
            if key[0] == "e" and key[1] == k and self.skip_same[k]:
                continue
            if self.waited[k].get(key, 0) >= val:
                continue
            eng.wait_ge(self._semh(key), val)
            self.waited[k][key] = val

    def op(self, k, fn, r=(), w=(), inc=True):
        if self.rec is not None:
            self.rec.append((self._op, (k, fn, list(r), list(w), inc)))
            return None
        return self._op(k, fn, r, w, inc)

    def _op(self, k, fn, r=(), w=(), inc=True):
        self._wait(k, self._deps(k, r, w))
        ins = fn(self.engs[k])
        c = self.cnt[k] + 1
        key = ("e", k, (c - 1) // self.EPOCH)
        tok = (key, (c - 1) % self.EPOCH + 1)
        if inc:
            ins.then_inc(self._semh(key), 1)
            self.cnt[k] = c
        for b in r:
            b.r[k] = tok
        for b in w:
            b.w = tok
            b.r = {}
        self.ninst += 1
        return ins

    def fence(self, k):
        if self.rec is not None:
            self.rec.append((self._fence, (k,)))
            return
        self._fence(k)

    def _fence(self, k):
        c = self.cnt[k]
        if c == 0:
            return
        key = ("e", k, (c - 1) // self.EPOCH)
        self.engs[k].wait_ge(self._semh(key), (c - 1) % self.EPOCH + 1)

    def dve(self, fn, r=(), w=()):
        return self.op("dve", fn, r, w)

    def act(self, fn, r=(), w=()):
        return self.op("act", fn, r, w)

    def pool(self, fn, r=(), w=()):
        return self.op("pool", fn, r, w)

    def pe(self, fn, r=(), w=(), inc=True):
        return self.op("pe", fn, r, w, inc)

    def dma(self, q, out, in_, r=(), w=(), is_out=False):
        if self.rec is not None:
            self.rec.append((self._dma, (q, out, in_, list(r), list(w), is_out)))
            return
        self._dma(q, out, in_, r, w, is_out)

    def _dma(self, q, out, in_, r=(), w=(), is_out=False):
        deps = self._deps(q, r, w)
        idx = self.dnext[q]
        self.dnext[q] = (idx + 1) % len(self.dq[q])
        key = ("d", q, idx)
        if self.dcnt[q][idx] > 0:
            v = 16 * self.dcnt[q][idx]
            if deps.get(key, 0) < v:
                deps[key] = v
        self._wait(q, deps)
        self.engs[q].dma_start(out=out, in_=in_).then_inc(self.dq[q][idx], 16)
        self.dcnt[q][idx] += 1
        tok = (key, 16 * self.dcnt[q][idx])
        for b in r:
            b.r[key] = tok
        for b in w:
            b.w = tok
            b.r = {}
        if is_out:
            self.out_tokens.append(tok)
        self.ninst += 1

    def finish(self):
        deps = {}
        for key, val in self.out_tokens:
            if deps.get(key, 0) < val:
                deps[key] = val
        self._wait("sp", deps)


def build(cfg):
    depth = cfg.get("depth", DEPTH)
    nc = bass.Bass("TRN2", target_bir_lowering=False)
    es = ExitStack()
    with es:
        _build(nc, es, cfg, depth)
    return nc


def _build(nc, es, cfg, depth):
    S = Sched(nc, es)

    def din(name, shape):
        return nc.dram_tensor(name, list(shape), F32, kind="ExternalInput").ap()

    def dout(name, shape):
        return nc.dram_tensor(name, list(shape), F32, kind="ExternalOutput").ap()

    def sb(name, shape, dt=F32):
        return es.enter_context(nc.sbuf_tensor(name, list(shape), dt))

    xp = din("xp", [SEQ, D])
    xs = din("xs", [NSB * TS, D])
    st_conv_a = din("st_conv_a", [NSB, 2, 30, 512])
    st_shift = din("st_shift", [NSB, 2, DBP])
    st_wkv = din("st_wkv", [NSB, 2, 8, 64, 64])
    st_re = din("st_re", [NSB, 2, 64, 64])
    st_im = din("st_im", [NSB, 2, 64, 64])
    st_ffn = din("st_ffn", [NSB, 4, 2, DFF])
    cst = din("cst", [128, C_W])
    W = {}
    for name, shape in WSHAPES:
        W[name] = din(name, shape)
    o_yp = dout("o_yp", [SEQ, D])
    o_ys = dout("o_ys", [NSB * TS, D])
    o_conv_p = dout("o_conv_p", [2, 30, 512])
    o_conv_s = dout("o_conv_s", [NSB, 2, 30, 512])
    o_shift_p = dout("o_shift_p", [2, DBP])
    o_shift_s = dout("o_shift_s", [NSB, 2, DBP])
    o_wkv_p = dout("o_wkv_p", [2, 8, 64, 64])
    o_wkv_s = dout("o_wkv_s", [NSB, 2, 8, 64, 64])
    o_re_p = dout("o_re_p", [2, 64, 64])
    o_re_s = dout("o_re_s", [NSB, 2, 64, 64])
    o_im_p = dout("o_im_p", [2, 64, 64])
    o_im_s = dout("o_im_s", [NSB, 2, 64, 64])
    o_ffn_p = dout("o_ffn_p", [4, 2, DFF])
    o_ffn_s = dout("o_ffn_s", [NSB, 4, 2, DFF])
    o_dbg = dout("o_dbg", [128, 2048]) if cfg.get("dbg") else None
    o_dbgB = dout("o_dbgB", [128, 9216]) if cfg.get("dbgr") else None
    o_dbgZ = dout("o_dbgZ", [128, 4096]) if cfg.get("dbgr") else None
    o_dbgS = dout("o_dbgS", [128, 256]) if cfg.get("dbgr") else None

    X = sb("X", [128, 8, 512])
    _bx = Buf("X")
    bX = [_bx] * 5
    XB = sb("XB", [128, 8, 512], BF16)
    bXB = Buf("XB")
    CST = sb("CST", [128, C_W])
    bCST = Buf("CST")
    IDB = None
    PAR = sb("PAR", [128, 512])
    bPAR = Buf("PAR")
    NWB = 2
    WB = [sb(f"WB{i}", [128, 22 * 128], BF16) for i in range(NWB)]
    bWB = [Buf(f"WB{i}") for i in range(NWB)]
    wb_i = [0]
    wb_cnt = [0]
    ZT = sb("ZT", [128, 8, 512])
    bZT = Buf("ZT")
    LNS = sb("LNS", [128, 4, 512])
    bLNS = [Buf(f"LNS{i}") for i in range(4)]
    BIG = sb("BIG", [128, 9, 1024])
    bBIG = [Buf(f"BIG{i}") for i in range(9)]
    QA = sb("QA", [128, 14 * 512])
    bQA = Buf("QA")
    QT = QA[:, :].rearrange("p (k n) -> p k n", k=14)
    bQT = bQA
    AT = QA[:, :].bitcast(BF16)[:, 0:NFF * 512].rearrange("p (k n) -> p k n", k=NFF)
    bAT = bQA
    ROWS = sb("ROWS", [128, 1024])
    bROWS = [Buf("ROWS0"), Buf("ROWS1")]
    ROWS2 = sb("ROWS2", [128, 1024])
    rows_t = [ROWS, ROWS2]
    rows_i = [0]
    FH = sb("FH", [128, DEPTH, NFF, 2])
    bFH = Buf("FH")
    HEXT = BIG[:, 0:2, 0:520]
    bHEXT = [bBIG[0], bBIG[1]]
    CV = BIG[:, 2:4, 0:512]
    bCV = [bBIG[2], bBIG[3]]
    SFH = BIG[:, 4, 0:NFF * 32].rearrange("p (c t) -> p c t", c=NFF)
    bSFH = bBIG[4]
    SEL = BIG[:, 5, 0:NFF * 32].rearrange("p (c t) -> p c t", c=NFF)
    bSEL = bBIG[5]

    PS = [es.enter_context(nc.psum_tensor(f"ps{i}", [128, 512], F32)) for i in range(8)]
    bPS = [Buf(f"ps{i}", excl=True) for i in range(8)]
    ps_i = [0]

    ps_pool = [None]
    ps_pi = [0, 0]

    def psum():
        if ps_pool[0] is not None:
            q = ps_pool[0]
            i = 4 * q + ps_pi[q]
            ps_pi[q] = (ps_pi[q] + 1) % 4
            return PS[i], bPS[i]
        i = ps_i[0]
        ps_i[0] = (i + 1) % 8
        return PS[i], bPS[i]

    def ident(n=128):
        return CST[0:n, C_ID:C_ID + n]

    S.dma("sp", CST[:], cst[:, :], w=[bCST])

    TILES = [(i * 512, 512, 1, 512) for i in range(4)] + [(2048, 64, NSB, TS)]

    def transpose_to(out_ps, in_ap, nrows, r, w):
        S.pe(lambda e: e.transpose(out_ps, in_ap, ident(nrows)), r=r + [bCST], w=w)

    def load_cols(rows_list, col0, dest=None, dbuf=None):
        i = rows_i[0]
        rows_i[0] ^= 1
        rt, rb = rows_t[i], bROWS[i]
        r0 = 0
        for ap in rows_list:
            n = ap.shape[0]
            S.dma("sp", rt[r0:r0 + n, 0:128], ap, w=[rb])
            r0 += n
        pt, pb = psum()
        transpose_to(pt[:, 0:r0], rt[0:r0, 0:128], r0, [rb], [pb])
        if dest is None:
            dest, dbuf = PAR, bPAR
        S.act(lambda e: e.copy(out=dest[:, col0:col0 + r0], in_=pt[:, 0:r0]), r=[pb], w=[dbuf])
        return r0

    wcache = {}

    def wslab(wd, k0, kt, f0, nf, key=None):
        i = wb_i[0]
        wb_i[0] = (i + 1) % NWB
        t, b = WB[i], bWB[i]
        v = t[:, 0:kt * nf].rearrange("p (k f) -> p k f", k=kt)
        if cfg.get("nodma") and wb_cnt[0] >= NWB:
            return v, b
        wb_cnt[0] += 1
        ck = (key, k0, kt, f0, nf)
        if key is not None and cfg.get("wcache", True) and ck in wcache:
            sc, scb = wcache[ck]
            S.dma("sp", t[:, 0:kt * nf], sc[:, :], r=[scb], w=[b])
            return v, b
        S.dma("pool", v, wd[k0:k0 + kt * 128, f0:f0 + nf].rearrange("(k p) f -> p k f", p=128), w=[b])
        if key is not None and cfg.get("wcache", True) and len(cfg.get("tiles", [0, 1, 2, 3, 4])) > 1:
            sc = nc.dram_tensor(f"wc{len(wcache)}", [128, kt * nf], BF16, kind="Internal").ap()
            scb = Buf(f"wc{len(wcache)}")
            wcache[ck] = (sc, scb)
            S.dma("sp", sc[:, :], t[:, 0:kt * nf], r=[b], w=[scb])
        return v, b

    def cast_xb(ti):
        c0, n, _, _ = TILES[ti]
        S.act(lambda e: e.copy(out=XB[:, :, 0:n], in_=X[:, :, 0:n]), r=[bX[ti]], w=[bXB])

    def layer_norm(zv, zb, nk, n, gcol, bcol, out_fn, eps=LN_EPS, ptile=None):
        inv = 1.0 / (nk * 128)
        if ptile is None:
            ptile = PAR
        zb = zb if isinstance(zb, list) else [zb]
        ones = CST[:, C_ONE:C_ONE + 128]
        pm, pmb = psum()
        for k in range(nk):
            S.pe(lambda e, k=k: e.matmul(pm[:, 0:n], ones, zv(k), start=(k == 0), stop=(k == nk - 1)),
                 r=zb + [bCST], w=[pmb], inc=(k == nk - 1))
        pq, pqb = psum()
        for k in range(nk):
            sq = LNS[:, 0, 0:n]
            S.act(lambda e, k=k: e.activation(out=sq, in_=zv(k), func=AF.Square), r=zb, w=[bLNS[0]])
            S.pe(lambda e, k=k: e.matmul(pq[:, 0:n], ones, sq, start=(k == 0), stop=(k == nk - 1)),
                 r=[bLNS[0], bCST], w=[pqb])
        mean = LNS[:, 1, 0:n]
        rstd = LNS[:, 2, 0:n]
        S.act(lambda e: e.mul(out=mean, in_=pm[:, 0:n], mul=inv), r=[pmb], w=[bLNS[1]])
        S.dve(lambda e: e.tensor_tensor(out=rstd, in0=mean, in1=mean, op=ALU.mult), r=[bLNS[1]], w=[bLNS[2]])
        S.dve(lambda e: e.scalar_tensor_tensor(out=rstd, in0=pq[:, 0:n], scalar=inv, in1=rstd,
                                               op0=ALU.mult, op1=ALU.subtract), r=[pqb, bLNS[2]], w=[bLNS[2]])
        S.dve(lambda e: e.tensor_scalar(out=rstd, in0=rstd, scalar1=eps, scalar2=None, op0=ALU.add),
              r=[bLNS[2]], w=[bLNS[2]])
        S.act(lambda e: e.activation(out=rstd, in_=rstd, func=AF.Sqrt), r=[bLNS[2]], w=[bLNS[2]])
        S.dve(lambda e: e.reciprocal(out=rstd, in_=rstd), r=[bLNS[2]], w=[bLNS[2]])
        for k in range(nk):
            tmp = LNS[:, 3, 0:n]
            S.dve(lambda e, k=k: e.tensor_tensor(out=tmp, in0=zv(k), in1=mean, op=ALU.subtract),
                  r=zb + [bLNS[1]], w=[bLNS[3]])
            S.dve(lambda e: e.tensor_tensor(out=tmp, in0=tmp, in1=rstd, op=ALU.mult),
                  r=[bLNS[3], bLNS[2]], w=[bLNS[3]])
            out_fn(k, tmp, bLNS[3], ptile[:, gcol + k:gcol + k + 1], ptile[:, bcol + k:bcol + k + 1])

    def load_x(ti):
        c0, n, _, _ = TILES[ti]
        for tb in range((n + 127) // 128):
            src = xp[c0 + tb * 128:c0 + (tb + 1) * 128, :] if ti < 4 else xs[:, :]
            nr = min(128, n)
            i = rows_i[0]
            rows_i[0] ^= 1
            rt, rb = rows_t[i], bROWS[i]
            S.dma("sp", rt[0:nr, :], src, w=[rb])
            for half in range(2):
                pt, pb = psum()
                for kk in range(4):
                    k = half * 4 + kk
                    S.pe(lambda e, k=k, kk=kk: e.transpose(pt[:, kk * 128:kk * 128 + nr],
                                                           rt[0:nr, k * 128:(k + 1) * 128], ident(nr)),
                         r=[rb, bCST], w=[pb])
                S.act(lambda e, half=half: e.copy(
                    out=X[:, half * 4:half * 4 + 4, tb * 128:tb * 128 + nr],
                    in_=pt[:, :].rearrange("p (k t) -> p k t", k=4)[:, :, 0:nr]), r=[pb], w=[bX[ti]])

    def store_y(ti):
        c0, n, _, _ = TILES[ti]
        for tb in range((n + 127) // 128):
            dst = o_yp[c0 + tb * 128:c0 + (tb + 1) * 128, :] if ti < 4 else o_ys[:, :]
            nr = min(128, n)
            i = rows_i[0]
            rows_i[0] ^= 1
            rt, rb = rows_t[i], bROWS[i]
            for half in range(2):
                pt, pb = psum()
                for kk in range(4):
                    k = half * 4 + kk
                    S.pe(lambda e, k=k, kk=kk: e.transpose(pt[0:nr, kk * 128:(kk + 1) * 128],
                                                           X[:, k, tb * 128:tb * 128 + nr], ident(128)),
                         r=[bX[ti], bCST], w=[pb])
                S.act(lambda e, half=half: e.copy(out=rt[0:nr, half * 512:(half + 1) * 512], in_=pt[0:nr, :]),
                      r=[pb], w=[rb])
            S.dma("sp", dst, rt[0:nr, :], r=[rb], is_out=True)

    def emit_rows(src_fn, nrows, nchunks, dst_fn, r):
        done = 0
        while done < nchunks:
            g = min(8, nchunks - done)
            i = rows_i[0]
            rows_i[0] ^= 1
            rt, rb = rows_t[i], bROWS[i]
            for h0 in range(0, g, 4):
                pt, pb = psum()
                hn = min(4, g - h0)
                for kk in range(hn):
                    c = done + h0 + kk
                    S.pe(lambda e, c=c, kk=kk: e.transpose(pt[0:nrows, kk * 128:(kk + 1) * 128],
                                                           src_fn(c), ident(128)), r=r + [bCST], w=[pb])
                S.act(lambda e, h0=h0, hn=hn: e.copy(out=rt[0:nrows, h0 * 128:(h0 + hn) * 128],
                                                      in_=pt[0:nrows, 0:hn * 128]), r=[pb], w=[rb])
            dst_fn(rt, rb, done, g)
            done += g

    def ffn_params(l):
        load_cols([W["conv_f_w"][l].rearrange("j (c p) -> (j c) p", p=128)], 0)
        load_cols([W["conv_f_b"][l].rearrange("(c p) -> c p", p=128),
                   W["ln2_g"][l].rearrange("(c p) -> c p", p=128),
                   W["ln2_b"][l].rearrange("(c p) -> c p", p=128),
                   W["ln1_g"][l].rearrange("(c p) -> c p", p=128),
                   W["ln1_b"][l].rearrange("(c p) -> c p", p=128)], 66)

    def ffn_sample_state(l):
        i = rows_i[0]
        for c0 in range(0, NFF, 8):
            g = min(8, NFF - c0)
            i = rows_i[0]
            rows_i[0] ^= 1
            rt, rb = rows_t[i], bROWS[i]
            for b in range(NSB):
                S.dma("sp", rt[2 * b:2 * b + 2, 0:g * 128], st_ffn[b, l, :, c0 * 128:(c0 + g) * 128], w=[rb])
            for h0 in range(0, g, 4):
                hn = min(4, g - h0)
                pt, pb = psum()
                for kk in range(hn):
                    S.pe(lambda e, kk=kk, h0=h0: e.transpose(pt[:, kk * 32:(kk + 1) * 32],
                                                             rt[0:32, (h0 + kk) * 128:(h0 + kk + 1) * 128], ident(32)),
                         r=[rb, bCST], w=[pb])
                S.act(lambda e, h0=h0, hn=hn, c0=c0: e.copy(
                    out=SFH[:, c0 + h0:c0 + h0 + hn, :],
                    in_=pt[:, 0:hn * 32].rearrange("p (k t) -> p k t", k=hn)), r=[pb], w=[bSFH])

    def ffn_tile(l, ti):
        c0, n, nseq, T = TILES[ti]
        E = T + 2
        cast_xb(ti)
        wup = W["w_up"][l]
        for g0 in range(0, NFF, 2):
            gn = min(2, NFF - g0)
            wv, wvb = wslab(wup, 0, 8, g0 * 128, gn * 128, key=("up", l))
            wg, wgb = wslab(wup, 0, 8, DFF + g0 * 128, gn * 128, key=("up", l))
            for j in range(gn):
                hc = g0 + j
                ph, phb = psum()
                for k in range(8):
                    S.pe(lambda e, k=k, j=j: e.matmul(ph[:, 0:n], wv[:, k, j * 128:(j + 1) * 128], XB[:, k, 0:n],
                                                      start=(k == 0), stop=(k == 7)),
                         r=[wvb, bXB], w=[phb], inc=(k == 7))
                pg, pgb = psum()
                for k in range(8):
                    S.pe(lambda e, k=k, j=j: e.matmul(pg[:, 0:n], wg[:, k, j * 128:(j + 1) * 128], XB[:, k, 0:n],
                                                      start=(k == 0), stop=(k == 7)),
                         r=[wgb, bXB], w=[pgb], inc=(k == 7))
                hi = hc % 2
                hx = HEXT[:, hi, 0:nseq * E].rearrange("p (s e) -> p s e", s=nseq)
                hb = bHEXT[hi]
                if nseq == 1:
                    if ti == 0:
                        S.dve(lambda e: e.memset(hx[:, :, 0:2], 0.0), w=[hb])
                    else:
                        S.dve(lambda e, hc=hc: e.tensor_copy(out=hx[:, 0, 0:2], in_=FH[:, l, hc, :]), r=[bFH], w=[hb])
                else:
                    S.dve(lambda e, hc=hc: e.tensor_copy(
                        out=hx[:, :, 0:2], in_=SFH[:, hc, :].rearrange("p (b r) -> p b r", r=2)), r=[bSFH], w=[hb])
                S.act(lambda e: e.copy(out=hx[:, :, 2:E], in_=ph[:, 0:n].rearrange("p (s t) -> p s t", s=nseq)),
                      r=[phb], w=[hb])
                cv = CV[:, hi, 0:n].rearrange("p (s t) -> p s t", s=nseq)
                cb = bCV[hi]
                S.dve(lambda e, hc=hc: e.tensor_scalar(out=cv, in0=hx[:, :, 0:T], scalar1=PAR[:, hc:hc + 1],
                                                       scalar2=PAR[:, 66 + hc:67 + hc], op0=ALU.mult, op1=ALU.add),
                      r=[hb, bPAR], w=[cb])
                for j2 in (1, 2):
                    S.dve(lambda e, hc=hc, j2=j2: e.scalar_tensor_tensor(
                        out=cv, in0=hx[:, :, j2:j2 + T], scalar=PAR[:, 22 * j2 + hc:22 * j2 + hc + 1], in1=cv,
                        op0=ALU.mult, op1=ALU.add), r=[hb, bPAR, cb], w=[cb])
                if nseq == 1:
                    S.pool(lambda e, hc=hc: e.tensor_copy(out=FH[:, l, hc, :], in_=hx[:, 0, T:T + 2]), r=[hb], w=[bFH])
                else:
                    S.pool(lambda e, hc=hc: e.tensor_copy(
                        out=SEL[:, hc, :].rearrange("p (r b) -> p r b", r=2),
                        in_=hx[:, :, T:T + 2].rearrange("p b r -> p r b")), r=[hb], w=[bSEL])
                S.act(lambda e: e.activation(out=CV[:, hi, 0:n], in_=CV[:, hi, 0:n], func=AF.Silu), r=[cb], w=[cb])
                S.dve(lambda e, hc=hc: e.tensor_tensor(out=AT[:, hc, 0:n], in0=CV[:, hi, 0:n], in1=pg[:, 0:n],
                                                       op=ALU.mult), r=[cb, pgb], w=[bAT])
        if ti == 3:
            def dst(rt, rb, done, g):
                S.dma("sp", o_ffn_p[l, :, done * 128:(done + g) * 128], rt[0:2, 0:g * 128], r=[rb], is_out=True)
            emit_rows(lambda c: FH[:, l, c, :], 2, NFF, dst, [bFH])
        if ti == 4:
            def dst(rt, rb, done, g):
                for r_ in range(2):
                    S.dma("sp", o_ffn_s[:, l, r_, done * 128:(done + g) * 128],
                          rt[16 * r_:16 * r_ + 16, 0:g * 128], r=[rb], is_out=True)
            emit_rows(lambda c: SEL[:, c, :], 32, NFF, dst, [bSEL])
        wdn = W["w_down"][l]
        for o0 in range(0, 8):
            wd_, wdb = wslab(wdn, 0, NFF, o0 * 128, 128, key=("dn", l))
            for j in range(1):
                oc = o0 + j
                pf, pfb = psum()
                for k in range(NFF):
                    S.pe(lambda e, k=k, j=j: e.matmul(pf[:, 0:n], wd_[:, k, j * 128:(j + 1) * 128], AT[:, k, 0:n],
                                                      start=(k == 0), stop=(k == NFF - 1)),
                         r=[wdb, bAT], w=[pfb], inc=(k == NFF - 1))
                S.dve(lambda e, oc=oc: e.scalar_tensor_tensor(out=ZT[:, oc, 0:n], in0=X[:, oc, 0:n], scalar=ALPHA,
                                                              in1=pf[:, 0:n], op0=ALU.mult, op1=ALU.add),
                      r=[bX[ti], pfb], w=[bZT])

        def outf(k, tmp, tb, g, b):
            S.act(lambda e: e.activation(out=X[:, k, 0:n], in_=tmp, func=AF.Identity, scale=g, bias=b),
                  r=[tb, bPAR], w=[bX[ti]])
        layer_norm(lambda k: ZT[:, k, 0:n], bZT, 8, n, 88, 96, outf)

    S5P = sb("S5P", [128, 24, 32])
    bS5P = Buf("S5P")
    BL = sb("BL", [128, 32, 2, 128], BF16)
    bBL = Buf("BL")
    CL = sb("CL", [128, 32, 2, 128], BF16)
    bCL = Buf("CL")
    HCS = sb("HCS", [128, 2, 2, 32])
    bHCS = Buf("HCS")
    BBX = sb("BBX", [128, 8, 128])
    bBBX = [Buf(f"BBX{i}") for i in range(8)]
    S.dve(lambda e: e.memset(HCS[:, :, :, :], 0.0), w=[bHCS])
    S.dve(lambda e: e.memset(BBX[:, :, :], 0.0), w=bBBX)
    P_LR, P_LI, P_DT, P_MAG, P_ANG, P_C, P_S, P_T0, P_T1, P_T2, P_FR, P_FI, P_AR0, P_AR1, P_AI, P_NAI, P_DEN = range(17)
    P_DSK = 17

    def s5_setup(j):
        alias_claim(0)
        sp = lambda i: S5P[:, i, :]
        dbg = cfg.get("s5stop", 99)
        load_cols([W["lam_re"][j].rearrange("(s g) p -> s (g p)", g=2)], 0, S5P[:, P_LR, :], bS5P)
        load_cols([W["lam_im"][j].rearrange("(s g) p -> s (g p)", g=2)], 0, S5P[:, P_LI, :], bS5P)
        load_cols([W["d_skip"][j].rearrange("(c p) -> c p", p=128)], 0, S5P[:, P_DSK, :], bS5P)
        i = rows_i[0]
        rows_i[0] ^= 1
        rt, rb = rows_t[i], bROWS[i]
        S.dma("sp", rt[0:32, 512:514], W["log_dt"][j].rearrange("(s g) -> s g", g=2), w=[rb])
        S.dve(lambda e: e.tensor_copy(out=rt[0:32, 0:128].rearrange("s (g p) -> s g p", g=2),
                                      in_=rt[0:32, 512:514].unsqueeze(2).to_broadcast([32, 2, 64])), r=[rb], w=[rb])
        pt, pb = psum()
        transpose_to(pt[:, 0:32], rt[0:32, 0:128], 32, [rb], [pb])
        S.act(lambda e: e.activation(out=sp(P_DT), in_=pt[:, 0:32], func=AF.Exp), r=[pb], w=[bS5P])
        R = [bS5P]
        tt = lambda o, a, b, op: S.dve(lambda e: e.tensor_tensor(out=sp(o), in0=sp(a), in1=sp(b), op=op), r=R, w=R)
        def ts(o, a, s1, s2, op0, op1=None):
            if s2 is None:
                S.dve(lambda e: e.tensor_scalar(out=sp(o), in0=sp(a), scalar1=s1, scalar2=None, op0=op0), r=R, w=R)
            else:
                S.dve(lambda e: e.tensor_scalar(out=sp(o), in0=sp(a), scalar1=s1, scalar2=s2, op0=op0, op1=op1), r=R, w=R)
        tt(P_MAG, P_LR, P_DT, ALU.mult)
        S.act(lambda e: e.activation(out=sp(P_MAG), in_=sp(P_MAG), func=AF.Exp), r=R, w=R)
        tt(P_ANG, P_LI, P_DT, ALU.mult)
        S.act(lambda e: e.activation(out=sp(P_T0), in_=sp(P_ANG), func=AF.Sin, scale=1.0 / 64), r=R, w=R)
        S.act(lambda e: e.activation(out=sp(P_S), in_=sp(P_ANG), func=AF.Sin, scale=1.0 / 32), r=R, w=R)
        tt(P_T0, P_T0, P_T0, ALU.mult)
        ts(P_C, P_T0, -2.0, 1.0, ALU.mult, ALU.add)
        for _ in range(5):
            tt(P_T0, P_C, P_C, ALU.mult)
            tt(P_T1, P_S, P_S, ALU.mult)
            tt(P_T2, P_C, P_S, ALU.mult)
            tt(P_C, P_T0, P_T1, ALU.subtract)
            ts(P_S, P_T2, 2.0, None, ALU.mult)
        tt(P_AR0, P_MAG, P_C, ALU.mult)
        S.dve(lambda e: e.tensor_copy(out=sp(P_AR1), in_=sp(P_AR0)), r=R, w=R)
        tt(P_AI, P_MAG, P_S, ALU.mult)
        ts(P_NAI, P_AI, -1.0, None, ALU.mult)
        tt(P_T0, P_LR, P_LR, ALU.mult)
        tt(P_T1, P_LI, P_LI, ALU.mult)
        tt(P_DEN, P_T0, P_T1, ALU.add)
        S.dve(lambda e: e.reciprocal(out=sp(P_DEN), in_=sp(P_DEN)), r=R, w=R)
        ts(P_T2, P_AR0, -1.0, None, ALU.add)
        tt(P_T0, P_T2, P_LR, ALU.mult)
        tt(P_T1, P_AI, P_LI, ALU.mult)
        tt(P_T0, P_T0, P_T1, ALU.add)
        tt(P_FR, P_T0, P_DEN, ALU.mult)
        tt(P_T0, P_AI, P_LR, ALU.mult)
        tt(P_T1, P_T2, P_LI, ALU.mult)
        tt(P_T0, P_T0, P_T1, ALU.subtract)
        tt(P_FI, P_T0, P_DEN, ALU.mult)
        if o_dbg is not None:
            S.dma("sp", o_dbg[:, 0:768], S5P[:, :, :].rearrange("p a b -> p (a b)"), r=[bS5P], is_out=True)
        BR = BIG[:, 0, 0:512].rearrange("p (s c) -> p s c", c=16)
        BI = BIG[:, 1, 0:512].rearrange("p (s c) -> p s c", c=16)
        BBR = BIG[:, 2, 0:512].rearrange("p (s c) -> p s c", c=16)
        BBI = BIG[:, 3, 0:512].rearrange("p (s c) -> p s c", c=16)
        TMP = BIG[:, 4, 0:512].rearrange("p (s c) -> p s c", c=16)
        for g2 in range(2):
            S.dma("sp", BR[64 * g2:64 * g2 + 64, :, :],
                  W["b_re"][j].rearrange("(s g) p c -> g p s c", g=2)[g2], w=[bBIG[0]])
            S.dma("sp", BI[64 * g2:64 * g2 + 64, :, :],
                  W["b_im"][j].rearrange("(s g) p c -> g p s c", g=2)[g2], w=[bBIG[1]])
        fr = S5P[:, P_FR, :].unsqueeze(2).to_broadcast([128, 32, 16])
        fi = S5P[:, P_FI, :].unsqueeze(2).to_broadcast([128, 32, 16])
        S.dve(lambda e: e.tensor_tensor(out=BBR, in0=BR, in1=fr, op=ALU.mult), r=[bBIG[0], bS5P], w=[bBIG[2]])
        S.dve(lambda e: e.tensor_tensor(out=TMP, in0=BI, in1=fi, op=ALU.mult), r=[bBIG[1], bS5P], w=[bBIG[4]])
        S.dve(lambda e: e.tensor_tensor(out=BBR, in0=BBR, in1=TMP, op=ALU.subtract), r=[bBIG[2], bBIG[4]], w=[bBIG[2]])
        S.dve(lambda e: e.tensor_tensor(out=BBI, in0=BI, in1=fr, op=ALU.mult), r=[bBIG[1], bS5P], w=[bBIG[3]])
        S.dve(lambda e: e.tensor_tensor(out=TMP, in0=BR, in1=fi, op=ALU.mult), r=[bBIG[0], bS5P], w=[bBIG[4]])
        S.dve(lambda e: e.tensor_tensor(out=BBI, in0=BBI, in1=TMP, op=ALU.add), r=[bBIG[3], bBIG[4]], w=[bBIG[3]])
        for s_ in range(32):
            q = s_ % 4
            for ri, src, sbuf_ in ((0, BBR, bBIG[2]), (1, BBI, bBIG[3])):
                bx = BBX[:, ri * 4 + q, :]
                bb = bBBX[ri * 4 + q]
                for g2 in range(2):
                    S.dve(lambda e, g2=g2, src=src, bx=bx: e.tensor_copy(
                        out=bx[64 * g2:64 * g2 + 64, 32 * q + 16 * g2:32 * q + 16 * g2 + 16],
                        in_=src[64 * g2:64 * g2 + 64, s_, :]), r=[sbuf_], w=[bb])
                pt, pb = psum()
                transpose_to(pt[:, 0:128], bx, 128, [bb], [pb])
                S.act(lambda e, ri=ri: e.copy(out=BL[:, s_, ri, :], in_=pt[:, 0:128]), r=[pb], w=[bBL])
        CR = BIG[:, 5, 0:512].rearrange("p (t q) -> p t q", q=64)
        CI = BIG[:, 6, 0:512].rearrange("p (t q) -> p t q", q=64)
        S.dma("sp", CR, W["c_re"][j].rearrange("(t g) c p -> (g c) t p", g=8), w=[bBIG[5]])
        S.dma("sp", CI, W["c_im"][j].rearrange("(t g) c p -> (g c) t p", g=8), w=[bBIG[6]])
        CX = BIG[:, 7, 0:256].rearrange("p (r x) -> p r x", r=2)
        for s_ in range(32):
            q, ct = s_ % 4, s_ // 4
            mk = CST[:, C_CM + 128 * q:C_CM + 128 * (q + 1)]
            for ri, src, sbuf_ in ((0, CR, bBIG[5]), (1, CI, bBIG[6])):
                cx = CX[:, ri, :]
                S.dve(lambda e, src=src, cx=cx: e.tensor_tensor(
                    out=cx.rearrange("p (g q) -> p g q", g=2),
                    in0=src[:, ct, :].unsqueeze(1).to_broadcast([128, 2, 64]),
                    in1=mk.rearrange("p (g q) -> p g q", g=2), op=ALU.mult), r=[sbuf_, bCST], w=[bBIG[7]])
                pt, pb = psum()
                transpose_to(pt[:, 0:128], cx, 128, [bBIG[7]], [pb])
                S.act(lambda e, ri=ri: e.mul(out=CL[:, s_, ri, :], in_=pt[:, 0:128], mul=(1.0 if ri == 0 else -1.0)),
                      r=[pb], w=[bCL])

    def s5_sample_state(j):
        S0 = BIG[:, 8, :].rearrange("p (r s b) -> p r s b", r=2, s=32)
        for ri, st in ((0, st_re), (1, st_im)):
            for q in range(4):
                i = rows_i[0]
                rows_i[0] ^= 1
                rt, rb = rows_t[i], bROWS[i]
                for b4 in range(4):
                    S.dma("sp", rt[32 * b4:32 * b4 + 32, 0:128],
                          st[4 * q + b4, j].rearrange("(s g) p -> s (g p)", g=2), w=[rb])
                pt, pb = psum()
                transpose_to(pt[:, 0:128], rt[:, 0:128], 128, [rb], [pb])
                S.act(lambda e, ri=ri, q=q: e.copy(
                    out=S0[:, ri, :, 4 * q:4 * q + 4].rearrange("p s b -> p b s"),
                    in_=pt[:, 0:128].rearrange("p (b s) -> p b s", b=4)), r=[pb], w=[bBIG[8]])

    def mixer_c(l, ti):
        j = l // 2
        c0, n, nseq, T = TILES[ti]
        U = BIG[:, 0:4, :].rearrange("p a (b n) -> p (a b) n", b=2)
        bU = bBIG[0:4]
        UBf = [BIG[:, 4 + i, :].bitcast(BF16) for i in range(2)]
        GYf = [BIG[:, 6 + i, :].bitcast(BF16) for i in range(2)]
        UB = lambda k: UBf[k // 4][:, (k % 4) * 512:(k % 4) * 512 + 512]
        GY = lambda k: GYf[k // 4][:, (k % 4) * 512:(k % 4) * 512 + 512]
        bUB = bBIG[4:6]
        bGY = bBIG[6:8]
        S0 = BIG[:, 8, :].rearrange("p (r s b) -> p r s b", r=2, s=32)
        cast_xb(ti)
        for oc in range(8):
            wv, wvb = wslab(W["w_in_c"][j], 0, 8, oc * 128, 128, key=("inc", j))
            pu, pub = psum()
            for k in range(8):
                S.pe(lambda e, k=k: e.matmul(pu[:, 0:n], wv[:, k, :], XB[:, k, 0:n], start=(k == 0), stop=(k == 7)),
                     r=[wvb, bXB], w=[pub], inc=(k == 7))
            S.act(lambda e, oc=oc: e.copy(out=U[:, oc, 0:n], in_=pu[:, 0:n]), r=[pub], w=bU)
            S.dve(lambda e, oc=oc: e.tensor_copy(out=UB(oc)[:, 0:n], in_=pu[:, 0:n]), r=[pub], w=bUB)
        cstop = cfg.get("cstop", 99)
        if cstop <= 1:
            return mixer_none(l, ti)
        qv = QA[:, :]
        BU = qv[:, 0:2048].rearrange("p (r s t) -> p r s t", r=2, s=32)
        HALL = qv[:, 2048:4096].rearrange("p (r s t) -> p r s t", r=2, s=32)
        HB = qv[:, 4096:5120].bitcast(BF16).rearrange("p (r s t) -> p r s t", r=2, s=32)
        T1f = qv[:, 5120:6144]
        T2f = qv[:, 6144:7168]
        bQ = [bQA]
        nsub = n // 32
        for st_i in range(nsub):
            cs = st_i * 32
            for ri in range(2):
                for h in range(2):
                    pt, pb = psum()
                    for s16 in range(16):
                        s_ = h * 16 + s16
                        S.pe(lambda e, s_=s_, s16=s16, ri=ri: e.matmul(
                            pt[:, s16 * 32:(s16 + 1) * 32], BL[:, s_, ri, :], UB(s_ // 4)[:, cs:cs + 32],
                            start=True, stop=True), r=[bBL] + bUB, w=[pb], inc=(s16 == 15))
                    S.act(lambda e, ri=ri, h=h: e.copy(
                        out=BU[:, ri, 16 * h:16 * h + 16, :], in_=pt[:, :].rearrange("p (s t) -> p s t", s=16)),
                        r=[pb], w=bQ)
            if cstop <= 2:
                continue
            if nseq == 1:
                nb, nt = 1, 32
                shp = [128, 2, 32]
                bu_t = lambda t: BU[:, :, :, t]
                h_t = lambda t: HALL[:, :, :, t]
                prev0 = HCS[:, j, :, :]
                prev0b = [bHCS]
                T1 = T1f[:, 0:64].rearrange("p (r s) -> p r s", r=2)
                T2 = T2f[:, 0:64].rearrange("p (r s) -> p r s", r=2)
                arar = S5P[:, P_AR0:P_AR0 + 2, :]
                ai = S5P[:, P_AI, :]
                nai = S5P[:, P_NAI, :]
            else:
                nb, nt = 8, 4
                BU5 = BU.rearrange("p r s (b t) -> p r s b t", b=8)
                H5 = HALL.rearrange("p r s (b t) -> p r s b t", b=8)
                bu_t = lambda t: BU5[:, :, :, :, t]
                h_t = lambda t: H5[:, :, :, :, t]
                prev0 = S0[:, :, :, 8 * st_i:8 * st_i + 8]
                prev0b = [bBIG[8]]
                T1 = T1f[:, 0:512].rearrange("p (r s b) -> p r s b", r=2, s=32)
                T2 = T2f[:, 0:512].rearrange("p (r s b) -> p r s b", r=2, s=32)
                arar = S5P[:, P_AR0:P_AR0 + 2, :].unsqueeze(3).to_broadcast([128, 2, 32, 8])
                ai = S5P[:, P_AI, :].unsqueeze(2).to_broadcast([128, 32, 8])
                nai = S5P[:, P_NAI, :].unsqueeze(2).to_broadcast([128, 32, 8])
            for t in range(nt):
                prev = prev0 if t == 0 else h_t(t - 1)
                pbf = prev0b if t == 0 else bQ
                S.dve(lambda e, prev=prev: e.tensor_tensor(out=T1, in0=prev, in1=arar, op=ALU.mult),
                      r=pbf + [bS5P], w=bQ)
                S.dve(lambda e, prev=prev: e.tensor_tensor(out=T2[:, 0], in0=prev[:, 1], in1=nai, op=ALU.mult),
                      r=pbf + [bS5P], w=bQ)
                S.dve(lambda e, prev=prev: e.tensor_tensor(out=T2[:, 1], in0=prev[:, 0], in1=ai, op=ALU.mult),
                      r=pbf + [bS5P], w=bQ)
                S.dve(lambda e: e.tensor_tensor(out=T1, in0=T1, in1=T2, op=ALU.add), r=bQ, w=bQ)
                S.dve(lambda e, t=t: e.tensor_tensor(out=h_t(t), in0=T1, in1=bu_t(t), op=ALU.add), r=bQ, w=bQ)
            if nseq == 1:
                S.dve(lambda e: e.tensor_copy(out=HCS[:, j, :, :], in_=HALL[:, :, :, 31]), r=bQ, w=[bHCS])
            else:
                S.dve(lambda e: e.tensor_copy(out=S0[:, :, :, 8 * st_i:8 * st_i + 8], in_=h_t(3)), r=bQ, w=[bBIG[8]])
            if cstop <= 3:
                continue
            S.act(lambda e: e.copy(out=HB, in_=HALL), r=bQ, w=bQ)
            py, pyb = psum()
            for ct in range(8):
                i_ = 0
                for s4 in range(4):
                    for ri in range(2):
                        s_ = ct * 4 + s4
                        S.pe(lambda e, s_=s_, ri=ri, ct=ct, i_=i_: e.matmul(
                            py[:, ct * 32:(ct + 1) * 32], CL[:, s_, ri, :], HB[:, ri, s_, :],
                            start=(i_ == 0), stop=(i_ == 7)), r=[bCL] + bQ, w=[pyb], inc=(i_ == 7 and ct == 7))
                        i_ += 1
            Y = T1f[:, 0:256].rearrange("p (c t) -> p c t", c=8)
            Y2 = T2f[:, 0:256].rearrange("p (c t) -> p c t", c=8)
            dsk = S5P[:, P_DSK, 0:8].unsqueeze(2).to_broadcast([128, 8, 32])
            S.dve(lambda e: e.tensor_tensor(out=Y, in0=U[:, :, cs:cs + 32], in1=dsk, op=ALU.mult), r=bU + [bS5P], w=bQ)
            S.dve(lambda e: e.tensor_tensor(out=Y, in0=Y, in1=py[:, 0:256].rearrange("p (c t) -> p c t", c=8),
                                            op=ALU.add), r=bQ + [pyb], w=bQ)
            S.dve(lambda e: e.tensor_tensor(out=Y2, in0=Y, in1=Y, op=ALU.mult), r=bQ, w=bQ)
            S.dve(lambda e: e.tensor_scalar(out=Y2, in0=Y2, scalar1=0.044715, scalar2=1.0, op0=ALU.mult, op1=ALU.add),
                  r=bQ, w=bQ)
            S.dve(lambda e: e.tensor_tensor(out=Y2, in0=Y2, in1=Y, op=ALU.mult), r=bQ, w=bQ)
            S.act(lambda e: e.activation(out=Y2, in_=Y2, func=AF.Sigmoid, scale=1.5957691216), r=bQ, w=bQ)
            for k in range(8):
                S.dve(lambda e, k=k: e.tensor_tensor(out=GY(k)[:, cs:cs + 32], in0=Y[:, k, :], in1=Y2[:, k, :],
                                                     op=ALU.mult), r=bQ, w=bGY)
        if cstop <= 4:
            return mixer_none(l, ti)
        if ti == 3:
            for ri, od in ((0, o_re_p), (1, o_im_p)):
                i = rows_i[0]
                rows_i[0] ^= 1
                rt, rb = rows_t[i], bROWS[i]
                pt, pb = psum()
                S.pe(lambda e, ri=ri: e.transpose(pt[0:32, 0:128], HCS[:, j, ri, :], ident(128)), r=[bHCS, bCST], w=[pb])
                S.act(lambda e: e.copy(out=rt[0:32, 0:128], in_=pt[0:32, 0:128]), r=[pb], w=[rb])
                S.dma("sp", od[j].rearrange("(s g) p -> s (g p)", g=2), rt[0:32, 0:128], r=[rb], is_out=True)
        if ti == 4:
            STG = T1f[:, 0:128]
            for ri, od in ((0, o_re_s), (1, o_im_s)):
                for q in range(4):
                    i = rows_i[0]
                    rows_i[0] ^= 1
                    rt, rb = rows_t[i], bROWS[i]
                    S.dve(lambda e, ri=ri, q=q: e.tensor_copy(
                        out=STG.rearrange("p (b s) -> p b s", b=4),
                        in_=S0[:, ri, :, 4 * q:4 * q + 4].rearrange("p s b -> p b s")), r=[bBIG[8]], w=bQ)
                    pt, pb = psum()
                    S.pe(lambda e: e.transpose(pt[:, 0:128], STG, ident(128)), r=bQ + [bCST], w=[pb])
                    S.act(lambda e: e.copy(out=rt[:, 0:128], in_=pt[:, 0:128]), r=[pb], w=[rb])
                    for b4 in range(4):
                        S.dma("sp", od[4 * q + b4, j].rearrange("(s g) p -> s (g p)", g=2),
                              rt[32 * b4:32 * b4 + 32, 0:128], r=[rb], is_out=True)
        for oc in range(8):
            w1, w1b = wslab(W["w_out_c"][j], 0, 8, oc * 128, 128, key=("outc", j))
            w2, w2b = wslab(W["w_out_c"][j], 0, 8, 1024 + oc * 128, 128, key=("outc", j))
            p1, p1b = psum()
            for k in range(8):
                S.pe(lambda e, k=k: e.matmul(p1[:, 0:n], w1[:, k, :], GY(k)[:, 0:n], start=(k == 0), stop=(k == 7)),
                     r=[w1b] + bGY, w=[p1b], inc=(k == 7))
            p2, p2b = psum()
            for k in range(8):
                S.pe(lambda e, k=k: e.matmul(p2[:, 0:n], w2[:, k, :], GY(k)[:, 0:n], start=(k == 0), stop=(k == 7)),
                     r=[w2b] + bGY, w=[p2b], inc=(k == 7))
            sg = LNS[:, 0, 0:n]
            S.act(lambda e: e.activation(out=sg, in_=p2[:, 0:n], func=AF.Sigmoid), r=[p2b], w=[bLNS[0]])
            S.dve(lambda e: e.tensor_tensor(out=sg, in0=sg, in1=p1[:, 0:n], op=ALU.mult), r=[bLNS[0], p1b], w=[bLNS[0]])
            S.dve(lambda e, oc=oc: e.scalar_tensor_tensor(out=ZT[:, oc, 0:n], in0=X[:, oc, 0:n], scalar=ALPHA, in1=sg,
                                                          op0=ALU.mult, op1=ALU.add), r=[bX[ti], bLNS[0]], w=[bZT])


    YAB = sb("YAB", [128, 8, 512], BF16)
    bYAB = Buf("YAB")
    PA = sb("PA", [128, 192])
    bPA = Buf("PA")
    LOR = sb("LOR", [128, 3, 512])
    bLOR = Buf("LOR")
    LNX = sb("LNX", [64, 2, 512])
    bLNX = Buf("LNX")
    CAH = sb("CAH", [128, 2, 4, 30])
    bCAH = Buf("CAH")
    CSH = sb("CSH", [128, 2, 14])
    bCSH = Buf("CSH")
    SHS = sb("SHS", [128, 14, 16])
    bSHS = Buf("SHS")
    STS = sb("STS", [128, 2, 4, 64])
    bSTS = Buf("STS")
    S.dve(lambda e: e.memset(CAH[:, :, :, :], 0.0), w=[bCAH])
    S.dve(lambda e: e.memset(CSH[:, :, :], 0.0), w=[bCSH])
    S.dve(lambda e: e.memset(STS[:, :, :, :], 0.0), w=[bSTS])
    A_MU, A_W0, A_A0, A_KK, A_KA, A_RK, A_CB, A_LG, A_LB, A_CW = 0, 14, 18, 22, 26, 30, 34, 38, 42, 64

    def ab_params(j):
        v4 = lambda name: W[name][j].rearrange("(c p) -> c p", p=128)
        load_cols([W["mu_b"][j].rearrange("(c p) -> c p", p=128), v4("w0_b"), v4("a0_b"), v4("k_k_b"), v4("k_a_b"),
                   W["r_k_b"][j].rearrange("(c h) k -> c (h k)", h=2), v4("conv_a_b"), v4("ln_a_g"), v4("ln_a_b")],
                  0, PA, bPA)
        load_cols([W["conv_a_w"][j].rearrange("t (c p) -> (t c) p", p=128)], A_CW, PA, bPA)
        S.dma("sp", LOR[0:64, 0, :], W["w2_b"][j], w=[bLOR])
        S.dma("sp", LOR[64:128, 1, :], W["a2_b"][j], w=[bLOR])
        S.dma("sp", LOR[:, 2, :], W["g2_b"][j], w=[bLOR])
        S.dma("sp", LNX[:, 0, :], W["ln_x_g"][j].partition_broadcast(64), w=[bLNX])
        S.dma("sp", LNX[:, 1, :], W["ln_x_b"][j].partition_broadcast(64), w=[bLNX])

    def wkv_store(j, dst):
        i = rows_i[0]
        rows_i[0] ^= 1
        rt, rb = rows_t[i], bROWS[i]
        pt, pb = psum()
        for hp in range(4):
            S.pe(lambda e, hp=hp: e.transpose(pt[0:64, hp * 128:(hp + 1) * 128], STS[:, j, hp, :], ident(128)),
                 r=[bSTS, bCST], w=[pb])
        S.act(lambda e: e.copy(out=rt[0:64, 0:512], in_=pt[0:64, 0:512]), r=[pb], w=[rb])
        S.dma("sp", dst.rearrange("(hp h2) v k -> v hp h2 k", h2=2),
              rt[0:64, 0:512].rearrange("v (hp h2 k) -> v hp h2 k", hp=4, h2=2), r=[rb], is_out=True)

    def wkv_load(j, src):
        i = rows_i[0]
        rows_i[0] ^= 1
        rt, rb = rows_t[i], bROWS[i]
        S.dma("sp", rt[0:64, 0:512].rearrange("v (hp h2 k) -> v hp h2 k", hp=4, h2=2),
              src.rearrange("(hp h2) v k -> v hp h2 k", h2=2), w=[rb])
        pt, pb = psum()
        for hp in range(4):
            S.pe(lambda e, hp=hp: e.transpose(pt[:, hp * 64:(hp + 1) * 64], rt[0:64, hp * 128:(hp + 1) * 128], ident(64)),
                 r=[rb, bCST], w=[pb])
        S.act(lambda e: e.copy(out=STS[:, j, :, :], in_=pt[:, 0:256].rearrange("p (h v) -> p h v", h=4)),
              r=[pb], w=[bSTS])

    dbg_done = [False]
    BLf = BL[:, :, :, :].rearrange("p a b c -> p (a b c)").bitcast(F32)
    CLf = CL[:, :, :, :].rearrange("p a b c -> p (a b c)").bitcast(F32)
    bART = [Buf("ART0"), Buf("ART1")]
    bTM = [Buf("TM0"), Buf("TM1")]
    bZN = [Buf("ZN0"), Buf("ZN1")]
    bBK = bBIG[2]
    bZPQ = Buf("ZPQ")
    alias_bufs = bART + bTM + bZN

    def alias_claim(for_rwkv):
        if for_rwkv == 1:
            S.dve(lambda e: e.memset(CLf[:, 4090:4092], 0.0), r=[bBL, bCL, bZT], w=alias_bufs + [bZPQ])
        elif for_rwkv == 2:
            S.dve(lambda e: e.memset(CLf[:, 4090:4092], 0.0), r=alias_bufs + [bZPQ], w=[bZT])
        else:
            S.dve(lambda e: e.memset(CLf[:, 4090:4092], 0.0), r=alias_bufs, w=[bBL, bCL])

    def rw_views(j, col, L, p):
        V = {}
        fm = lambda slot, idx: BIG[:, slot, idx * 256:idx * 256 + 4 * L].rearrange("p (h l) -> p h l", h=4)
        V["LD"], V["LP"], V["EE"], V["KKn"] = fm(0, 0), fm(0, 1), fm(0, 2), fm(0, 3)
        V["KP"], V["AAv"], V["Bv"], V["RKR"] = fm(1, 0), fm(1, 1), fm(1, 2), fm(1, 3)
        V["BT"] = BIG[:, 2, 512:512 + 4 * L].rearrange("p (h l) -> p h l", h=4)
        V["KT"] = BIG[:, 2, 768:768 + 4 * L].rearrange("p (h l) -> p h l", h=4)
        V["BHf"], V["KHf"], V["TMPf"] = fm(3, 0), fm(3, 1), fm(3, 2)
        V["SGL"] = BIG[:, 3, 768:768 + L]
        V["TW"] = BIG[0:64, 3, 832:832 + L]
        ar0 = p * 520
        V["ART"] = CLf[:, ar0:ar0 + 8 * L].rearrange("p (h a l) -> p h a l", h=4, a=2)
        V["PL"] = CLf[:, ar0 + 512:ar0 + 516]
        tm0 = p * 2048
        V["Vt"], V["BHt"] = BLf[0:L, tm0:tm0 + 512], BLf[0:L, tm0 + 512:tm0 + 1024]
        V["KHt"], V["Gt"] = BLf[0:L, tm0 + 1024:tm0 + 1536], BLf[0:L, tm0 + 1536:tm0 + 2048]
        V["RKt"] = CLf[0:L, 1040 + 8 * p:1048 + 8 * p]
        zr = lambda i: ZT[0:L, i, 0:8 * L].rearrange("p (h l) -> p h l", h=8)
        V["Pq"] = [zr(0), zr(1)]
        V["Qq"] = [zr(2), zr(3)]
        if p == 0:
            V["Mv"], V["NAK"], V["NRB"], V["NRK"] = zr(4), zr(5), zr(6), zr(7)
        else:
            zc = lambda i: CLf[0:L, 2048 + 512 * i:2048 + 512 * i + 8 * L].rearrange("p (h l) -> p h l", h=8)
            V["Mv"], V["NAK"], V["NRB"], V["NRK"] = zc(0), zc(1), zc(2), zc(3)
        V["Zs"], V["Ws"] = BIG[0:L, 6, 0:512], BIG[0:L, 6, 512:1024]
        V["Ys"], V["Y2"] = BIG[0:L, 7, 0:512], BIG[0:L, 7, 512:1024]
        V["MU"], V["RS"] = BIG[0:L, 8, 0:8], BIG[0:L, 8, 8:16]
        return V

    hd = lambda h: (h // 2, 64 * (h % 2))
    HORD = [0, 2, 4, 6, 1, 3, 5, 7]

    def rwkv_part1(j, col, L, nlev, p):
        B0, B1, B2, B3 = bBIG[0], bBIG[1], bBIG[2], bBIG[3]
        V = rw_views(j, col, L, p)
        LD, LP, EE, KKn, KP, AAv, Bv, RKR = (V[k] for k in ("LD", "LP", "EE", "KKn", "KP", "AAv", "Bv", "RKR"))
        BT, KT, BHf, KHf, TMPf, SGL, TW = (V[k] for k in ("BT", "KT", "BHf", "KHf", "TMPf", "SGL", "TW"))
        ART, PL, Vt, BHt, KHt, Gt, RKt = (V[k] for k in ("ART", "PL", "Vt", "BHt", "KHt", "Gt", "RKt"))
        Pq, Qq, Mv, NAK, NRB, NRK = (V[k] for k in ("Pq", "Qq", "Mv", "NAK", "NRB", "NRK"))
        bA, bT, bZ = bART[p], bTM[p], bZN[p]
        R_ = QT[:, 0:4, col:col + L]
        K_ = QT[:, 4:8, col:col + L]
        V_ = QT[:, 8:12, col:col + L]
        WLv = QT[0:64, 12, col:col + L]
        ALv = QT[64:128, 12, col:col + L]
        GLv = QT[:, 13, col:col + L]
        bc4 = lambda c: PA[:, c:c + 4].unsqueeze(2).to_broadcast([128, 4, L])
        ones = CST[:, C_ONE:C_ONE + L]
        TT = lambda o, a, b, op, r, w: S.dve(lambda e: e.tensor_tensor(out=o, in0=a, in1=b, op=op), r=r, w=w)
        S.act(lambda e: e.activation(out=TW, in_=WLv, func=AF.Tanh), r=[bQA], w=[B3])
        pw, pwb = psum()
        for hp in range(4):
            S.pe(lambda e, hp=hp: e.matmul(pw[:, hp * L:(hp + 1) * L], LOR[0:64, 0, hp * 128:(hp + 1) * 128], TW,
                                           start=True, stop=True), r=[bLOR, B3], w=[pwb], inc=(hp == 3))
        pa_, pab = psum()
        for hp in range(4):
            S.pe(lambda e, hp=hp: e.matmul(pa_[:, hp * L:(hp + 1) * L], LOR[64:128, 1, hp * 128:(hp + 1) * 128], ALv,
                                           start=True, stop=True), r=[bLOR, bQA], w=[pab], inc=(hp == 3))
        for hp in range(4):
            S.act(lambda e, hp=hp: e.activation(out=LD[:, hp, :], in_=pw[:, hp * L:(hp + 1) * L], func=AF.Sigmoid,
                                                bias=PA[:, A_W0 + hp:A_W0 + hp + 1]), r=[pwb, bPA], w=[B0])
            S.act(lambda e, hp=hp: e.activation(out=AAv[:, hp, :], in_=pa_[:, hp * L:(hp + 1) * L], func=AF.Sigmoid,
                                                bias=PA[:, A_A0 + hp:A_A0 + hp + 1]), r=[pab, bPA], w=[B1])
        S.dve(lambda e: e.tensor_scalar(out=LD, in0=LD, scalar1=-0.6065306597126334, scalar2=None, op0=ALU.mult),
              r=[B0], w=[B0])
        for hp in range(4):
            S.dve(lambda e, hp=hp: e.tensor_tensor_scan(out=LP[:, hp, :], data0=ones, data1=LD[:, hp, :], initial=0.0,
                                                        op0=ALU.mult, op1=ALU.add), r=[B0, bCST], w=[B0])
        TT(KKn, K_, bc4(A_KK), ALU.mult, [bQA, bPA], [B0])
        TT(TMPf, KKn, KKn, ALU.mult, [B0], [B3])
        pss, pssb = psum()
        for hp in range(4):
            S.pe(lambda e, hp=hp: e.matmul(pss[:, hp * L:(hp + 1) * L], CST[:, C_BONE:C_BONE + 128], TMPf[:, hp, :],
                                           start=True, stop=True), r=[bCST, B3], w=[pssb], inc=(hp == 3))
        S.dve(lambda e: e.tensor_scalar(out=TMPf, in0=pss[:, 0:4 * L].rearrange("p (h l) -> p h l", h=4),
                                        scalar1=1e-24, scalar2=None, op0=ALU.max), r=[pssb], w=[B3])
        S.act(lambda e: e.activation(out=TMPf, in_=TMPf, func=AF.Sqrt), r=[B3], w=[B3])
        S.dve(lambda e: e.reciprocal(out=TMPf, in_=TMPf), r=[B3], w=[B3])
        TT(KKn, KKn, TMPf, ALU.mult, [B0, B3], [B0])
        S.dve(lambda e: e.tensor_scalar(out=TMPf, in0=AAv, scalar1=-1.0, scalar2=None, op0=ALU.add), r=[B1], w=[B3])
        TT(TMPf, TMPf, bc4(A_KA), ALU.mult, [B3, bPA], [B3])
        S.dve(lambda e: e.scalar_tensor_tensor(out=KP, in0=TMPf, scalar=1.0, in1=K_, op0=ALU.add, op1=ALU.mult),
              r=[B3, bQA], w=[B1])
        TT(Bv, KKn, AAv, ALU.mult, [B0, B1], [B1])
        S.act(lambda e: e.activation(out=EE, in_=LP, func=AF.Exp, scale=-1.0), r=[B0], w=[B0])
        TT(BT, Bv, EE, ALU.mult, [B1, B0], [bBK])
        TT(KT, KP, EE, ALU.mult, [B1, B0], [bBK])
        S.act(lambda e: e.activation(out=EE, in_=LP, func=AF.Exp), r=[B0], w=[B0])
        TT(ART[:, :, 1, :], R_, EE, ALU.mult, [bQA, B0], [bA])
        TT(TMPf, LP, LD, ALU.subtract, [B0], [B3])
        S.act(lambda e: e.activation(out=EE, in_=TMPf, func=AF.Exp), r=[B3], w=[B0])
        S.dve(lambda e: e.scalar_tensor_tensor(out=ART[:, :, 0, :], in0=KKn, scalar=-1.0, in1=EE,
                                               op0=ALU.mult, op1=ALU.mult), r=[B0], w=[bA])
        TT(TMPf, LP[:, :, L - 1:L].to_broadcast([128, 4, L]), LP, ALU.subtract, [B0], [B3])
        S.act(lambda e: e.activation(out=EE, in_=TMPf, func=AF.Exp), r=[B3], w=[B0])
        TT(BHf, Bv, EE, ALU.mult, [B1, B0], [B3])
        TT(KHf, KP, EE, ALU.mult, [B1, B0], [B3])
        S.act(lambda e: e.activation(out=PL, in_=LP[:, :, L - 1], func=AF.Exp), r=[B0], w=[bA])
        TT(RKR, R_, KP, ALU.mult, [bQA, B1], [B1])
        TT(RKR, RKR, bc4(A_RK), ALU.mult, [B1, bPA], [B1])
        S.act(lambda e: e.activation(out=SGL, in_=GLv, func=AF.Sigmoid), r=[bQA], w=[B3])
        for src, sbf, dst in ((V_, bQA, Vt), (BHf, B3, BHt), (KHf, B3, KHt)):
            pt, pb = psum()
            for hp in range(4):
                S.pe(lambda e, hp=hp, src=src, pt=pt: e.transpose(pt[0:L, hp * 128:(hp + 1) * 128], src[:, hp, :], ident(128)),
                     r=[sbf, bCST], w=[pb])
            S.act(lambda e, dst=dst, pt=pt: e.copy(out=dst, in_=pt[0:L, 0:512]), r=[pb], w=[bT])
        prk, prkb = psum()
        for hp in range(4):
            S.pe(lambda e, hp=hp: e.matmul(prk[0:L, 2 * hp:2 * hp + 2], RKR[:, hp, :], CST[:, C_BIND:C_BIND + 2],
                                           start=True, stop=True), r=[B1, bCST], w=[prkb], inc=(hp == 3))
        S.act(lambda e: e.copy(out=RKt, in_=prk[0:L, 0:8]), r=[prkb], w=[bT])
        pg, pgb = psum()
        S.pe(lambda e: e.matmul(pg[0:L, 0:512], SGL, LOR[:, 2, :], start=True, stop=True), r=[B3, bLOR], w=[pgb])
        S.act(lambda e: e.copy(out=Gt, in_=pg[0:L, 0:512]), r=[pgb], w=[bT])
        specs = [("bt", 0, Qq[0], C_MSU, bZPQ), ("bt", 1, NRB, C_MUI, bZ), ("kt", 0, NAK, C_MSU, bZ), ("kt", 1, NRK, C_MUI, bZ)]
        for lname, a_idx, dst, mcol, dbf in specs:
            pt, pb = psum()
            for h in HORD:
                hp, ho = hd(h)
                lh = (BT if lname == "bt" else KT)[ho:ho + 64, hp, :]
                if h == 1:
                    S.fence("pe")
                S.pe(lambda e, h=h, lh=lh, hp=hp, ho=ho, a_idx=a_idx, pt=pt: e.matmul(
                    pt[0:L, h * L:(h + 1) * L], lh, ART[ho:ho + 64, hp, a_idx, :], start=True, stop=True),
                    r=[bBK, bA], w=[pb], inc=(h in (6, 7)))
            S.dve(lambda e, dst=dst, mcol=mcol, pt=pt: e.tensor_tensor(
                out=dst, in0=pt[0:L, 0:8 * L].rearrange("p (h l) -> p h l", h=8),
                in1=CST[0:L, mcol:mcol + L].unsqueeze(1).to_broadcast([L, 8, L]), op=ALU.mult),
                r=[pb, bCST], w=[dbf])
        pt, pb = psum()
        for h in HORD:
            hp, ho = hd(h)
            if h == 1:
                S.fence("pe")
            S.pe(lambda e, h=h, hp=hp, ho=ho, pt=pt: e.matmul(pt[0:L, h * L:(h + 1) * L], ART[ho:ho + 64, hp, 0, :],
                                                              BT[ho:ho + 64, hp, :], start=True, stop=True),
                 r=[bBK, bA], w=[pb], inc=(h in (6, 7)))
        S.dve(lambda e, pt=pt: e.tensor_tensor(
            out=Pq[0], in0=pt[0:L, 0:8 * L].rearrange("p (h l) -> p h l", h=8),
            in1=CST[0:L, C_MSL:C_MSL + L].unsqueeze(1).to_broadcast([L, 8, L]), op=ALU.mult), r=[pb, bCST], w=[bZPQ])
        S.dve(lambda e: e.tensor_tensor(out=Mv, in0=Qq[0], in1=CST[0:L, C_ID:C_ID + L].unsqueeze(1).to_broadcast([L, 8, L]),
                                        op=ALU.add), r=[bZPQ, bCST], w=[bZ])
        cur = 0
        for lev in range(1, nlev):
            nxt = 1 - cur
            pP, pPb = psum()
            for h in range(8):
                S.pe(lambda e, h=h, cur=cur, pP=pP: e.matmul(pP[0:L, h * L:(h + 1) * L], Qq[cur][:, h, :], Pq[cur][:, h, :],
                                                             start=True, stop=True), r=[bZPQ], w=[pPb], inc=(h == 7))
            need_q = lev < nlev - 1
            if need_q:
                pQ, pQb = psum()
                for h in range(8):
                    S.pe(lambda e, h=h, cur=cur, pQ=pQ: e.matmul(pQ[0:L, h * L:(h + 1) * L], Pq[cur][:, h, :], Qq[cur][:, h, :],
                                                                 start=True, stop=True), r=[bZPQ], w=[pQb], inc=(h == 7))
            S.act(lambda e, nxt=nxt, pP=pP: e.copy(out=Pq[nxt], in_=pP[0:L, 0:8 * L].rearrange("p (h l) -> p h l", h=8)),
                  r=[pPb], w=[bZPQ])
            if need_q:
                S.dve(lambda e, nxt=nxt, pQ=pQ: e.tensor_copy(out=Qq[nxt], in_=pQ[0:L, 0:8 * L].rearrange("p (h l) -> p h l", h=8)),
                      r=[pQb], w=[bZPQ])
            pM, pMb = psum()
            for h in range(8):
                S.pe(lambda e, h=h, nxt=nxt, pM=pM: e.matmul(pM[0:L, h * L:(h + 1) * L], Pq[nxt][:, h, :], Mv[:, h, :],
                                                             start=True, stop=True), r=[bZPQ, bZ], w=[pMb], inc=(h == 7))
            S.dve(lambda e, pM=pM: e.tensor_tensor(out=Mv, in0=Mv, in1=pM[0:L, 0:8 * L].rearrange("p (h l) -> p h l", h=8),
                                                   op=ALU.add), r=[bZ, pMb], w=[bZ])
            cur = nxt

    def rwkv_part2(j, col, L, p):
        B6, B7, B8 = bBIG[6], bBIG[7], bBIG[8]
        V = rw_views(j, col, L, p)
        ART, PL, Vt, BHt, KHt, Gt, RKt = (V[k] for k in ("ART", "PL", "Vt", "BHt", "KHt", "Gt", "RKt"))
        Mv, NAK, NRB, NRK = (V[k] for k in ("Mv", "NAK", "NRB", "NRK"))
        Zs, Ws, Ys, Y2, MU, RS = (V[k] for k in ("Zs", "Ws", "Ys", "Y2", "MU", "RS"))
        bA, bT, bZ = bART[p], bTM[p], bZN[p]
        ST = STS[:, j, :, :]
        TT = lambda o, a, b, op, r, w: S.dve(lambda e: e.tensor_tensor(out=o, in0=a, in1=b, op=op), r=r, w=w)
        pz, pzb = psum()
        S.fence("pe")
        for i_, h in enumerate((1, 3, 5, 7)):
            hp, ho = hd(h)
            S.pe(lambda e, h=h, hp=hp, ho=ho, i_=i_: e.matmul(pz[0:L, h * 64:(h + 1) * 64], ART[ho:ho + 64, hp, 0, :],
                                                              ST[ho:ho + 64, hp, :], start=(i_ == 0), stop=False,
                                                              skip_group_check=True),
                 r=[bA, bSTS], w=[pzb], inc=True)
        S.fence("pe")
        for h in (0, 2, 4, 6):
            hp, ho = hd(h)
            S.pe(lambda e, h=h, hp=hp, ho=ho: e.matmul(pz[0:L, h * 64:(h + 1) * 64], ART[ho:ho + 64, hp, 0, :],
                                                       ST[ho:ho + 64, hp, :], start=False, stop=False,
                                                       skip_group_check=True),
                 r=[bA, bSTS], w=[pzb], inc=False)
        for h in range(8):
            S.pe(lambda e, h=h: e.matmul(pz[0:L, h * 64:(h + 1) * 64], NAK[:, h, :], Vt[:, h * 64:(h + 1) * 64],
                                         start=False, stop=True, skip_group_check=True),
                 r=[bZ, bT], w=[pzb], inc=(h == 7))
        S.act(lambda e: e.copy(out=Zs, in_=pz[0:L, 0:512]), r=[pzb], w=[B6])
        pw2, pw2b = psum()
        for h in range(8):
            S.pe(lambda e, h=h: e.matmul(pw2[0:L, h * 64:(h + 1) * 64], Mv[:, h, :], Zs[:, h * 64:(h + 1) * 64],
                                         start=True, stop=True), r=[bZ, B6], w=[pw2b], inc=(h == 7))
        S.act(lambda e: e.copy(out=Ws, in_=pw2[0:L, 0:512]), r=[pw2b], w=[B6])
        py, pyb = psum()
        S.fence("pe")
        for i_, h in enumerate((1, 3, 5, 7)):
            hp, ho = hd(h)
            S.pe(lambda e, h=h, hp=hp, ho=ho, i_=i_: e.matmul(py[0:L, h * 64:(h + 1) * 64], ART[ho:ho + 64, hp, 1, :],
                                                              ST[ho:ho + 64, hp, :], start=(i_ == 0), stop=False,
                                                              skip_group_check=True),
                 r=[bA, bSTS], w=[pyb], inc=True)
        S.fence("pe")
        for h in (0, 2, 4, 6):
            hp, ho = hd(h)
            S.pe(lambda e, h=h, hp=hp, ho=ho: e.matmul(py[0:L, h * 64:(h + 1) * 64], ART[ho:ho + 64, hp, 1, :],
                                                       ST[ho:ho + 64, hp, :], start=False, stop=False,
                                                       skip_group_check=True),
                 r=[bA, bSTS], w=[pyb], inc=False)
        for h in range(8):
            S.pe(lambda e, h=h: e.matmul(py[0:L, h * 64:(h + 1) * 64], NRB[:, h, :], Ws[:, h * 64:(h + 1) * 64],
                                         start=False, stop=False, skip_group_check=True),
                 r=[bZ, B6], w=[pyb], inc=False)
            S.pe(lambda e, h=h: e.matmul(py[0:L, h * 64:(h + 1) * 64], NRK[:, h, :], Vt[:, h * 64:(h + 1) * 64],
                                         start=False, stop=True, skip_group_check=True),
                 r=[bZ, bT], w=[pyb], inc=(h == 7))
        psn, psnb = psum()
        for h in range(8):
            hp, ho = hd(h)
            S.pe(lambda e, h=h, hp=hp, ho=ho: e.matmul(psn[ho:ho + 64, hp * 64:(hp + 1) * 64], BHt[:, h * 64:(h + 1) * 64],
                                                       Ws[:, h * 64:(h + 1) * 64], start=True, stop=False),
                 r=[bT, B6], w=[psnb], inc=False)
            S.pe(lambda e, h=h, hp=hp, ho=ho: e.matmul(psn[ho:ho + 64, hp * 64:(hp + 1) * 64], KHt[:, h * 64:(h + 1) * 64],
                                                       Vt[:, h * 64:(h + 1) * 64], start=False, stop=True),
                 r=[bT], w=[psnb], inc=(h == 7))
        S.act(lambda e: e.copy(out=Ys, in_=py[0:L, 0:512]), r=[pyb], w=[B7])
        TT(ST, ST, PL.unsqueeze(2).to_broadcast([128, 4, 64]), ALU.mult, [bSTS, bA], [bSTS])
        TT(ST, ST, psn[:, 0:256].rearrange("p (h v) -> p h v", h=4), ALU.add, [bSTS, psnb], [bSTS])
        Y3 = Ys.rearrange("p (h v) -> p h v", h=8)
        Y23 = Y2.rearrange("p (h v) -> p h v", h=8)
        V3 = Vt.rearrange("p (h v) -> p h v", h=8)
        b8 = lambda ap: ap.unsqueeze(2).to_broadcast([L, 8, 64])
        S.dve(lambda e: e.reduce_sum(out=MU, in_=Y3, axis=AX.X), r=[B7], w=[B8])
        S.dve(lambda e: e.tensor_scalar(out=MU, in0=MU, scalar1=1.0 / 64, scalar2=None, op0=ALU.mult), r=[B8], w=[B8])
        TT(Y3, Y3, b8(MU), ALU.subtract, [B7, B8], [B7])
        TT(Y23, Y3, Y3, ALU.mult, [B7], [B7])
        S.dve(lambda e: e.reduce_sum(out=RS, in_=Y23, axis=AX.X), r=[B7], w=[B8])
        S.dve(lambda e: e.tensor_scalar(out=RS, in0=RS, scalar1=1.0 / 64, scalar2=GN_EPS, op0=ALU.mult, op1=ALU.add),
              r=[B8], w=[B8])
        S.act(lambda e: e.activation(out=RS, in_=RS, func=AF.Sqrt), r=[B8], w=[B8])
        S.dve(lambda e: e.reciprocal(out=RS, in_=RS), r=[B8], w=[B8])
        TT(Y3, Y3, b8(RS), ALU.mult, [B7, B8], [B7])
        TT(Ys, Ys, LNX[0:L, 0, :], ALU.mult, [B7, bLNX], [B7])
        TT(Ys, Ys, LNX[0:L, 1, :], ALU.add, [B7, bLNX], [B7])
        TT(Y23, V3, b8(RKt), ALU.mult, [bT], [B7])
        TT(Ys, Ys, Y2, ALU.add, [B7], [B7])
        TT(Ys, Ys, Gt, ALU.mult, [B7, bT], [B7])
        pT, pTb = psum()
        for hp in range(4):
            S.pe(lambda e, hp=hp: e.transpose(pT[:, hp * L:(hp + 1) * L], Ys[:, hp * 128:(hp + 1) * 128], ident(L)),
                 r=[B7, bCST], w=[pTb])
        S.act(lambda e: e.copy(out=YAB[:, 4:8, col:col + L], in_=pT[:, 0:4 * L].rearrange("p (h l) -> p h l", h=4)),
              r=[pTb], w=[bYAB])

    def mixer_ab(l, ti):
        j = l // 2
        c0, n, nseq, T = TILES[ti]
        E = 30 + T
        w_in = W["w_in_ab"][j]
        cast_xb(ti)
        UEXT = lambda c: BIG[:, c, 0:nseq * E].rearrange("p (s e) -> p s e", s=nseq)
        CVA = lambda c: BIG[:, 4 + c, 0:n]
        UALL = BIG[:, 0:4, 0:nseq * E].rearrange("p c (s e) -> p c s e", s=nseq)
        if nseq == 1:
            if ti == 0:
                S.dve(lambda e: e.memset(UALL[:, :, 0, 0:30], 0.0), w=bBIG[0:4])
            else:
                S.dve(lambda e: e.tensor_copy(out=UALL[:, :, 0, 0:30], in_=CAH[:, j, :, :]), r=[bCAH], w=bBIG[0:4])
        else:
            for q in range(4):
                i = rows_i[0]
                rows_i[0] ^= 1
                rt, rb = rows_t[i], bROWS[i]
                for b4 in range(4):
                    S.dma("sp", rt[30 * b4:30 * b4 + 30, 0:512], st_conv_a[4 * q + b4, j], w=[rb])
                pt, pb = psum()
                for c in range(4):
                    S.pe(lambda e, c=c: e.transpose(pt[:, c * 120:(c + 1) * 120], rt[0:120, c * 128:(c + 1) * 128], ident(120)),
                         r=[rb, bCST], w=[pb])
                for c in range(4):
                    S.act(lambda e, c=c, q=q: e.copy(out=UEXT(c)[:, 4 * q:4 * q + 4, 0:30],
                                                     in_=pt[:, c * 120:(c + 1) * 120].rearrange("p (b r) -> p b r", b=4)),
                          r=[pb], w=[bBIG[c]])
        for c in range(4):
            wv, wvb = wslab(w_in, 0, 8, c * 128, 128, key=("inab", j))
            wg, wgb = wslab(w_in, 0, 8, 512 + c * 128, 128, key=("inab", j))
            pv, pvb = psum()
            for k in range(8):
                S.pe(lambda e, k=k: e.matmul(pv[:, 0:n], wv[:, k, :], XB[:, k, 0:n], start=(k == 0), stop=(k == 7)),
                     r=[wvb, bXB], w=[pvb], inc=(k == 7))
            pg, pgb = psum()
            for k in range(8):
                S.pe(lambda e, k=k: e.matmul(pg[:, 0:n], wg[:, k, :], XB[:, k, 0:n], start=(k == 0), stop=(k == 7)),
                     r=[wgb, bXB], w=[pgb], inc=(k == 7))
            sg = LNS[:, 0, 0:n]
            S.act(lambda e: e.activation(out=sg, in_=pg[:, 0:n], func=AF.Sigmoid), r=[pgb], w=[bLNS[0]])
            S.dve(lambda e, c=c: e.tensor_tensor(out=UEXT(c)[:, :, 30:E], in0=sg.rearrange("p (s t) -> p s t", s=nseq),
                                                 in1=pv[:, 0:n].rearrange("p (s t) -> p s t", s=nseq), op=ALU.mult),
                  r=[bLNS[0], pvb], w=[bBIG[c]])
        for c in range(4):
            acc = CVA(c).rearrange("p (s t) -> p s t", s=nseq)
            S.dve(lambda e, c=c, acc=acc: e.tensor_scalar(out=acc, in0=UEXT(c)[:, :, 0:T], scalar1=PA[:, A_CW + c:A_CW + c + 1],
                                                          scalar2=PA[:, A_CB + c:A_CB + c + 1], op0=ALU.mult, op1=ALU.add),
                  r=[bBIG[c], bPA], w=[bBIG[4 + c]])
            for t_ in range(1, 31):
                S.dve(lambda e, c=c, acc=acc, t_=t_: e.scalar_tensor_tensor(
                    out=acc, in0=UEXT(c)[:, :, t_:t_ + T], scalar=PA[:, A_CW + 4 * t_ + c:A_CW + 4 * t_ + c + 1], in1=acc,
                    op0=ALU.mult, op1=ALU.add), r=[bBIG[c], bPA, bBIG[4 + c]], w=[bBIG[4 + c]])
        if nseq == 1:
            S.dve(lambda e: e.tensor_copy(out=CAH[:, j, :, :], in_=UALL[:, :, 0, T:T + 30]), r=bBIG[0:4], w=[bCAH])
            if ti == 3:
                def dst(rt, rb, done, g):
                    S.dma("sp", o_conv_p[j], rt[0:30, 0:512], r=[rb], is_out=True)
                emit_rows(lambda c: CAH[:, j, c, :], 30, 4, dst, [bCAH])
        else:
            for q in range(4):
                i = rows_i[0]
                rows_i[0] ^= 1
                rt, rb = rows_t[i], bROWS[i]
                pt, pb = psum()
                for c in range(4):
                    stg = LNS[:, c, 0:120]
                    S.dve(lambda e, c=c, q=q, stg=stg: e.tensor_copy(out=stg.rearrange("p (b r) -> p b r", b=4),
                                                                     in_=UEXT(c)[:, 4 * q:4 * q + 4, T:T + 30]),
                          r=[bBIG[c]], w=[bLNS[c]])
                    S.pe(lambda e, c=c, stg=stg: e.transpose(pt[0:120, c * 128:(c + 1) * 128], stg, ident(128)),
                         r=[bLNS[c], bCST], w=[pb])
                S.act(lambda e: e.copy(out=rt[0:120, 0:512], in_=pt[0:120, 0:512]), r=[pb], w=[rb])
                for b4 in range(4):
                    S.dma("sp", o_conv_s[4 * q + b4, j], rt[30 * b4:30 * b4 + 30, 0:512], r=[rb], is_out=True)
        def outf(k, tmp, tb, g, b):
            S.act(lambda e: e.activation(out=YAB[:, k, 0:n], in_=tmp, func=AF.Silu, scale=g, bias=b),
                  r=[tb, bPA], w=[bYAB])
        layer_norm(lambda k: CVA(k), bBIG[4:8], 4, n, A_LG, A_LB, outf, ptile=PA)
        ZTf = ZT[:, :, :].rearrange("p a b -> p (a b)")
        PBX = lambda i: ZTf[:, i * 1024:i * 1024 + nseq * (1 + T)].rearrange("p (s e) -> p s e", s=nseq)
        DTMP = ZTf[:, 2048:2048 + n].rearrange("p (s t) -> p s t", s=nseq)
        if nseq > 1:
            for half in range(2):
                i = rows_i[0]
                rows_i[0] ^= 1
                rt, rb = rows_t[i], bROWS[i]
                S.dma("sp", rt[0:16, 0:896], st_shift[:, j, half * 896:(half + 1) * 896], w=[rb])
                pt, pb = psum()
                for m in range(7):
                    S.pe(lambda e, m=m: e.transpose(pt[:, m * 16:(m + 1) * 16], rt[0:16, m * 128:(m + 1) * 128], ident(16)),
                         r=[rb, bCST], w=[pb])
                S.act(lambda e, half=half: e.copy(out=SHS[:, 7 * half:7 * half + 7, :],
                                                  in_=pt[:, 0:112].rearrange("p (m b) -> p m b", m=7)), r=[pb], w=[bSHS])
        for m in range(14):
            if m % 2 == 0:
                wp, wpb = wslab(w_in, 0, 8, 1024 + m * 128, 256, key=("inab", j))
            pp, ppb = psum()
            for k in range(8):
                S.pe(lambda e, k=k, m=m: e.matmul(pp[:, 0:n], wp[:, k, (m % 2) * 128:(m % 2 + 1) * 128], XB[:, k, 0:n],
                                                  start=(k == 0), stop=(k == 7)), r=[wpb, bXB], w=[ppb], inc=(k == 7))
            pbx = PBX(m % 2)
            S.act(lambda e, pbx=pbx: e.copy(out=pbx[:, :, 1:1 + T], in_=pp[:, 0:n].rearrange("p (s t) -> p s t", s=nseq)),
                  r=[ppb], w=[bZT])
            if nseq == 1:
                if ti == 0:
                    S.dve(lambda e, pbx=pbx: e.memset(pbx[:, :, 0:1], 0.0), w=[bZT])
                else:
                    S.dve(lambda e, pbx=pbx, m=m: e.tensor_copy(out=pbx[:, 0, 0:1], in_=CSH[:, j, m:m + 1]), r=[bCSH], w=[bZT])
            else:
                S.dve(lambda e, pbx=pbx, m=m: e.tensor_copy(out=pbx[:, :, 0], in_=SHS[:, m, :]), r=[bSHS], w=[bZT])
            S.dve(lambda e, pbx=pbx: e.tensor_tensor(out=DTMP, in0=pbx[:, :, 0:T], in1=pbx[:, :, 1:1 + T], op=ALU.subtract),
                  r=[bZT], w=[bZT])
            S.dve(lambda e, pbx=pbx, m=m: e.scalar_tensor_tensor(
                out=QT[:, m, 0:n].rearrange("p (s t) -> p s t", s=nseq), in0=DTMP, scalar=PA[:, A_MU + m:A_MU + m + 1],
                in1=pbx[:, :, 1:1 + T], op0=ALU.mult, op1=ALU.add), r=[bZT, bPA], w=[bQA])
            if nseq == 1:
                S.dve(lambda e, pbx=pbx, m=m: e.tensor_copy(out=CSH[:, j, m:m + 1], in_=pbx[:, 0, T:T + 1]), r=[bZT], w=[bCSH])
            else:
                S.dve(lambda e, pbx=pbx, m=m: e.tensor_copy(out=SHS[:, m, :], in_=pbx[:, :, T]), r=[bZT], w=[bSHS])
        if ti == 3:
            i = rows_i[0]
            rows_i[0] ^= 1
            rt, rb = rows_t[i], bROWS[i]
            pt, pb = psum()
            S.pe(lambda e: e.transpose(pt[0:14, 0:128], CSH[:, j, :], ident(128)), r=[bCSH, bCST], w=[pb])
            S.act(lambda e: e.copy(out=rt[0:14, 0:128], in_=pt[0:14, 0:128]), r=[pb], w=[rb])
            S.dma("sp", o_shift_p[j].rearrange("(c p) -> c p", p=128), rt[0:14, 0:128], r=[rb], is_out=True)
        if ti == 4:
            for half in range(2):
                i = rows_i[0]
                rows_i[0] ^= 1
                rt, rb = rows_t[i], bROWS[i]
                for m0, mn in ((0, 4), (4, 3)):
                    pt, pb = psum()
                    for mm in range(mn):
                        m = m0 + mm
                        S.pe(lambda e, m=m, mm=mm, half=half: e.transpose(pt[0:16, mm * 128:(mm + 1) * 128],
                                                                          SHS[:, 7 * half + m, :], ident(128)),
                             r=[bSHS, bCST], w=[pb])
                    S.act(lambda e, m0=m0, mn=mn, pt=pt: e.copy(out=rt[0:16, m0 * 128:(m0 + mn) * 128], in_=pt[0:16, 0:mn * 128]),
                          r=[pb], w=[rb])
                S.dma("sp", o_shift_s[:, j, half * 896:(half + 1) * 896], rt[0:16, 0:896], r=[rb], is_out=True)
        if cfg.get("norwkv"):
            S.dve(lambda e: e.memset(YAB[:, 4:8, 0:n], 0.0), w=[bYAB])
        elif nseq == 1:
            alias_claim(1)
            rwkv_part1(j, 0, 64, 6, 0)
            for ch in range(8):
                S.begin_rec()
                ps_pool[0] = 0
                if ch + 1 < 8:
                    rwkv_part1(j, (ch + 1) * 64, 64, 6, (ch + 1) % 2)
                ra = S.end_rec()
                S.begin_rec()
                ps_pool[0] = 1
                rwkv_part2(j, ch * 64, 64, ch % 2)
                rb_ = S.end_rec()
                ps_pool[0] = None
                S.merge(ra, rb_)
            if ti == 3:
                wkv_store(j, o_wkv_p[j])
        else:
            alias_claim(1)
            rwkv_part1(j, 0, 4, 2, 0)
            for b in range(NSB):
                S.begin_rec()
                ps_pool[0] = 0
                if b + 1 < NSB:
                    rwkv_part1(j, (b + 1) * 4, 4, 2, (b + 1) % 2)
                ra = S.end_rec()
                S.begin_rec()
                ps_pool[0] = 1
                wkv_load(j, st_wkv[b, j])
                rwkv_part2(j, b * 4, 4, b % 2)
                wkv_store(j, o_wkv_s[b, j])
                rb_ = S.end_rec()
                ps_pool[0] = None
                S.merge(ra, rb_)
        if not cfg.get("norwkv"):
            alias_claim(2)
        for oc in range(8):
            wo, wob = wslab(W["w_out_ab"][j], 0, 8, oc * 128, 128, key=("outab", j))
            po, pob = psum()
            for k in range(8):
                S.pe(lambda e, k=k: e.matmul(po[:, 0:n], wo[:, k, :], YAB[:, k, 0:n], start=(k == 0), stop=(k == 7)),
                     r=[wob, bYAB], w=[pob], inc=(k == 7))
            S.dve(lambda e, oc=oc: e.scalar_tensor_tensor(out=ZT[:, oc, 0:n], in0=X[:, oc, 0:n], scalar=ALPHA, in1=po[:, 0:n],
                                                          op0=ALU.mult, op1=ALU.add), r=[bX[ti], pob], w=[bZT])


    def mixer_none(l, ti):
        c0, n, nseq, T = TILES[ti]
        for k in range(8):
            S.dve(lambda e, k=k: e.tensor_scalar(out=ZT[:, k, 0:n], in0=X[:, k, 0:n], scalar1=ALPHA, scalar2=None,
                                                 op0=ALU.mult), r=[bX[ti]], w=[bZT])

    def ln1(l, ti):
        c0, n, nseq, T = TILES[ti]

        def outf(k, tmp, tb, g, b):
            S.act(lambda e: e.activation(out=X[:, k, 0:n], in_=tmp, func=AF.Identity, scale=g, bias=b),
                  r=[tb, bPAR], w=[bX[ti]])
        layer_norm(lambda k: ZT[:, k, 0:n], bZT, 8, n, 104, 112, outf)

    tiles = cfg.get("tiles", [0, 1, 2, 3, 4])
    for ti in tiles:
        load_x(ti)
        for l in range(depth):
            ffn_params(l)
            mix = cfg.get("mix", "full")
            if l % 2 == 1 and mix in ("full", "c"):
                s5_setup(l // 2)
                if ti == 4:
                    s5_sample_state(l // 2)
                if cfg.get("s5dbg", 0) == 1:
                    mixer_none(l, ti)
                else:
                    mixer_c(l, ti)
            elif l % 2 == 0 and mix in ("full", "ab"):
                ab_params(l // 2)
                mixer_ab(l, ti)
            else:
                mixer_none(l, ti)
            ln1(l, ti)
            if ti == 4:
                ffn_sample_state(l)
            ffn_tile(l, ti)
        store_y(ti)
    S.finish()
    print("instructions:", S.ninst, {k: v for k, v in S.cnt.items()}, "sbuf_left", nc.sbuf_bytes_remaining)


WSHAPES = [
    ("w_in_ab", (2, 1024, 2816)), ("conv_a_w", (2, 31, 512)), ("conv_a_b", (2, 512)), ("ln_a_g", (2, 512)),
    ("ln_a_b", (2, 512)), ("mu_b", (2, 1792)), ("w0_b", (2, 512)), ("w2_b", (2, 64, 512)), ("a0_b", (2, 512)),
    ("a2_b", (2, 64, 512)), ("g2_b", (2, 128, 512)), ("k_k_b", (2, 512)), ("k_a_b", (2, 512)), ("r_k_b", (2, 8, 64)),
    ("ln_x_g", (2, 512)), ("ln_x_b", (2, 512)), ("w_out_ab", (2, 1024, 1024)), ("w_in_c", (2, 1024, 1024)),
    ("lam_re", (2, 64, 64)), ("lam_im", (2, 64, 64)), ("log_dt", (2, 64)), ("b_re", (2, 64, 64, 16)),
    ("b_im", (2, 64, 64, 16)), ("c_re", (2, 64, 16, 64)), ("c_im", (2, 64, 16, 64)), ("d_skip", (2, 1024)),
    ("w_out_c", (2, 1024, 2048)), ("w_up", (4, 1024, 5632)), ("conv_f_w", (4, 3, 2816)), ("conv_f_b", (4, 2816)),
    ("w_down", (4, 2816, 1024)), ("ln1_g", (4, 1024)), ("ln1_b", (4, 1024)), ("ln2_g", (4, 1024)), ("ln2_b", (4, 1024)),
]

OUT_NAMES = ["o_yp", "o_ys", "o_conv_p", "o_conv_s", "o_shift_p", "o_shift_s", "o_wkv_p", "o_wkv_s",
             "o_re_p", "o_re_s", "o_im_p", "o_im_s", "o_ffn_p", "o_ffn_s"]


def run(inputs, cfg=None, cores=None):
    cfg = cfg or {}
    cores = list(range(NCORES)) if cores is None else cores
    f = lambda a: np.ascontiguousarray(np.asarray(a, dtype=np.float32))
    consts = make_consts()
    wmap = {name: f(inputs[name]) for name, _ in WSHAPES}
    in_maps = []
    for c in cores:
        sl = slice(c * NSB, (c + 1) * NSB)
        m = dict(wmap)
        m["xp"] = f(inputs["x_prompt"][c])
        m["xs"] = f(inputs["x_sample"][sl]).reshape(NSB * TS, D)
        m["st_conv_a"] = f(inputs["state_conv_a"][sl])
        m["st_shift"] = f(inputs["state_shift_b"][sl])
        m["st_wkv"] = f(inputs["state_wkv_b"][sl])
        m["st_re"] = f(inputs["state_ssm_re"][sl])
        m["st_im"] = f(inputs["state_ssm_im"][sl])
        m["st_ffn"] = f(inputs["state_conv_ffn"][sl])
        m["cst"] = consts
        in_maps.append(m)
    nc = build(cfg)
    res = run_bass_kernel_spmd(nc, in_maps, core_ids=cores)
    return res.results


def kernel(**inputs):
    rs = run(inputs)
    outs = []
    for name in OUT_NAMES:
        if name.endswith("p"):
            outs.append(np.stack([np.asarray(r[name]) for r in rs], 0))
        else:
            outs.append(np.concatenate([np.asarray(r[name]) for r in rs], 0))
    outs[0] = outs[0].reshape(8, SEQ, D)
    outs[1] = outs[1].reshape(128, TS, D)
    return tuple(o.astype(np.float32) for o in outs)
```

```python
import numpy as np
from contextlib import ExitStack
import concourse.bass as bass
import concourse.mybir as mybir
from concourse.bass_utils import run_bass_kernel_spmd

F32 = mybir.dt.float32
BF16 = mybir.dt.bfloat16
AF = mybir.ActivationFunctionType
ALU = mybir.AluOpType
AX = mybir.AxisListType

D = 1024
SEQ = 2048
NSB = 16
TS = 4
NTOK = SEQ + NSB * TS
DEPTH = 4
DFF = 2816
NFF = 22
DBP = 1792
ALPHA = (2 * DEPTH) ** 0.25
LN_EPS = 1e-5
GN_EPS = 64 * 1e-5
NCORES = 8

C_ID = 0
C_MSU = 128
C_MUI = 192
C_MSL = 256
C_BONE = 320
C_BIND = 448
C_CM = 450
C_ONE = 962
C_W = 1090


def make_consts():
    c = np.zeros((128, C_W), np.float32)
    c[:, C_ID:C_ID + 128] = np.eye(128)
    i = np.arange(64)
    c[:64, C_MSU:C_MSU + 64] = (i[:, None] < i[None, :])
    c[:64, C_MUI:C_MUI + 64] = (i[:, None] <= i[None, :])
    c[:64, C_MSL:C_MSL + 64] = (i[:, None] > i[None, :])
    c[:64, C_BONE:C_BONE + 64] = 1.0
    c[64:, C_BONE + 64:C_BONE + 128] = 1.0
    c[:64, C_BIND] = 1.0
    c[64:, C_BIND + 1] = 1.0
    for q in range(4):
        m = np.zeros((128, 128), np.float32)
        m[32 * q:32 * q + 16, 0:64] = 1.0
        m[32 * q + 16:32 * q + 32, 64:128] = 1.0
        c[:, C_CM + 128 * q:C_CM + 128 * (q + 1)] = m
    c[:, C_ONE:C_ONE + 128] = 1.0
    return c


class Buf:
    __slots__ = ("name", "w", "r", "excl")

    def __init__(self, name, excl=False):
        self.name = name
        self.w = None
        self.r = {}
        self.excl = excl


class Sched:
    EPOCH = 30000

    def __init__(self, nc, es):
        self.nc = nc
        self.es = es
        self.engs = {"pe": nc.tensor, "dve": nc.vector, "act": nc.scalar, "pool": nc.gpsimd, "sp": nc.sync}
        self.cnt = {k: 0 for k in self.engs}
        self.esem = {}
        self.waited = {k: {} for k in self.engs}
        self.dq = {"sp": [es.enter_context(nc.semaphore(f"dsp{i}")) for i in range(24)],
                   "pool": [es.enter_context(nc.semaphore(f"dpl{i}")) for i in range(8)]}
        self.dcnt = {q: [0] * len(v) for q, v in self.dq.items()}
        self.dnext = {q: 0 for q in self.dq}
        self.out_tokens = []
        self.skip_same = {"pe": True, "dve": False, "act": False, "pool": False, "sp": True}
        self.ninst = 0
        self.rec = None

    def begin_rec(self):
        self.rec = []

    def end_rec(self):
        r, self.rec = self.rec, None
        return r

    @staticmethod
    def merge(a, b):
        i = j = 0
        while i < len(a) or j < len(b):
            if j >= len(b) or (i < len(a) and i * max(len(b), 1) <= j * max(len(a), 1)):
                f, args = a[i]
                i += 1
            else:
                f, args = b[j]
                j += 1
            f(*args)

    def _semh(self, key):
        if key[0] == "e":
            if key not in self.esem:
                self.esem[key] = self.es.enter_context(self.nc.semaphore(f"e_{key[1]}_{key[2]}"))
            return self.esem[key]
        return self.dq[key[1]][key[2]]

    def _deps(self, k, r, w):
        deps = {}

        def add(t):
            if t is None:
                return
            key, val = t
            if deps.get(key, 0) < val:
                deps[key] = val
        for b in r:
            add(b.w)
            if b.excl:
                for t in b.r.values():
                    add(t)
        for b in w:
            add(b.w)
            for t in b.r.values():
                add(t)
        return deps

    def _wait(self, k, deps):
        eng = self.engs[k]
        for key, val in deps.items():
            if key[0] == "e" and key[1] == k and self.skip_same[k]:
                continue
            if self.waited[k].get(key, 0) >= val:
                continue
            eng.wait_ge(self._semh(key), val)
            self.waited[k][key] = val

    def op(self, k, fn, r=(), w=(), inc=True):
        if self.rec is not None:
            self.rec.append((self._op, (k, fn, list(r), list(w), inc)))
            return None
        return self._op(k, fn, r, w, inc)

    def _op(self, k, fn, r=(), w=(), inc=True):
        self._wait(k, self._deps(k, r, w))
        ins = fn(self.engs[k])
        c = self.cnt[k] + 1
        key = ("e", k, (c - 1) // self.EPOCH)
        tok = (key, (c - 1) % self.EPOCH + 1)
        if inc:
            ins.then_inc(self._semh(key), 1)
            self.cnt[k] = c
        for b in r:
            b.r[k] = tok
        for b in w:
            b.w = tok
            b.r = {}
        self.ninst += 1
        return ins

    def fence(self, k):
        if self.rec is not None:
            self.rec.append((self._fence, (k,)))
            return
        self._fence(k)

    def _fence(self, k):
        c = self.cnt[k]
        if c == 0:
            return
        key = ("e", k, (c - 1) // self.EPOCH)
        self.engs[k].wait_ge(self._semh(key), (c - 1) % self.EPOCH + 1)

    def dve(self, fn, r=(), w=()):
        return self.op("dve", fn, r, w)

    def act(self, fn, r=(), w=()):
        return self.op("act", fn, r, w)

    def pool(self, fn, r=(), w=()):
        return self.op("pool", fn, r, w)

    def pe(self, fn, r=(), w=(), inc=True):
        return self.op("pe", fn, r, w, inc)

    def dma(self, q, out, in_, r=(), w=(), is_out=False):
        if self.rec is not None:
            self.rec.append((self._dma, (q, out, in_, list(r), list(w), is_out)))
            return
        self._dma(q, out, in_, r, w, is_out)

    def _dma(self, q, out, in_, r=(), w=(), is_out=False):
        deps = self._deps(q, r, w)
        idx = self.dnext[q]
        self.dnext[q] = (idx + 1) % len(self.dq[q])
        key = ("d", q, idx)
        if self.dcnt[q][idx] > 0:
            v = 16 * self.dcnt[q][idx]
            if deps.get(key, 0) < v:
                deps[key] = v
        self._wait(q, deps)
        self.engs[q].dma_start(out=out, in_=in_).then_inc(self.dq[q][idx], 16)
        self.dcnt[q][idx] += 1
        tok = (key, 16 * self.dcnt[q][idx])
        for b in r:
            b.r[key] = tok
        for b in w:
            b.w = tok
            b.r = {}
        if is_out:
            self.out_tokens.append(tok)
        self.ninst += 1

    def finish(self):
        deps = {}
        for key, val in self.out_tokens:
            if deps.get(key, 0) < val:
                deps[key] = val
        self._wait("sp", deps)


def build(cfg):
    depth = cfg.get("depth", DEPTH)
    nc = bass.Bass("TRN2", target_bir_lowering=False)
    es = ExitStack()
    with es:
        _build(nc, es, cfg, depth)
    return nc


def _build(nc, es, cfg, depth):
    S = Sched(nc, es)

    def din(name, shape):
        return nc.dram_tensor(name, list(shape), F32, kind="ExternalInput").ap()

    def dout(name, shape):
        return nc.dram_tensor(name, list(shape), F32, kind="ExternalOutput").ap()

    def sb(name, shape, dt=F32):
        return es.enter_context(nc.sbuf_tensor(name, list(shape), dt))

    xp = din("xp", [SEQ, D])
    xs = din("xs", [NSB * TS, D])
    st_conv_a = din("st_conv_a", [NSB, 2, 30, 512])
    st_shift = din("st_shift", [NSB, 2, DBP])
    st_wkv = din("st_wkv", [NSB, 2, 8, 64, 64])
    st_re = din("st_re", [NSB, 2, 64, 64])
    st_im = din("st_im", [NSB, 2, 64, 64])
    st_ffn = din("st_ffn", [NSB, 4, 2, DFF])
    cst = din("cst", [128, C_W])
    W = {}
    for name, shape in WSHAPES:
        W[name] = din(name, shape)
    o_yp = dout("o_yp", [SEQ, D])
    o_ys = dout("o_ys", [NSB * TS, D])
    o_conv_p = dout("o_conv_p", [2, 30, 512])
    o_conv_s = dout("o_conv_s", [NSB, 2, 30, 512])
    o_shift_p = dout("o_shift_p", [2, DBP])
    o_shift_s = dout("o_shift_s", [NSB, 2, DBP])
    o_wkv_p = dout("o_wkv_p", [2, 8, 64, 64])
    o_wkv_s = dout("o_wkv_s", [NSB, 2, 8, 64, 64])
    o_re_p = dout("o_re_p", [2, 64, 64])
    o_re_s = dout("o_re_s", [NSB, 2, 64, 64])
    o_im_p = dout("o_im_p", [2, 64, 64])
    o_im_s = dout("o_im_s", [NSB, 2, 64, 64])
    o_ffn_p = dout("o_ffn_p", [4, 2, DFF])
    o_ffn_s = dout("o_ffn_s", [NSB, 4, 2, DFF])
    o_dbg = dout("o_dbg", [128, 2048]) if cfg.get("dbg") else None
    o_dbgB = dout("o_dbgB", [128, 9216]) if cfg.get("dbgr") else None
    o_dbgZ = dout("o_dbgZ", [128, 4096]) if cfg.get("dbgr") else None
    o_dbgS = dout("o_dbgS", [128, 256]) if cfg.get("dbgr") else None

    X = sb("X", [128, 8, 512])
    _bx = Buf("X")
    bX = [_bx] * 5
    XB = sb("XB", [128, 8, 512], BF16)
    bXB = Buf("XB")
    CST = sb("CST", [128, C_W])
    bCST = Buf("CST")
    IDB = None
    PAR = sb("PAR", [128, 512])
    bPAR = Buf("PAR")
    NWB = 2
    WB = [sb(f"WB{i}", [128, 22 * 128], BF16) for i in range(NWB)]
    bWB = [Buf(f"WB{i}") for i in range(NWB)]
    wb_i = [0]
    wb_cnt = [0]
    ZT = sb("ZT", [128, 8, 512])
    bZT = Buf("ZT")
    LNS = sb("LNS", [128, 4, 512])
    bLNS = [Buf(f"LNS{i}") for i in range(4)]
    BIG = sb("BIG", [128, 9, 1024])
    bBIG = [Buf(f"BIG{i}") for i in range(9)]
    QA = sb("QA", [128, 14 * 512])
    bQA = Buf("QA")
    QT = QA[:, :].rearrange("p (k n) -> p k n", k=14)
    bQT = bQA
    AT = QA[:, :].bitcast(BF16)[:, 0:NFF * 512].rearrange("p (k n) -> p k n", k=NFF)
    bAT = bQA
    ROWS = sb("ROWS", [128, 1024])
    bROWS = [Buf("ROWS0"), Buf("ROWS1")]
    ROWS2 = sb("ROWS2", [128, 1024])
    rows_t = [ROWS, ROWS2]
    rows_i = [0]
    FH = sb("FH", [128, DEPTH, NFF, 2])
    bFH = Buf("FH")
    HEXT = BIG[:, 0:2, 0:520]
    bHEXT = [bBIG[0], bBIG[1]]
    CV = BIG[:, 2:4, 0:512]
    bCV = [bBIG[2], bBIG[3]]
    SFH = BIG[:, 4, 0:NFF * 32].rearrange("p (c t) -> p c t", c=NFF)
    bSFH = bBIG[4]
    SEL = BIG[:, 5, 0:NFF * 32].rearrange("p (c t) -> p c t", c=NFF)
    bSEL = bBIG[5]

    PS = [es.enter_context(nc.psum_tensor(f"ps{i}", [128, 512], F32)) for i in range(8)]
    bPS = [Buf(f"ps{i}", excl=True) for i in range(8)]
    ps_i = [0]

    ps_pool = [None]
    ps_pi = [0, 0]

    def psum():
        if ps_pool[0] is not None:
            q = ps_pool[0]
            i = 4 * q + ps_pi[q]
            ps_pi[q] = (ps_pi[q] + 1) % 4
            return PS[i], bPS[i]
        i = ps_i[0]
        ps_i[0] = (i + 1) % 8
        return PS[i], bPS[i]

    def ident(n=128):
        return CST[0:n, C_ID:C_ID + n]

    S.dma("sp", CST[:], cst[:, :], w=[bCST])

    TILES = [(i * 512, 512, 1, 512) for i in range(4)] + [(2048, 64, NSB, TS)]

    def transpose_to(out_ps, in_ap, nrows, r, w):
        S.pe(lambda e: e.transpose(out_ps, in_ap, ident(nrows)), r=r + [bCST], w=w)

    def load_cols(rows_list, col0, dest=None, dbuf=None):
        i = rows_i[0]
        rows_i[0] ^= 1
        rt, rb = rows_t[i], bROWS[i]
        r0 = 0
        for ap in rows_list:
            n = ap.shape[0]
            S.dma("sp", rt[r0:r0 + n, 0:128], ap, w=[rb])
            r0 += n
        pt, pb = psum()
        transpose_to(pt[:, 0:r0], rt[0:r0, 0:128], r0, [rb], [pb])
        if dest is None:
            dest, dbuf = PAR, bPAR
        S.act(lambda e: e.copy(out=dest[:, col0:col0 + r0], in_=pt[:, 0:r0]), r=[pb], w=[dbuf])
        return r0

    wcache = {}

    def wslab(wd, k0, kt, f0, nf, key=None):
        i = wb_i[0]
        wb_i[0] = (i + 1) % NWB
        t, b = WB[i], bWB[i]
        v = t[:, 0:kt * nf].rearrange("p (k f) -> p k f", k=kt)
        if cfg.get("nodma") and wb_cnt[0] >= NWB:
            return v, b
        wb_cnt[0] += 1
        ck = (key, k0, kt, f0, nf)
        if key is not None and cfg.get("wcache", True) and ck in wcache:
            sc, scb = wcache[ck]
            S.dma("sp", t[:, 0:kt * nf], sc[:, :], r=[scb], w=[b])
            return v, b
        S.dma("pool", v, wd[k0:k0 + kt * 128, f0:f0 + nf].rearrange("(k p) f -> p k f", p=128), w=[b])
        if key is not None and cfg.get("wcache", True) and len(cfg.get("tiles", [0, 1, 2, 3, 4])) > 1:
            sc = nc.dram_tensor(f"wc{len(wcache)}", [128, kt * nf], BF16, kind="Internal").ap()
            scb = Buf(f"wc{len(wcache)}")
            wcache[ck] = (sc, scb)
            S.dma("sp", sc[:, :], t[:, 0:kt * nf], r=[b], w=[scb])
        return v, b

    def cast_xb(ti):
        c0, n, _, _ = TILES[ti]
        S.act(lambda e: e.copy(out=XB[:, :, 0:n], in_=X[:, :, 0:n]), r=[bX[ti]], w=[bXB])

    def layer_norm(zv, zb, nk, n, gcol, bcol, out_fn, eps=LN_EPS, ptile=None):
        inv = 1.0 / (nk * 128)
        if ptile is None:
            ptile = PAR
        zb = zb if isinstance(zb, list) else [zb]
        ones = CST[:, C_ONE:C_ONE + 128]
        pm, pmb = psum()
        for k in range(nk):
            S.pe(lambda e, k=k: e.matmul(pm[:, 0:n], ones, zv(k), start=(k == 0), stop=(k == nk - 1)),
                 r=zb + [bCST], w=[pmb], inc=(k == nk - 1))
        pq, pqb = psum()
        for k in range(nk):
            sq = LNS[:, 0, 0:n]
            S.act(lambda e, k=k: e.activation(out=sq, in_=zv(k), func=AF.Square), r=zb, w=[bLNS[0]])
            S.pe(lambda e, k=k: e.matmul(pq[:, 0:n], ones, sq, start=(k == 0), stop=(k == nk - 1)),
                 r=[bLNS[0], bCST], w=[pqb])
        mean = LNS[:, 1, 0:n]
        rstd = LNS[:, 2, 0:n]
        S.act(lambda e: e.mul(out=mean, in_=pm[:, 0:n], mul=inv), r=[pmb], w=[bLNS[1]])
        S.dve(lambda e: e.tensor_tensor(out=rstd, in0=mean, in1=mean, op=ALU.mult), r=[bLNS[1]], w=[bLNS[2]])
        S.dve(lambda e: e.scalar_tensor_tensor(out=rstd, in0=pq[:, 0:n], scalar=inv, in1=rstd,
                                               op0=ALU.mult, op1=ALU.subtract), r=[pqb, bLNS[2]], w=[bLNS[2]])
        S.dve(lambda e: e.tensor_scalar(out=rstd, in0=rstd, scalar1=eps, scalar2=None, op0=ALU.add),
              r=[bLNS[2]], w=[bLNS[2]])
        S.act(lambda e: e.activation(out=rstd, in_=rstd, func=AF.Sqrt), r=[bLNS[2]], w=[bLNS[2]])
        S.dve(lambda e: e.reciprocal(out=rstd, in_=rstd), r=[bLNS[2]], w=[bLNS[2]])
        for k in range(nk):
            tmp = LNS[:, 3, 0:n]
            S.dve(lambda e, k=k: e.tensor_tensor(out=tmp, in0=zv(k), in1=mean, op=ALU.subtract),
                  r=zb + [bLNS[1]], w=[bLNS[3]])
            S.dve(lambda e: e.tensor_tensor(out=tmp, in0=tmp, in1=rstd, op=ALU.mult),
                  r=[bLNS[3], bLNS[2]], w=[bLNS[3]])
            out_fn(k, tmp, bLNS[3], ptile[:, gcol + k:gcol + k + 1], ptile[:, bcol + k:bcol + k + 1])

    def load_x(ti):
        c0, n, _, _ = TILES[ti]
        for tb in range((n + 127) // 128):
            src = xp[c0 + tb * 128:c0 + (tb + 1) * 128, :] if ti < 4 else xs[:, :]
            nr = min(128, n)
            i = rows_i[0]
            rows_i[0] ^= 1
            rt, rb = rows_t[i], bROWS[i]
            S.dma("sp", rt[0:nr, :], src, w=[rb])
            for half in range(2):
                pt, pb = psum()
                for kk in range(4):
                    k = half * 4 + kk
                    S.pe(lambda e, k=k, kk=kk: e.transpose(pt[:, kk * 128:kk * 128 + nr],
                                                           rt[0:nr, k * 128:(k + 1) * 128], ident(nr)),
                         r=[rb, bCST], w=[pb])
                S.act(lambda e, half=half: e.copy(
                    out=X[:, half * 4:half * 4 + 4, tb * 128:tb * 128 + nr],
                    in_=pt[:, :].rearrange("p (k t) -> p k t", k=4)[:, :, 0:nr]), r=[pb], w=[bX[ti]])

    def store_y(ti):
        c0, n, _, _ = TILES[ti]
        for tb in range((n + 127) // 128):
            dst = o_yp[c0 + tb * 128:c0 + (tb + 1) * 128, :] if ti < 4 else o_ys[:, :]
            nr = min(128, n)
            i = rows_i[0]
            rows_i[0] ^= 1
            rt, rb = rows_t[i], bROWS[i]
            for half in range(2):
                pt, pb = psum()
                for kk in range(4):
                    k = half * 4 + kk
                    S.pe(lambda e, k=k, kk=kk: e.transpose(pt[0:nr, kk * 128:(kk + 1) * 128],
                                                           X[:, k, tb * 128:tb * 128 + nr], ident(128)),
                         r=[bX[ti], bCST], w=[pb])
                S.act(lambda e, half=half: e.copy(out=rt[0:nr, half * 512:(half + 1) * 512], in_=pt[0:nr, :]),
                      r=[pb], w=[rb])
            S.dma("sp", dst, rt[0:nr, :], r=[rb], is_out=True)

    def emit_rows(src_fn, nrows, nchunks, dst_fn, r):
        done = 0
        while done < nchunks:
            g = min(8, nchunks - done)
            i = rows_i[0]
            rows_i[0] ^= 1
            rt, rb = rows_t[i], bROWS[i]
            for h0 in range(0, g, 4):
                pt, pb = psum()
                hn = min(4, g - h0)
                for kk in range(hn):
                    c = done + h0 + kk
                    S.pe(lambda e, c=c, kk=kk: e.transpose(pt[0:nrows, kk * 128:(kk + 1) * 128],
                                                           src_fn(c), ident(128)), r=r + [bCST], w=[pb])
                S.act(lambda e, h0=h0, hn=hn: e.copy(out=rt[0:nrows, h0 * 128:(h0 + hn) * 128],
                                                      in_=pt[0:nrows, 0:hn * 128]), r=[pb], w=[rb])
            dst_fn(rt, rb, done, g)
            done += g

    def ffn_params(l):
        load_cols([W["conv_f_w"][l].rearrange("j (c p) -> (j c) p", p=128)], 0)
        load_cols([W["conv_f_b"][l].rearrange("(c p) -> c p", p=128),
                   W["ln2_g"][l].rearrange("(c p) -> c p", p=128),
                   W["ln2_b"][l].rearrange("(c p) -> c p", p=128),
                   W["ln1_g"][l].rearrange("(c p) -> c p", p=128),
                   W["ln1_b"][l].rearrange("(c p) -> c p", p=128)], 66)

    def ffn_sample_state(l):
        i = rows_i[0]
        for c0 in range(0, NFF, 8):
            g = min(8, NFF - c0)
            i = rows_i[0]
            rows_i[0] ^= 1
            rt, rb = rows_t[i], bROWS[i]
            for b in range(NSB):
                S.dma("sp", rt[2 * b:2 * b + 2, 0:g * 128], st_ffn[b, l, :, c0 * 128:(c0 + g) * 128], w=[rb])
            for h0 in range(0, g, 4):
                hn = min(4, g - h0)
                pt, pb = psum()
                for kk in range(hn):
                    S.pe(lambda e, kk=kk, h0=h0: e.transpose(pt[:, kk * 32:(kk + 1) * 32],
                                                             rt[0:32, (h0 + kk) * 128:(h0 + kk + 1) * 128], ident(32)),
                         r=[rb, bCST], w=[pb])
                S.act(lambda e, h0=h0, hn=hn, c0=c0: e.copy(
                    out=SFH[:, c0 + h0:c0 + h0 + hn, :],
                    in_=pt[:, 0:hn * 32].rearrange("p (k t) -> p k t", k=hn)), r=[pb], w=[bSFH])

    def ffn_tile(l, ti):
        c0, n, nseq, T = TILES[ti]
        E = T + 2
        cast_xb(ti)
        wup = W["w_up"][l]
        for g0 in range(0, NFF, 2):
            gn = min(2, NFF - g0)
            wv, wvb = wslab(wup, 0, 8, g0 * 128, gn * 128, key=("up", l))
            wg, wgb = wslab(wup, 0, 8, DFF + g0 * 128, gn * 128, key=("up", l))
            for j in range(gn):
                hc = g0 + j
                ph, phb = psum()
                for k in range(8):
                    S.pe(lambda e, k=k, j=j: e.matmul(ph[:, 0:n], wv[:, k, j * 128:(j + 1) * 128], XB[:, k, 0:n],
                                                      start=(k == 0), stop=(k == 7)),
                         r=[wvb, bXB], w=[phb], inc=(k == 7))
                pg, pgb = psum()
                for k in range(8):
                    S.pe(lambda e, k=k, j=j: e.matmul(pg[:, 0:n], wg[:, k, j * 128:(j + 1) * 128], XB[:, k, 0:n],
                                                      start=(k == 0), stop=(k == 7)),
                         r=[wgb, bXB], w=[pgb], inc=(k == 7))
                hi = hc % 2
                hx = HEXT[:, hi, 0:nseq * E].rearrange("p (s e) -> p s e", s=nseq)
                hb = bHEXT[hi]
                if nseq == 1:
                    if ti == 0:
                        S.dve(lambda e: e.memset(hx[:, :, 0:2], 0.0), w=[hb])
                    else:
                        S.dve(lambda e, hc=hc: e.tensor_copy(out=hx[:, 0, 0:2], in_=FH[:, l, hc, :]), r=[bFH], w=[hb])
                else:
                    S.dve(lambda e, hc=hc: e.tensor_copy(
                        out=hx[:, :, 0:2], in_=SFH[:, hc, :].rearrange("p (b r) -> p b r", r=2)), r=[bSFH], w=[hb])
                S.act(lambda e: e.copy(out=hx[:, :, 2:E], in_=ph[:, 0:n].rearrange("p (s t) -> p s t", s=nseq)),
                      r=[phb], w=[hb])
                cv = CV[:, hi, 0:n].rearrange("p (s t) -> p s t", s=nseq)
                cb = bCV[hi]
                S.dve(lambda e, hc=hc: e.tensor_scalar(out=cv, in0=hx[:, :, 0:T], scalar1=PAR[:, hc:hc + 1],
                                                       scalar2=PAR[:, 66 + hc:67 + hc], op0=ALU.mult, op1=ALU.add),
                      r=[hb, bPAR], w=[cb])
                for j2 in (1, 2):
                    S.dve(lambda e, hc=hc, j2=j2: e.scalar_tensor_tensor(
                        out=cv, in0=hx[:, :, j2:j2 + T], scalar=PAR[:, 22 * j2 + hc:22 * j2 + hc + 1], in1=cv,
                        op0=ALU.mult, op1=ALU.add), r=[hb, bPAR, cb], w=[cb])
                if nseq == 1:
                    S.pool(lambda e, hc=hc: e.tensor_copy(out=FH[:, l, hc, :], in_=hx[:, 0, T:T + 2]), r=[hb], w=[bFH])
                else:
                    S.pool(lambda e, hc=hc: e.tensor_copy(
                        out=SEL[:, hc, :].rearrange("p (r b) -> p r b", r=2),
                        in_=hx[:, :, T:T + 2].rearrange("p b r -> p r b")), r=[hb], w=[bSEL])
                S.act(lambda e: e.activation(out=CV[:, hi, 0:n], in_=CV[:, hi, 0:n], func=AF.Silu), r=[cb], w=[cb])
                S.dve(lambda e, hc=hc: e.tensor_tensor(out=AT[:, hc, 0:n], in0=CV[:, hi, 0:n], in1=pg[:, 0:n],
                                                       op=ALU.mult), r=[cb, pgb], w=[bAT])
        if ti == 3:
            def dst(rt, rb, done, g):
                S.dma("sp", o_ffn_p[l, :, done * 128:(done + g) * 128], rt[0:2, 0:g * 128], r=[rb], is_out=True)
            emit_rows(lambda c: FH[:, l, c, :], 2, NFF, dst, [bFH])
        if ti == 4:
            def dst(rt, rb, done, g):
                for r_ in range(2):
                    S.dma("sp", o_ffn_s[:, l, r_, done * 128:(done + g) * 128],
                          rt[16 * r_:16 * r_ + 16, 0:g * 128], r=[rb], is_out=True)
            emit_rows(lambda c: SEL[:, c, :], 32, NFF, dst, [bSEL])
        wdn = W["w_down"][l]
        for o0 in range(0, 8):
            wd_, wdb = wslab(wdn, 0, NFF, o0 * 128, 128, key=("dn", l))
            for j in range(1):
                oc = o0 + j
                pf, pfb = psum()
                for k in range(NFF):
                    S.pe(lambda e, k=k, j=j: e.matmul(pf[:, 0:n], wd_[:, k, j * 128:(j + 1) * 128], AT[:, k, 0:n],
                                                      start=(k == 0), stop=(k == NFF - 1)),
                         r=[wdb, bAT], w=[pfb], inc=(k == NFF - 1))
                S.dve(lambda e, oc=oc: e.scalar_tensor_tensor(out=ZT[:, oc, 0:n], in0=X[:, oc, 0:n], scalar=ALPHA,
                                                              in1=pf[:, 0:n], op0=ALU.mult, op1=ALU.add),
                      r=[bX[ti], pfb], w=[bZT])

        def outf(k, tmp, tb, g, b):
            S.act(lambda e: e.activation(out=X[:, k, 0:n], in_=tmp, func=AF.Identity, scale=g, bias=b),
                  r=[tb, bPAR], w=[bX[ti]])
        layer_norm(lambda k: ZT[:, k, 0:n], bZT, 8, n, 88, 96, outf)

    S5P = sb("S5P", [128, 24, 32])
    bS5P = Buf("S5P")
    BL = sb("BL", [128, 32, 2, 128], BF16)
    bBL = Buf("BL")
    CL = sb("CL", [128, 32, 2, 128], BF16)
    bCL = Buf("CL")
    HCS = sb("HCS", [128, 2, 2, 32])
    bHCS = Buf("HCS")
    BBX = sb("BBX", [128, 8, 128])
    bBBX = [Buf(f"BBX{i}") for i in range(8)]
    S.dve(lambda e: e.memset(HCS[:, :, :, :], 0.0), w=[bHCS])
    S.dve(lambda e: e.memset(BBX[:, :, :], 0.0), w=bBBX)
    P_LR, P_LI, P_DT, P_MAG, P_ANG, P_C, P_S, P_T0, P_T1, P_T2, P_FR, P_FI, P_AR0, P_AR1, P_AI, P_NAI, P_DEN = range(17)
    P_DSK = 17

    def s5_setup(j):
        alias_claim(0)
        sp = lambda i: S5P[:, i, :]
        dbg = cfg.get("s5stop", 99)
        load_cols([W["lam_re"][j].rearrange("(s g) p -> s (g p)", g=2)], 0, S5P[:, P_LR, :], bS5P)
        load_cols([W["lam_im"][j].rearrange("(s g) p -> s (g p)", g=2)], 0, S5P[:, P_LI, :], bS5P)
        load_cols([W["d_skip"][j].rearrange("(c p) -> c p", p=128)], 0, S5P[:, P_DSK, :], bS5P)
        i = rows_i[0]
        rows_i[0] ^= 1
        rt, rb = rows_t[i], bROWS[i]
        S.dma("sp", rt[0:32, 512:514], W["log_dt"][j].rearrange("(s g) -> s g", g=2), w=[rb])
        S.dve(lambda e: e.tensor_copy(out=rt[0:32, 0:128].rearrange("s (g p) -> s g p", g=2),
                                      in_=rt[0:32, 512:514].unsqueeze(2).to_broadcast([32, 2, 64])), r=[rb], w=[rb])
        pt, pb = psum()
        transpose_to(pt[:, 0:32], rt[0:32, 0:128], 32, [rb], [pb])
        S.act(lambda e: e.activation(out=sp(P_DT), in_=pt[:, 0:32], func=AF.Exp), r=[pb], w=[bS5P])
        R = [bS5P]
        tt = lambda o, a, b, op: S.dve(lambda e: e.tensor_tensor(out=sp(o), in0=sp(a), in1=sp(b), op=op), r=R, w=R)
        def ts(o, a, s1, s2, op0, op1=None):
            if s2 is None:
                S.dve(lambda e: e.tensor_scalar(out=sp(o), in0=sp(a), scalar1=s1, scalar2=None, op0=op0), r=R, w=R)
            else:
                S.dve(lambda e: e.tensor_scalar(out=sp(o), in0=sp(a), scalar1=s1, scalar2=s2, op0=op0, op1=op1), r=R, w=R)
        tt(P_MAG, P_LR, P_DT, ALU.mult)
        S.act(lambda e: e.activation(out=sp(P_MAG), in_=sp(P_MAG), func=AF.Exp), r=R, w=R)
        tt(P_ANG, P_LI, P_DT, ALU.mult)
        S.act(lambda e: e.activation(out=sp(P_T0), in_=sp(P_ANG), func=AF.Sin, scale=1.0 / 64), r=R, w=R)
        S.act(lambda e: e.activation(out=sp(P_S), in_=sp(P_ANG), func=AF.Sin, scale=1.0 / 32), r=R, w=R)
        tt(P_T0, P_T0, P_T0, ALU.mult)
        ts(P_C, P_T0, -2.0, 1.0, ALU.mult, ALU.add)
        for _ in range(5):
            tt(P_T0, P_C, P_C, ALU.mult)
            tt(P_T1, P_S, P_S, ALU.mult)
            tt(P_T2, P_C, P_S, ALU.mult)
            tt(P_C, P_T0, P_T1, ALU.subtract)
            ts(P_S, P_T2, 2.0, None, ALU.mult)
        tt(P_AR0, P_MAG, P_C, ALU.mult)
        S.dve(lambda e: e.tensor_copy(out=sp(P_AR1), in_=sp(P_AR0)), r=R, w=R)
        tt(P_AI, P_MAG, P_S, ALU.mult)
        ts(P_NAI, P_AI, -1.0, None, ALU.mult)
        tt(P_T0, P_LR, P_LR, ALU.mult)
        tt(P_T1, P_LI, P_LI, ALU.mult)
        tt(P_DEN, P_T0, P_T1, ALU.add)
        S.dve(lambda e: e.reciprocal(out=sp(P_DEN), in_=sp(P_DEN)), r=R, w=R)
        ts(P_T2, P_AR0, -1.0, None, ALU.add)
        tt(P_T0, P_T2, P_LR, ALU.mult)
        tt(P_T1, P_AI, P_LI, ALU.mult)
        tt(P_T0, P_T0, P_T1, ALU.add)
        tt(P_FR, P_T0, P_DEN, ALU.mult)
        tt(P_T0, P_AI, P_LR, ALU.mult)
        tt(P_T1, P_T2, P_LI, ALU.mult)
        tt(P_T0, P_T0, P_T1, ALU.subtract)
        tt(P_FI, P_T0, P_DEN, ALU.mult)
        if o_dbg is not None:
            S.dma("sp", o_dbg[:, 0:768], S5P[:, :, :].rearrange("p a b -> p (a b)"), r=[bS5P], is_out=True)
        BR = BIG[:, 0, 0:512].rearrange("p (s c) -> p s c", c=16)
        BI = BIG[:, 1, 0:512].rearrange("p (s c) -> p s c", c=16)
        BBR = BIG[:, 2, 0:512].rearrange("p (s c) -> p s c", c=16)
        BBI = BIG[:, 3, 0:512].rearrange("p (s c) -> p s c", c=16)
        TMP = BIG[:, 4, 0:512].rearrange("p (s c) -> p s c", c=16)
        for g2 in range(2):
            S.dma("sp", BR[64 * g2:64 * g2 + 64, :, :],
                  W["b_re"][j].rearrange("(s g) p c -> g p s c", g=2)[g2], w=[bBIG[0]])
            S.dma("sp", BI[64 * g2:64 * g2 + 64, :, :],
                  W["b_im"][j].rearrange("(s g) p c -> g p s c", g=2)[g2], w=[bBIG[1]])
        fr = S5P[:, P_FR, :].unsqueeze(2).to_broadcast([128, 32, 16])
        fi = S5P[:, P_FI, :].unsqueeze(2).to_broadcast([128, 32, 16])
        S.dve(lambda e: e.tensor_tensor(out=BBR, in0=BR, in1=fr, op=ALU.mult), r=[bBIG[0], bS5P], w=[bBIG[2]])
        S.dve(lambda e: e.tensor_tensor(out=TMP, in0=BI, in1=fi, op=ALU.mult), r=[bBIG[1], bS5P], w=[bBIG[4]])
        S.dve(lambda e: e.tensor_tensor(out=BBR, in0=BBR, in1=TMP, op=ALU.subtract), r=[bBIG[2], bBIG[4]], w=[bBIG[2]])
        S.dve(lambda e: e.tensor_tensor(out=BBI, in0=BI, in1=fr, op=ALU.mult), r=[bBIG[1], bS5P], w=[bBIG[3]])
        S.dve(lambda e: e.tensor_tensor(out=TMP, in0=BR, in1=fi, op=ALU.mult), r=[bBIG[0], bS5P], w=[bBIG[4]])
        S.dve(lambda e: e.tensor_tensor(out=BBI, in0=BBI, in1=TMP, op=ALU.add), r=[bBIG[3], bBIG[4]], w=[bBIG[3]])
        for s_ in range(32):
            q = s_ % 4
            for ri, src, sbuf_ in ((0, BBR, bBIG[2]), (1, BBI, bBIG[3])):
                bx = BBX[:, ri * 4 + q, :]
                bb = bBBX[ri * 4 + q]
                for g2 in range(2):
                    S.dve(lambda e, g2=g2, src=src, bx=bx: e.tensor_copy(
                        out=bx[64 * g2:64 * g2 + 64, 32 * q + 16 * g2:32 * q + 16 * g2 + 16],
                        in_=src[64 * g2:64 * g2 + 64, s_, :]), r=[sbuf_], w=[bb])
                pt, pb = psum()
                transpose_to(pt[:, 0:128], bx, 128, [bb], [pb])
                S.act(lambda e, ri=ri: e.copy(out=BL[:, s_, ri, :], in_=pt[:, 0:128]), r=[pb], w=[bBL])
        CR = BIG[:, 5, 0:512].rearrange("p (t q) -> p t q", q=64)
        CI = BIG[:, 6, 0:512].rearrange("p (t q) -> p t q", q=64)
        S.dma("sp", CR, W["c_re"][j].rearrange("(t g) c p -> (g c) t p", g=8), w=[bBIG[5]])
        S.dma("sp", CI, W["c_im"][j].rearrange("(t g) c p -> (g c) t p", g=8), w=[bBIG[6]])
        CX = BIG[:, 7, 0:256].rearrange("p (r x) -> p r x", r=2)
        for s_ in range(32):
            q, ct = s_ % 4, s_ // 4
            mk = CST[:, C_CM + 128 * q:C_CM + 128 * (q + 1)]
            for ri, src, sbuf_ in ((0, CR, bBIG[5]), (1, CI, bBIG[6])):
                cx = CX[:, ri, :]
                S.dve(lambda e, src=src, cx=cx: e.tensor_tensor(
                    out=cx.rearrange("p (g q) -> p g q", g=2),
                    in0=src[:, ct, :].unsqueeze(1).to_broadcast([128, 2, 64]),
                    in1=mk.rearrange("p (g q) -> p g q", g=2), op=ALU.mult), r=[sbuf_, bCST], w=[bBIG[7]])
                pt, pb = psum()
                transpose_to(pt[:, 0:128], cx, 128, [bBIG[7]], [pb])
                S.act(lambda e, ri=ri: e.mul(out=CL[:, s_, ri, :], in_=pt[:, 0:128], mul=(1.0 if ri == 0 else -1.0)),
                      r=[pb], w=[bCL])

    def s5_sample_state(j):
        S0 = BIG[:, 8, :].rearrange("p (r s b) -> p r s b", r=2, s=32)
        for ri, st in ((0, st_re), (1, st_im)):
            for q in range(4):
                i = rows_i[0]
                rows_i[0] ^= 1
                rt, rb = rows_t[i], bROWS[i]
                for b4 in range(4):
                    S.dma("sp", rt[32 * b4:32 * b4 + 32, 0:128],
                          st[4 * q + b4, j].rearrange("(s g) p -> s (g p)", g=2), w=[rb])
                pt, pb = psum()
                transpose_to(pt[:, 0:128], rt[:, 0:128], 128, [rb], [pb])
                S.act(lambda e, ri=ri, q=q: e.copy(
                    out=S0[:, ri, :, 4 * q:4 * q + 4].rearrange("p s b -> p b s"),
                    in_=pt[:, 0:128].rearrange("p (b s) -> p b s", b=4)), r=[pb], w=[bBIG[8]])

    tabD = [nc.dram_tensor(f"s5tab{j_}", [32, 128, 1024], F32, kind="Internal").ap() for j_ in range(2)]
    btabD = [Buf("tab0"), Buf("tab1")]
    bQR = [Buf(f"QR{i}") for i in range(14)]
    QR = lambda i: QA[:, i * 512:(i + 1) * 512]

    def qa_claim(take):
        if take:
            S.dve(lambda e: e.memset(QA[:, 0:2], 0.0), r=[bQA], w=bQR)
        else:
            S.dve(lambda e: e.memset(QA[:, 0:2], 0.0), r=bQR, w=[bQA])

    def s5_tables(j):
        for s_ in range(32):
            i = s_ % 2
            tc, tsn, tmp = QR(2 * i), QR(2 * i + 1), QR(4 + i)
            bb = [bQR[2 * i], bQR[2 * i + 1], bQR[4 + i]]
            S.dve(lambda e, tc=tc: e.tensor_copy(out=tc[:, 0:1], in_=S5P[:, P_C, s_:s_ + 1]), r=[bS5P], w=bb)
            S.dve(lambda e, tsn=tsn: e.tensor_copy(out=tsn[:, 0:1], in_=S5P[:, P_S, s_:s_ + 1]), r=[bS5P], w=bb)
            n_ = 1
            while n_ < 512:
                cn, sn = tc[:, n_ - 1:n_], tsn[:, n_ - 1:n_]
                S.dve(lambda e, n_=n_, sn=sn, tsn=tsn, tmp=tmp: e.tensor_scalar(
                    out=tmp[:, 0:n_], in0=tsn[:, 0:n_], scalar1=sn, scalar2=None, op0=ALU.mult), r=bb, w=bb)
                S.dve(lambda e, n_=n_, sn=sn, tc=tc, tmp=tmp: e.tensor_scalar(
                    out=tmp[:, 256:256 + n_], in0=tc[:, 0:n_], scalar1=sn, scalar2=None, op0=ALU.mult), r=bb, w=bb)
                S.dve(lambda e, n_=n_, cn=cn, tc=tc, tmp=tmp: e.scalar_tensor_tensor(
                    out=tc[:, n_:2 * n_], in0=tc[:, 0:n_], scalar=cn, in1=tmp[:, 0:n_], op0=ALU.mult, op1=ALU.subtract),
                    r=bb, w=bb)
                S.dve(lambda e, n_=n_, cn=cn, tsn=tsn, tmp=tmp: e.scalar_tensor_tensor(
                    out=tsn[:, n_:2 * n_], in0=tsn[:, 0:n_], scalar=cn, in1=tmp[:, 256:256 + n_], op0=ALU.mult, op1=ALU.add),
                    r=bb, w=bb)
                n_ *= 2
            S.dma("sp", tabD[j][s_, :, 0:512], tc, r=bb, w=[btabD[j]])
            S.dma("sp", tabD[j][s_, :, 512:1024], tsn, r=bb, w=[btabD[j]])

    def s5_scan_tile(j, ti, U, bU, UB, bUB, GY, bGY):
        qa_claim(True)
        if ti == 0:
            s5_tables(j)
        HBq = [QA[:, 6144 + 512 * i:6144 + 512 * (i + 1)].bitcast(BF16) for i in range(2)]
        HBv = lambda i: HBq[i // 2][:, (i % 2) * 512:(i % 2) * 512 + 512]
        bHB = [bQR[12], bQR[13]]
        py = pyb = None
        for s_ in range(32):
            ct, q = s_ // 4, s_ % 4
            i = s_ % 2
            tcs, tsn = QR(2 * i), QR(2 * i + 1)
            btab = [bQR[2 * i], bQR[2 * i + 1]]
            S.dma("sp", tcs, tabD[j][s_, :, 0:512], r=[btabD[j]], w=[btab[0]])
            S.dma("sp", tsn, tabD[j][s_, :, 512:1024], r=[btabD[j]], w=[btab[1]])
            pr, prb = psum()
            S.pe(lambda e: e.matmul(pr[:, 0:512], BL[:, s_, 0, :], UB(ct)[:, 0:512], start=True, stop=True),
                 r=[bBL] + bUB, w=[prb])
            pi, pib = psum()
            S.pe(lambda e: e.matmul(pi[:, 0:512], BL[:, s_, 1, :], UB(ct)[:, 0:512], start=True, stop=True),
                 r=[bBL] + bUB, w=[pib])
            BR, BI, GR, GI, T1, T2, T3, T4 = (QR(k) for k in range(4, 12))
            bBR, bBI, bGR, bGI, bT1, bT2, bT3, bT4 = (bQR[k] for k in range(4, 12))
            S.act(lambda e: e.copy(out=BR, in_=pr[:, 0:512]), r=[prb], w=[bBR])
            S.act(lambda e: e.copy(out=BI, in_=pi[:, 0:512]), r=[pib], w=[bBI])
            mag = S5P[:, P_MAG, s_:s_ + 1].to_broadcast([128, 512])
            S.dve(lambda e: e.tensor_tensor(out=T1, in0=BR, in1=tcs, op=ALU.mult), r=[bBR, btab[0]], w=[bT1])
            S.dve(lambda e: e.tensor_tensor(out=T2, in0=BI, in1=tsn, op=ALU.mult), r=[bBI, btab[1]], w=[bT2])
            S.dve(lambda e: e.tensor_tensor(out=T1, in0=T1, in1=T2, op=ALU.add), r=[bT1, bT2], w=[bT1])
            S.pool(lambda e: e.tensor_tensor(out=T3, in0=BI, in1=tcs, op=ALU.mult), r=[bBI, btab[0]], w=[bT3])
            S.pool(lambda e: e.tensor_tensor(out=T4, in0=BR, in1=tsn, op=ALU.mult), r=[bBR, btab[1]], w=[bT4])
            S.pool(lambda e: e.tensor_tensor(out=T3, in0=T3, in1=T4, op=ALU.subtract), r=[bT3, bT4], w=[bT3])
            S.dve(lambda e: e.tensor_tensor_scan(out=GR, data0=mag, data1=T1, initial=HCS[:, j, 0, s_:s_ + 1],
                                                 op0=ALU.mult, op1=ALU.add), r=[bT1, bS5P, bHCS], w=[bGR])
            S.dve(lambda e: e.tensor_tensor_scan(out=GI, data0=mag, data1=T3, initial=HCS[:, j, 1, s_:s_ + 1],
                                                 op0=ALU.mult, op1=ALU.add), r=[bT3, bS5P, bHCS], w=[bGI])
            S.dve(lambda e: e.tensor_tensor(out=T1, in0=GR, in1=tcs, op=ALU.mult), r=[bGR, btab[0]], w=[bT1])
            S.dve(lambda e: e.tensor_tensor(out=T2, in0=GI, in1=tsn, op=ALU.mult), r=[bGI, btab[1]], w=[bT2])
            S.dve(lambda e: e.tensor_tensor(out=T1, in0=T1, in1=T2, op=ALU.subtract), r=[bT1, bT2], w=[bT1])
            S.pool(lambda e: e.tensor_tensor(out=T3, in0=GR, in1=tsn, op=ALU.mult), r=[bGR, btab[1]], w=[bT3])
            S.pool(lambda e: e.tensor_tensor(out=T4, in0=GI, in1=tcs, op=ALU.mult), r=[bGI, btab[0]], w=[bT4])
            S.pool(lambda e: e.tensor_tensor(out=T3, in0=T3, in1=T4, op=ALU.add), r=[bT3, bT4], w=[bT3])
            S.dve(lambda e: e.tensor_copy(out=HCS[:, j, 0, s_:s_ + 1], in_=T1[:, 511:512]), r=[bT1], w=[bHCS])
            S.dve(lambda e: e.tensor_copy(out=HCS[:, j, 1, s_:s_ + 1], in_=T3[:, 511:512]), r=[bT3], w=[bHCS])
            hbr, hbi = HBv(2 * i), HBv(2 * i + 1)
            S.act(lambda e, hbr=hbr: e.copy(out=hbr, in_=T1), r=[bT1], w=[bHB[i]])
            S.act(lambda e, hbi=hbi: e.copy(out=hbi, in_=T3), r=[bT3], w=[bHB[i]])
            if q == 0:
                py, pyb = psum()
            S.pe(lambda e, hbr=hbr, py=py, q=q: e.matmul(py[:, 0:512], CL[:, s_, 0, :], hbr, start=(q == 0), stop=False),
                 r=[bCL, bHB[i]], w=[pyb], inc=False)
            S.pe(lambda e, hbi=hbi, py=py, q=q: e.matmul(py[:, 0:512], CL[:, s_, 1, :], hbi, start=False, stop=(q == 3)),
                 r=[bCL, bHB[i]], w=[pyb], inc=True)
            if q == 3:
                Y, Y2 = T2, T4
                S.dve(lambda e, ct=ct: e.scalar_tensor_tensor(out=Y, in0=U[:, ct, 0:512], scalar=S5P[:, P_DSK, ct:ct + 1],
                                                              in1=py[:, 0:512], op0=ALU.mult, op1=ALU.add),
                      r=bU + [bS5P, pyb], w=[bT2])
                S.pool(lambda e: e.tensor_tensor(out=Y2, in0=Y, in1=Y, op=ALU.mult), r=[bT2], w=[bT4])
                S.pool(lambda e: e.tensor_scalar(out=Y2, in0=Y2, scalar1=0.044715, scalar2=1.0, op0=ALU.mult, op1=ALU.add),
                       r=[bT4], w=[bT4])
                S.pool(lambda e: e.tensor_tensor(out=Y2, in0=Y2, in1=Y, op=ALU.mult), r=[bT4, bT2], w=[bT4])
                S.act(lambda e: e.activation(out=Y2, in_=Y2, func=AF.Sigmoid, scale=1.5957691216), r=[bT4], w=[bT4])
                S.dve(lambda e, ct=ct: e.tensor_tensor(out=GY(ct)[:, 0:512], in0=Y, in1=Y2, op=ALU.mult),
                      r=[bT2, bT4], w=bGY)
        qa_claim(False)


    def mixer_c(l, ti):
        j = l // 2
        c0, n, nseq, T = TILES[ti]
        U = BIG[:, 0:4, :].rearrange("p a (b n) -> p (a b) n", b=2)
        bU = bBIG[0:4]
        UBf = [BIG[:, 4 + i, :].bitcast(BF16) for i in range(2)]
        GYf = [BIG[:, 6 + i, :].bitcast(BF16) for i in range(2)]
        UB = lambda k: UBf[k // 4][:, (k % 4) * 512:(k % 4) * 512 + 512]
        GY = lambda k: GYf[k // 4][:, (k % 4) * 512:(k % 4) * 512 + 512]
        bUB = bBIG[4:6]
        bGY = bBIG[6:8]
        S0 = BIG[:, 8, :].rearrange("p (r s b) -> p r s b", r=2, s=32)
        cast_xb(ti)
        for oc in range(8):
            wv, wvb = wslab(W["w_in_c"][j], 0, 8, oc * 128, 128, key=("inc", j))
            pu, pub = psum()
            for k in range(8):
                S.pe(lambda e, k=k: e.matmul(pu[:, 0:n], wv[:, k, :], XB[:, k, 0:n], start=(k == 0), stop=(k == 7)),
                     r=[wvb, bXB], w=[pub], inc=(k == 7))
            S.act(lambda e, oc=oc: e.copy(out=U[:, oc, 0:n], in_=pu[:, 0:n]), r=[pub], w=bU)
            S.dve(lambda e, oc=oc: e.tensor_copy(out=UB(oc)[:, 0:n], in_=pu[:, 0:n]), r=[pub], w=bUB)
        cstop = cfg.get("cstop", 99)
        if cstop <= 1:
            return mixer_none(l, ti)
        if nseq == 1 and cfg.get("s5scan", True):
            s5_scan_tile(j, ti, U, bU, UB, bUB, GY, bGY)
            nsub_skip = True
        else:
            nsub_skip = False
        qv = QA[:, :]
        BU = qv[:, 0:2048].rearrange("p (r s t) -> p r s t", r=2, s=32)
        HALL = qv[:, 2048:4096].rearrange("p (r s t) -> p r s t", r=2, s=32)
        HB = qv[:, 4096:5120].bitcast(BF16).rearrange("p (r s t) -> p r s t", r=2, s=32)
        T1f = qv[:, 5120:6144]
        T2f = qv[:, 6144:7168]
        bQ = [bQA]
        nsub = 0 if nsub_skip else n // 32
        for st_i in range(nsub):
            cs = st_i * 32
            for ri in range(2):
                for h in range(2):
                    pt, pb = psum()
                    for s16 in range(16):
                        s_ = h * 16 + s16
                        S.pe(lambda e, s_=s_, s16=s16, ri=ri: e.matmul(
                            pt[:, s16 * 32:(s16 + 1) * 32], BL[:, s_, ri, :], UB(s_ // 4)[:, cs:cs + 32],
                            start=True, stop=True), r=[bBL] + bUB, w=[pb], inc=(s16 == 15))
                    S.act(lambda e, ri=ri, h=h: e.copy(
                        out=BU[:, ri, 16 * h:16 * h + 16, :], in_=pt[:, :].rearrange("p (s t) -> p s t", s=16)),
                        r=[pb], w=bQ)
            if cstop <= 2:
                continue
            if nseq == 1:
                nb, nt = 1, 32
                shp = [128, 2, 32]
                bu_t = lambda t: BU[:, :, :, t]
                h_t = lambda t: HALL[:, :, :, t]
                prev0 = HCS[:, j, :, :]
                prev0b = [bHCS]
                T1 = T1f[:, 0:64].rearrange("p (r s) -> p r s", r=2)
                T2 = T2f[:, 0:64].rearrange("p (r s) -> p r s", r=2)
                arar = S5P[:, P_AR0:P_AR0 + 2, :]
                ai = S5P[:, P_AI, :]
                nai = S5P[:, P_NAI, :]
            else:
                nb, nt = 8, 4
                BU5 = BU.rearrange("p r s (b t) -> p r s b t", b=8)
                H5 = HALL.rearrange("p r s (b t) -> p r s b t", b=8)
                bu_t = lambda t: BU5[:, :, :, :, t]
                h_t = lambda t: H5[:, :, :, :, t]
                prev0 = S0[:, :, :, 8 * st_i:8 * st_i + 8]
                prev0b = [bBIG[8]]
                T1 = T1f[:, 0:512].rearrange("p (r s b) -> p r s b", r=2, s=32)
                T2 = T2f[:, 0:512].rearrange("p (r s b) -> p r s b", r=2, s=32)
                arar = S5P[:, P_AR0:P_AR0 + 2, :].unsqueeze(3).to_broadcast([128, 2, 32, 8])
                ai = S5P[:, P_AI, :].unsqueeze(2).to_broadcast([128, 32, 8])
                nai = S5P[:, P_NAI, :].unsqueeze(2).to_broadcast([128, 32, 8])
            for t in range(nt):
                prev = prev0 if t == 0 else h_t(t - 1)
                pbf = prev0b if t == 0 else bQ
                S.dve(lambda e, prev=prev: e.tensor_tensor(out=T1, in0=prev, in1=arar, op=ALU.mult),
                      r=pbf + [bS5P], w=bQ)
                S.dve(lambda e, prev=prev: e.tensor_tensor(out=T2[:, 0], in0=prev[:, 1], in1=nai, op=ALU.mult),
                      r=pbf + [bS5P], w=bQ)
                S.dve(lambda e, prev=prev: e.tensor_tensor(out=T2[:, 1], in0=prev[:, 0], in1=ai, op=ALU.mult),
                      r=pbf + [bS5P], w=bQ)
                S.dve(lambda e: e.tensor_tensor(out=T1, in0=T1, in1=T2, op=ALU.add), r=bQ, w=bQ)
                S.dve(lambda e, t=t: e.tensor_tensor(out=h_t(t), in0=T1, in1=bu_t(t), op=ALU.add), r=bQ, w=bQ)
            if nseq == 1:
                S.dve(lambda e: e.tensor_copy(out=HCS[:, j, :, :], in_=HALL[:, :, :, 31]), r=bQ, w=[bHCS])
            else:
                S.dve(lambda e: e.tensor_copy(out=S0[:, :, :, 8 * st_i:8 * st_i + 8], in_=h_t(3)), r=bQ, w=[bBIG[8]])
            if cstop <= 3:
                continue
            S.act(lambda e: e.copy(out=HB, in_=HALL), r=bQ, w=bQ)
            py, pyb = psum()
            for ct in range(8):
                i_ = 0
                for s4 in range(4):
                    for ri in range(2):
                        s_ = ct * 4 + s4
                        S.pe(lambda e, s_=s_, ri=ri, ct=ct, i_=i_: e.matmul(
                            py[:, ct * 32:(ct + 1) * 32], CL[:, s_, ri, :], HB[:, ri, s_, :],
                            start=(i_ == 0), stop=(i_ == 7)), r=[bCL] + bQ, w=[pyb], inc=(i_ == 7 and ct == 7))
                        i_ += 1
            Y = T1f[:, 0:256].rearrange("p (c t) -> p c t", c=8)
            Y2 = T2f[:, 0:256].rearrange("p (c t) -> p c t", c=8)
            dsk = S5P[:, P_DSK, 0:8].unsqueeze(2).to_broadcast([128, 8, 32])
            S.dve(lambda e: e.tensor_tensor(out=Y, in0=U[:, :, cs:cs + 32], in1=dsk, op=ALU.mult), r=bU + [bS5P], w=bQ)
            S.dve(lambda e: e.tensor_tensor(out=Y, in0=Y, in1=py[:, 0:256].rearrange("p (c t) -> p c t", c=8),
                                            op=ALU.add), r=bQ + [pyb], w=bQ)
            S.dve(lambda e: e.tensor_tensor(out=Y2, in0=Y, in1=Y, op=ALU.mult), r=bQ, w=bQ)
            S.dve(lambda e: e.tensor_scalar(out=Y2, in0=Y2, scalar1=0.044715, scalar2=1.0, op0=ALU.mult, op1=ALU.add),
                  r=bQ, w=bQ)
            S.dve(lambda e: e.tensor_tensor(out=Y2, in0=Y2, in1=Y, op=ALU.mult), r=bQ, w=bQ)
            S.act(lambda e: e.activation(out=Y2, in_=Y2, func=AF.Sigmoid, scale=1.5957691216), r=bQ, w=bQ)
            for k in range(8):
                S.dve(lambda e, k=k: e.tensor_tensor(out=GY(k)[:, cs:cs + 32], in0=Y[:, k, :], in1=Y2[:, k, :],
                                                     op=ALU.mult), r=bQ, w=bGY)
        if cstop <= 4:
            return mixer_none(l, ti)
        if ti == 3:
            for ri, od in ((0, o_re_p), (1, o_im_p)):
                i = rows_i[0]
                rows_i[0] ^= 1
                rt, rb = rows_t[i], bROWS[i]
                pt, pb = psum()
                S.pe(lambda e, ri=ri: e.transpose(pt[0:32, 0:128], HCS[:, j, ri, :], ident(128)), r=[bHCS, bCST], w=[pb])
                S.act(lambda e: e.copy(out=rt[0:32, 0:128], in_=pt[0:32, 0:128]), r=[pb], w=[rb])
                S.dma("sp", od[j].rearrange("(s g) p -> s (g p)", g=2), rt[0:32, 0:128], r=[rb], is_out=True)
        if ti == 4:
            STG = T1f[:, 0:128]
            for ri, od in ((0, o_re_s), (1, o_im_s)):
                for q in range(4):
                    i = rows_i[0]
                    rows_i[0] ^= 1
                    rt, rb = rows_t[i], bROWS[i]
                    S.dve(lambda e, ri=ri, q=q: e.tensor_copy(
                        out=STG.rearrange("p (b s) -> p b s", b=4),
                        in_=S0[:, ri, :, 4 * q:4 * q + 4].rearrange("p s b -> p b s")), r=[bBIG[8]], w=bQ)
                    pt, pb = psum()
                    S.pe(lambda e: e.transpose(pt[:, 0:128], STG, ident(128)), r=bQ + [bCST], w=[pb])
                    S.act(lambda e: e.copy(out=rt[:, 0:128], in_=pt[:, 0:128]), r=[pb], w=[rb])
                    for b4 in range(4):
                        S.dma("sp", od[4 * q + b4, j].rearrange("(s g) p -> s (g p)", g=2),
                              rt[32 * b4:32 * b4 + 32, 0:128], r=[rb], is_out=True)
        for oc in range(8):
            w1, w1b = wslab(W["w_out_c"][j], 0, 8, oc * 128, 128, key=("outc", j))
            w2, w2b = wslab(W["w_out_c"][j], 0, 8, 1024 + oc * 128, 128, key=("outc", j))
            p1, p1b = psum()
            for k in range(8):
                S.pe(lambda e, k=k: e.matmul(p1[:, 0:n], w1[:, k, :], GY(k)[:, 0:n], start=(k == 0), stop=(k == 7)),
                     r=[w1b] + bGY, w=[p1b], inc=(k == 7))
            p2, p2b = psum()
            for k in range(8):
                S.pe(lambda e, k=k: e.matmul(p2[:, 0:n], w2[:, k, :], GY(k)[:, 0:n], start=(k == 0), stop=(k == 7)),
                     r=[w2b] + bGY, w=[p2b], inc=(k == 7))
            sg = LNS[:, 0, 0:n]
            S.act(lambda e: e.activation(out=sg, in_=p2[:, 0:n], func=AF.Sigmoid), r=[p2b], w=[bLNS[0]])
            S.dve(lambda e: e.tensor_tensor(out=sg, in0=sg, in1=p1[:, 0:n], op=ALU.mult), r=[bLNS[0], p1b], w=[bLNS[0]])
            S.dve(lambda e, oc=oc: e.scalar_tensor_tensor(out=ZT[:, oc, 0:n], in0=X[:, oc, 0:n], scalar=ALPHA, in1=sg,
                                                          op0=ALU.mult, op1=ALU.add), r=[bX[ti], bLNS[0]], w=[bZT])


    YAB = sb("YAB", [128, 8, 512], BF16)
    bYAB = Buf("YAB")
    PA = sb("PA", [128, 192])
    bPA = Buf("PA")
    LOR = sb("LOR", [128, 3, 512])
    bLOR = Buf("LOR")
    LNX = sb("LNX", [64, 2, 512])
    bLNX = Buf("LNX")
    CAH = sb("CAH", [128, 2, 4, 30])
    bCAH = Buf("CAH")
    CSH = sb("CSH", [128, 2, 14])
    bCSH = Buf("CSH")
    SHS = sb("SHS", [128, 14, 16])
    bSHS = Buf("SHS")
    STS = sb("STS", [128, 2, 4, 64])
    bSTS = Buf("STS")
    S.dve(lambda e: e.memset(CAH[:, :, :, :], 0.0), w=[bCAH])
    S.dve(lambda e: e.memset(CSH[:, :, :], 0.0), w=[bCSH])
    S.dve(lambda e: e.memset(STS[:, :, :, :], 0.0), w=[bSTS])
    A_MU, A_W0, A_A0, A_KK, A_KA, A_RK, A_CB, A_LG, A_LB, A_CW = 0, 14, 18, 22, 26, 30, 34, 38, 42, 64

    def ab_params(j):
        v4 = lambda name: W[name][j].rearrange("(c p) -> c p", p=128)
        load_cols([W["mu_b"][j].rearrange("(c p) -> c p", p=128), v4("w0_b"), v4("a0_b"), v4("k_k_b"), v4("k_a_b"),
                   W["r_k_b"][j].rearrange("(c h) k -> c (h k)", h=2), v4("conv_a_b"), v4("ln_a_g"), v4("ln_a_b")],
                  0, PA, bPA)
        load_cols([W["conv_a_w"][j].rearrange("t (c p) -> (t c) p", p=128)], A_CW, PA, bPA)
        S.dma("sp", LOR[0:64, 0, :], W["w2_b"][j], w=[bLOR])
        S.dma("sp", LOR[64:128, 1, :], W["a2_b"][j], w=[bLOR])
        S.dma("sp", LOR[:, 2, :], W["g2_b"][j], w=[bLOR])
        S.dma("sp", LNX[:, 0, :], W["ln_x_g"][j].partition_broadcast(64), w=[bLNX])
        S.dma("sp", LNX[:, 1, :], W["ln_x_b"][j].partition_broadcast(64), w=[bLNX])

    def wkv_store(j, dst):
        i = rows_i[0]
        rows_i[0] ^= 1
        rt, rb = rows_t[i], bROWS[i]
        pt, pb = psum()
        for hp in range(4):
            S.pe(lambda e, hp=hp: e.transpose(pt[0:64, hp * 128:(hp + 1) * 128], STS[:, j, hp, :], ident(128)),
                 r=[bSTS, bCST], w=[pb])
        S.act(lambda e: e.copy(out=rt[0:64, 0:512], in_=pt[0:64, 0:512]), r=[pb], w=[rb])
        S.dma("sp", dst.rearrange("(hp h2) v k -> v hp h2 k", h2=2),
              rt[0:64, 0:512].rearrange("v (hp h2 k) -> v hp h2 k", hp=4, h2=2), r=[rb], is_out=True)

    def wkv_load(j, src):
        i = rows_i[0]
        rows_i[0] ^= 1
        rt, rb = rows_t[i], bROWS[i]
        S.dma("sp", rt[0:64, 0:512].rearrange("v (hp h2 k) -> v hp h2 k", hp=4, h2=2),
              src.rearrange("(hp h2) v k -> v hp h2 k", h2=2), w=[rb])
        pt, pb = psum()
        for hp in range(4):
            S.pe(lambda e, hp=hp: e.transpose(pt[:, hp * 64:(hp + 1) * 64], rt[0:64, hp * 128:(hp + 1) * 128], ident(64)),
                 r=[rb, bCST], w=[pb])
        S.act(lambda e: e.copy(out=STS[:, j, :, :], in_=pt[:, 0:256].rearrange("p (h v) -> p h v", h=4)),
              r=[pb], w=[bSTS])

    dbg_done = [False]
    BLf = BL[:, :, :, :].rearrange("p a b c -> p (a b c)").bitcast(F32)
    CLf = CL[:, :, :, :].rearrange("p a b c -> p (a b c)").bitcast(F32)
    bART = [Buf("ART0"), Buf("ART1")]
    bTM = [Buf("TM0"), Buf("TM1")]
    bZN = [Buf("ZN0"), Buf("ZN1")]
    bBK = bBIG[2]
    bZPQ = Buf("ZPQ")
    alias_bufs = bART + bTM + bZN

    def alias_claim(for_rwkv):
        if for_rwkv == 1:
            S.dve(lambda e: e.memset(CLf[:, 4090:4092], 0.0), r=[bBL, bCL, bZT], w=alias_bufs + [bZPQ])
        elif for_rwkv == 2:
            S.dve(lambda e: e.memset(CLf[:, 4090:4092], 0.0), r=alias_bufs + [bZPQ], w=[bZT])
        else:
            S.dve(lambda e: e.memset(CLf[:, 4090:4092], 0.0), r=alias_bufs, w=[bBL, bCL])

    def rw_views(j, col, L, p):
        V = {}
        fm = lambda slot, idx: BIG[:, slot, idx * 256:idx * 256 + 4 * L].rearrange("p (h l) -> p h l", h=4)
        V["LD"], V["LP"], V["EE"], V["KKn"] = fm(0, 0), fm(0, 1), fm(0, 2), fm(0, 3)
        V["KP"], V["AAv"], V["Bv"], V["RKR"] = fm(1, 0), fm(1, 1), fm(1, 2), fm(1, 3)
        V["BT"] = BIG[:, 2, 512:512 + 4 * L].rearrange("p (h l) -> p h l", h=4)
        V["KT"] = BIG[:, 2, 768:768 + 4 * L].rearrange("p (h l) -> p h l", h=4)
        V["BHf"], V["KHf"], V["TMPf"] = fm(3, 0), fm(3, 1), fm(3, 2)
        V["SGL"] = BIG[:, 3, 768:768 + L]
        V["TW"] = BIG[0:64, 3, 832:832 + L]
        ar0 = p * 520
        V["ART"] = CLf[:, ar0:ar0 + 8 * L].rearrange("p (h a l) -> p h a l", h=4, a=2)
        V["PL"] = CLf[:, ar0 + 512:ar0 + 516]
        tm0 = p * 2048
        V["Vt"], V["BHt"] = BLf[0:L, tm0:tm0 + 512], BLf[0:L, tm0 + 512:tm0 + 1024]
        V["KHt"], V["Gt"] = BLf[0:L, tm0 + 1024:tm0 + 1536], BLf[0:L, tm0 + 1536:tm0 + 2048]
        V["RKt"] = CLf[0:L, 1040 + 8 * p:1048 + 8 * p]
        zr = lambda i: ZT[0:L, i, 0:8 * L].rearrange("p (h l) -> p h l", h=8)
        V["Pq"] = [zr(0), zr(1)]
        V["Qq"] = [zr(2), zr(3)]
        if p == 0:
            V["Mv"], V["NAK"], V["NRB"], V["NRK"] = zr(4), zr(5), zr(6), zr(7)
        else:
            zc = lambda i: CLf[0:L, 2048 + 512 * i:2048 + 512 * i + 8 * L].rearrange("p (h l) -> p h l", h=8)
            V["Mv"], V["NAK"], V["NRB"], V["NRK"] = zc(0), zc(1), zc(2), zc(3)
        V["Zs"], V["Ws"] = BIG[0:L, 6, 0:512], BIG[0:L, 6, 512:1024]
        V["Ys"], V["Y2"] = BIG[0:L, 7, 0:512], BIG[0:L, 7, 512:1024]
        V["MU"], V["RS"] = BIG[0:L, 8, 0:8], BIG[0:L, 8, 8:16]
        return V

    hd = lambda h: (h // 2, 64 * (h % 2))
    HORD = [0, 2, 4, 6, 1, 3, 5, 7]

    def rwkv_part1(j, col, L, nlev, p):
        B0, B1, B2, B3 = bBIG[0], bBIG[1], bBIG[2], bBIG[3]
        V = rw_views(j, col, L, p)
        LD, LP, EE, KKn, KP, AAv, Bv, RKR = (V[k] for k in ("LD", "LP", "EE", "KKn", "KP", "AAv", "Bv", "RKR"))
        BT, KT, BHf, KHf, TMPf, SGL, TW = (V[k] for k in ("BT", "KT", "BHf", "KHf", "TMPf", "SGL", "TW"))
        ART, PL, Vt, BHt, KHt, Gt, RKt = (V[k] for k in ("ART", "PL", "Vt", "BHt", "KHt", "Gt", "RKt"))
        Pq, Qq, Mv, NAK, NRB, NRK = (V[k] for k in ("Pq", "Qq", "Mv", "NAK", "NRB", "NRK"))
        bA, bT, bZ = bART[p], bTM[p], bZN[p]
        R_ = QT[:, 0:4, col:col + L]
        K_ = QT[:, 4:8, col:col + L]
        V_ = QT[:, 8:12, col:col + L]
        WLv = QT[0:64, 12, col:col + L]
        ALv = QT[64:128, 12, col:col + L]
        GLv = QT[:, 13, col:col + L]
        bc4 = lambda c: PA[:, c:c + 4].unsqueeze(2).to_broadcast([128, 4, L])
        ones = CST[:, C_ONE:C_ONE + L]
        TT = lambda o, a, b, op, r, w: S.dve(lambda e: e.tensor_tensor(out=o, in0=a, in1=b, op=op), r=r, w=w)
        S.act(lambda e: e.activation(out=TW, in_=WLv, func=AF.Tanh), r=[bQA], w=[B3])
        pw, pwb = psum()
        for hp in range(4):
            S.pe(lambda e, hp=hp: e.matmul(pw[:, hp * L:(hp + 1) * L], LOR[0:64, 0, hp * 128:(hp + 1) * 128], TW,
                                           start=True, stop=True), r=[bLOR, B3], w=[pwb], inc=(hp == 3))
        pa_, pab = psum()
        for hp in range(4):
            S.pe(lambda e, hp=hp: e.matmul(pa_[:, hp * L:(hp + 1) * L], LOR[64:128, 1, hp * 128:(hp + 1) * 128], ALv,
                                           start=True, stop=True), r=[bLOR, bQA], w=[pab], inc=(hp == 3))
        for hp in range(4):
            S.act(lambda e, hp=hp: e.activation(out=LD[:, hp, :], in_=pw[:, hp * L:(hp + 1) * L], func=AF.Sigmoid,
                                                bias=PA[:, A_W0 + hp:A_W0 + hp + 1]), r=[pwb, bPA], w=[B0])
            S.act(lambda e, hp=hp: e.activation(out=AAv[:, hp, :], in_=pa_[:, hp * L:(hp + 1) * L], func=AF.Sigmoid,
                                                bias=PA[:, A_A0 + hp:A_A0 + hp + 1]), r=[pab, bPA], w=[B1])
        S.dve(lambda e: e.tensor_scalar(out=LD, in0=LD, scalar1=-0.6065306597126334, scalar2=None, op0=ALU.mult),
              r=[B0], w=[B0])
        for hp in range(4):
            S.dve(lambda e, hp=hp: e.tensor_tensor_scan(out=LP[:, hp, :], data0=ones, data1=LD[:, hp, :], initial=0.0,
                                                        op0=ALU.mult, op1=ALU.add), r=[B0, bCST], w=[B0])
        TT(KKn, K_, bc4(A_KK), ALU.mult, [bQA, bPA], [B0])
        TT(TMPf, KKn, KKn, ALU.mult, [B0], [B3])
        pss, pssb = psum()
        for hp in range(4):
            S.pe(lambda e, hp=hp: e.matmul(pss[:, hp * L:(hp + 1) * L], CST[:, C_BONE:C_BONE + 128], TMPf[:, hp, :],
                                           start=True, stop=True), r=[bCST, B3], w=[pssb], inc=(hp == 3))
        S.dve(lambda e: e.tensor_scalar(out=TMPf, in0=pss[:, 0:4 * L].rearrange("p (h l) -> p h l", h=4),
                                        scalar1=1e-24, scalar2=None, op0=ALU.max), r=[pssb], w=[B3])
        S.act(lambda e: e.activation(out=TMPf, in_=TMPf, func=AF.Sqrt), r=[B3], w=[B3])
        S.dve(lambda e: e.reciprocal(out=TMPf, in_=TMPf), r=[B3], w=[B3])
        TT(KKn, KKn, TMPf, ALU.mult, [B0, B3], [B0])
        S.dve(lambda e: e.tensor_scalar(out=TMPf, in0=AAv, scalar1=-1.0, scalar2=None, op0=ALU.add), r=[B1], w=[B3])
        TT(TMPf, TMPf, bc4(A_KA), ALU.mult, [B3, bPA], [B3])
        S.dve(lambda e: e.scalar_tensor_tensor(out=KP, in0=TMPf, scalar=1.0, in1=K_, op0=ALU.add, op1=ALU.mult),
              r=[B3, bQA], w=[B1])
        TT(Bv, KKn, AAv, ALU.mult, [B0, B1], [B1])
        S.act(lambda e: e.activation(out=EE, in_=LP, func=AF.Exp, scale=-1.0), r=[B0], w=[B0])
        TT(BT, Bv, EE, ALU.mult, [B1, B0], [bBK])
        TT(KT, KP, EE, ALU.mult, [B1, B0], [bBK])
        S.act(lambda e: e.activation(out=EE, in_=LP, func=AF.Exp), r=[B0], w=[B0])
        TT(ART[:, :, 1, :], R_, EE, ALU.mult, [bQA, B0], [bA])
        TT(TMPf, LP, LD, ALU.subtract, [B0], [B3])
        S.act(lambda e: e.activation(out=EE, in_=TMPf, func=AF.Exp), r=[B3], w=[B0])
        S.dve(lambda e: e.scalar_tensor_tensor(out=ART[:, :, 0, :], in0=KKn, scalar=-1.0, in1=EE,
                                               op0=ALU.mult, op1=ALU.mult), r=[B0], w=[bA])
        TT(TMPf, LP[:, :, L - 1:L].to_broadcast([128, 4, L]), LP, ALU.subtract, [B0], [B3])
        S.act(lambda e: e.activation(out=EE, in_=TMPf, func=AF.Exp), r=[B3], w=[B0])
        TT(BHf, Bv, EE, ALU.mult, [B1, B0], [B3])
        TT(KHf, KP, EE, ALU.mult, [B1, B0], [B3])
        S.act(lambda e: e.activation(out=PL, in_=LP[:, :, L - 1], func=AF.Exp), r=[B0], w=[bA])
        TT(RKR, R_, KP, ALU.mult, [bQA, B1], [B1])
        TT(RKR, RKR, bc4(A_RK), ALU.mult, [B1, bPA], [B1])
        S.act(lambda e: e.activation(out=SGL, in_=GLv, func=AF.Sigmoid), r=[bQA], w=[B3])
        for src, sbf, dst in ((V_, bQA, Vt), (BHf, B3, BHt), (KHf, B3, KHt)):
            pt, pb = psum()
            for hp in range(4):
                S.pe(lambda e, hp=hp, src=src, pt=pt: e.transpose(pt[0:L, hp * 128:(hp + 1) * 128], src[:, hp, :], ident(128)),
                     r=[sbf, bCST], w=[pb])
            S.act(lambda e, dst=dst, pt=pt: e.copy(out=dst, in_=pt[0:L, 0:512]), r=[pb], w=[bT])
        prk, prkb = psum()
        for hp in range(4):
            S.pe(lambda e, hp=hp: e.matmul(prk[0:L, 2 * hp:2 * hp + 2], RKR[:, hp, :], CST[:, C_BIND:C_BIND + 2],
                                           start=True, stop=True), r=[B1, bCST], w=[prkb], inc=(hp == 3))
        S.act(lambda e: e.copy(out=RKt, in_=prk[0:L, 0:8]), r=[prkb], w=[bT])
        pg, pgb = psum()
        S.pe(lambda e: e.matmul(pg[0:L, 0:512], SGL, LOR[:, 2, :], start=True, stop=True), r=[B3, bLOR], w=[pgb])
        S.act(lambda e: e.copy(out=Gt, in_=pg[0:L, 0:512]), r=[pgb], w=[bT])
        specs = [("bt", 0, Qq[0], C_MSU, bZPQ), ("bt", 1, NRB, C_MUI, bZ), ("kt", 0, NAK, C_MSU, bZ), ("kt", 1, NRK, C_MUI, bZ)]
        for lname, a_idx, dst, mcol, dbf in specs:
            pt, pb = psum()
            for h in HORD:
                hp, ho = hd(h)
                lh = (BT if lname == "bt" else KT)[ho:ho + 64, hp, :]
                if h == 1:
                    S.fence("pe")
                S.pe(lambda e, h=h, lh=lh, hp=hp, ho=ho, a_idx=a_idx, pt=pt: e.matmul(
                    pt[0:L, h * L:(h + 1) * L], lh, ART[ho:ho + 64, hp, a_idx, :], start=True, stop=True),
                    r=[bBK, bA], w=[pb], inc=(h in (6, 7)))
            S.dve(lambda e, dst=dst, mcol=mcol, pt=pt: e.tensor_tensor(
                out=dst, in0=pt[0:L, 0:8 * L].rearrange("p (h l) -> p h l", h=8),
                in1=CST[0:L, mcol:mcol + L].unsqueeze(1).to_broadcast([L, 8, L]), op=ALU.mult),
                r=[pb, bCST], w=[dbf])
        pt, pb = psum()
        for h in HORD:
            hp, ho = hd(h)
            if h == 1:
                S.fence("pe")
            S.pe(lambda e, h=h, hp=hp, ho=ho, pt=pt: e.matmul(pt[0:L, h * L:(h + 1) * L], ART[ho:ho + 64, hp, 0, :],
                                                              BT[ho:ho + 64, hp, :], start=True, stop=True),
                 r=[bBK, bA], w=[pb], inc=(h in (6, 7)))
        S.dve(lambda e, pt=pt: e.tensor_tensor(
            out=Pq[0], in0=pt[0:L, 0:8 * L].rearrange("p (h l) -> p h l", h=8),
            in1=CST[0:L, C_MSL:C_MSL + L].unsqueeze(1).to_broadcast([L, 8, L]), op=ALU.mult), r=[pb, bCST], w=[bZPQ])
        S.dve(lambda e: e.tensor_tensor(out=Mv, in0=Qq[0], in1=CST[0:L, C_ID:C_ID + L].unsqueeze(1).to_broadcast([L, 8, L]),
                                        op=ALU.add), r=[bZPQ, bCST], w=[bZ])
        cur = 0
        for lev in range(1, nlev):
            nxt = 1 - cur
            pP, pPb = psum()
            for h in range(8):
                S.pe(lambda e, h=h, cur=cur, pP=pP: e.matmul(pP[0:L, h * L:(h + 1) * L], Qq[cur][:, h, :], Pq[cur][:, h, :],
                                                             start=True, stop=True), r=[bZPQ], w=[pPb], inc=(h == 7))
            need_q = lev < nlev - 1
            if need_q:
                pQ, pQb = psum()
                for h in range(8):
                    S.pe(lambda e, h=h, cur=cur, pQ=pQ: e.matmul(pQ[0:L, h * L:(h + 1) * L], Pq[cur][:, h, :], Qq[cur][:, h, :],
                                                                 start=True, stop=True), r=[bZPQ], w=[pQb], inc=(h == 7))
            S.act(lambda e, nxt=nxt, pP=pP: e.copy(out=Pq[nxt], in_=pP[0:L, 0:8 * L].rearrange("p (h l) -> p h l", h=8)),
                  r=[pPb], w=[bZPQ])
            if need_q:
                S.dve(lambda e, nxt=nxt, pQ=pQ: e.tensor_copy(out=Qq[nxt], in_=pQ[0:L, 0:8 * L].rearrange("p (h l) -> p h l", h=8)),
                      r=[pQb], w=[bZPQ])
            pM, pMb = psum()
            for h in range(8):
                S.pe(lambda e, h=h, nxt=nxt, pM=pM: e.matmul(pM[0:L, h * L:(h + 1) * L], Pq[nxt][:, h, :], Mv[:, h, :],
                                                             start=True, stop=True), r=[bZPQ, bZ], w=[pMb], inc=(h == 7))
            S.dve(lambda e, pM=pM: e.tensor_tensor(out=Mv, in0=Mv, in1=pM[0:L, 0:8 * L].rearrange("p (h l) -> p h l", h=8),
                                                   op=ALU.add), r=[bZ, pMb], w=[bZ])
            cur = nxt

    def rwkv_part2(j, col, L, p):
        B6, B7, B8 = bBIG[6], bBIG[7], bBIG[8]
        V = rw_views(j, col, L, p)
        ART, PL, Vt, BHt, KHt, Gt, RKt = (V[k] for k in ("ART", "PL", "Vt", "BHt", "KHt", "Gt", "RKt"))
        Mv, NAK, NRB, NRK = (V[k] for k in ("Mv", "NAK", "NRB", "NRK"))
        Zs, Ws, Ys, Y2, MU, RS = (V[k] for k in ("Zs", "Ws", "Ys", "Y2", "MU", "RS"))
        bA, bT, bZ = bART[p], bTM[p], bZN[p]
        ST = STS[:, j, :, :]
        TT = lambda o, a, b, op, r, w: S.dve(lambda e: e.tensor_tensor(out=o, in0=a, in1=b, op=op), r=r, w=w)
        pz, pzb = psum()
        S.fence("pe")
        for i_, h in enumerate((1, 3, 5, 7)):
            hp, ho = hd(h)
            S.pe(lambda e, h=h, hp=hp, ho=ho, i_=i_: e.matmul(pz[0:L, h * 64:(h + 1) * 64], ART[ho:ho + 64, hp, 0, :],
                                                              ST[ho:ho + 64, hp, :], start=(i_ == 0), stop=False,
                                                              skip_group_check=True),
                 r=[bA, bSTS], w=[pzb], inc=True)
        S.fence("pe")
        for h in (0, 2, 4, 6):
            hp, ho = hd(h)
            S.pe(lambda e, h=h, hp=hp, ho=ho: e.matmul(pz[0:L, h * 64:(h + 1) * 64], ART[ho:ho + 64, hp, 0, :],
                                                       ST[ho:ho + 64, hp, :], start=False, stop=False,
                                                       skip_group_check=True),
                 r=[bA, bSTS], w=[pzb], inc=False)
        for h in range(8):
            S.pe(lambda e, h=h: e.matmul(pz[0:L, h * 64:(h + 1) * 64], NAK[:, h, :], Vt[:, h * 64:(h + 1) * 64],
                                         start=False, stop=True, skip_group_check=True),
                 r=[bZ, bT], w=[pzb], inc=(h == 7))
        S.act(lambda e: e.copy(out=Zs, in_=pz[0:L, 0:512]), r=[pzb], w=[B6])
        pw2, pw2b = psum()
        for h in range(8):
            S.pe(lambda e, h=h: e.matmul(pw2[0:L, h * 64:(h + 1) * 64], Mv[:, h, :], Zs[:, h * 64:(h + 1) * 64],
                                         start=True, stop=True), r=[bZ, B6], w=[pw2b], inc=(h == 7))
        S.act(lambda e: e.copy(out=Ws, in_=pw2[0:L, 0:512]), r=[pw2b], w=[B6])
        py, pyb = psum()
        S.fence("pe")
        for i_, h in enumerate((1, 3, 5, 7)):
            hp, ho = hd(h)
            S.pe(lambda e, h=h, hp=hp, ho=ho, i_=i_: e.matmul(py[0:L, h * 64:(h + 1) * 64], ART[ho:ho + 64, hp, 1, :],
                                                              ST[ho:ho + 64, hp, :], start=(i_ == 0), stop=False,
                                                              skip_group_check=True),
                 r=[bA, bSTS], w=[pyb], inc=True)
        S.fence("pe")
        for h in (0, 2, 4, 6):
            hp, ho = hd(h)
            S.pe(lambda e, h=h, hp=hp, ho=ho: e.matmul(py[0:L, h * 64:(h + 1) * 64], ART[ho:ho + 64, hp, 1, :],
                                                       ST[ho:ho + 64, hp, :], start=False, stop=False,
                                                       skip_group_check=True),
                 r=[bA, bSTS], w=[pyb], inc=False)
        for h in range(8):
            S.pe(lambda e, h=h: e.matmul(py[0:L, h * 64:(h + 1) * 64], NRB[:, h, :], Ws[:, h * 64:(h + 1) * 64],
                                         start=False, stop=False, skip_group_check=True),
                 r=[bZ, B6], w=[pyb], inc=False)
            S.pe(lambda e, h=h: e.matmul(py[0:L, h * 64:(h + 1) * 64], NRK[:, h, :], Vt[:, h * 64:(h + 1) * 64],
                                         start=False, stop=True, skip_group_check=True),
                 r=[bZ, bT], w=[pyb], inc=(h == 7))
        psn, psnb = psum()
        for h in range(8):
            hp, ho = hd(h)
            S.pe(lambda e, h=h, hp=hp, ho=ho: e.matmul(psn[ho:ho + 64, hp * 64:(hp + 1) * 64], BHt[:, h * 64:(h + 1) * 64],
                                                       Ws[:, h * 64:(h + 1) * 64], start=True, stop=False),
                 r=[bT, B6], w=[psnb], inc=False)
            S.pe(lambda e, h=h, hp=hp, ho=ho: e.matmul(psn[ho:ho + 64, hp * 64:(hp + 1) * 64], KHt[:, h * 64:(h + 1) * 64],
                                                       Vt[:, h * 64:(h + 1) * 64], start=False, stop=True),
                 r=[bT], w=[psnb], inc=(h == 7))
        S.act(lambda e: e.copy(out=Ys, in_=py[0:L, 0:512]), r=[pyb], w=[B7])
        TT(ST, ST, PL.unsqueeze(2).to_broadcast([128, 4, 64]), ALU.mult, [bSTS, bA], [bSTS])
        TT(ST, ST, psn[:, 0:256].rearrange("p (h v) -> p h v", h=4), ALU.add, [bSTS, psnb], [bSTS])
        Y3 = Ys.rearrange("p (h v) -> p h v", h=8)
        Y23 = Y2.rearrange("p (h v) -> p h v", h=8)
        V3 = Vt.rearrange("p (h v) -> p h v", h=8)
        b8 = lambda ap: ap.unsqueeze(2).to_broadcast([L, 8, 64])
        S.dve(lambda e: e.reduce_sum(out=MU, in_=Y3, axis=AX.X), r=[B7], w=[B8])
        S.dve(lambda e: e.tensor_scalar(out=MU, in0=MU, scalar1=1.0 / 64, scalar2=None, op0=ALU.mult), r=[B8], w=[B8])
        TT(Y3, Y3, b8(MU), ALU.subtract, [B7, B8], [B7])
        TT(Y23, Y3, Y3, ALU.mult, [B7], [B7])
        S.dve(lambda e: e.reduce_sum(out=RS, in_=Y23, axis=AX.X), r=[B7], w=[B8])
        S.dve(lambda e: e.tensor_scalar(out=RS, in0=RS, scalar1=1.0 / 64, scalar2=GN_EPS, op0=ALU.mult, op1=ALU.add),
              r=[B8], w=[B8])
        S.act(lambda e: e.activation(out=RS, in_=RS, func=AF.Sqrt), r=[B8], w=[B8])
        S.dve(lambda e: e.reciprocal(out=RS, in_=RS), r=[B8], w=[B8])
        TT(Y3, Y3, b8(RS), ALU.mult, [B7, B8], [B7])
        TT(Ys, Ys, LNX[0:L, 0, :], ALU.mult, [B7, bLNX], [B7])
        TT(Ys, Ys, LNX[0:L, 1, :], ALU.add, [B7, bLNX], [B7])
        TT(Y23, V3, b8(RKt), ALU.mult, [bT], [B7])
        TT(Ys, Ys, Y2, ALU.add, [B7], [B7])
        TT(Ys, Ys, Gt, ALU.mult, [B7, bT], [B7])
        pT, pTb = psum()
        for hp in range(4):
            S.pe(lambda e, hp=hp: e.transpose(pT[:, hp * L:(hp + 1) * L], Ys[:, hp * 128:(hp + 1) * 128], ident(L)),
                 r=[B7, bCST], w=[pTb])
        S.act(lambda e: e.copy(out=YAB[:, 4:8, col:col + L], in_=pT[:, 0:4 * L].rearrange("p (h l) -> p h l", h=4)),
              r=[pTb], w=[bYAB])

    def mixer_ab(l, ti):
        j = l // 2
        c0, n, nseq, T = TILES[ti]
        E = 30 + T
        w_in = W["w_in_ab"][j]
        cast_xb(ti)
        UEXT = lambda c: BIG[:, c, 0:nseq * E].rearrange("p (s e) -> p s e", s=nseq)
        CVA = lambda c: BIG[:, 4 + c, 0:n]
        UALL = BIG[:, 0:4, 0:nseq * E].rearrange("p c (s e) -> p c s e", s=nseq)
        if nseq == 1:
            if ti == 0:
                S.dve(lambda e: e.memset(UALL[:, :, 0, 0:30], 0.0), w=bBIG[0:4])
            else:
                S.dve(lambda e: e.tensor_copy(out=UALL[:, :, 0, 0:30], in_=CAH[:, j, :, :]), r=[bCAH], w=bBIG[0:4])
        else:
            for q in range(4):
                i = rows_i[0]
                rows_i[0] ^= 1
                rt, rb = rows_t[i], bROWS[i]
                for b4 in range(4):
                    S.dma("sp", rt[30 * b4:30 * b4 + 30, 0:512], st_conv_a[4 * q + b4, j], w=[rb])
                pt, pb = psum()
                for c in range(4):
                    S.pe(lambda e, c=c: e.transpose(pt[:, c * 120:(c + 1) * 120], rt[0:120, c * 128:(c + 1) * 128], ident(120)),
                         r=[rb, bCST], w=[pb])
                for c in range(4):
                    S.act(lambda e, c=c, q=q: e.copy(out=UEXT(c)[:, 4 * q:4 * q + 4, 0:30],
                                                     in_=pt[:, c * 120:(c + 1) * 120].rearrange("p (b r) -> p b r", b=4)),
                          r=[pb], w=[bBIG[c]])
        for c in range(4):
            wv, wvb = wslab(w_in, 0, 8, c * 128, 128, key=("inab", j))
            wg, wgb = wslab(w_in, 0, 8, 512 + c * 128, 128, key=("inab", j))
            pv, pvb = psum()
            for k in range(8):
                S.pe(lambda e, k=k: e.matmul(pv[:, 0:n], wv[:, k, :], XB[:, k, 0:n], start=(k == 0), stop=(k == 7)),
                     r=[wvb, bXB], w=[pvb], inc=(k == 7))
            pg, pgb = psum()
            for k in range(8):
                S.pe(lambda e, k=k: e.matmul(pg[:, 0:n], wg[:, k, :], XB[:, k, 0:n], start=(k == 0), stop=(k == 7)),
                     r=[wgb, bXB], w=[pgb], inc=(k == 7))
            sg = LNS[:, 0, 0:n]
            S.act(lambda e: e.activation(out=sg, in_=pg[:, 0:n], func=AF.Sigmoid), r=[pgb], w=[bLNS[0]])
            S.dve(lambda e, c=c: e.tensor_tensor(out=UEXT(c)[:, :, 30:E], in0=sg.rearrange("p (s t) -> p s t", s=nseq),
                                                 in1=pv[:, 0:n].rearrange("p (s t) -> p s t", s=nseq), op=ALU.mult),
                  r=[bLNS[0], pvb], w=[bBIG[c]])
        for c in range(4):
            acc = CVA(c).rearrange("p (s t) -> p s t", s=nseq)
            S.dve(lambda e, c=c, acc=acc: e.tensor_scalar(out=acc, in0=UEXT(c)[:, :, 0:T], scalar1=PA[:, A_CW + c:A_CW + c + 1],
                                                          scalar2=PA[:, A_CB + c:A_CB + c + 1], op0=ALU.mult, op1=ALU.add),
                  r=[bBIG[c], bPA], w=[bBIG[4 + c]])
            for t_ in range(1, 31):
                S.dve(lambda e, c=c, acc=acc, t_=t_: e.scalar_tensor_tensor(
                    out=acc, in0=UEXT(c)[:, :, t_:t_ + T], scalar=PA[:, A_CW + 4 * t_ + c:A_CW + 4 * t_ + c + 1], in1=acc,
                    op0=ALU.mult, op1=ALU.add), r=[bBIG[c], bPA, bBIG[4 + c]], w=[bBIG[4 + c]])
        if nseq == 1:
            S.dve(lambda e: e.tensor_copy(out=CAH[:, j, :, :], in_=UALL[:, :, 0, T:T + 30]), r=bBIG[0:4], w=[bCAH])
            if ti == 3:
                def dst(rt, rb, done, g):
                    S.dma("sp", o_conv_p[j], rt[0:30, 0:512], r=[rb], is_out=True)
                emit_rows(lambda c: CAH[:, j, c, :], 30, 4, dst, [bCAH])
        else:
            for q in range(4):
                i = rows_i[0]
                rows_i[0] ^= 1
                rt, rb = rows_t[i], bROWS[i]
                pt, pb = psum()
                for c in range(4):
                    stg = LNS[:, c, 0:120]
                    S.dve(lambda e, c=c, q=q, stg=stg: e.tensor_copy(out=stg.rearrange("p (b r) -> p b r", b=4),
                                                                     in_=UEXT(c)[:, 4 * q:4 * q + 4, T:T + 30]),
                          r=[bBIG[c]], w=[bLNS[c]])
                    S.pe(lambda e, c=c, stg=stg: e.transpose(pt[0:120, c * 128:(c + 1) * 128], stg, ident(128)),
                         r=[bLNS[c], bCST], w=[pb])
                S.act(lambda e: e.copy(out=rt[0:120, 0:512], in_=pt[0:120, 0:512]), r=[pb], w=[rb])
                for b4 in range(4):
                    S.dma("sp", o_conv_s[4 * q + b4, j], rt[30 * b4:30 * b4 + 30, 0:512], r=[rb], is_out=True)
        def outf(k, tmp, tb, g, b):
            S.act(lambda e: e.activation(out=YAB[:, k, 0:n], in_=tmp, func=AF.Silu, scale=g, bias=b),
                  r=[tb, bPA], w=[bYAB])
        layer_norm(lambda k: CVA(k), bBIG[4:8], 4, n, A_LG, A_LB, outf, ptile=PA)
        ZTf = ZT[:, :, :].rearrange("p a b -> p (a b)")
        PBX = lambda i: ZTf[:, i * 1024:i * 1024 + nseq * (1 + T)].rearrange("p (s e) -> p s e", s=nseq)
        DTMP = ZTf[:, 2048:2048 + n].rearrange("p (s t) -> p s t", s=nseq)
        if nseq > 1:
            for half in range(2):
                i = rows_i[0]
                rows_i[0] ^= 1
                rt, rb = rows_t[i], bROWS[i]
                S.dma("sp", rt[0:16, 0:896], st_shift[:, j, half * 896:(half + 1) * 896], w=[rb])
                pt, pb = psum()
                for m in range(7):
                    S.pe(lambda e, m=m: e.transpose(pt[:, m * 16:(m + 1) * 16], rt[0:16, m * 128:(m + 1) * 128], ident(16)),
                         r=[rb, bCST], w=[pb])
                S.act(lambda e, half=half: e.copy(out=SHS[:, 7 * half:7 * half + 7, :],
                                                  in_=pt[:, 0:112].rearrange("p (m b) -> p m b", m=7)), r=[pb], w=[bSHS])
        for m in range(14):
            if m % 2 == 0:
                wp, wpb = wslab(w_in, 0, 8, 1024 + m * 128, 256, key=("inab", j))
            pp, ppb = psum()
            for k in range(8):
                S.pe(lambda e, k=k, m=m: e.matmul(pp[:, 0:n], wp[:, k, (m % 2) * 128:(m % 2 + 1) * 128], XB[:, k, 0:n],
                                                  start=(k == 0), stop=(k == 7)), r=[wpb, bXB], w=[ppb], inc=(k == 7))
            pbx = PBX(m % 2)
            S.act(lambda e, pbx=pbx: e.copy(out=pbx[:, :, 1:1 + T], in_=pp[:, 0:n].rearrange("p (s t) -> p s t", s=nseq)),
                  r=[ppb], w=[bZT])
            if nseq == 1:
                if ti == 0:
                    S.dve(lambda e, pbx=pbx: e.memset(pbx[:, :, 0:1], 0.0), w=[bZT])
                else:
                    S.dve(lambda e, pbx=pbx, m=m: e.tensor_copy(out=pbx[:, 0, 0:1], in_=CSH[:, j, m:m + 1]), r=[bCSH], w=[bZT])
            else:
                S.dve(lambda e, pbx=pbx, m=m: e.tensor_copy(out=pbx[:, :, 0], in_=SHS[:, m, :]), r=[bSHS], w=[bZT])
            S.dve(lambda e, pbx=pbx: e.tensor_tensor(out=DTMP, in0=pbx[:, :, 0:T], in1=pbx[:, :, 1:1 + T], op=ALU.subtract),
                  r=[bZT], w=[bZT])
            S.dve(lambda e, pbx=pbx, m=m: e.scalar_tensor_tensor(
                out=QT[:, m, 0:n].rearrange("p (s t) -> p s t", s=nseq), in0=DTMP, scalar=PA[:, A_MU + m:A_MU + m + 1],
                in1=pbx[:, :, 1:1 + T], op0=ALU.mult, op1=ALU.add), r=[bZT, bPA], w=[bQA])
            if nseq == 1:
                S.dve(lambda e, pbx=pbx, m=m: e.tensor_copy(out=CSH[:, j, m:m + 1], in_=pbx[:, 0, T:T + 1]), r=[bZT], w=[bCSH])
            else:
                S.dve(lambda e, pbx=pbx, m=m: e.tensor_copy(out=SHS[:, m, :], in_=pbx[:, :, T]), r=[bZT], w=[bSHS])
        if ti == 3:
            i = rows_i[0]
            rows_i[0] ^= 1
            rt, rb = rows_t[i], bROWS[i]
            pt, pb = psum()
            S.pe(lambda e: e.transpose(pt[0:14, 0:128], CSH[:, j, :], ident(128)), r=[bCSH, bCST], w=[pb])
            S.act(lambda e: e.copy(out=rt[0:14, 0:128], in_=pt[0:14, 0:128]), r=[pb], w=[rb])
            S.dma("sp", o_shift_p[j].rearrange("(c p) -> c p", p=128), rt[0:14, 0:128], r=[rb], is_out=True)
        if ti == 4:
            for half in range(2):
                i = rows_i[0]
                rows_i[0] ^= 1
                rt, rb = rows_t[i], bROWS[i]
                for m0, mn in ((0, 4), (4, 3)):
                    pt, pb = psum()
                    for mm in range(mn):
                        m = m0 + mm
                        S.pe(lambda e, m=m, mm=mm, half=half: e.transpose(pt[0:16, mm * 128:(mm + 1) * 128],
                                                                          SHS[:, 7 * half + m, :], ident(128)),
                             r=[bSHS, bCST], w=[pb])
                    S.act(lambda e, m0=m0, mn=mn, pt=pt: e.copy(out=rt[0:16, m0 * 128:(m0 + mn) * 128], in_=pt[0:16, 0:mn * 128]),
                          r=[pb], w=[rb])
                S.dma("sp", o_shift_s[:, j, half * 896:(half + 1) * 896], rt[0:16, 0:896], r=[rb], is_out=True)
        if cfg.get("norwkv"):
            S.dve(lambda e: e.memset(YAB[:, 4:8, 0:n], 0.0), w=[bYAB])
        elif nseq == 1:
            alias_claim(1)
            rwkv_part1(j, 0, 64, 6, 0)
            for ch in range(8):
                S.begin_rec()
                ps_pool[0] = 0
                if ch + 1 < 8:
                    rwkv_part1(j, (ch + 1) * 64, 64, 6, (ch + 1) % 2)
                ra = S.end_rec()
                S.begin_rec()
                ps_pool[0] = 1
                rwkv_part2(j, ch * 64, 64, ch % 2)
                rb_ = S.end_rec()
                ps_pool[0] = None
                S.merge(ra, rb_)
            if ti == 3:
                wkv_store(j, o_wkv_p[j])
        else:
            alias_claim(1)
            rwkv_part1(j, 0, 4, 2, 0)
            for b in range(NSB):
                S.begin_rec()
                ps_pool[0] = 0
                if b + 1 < NSB:
                    rwkv_part1(j, (b + 1) * 4, 4, 2, (b + 1) % 2)
                ra = S.end_rec()
                S.begin_rec()
                ps_pool[0] = 1
                wkv_load(j, st_wkv[b, j])
                rwkv_part2(j, b * 4, 4, b % 2)
                wkv_store(j, o_wkv_s[b, j])
                rb_ = S.end_rec()
                ps_pool[0] = None
                S.merge(ra, rb_)
        if not cfg.get("norwkv"):
            alias_claim(2)
        for oc in range(8):
            wo, wob = wslab(W["w_out_ab"][j], 0, 8, oc * 128, 128, key=("outab", j))
            po, pob = psum()
            for k in range(8):
                S.pe(lambda e, k=k: e.matmul(po[:, 0:n], wo[:, k, :], YAB[:, k, 0:n], start=(k == 0), stop=(k == 7)),
                     r=[wob, bYAB], w=[pob], inc=(k == 7))
            S.dve(lambda e, oc=oc: e.scalar_tensor_tensor(out=ZT[:, oc, 0:n], in0=X[:, oc, 0:n], scalar=ALPHA, in1=po[:, 0:n],
                                                          op0=ALU.mult, op1=ALU.add), r=[bX[ti], pob], w=[bZT])


    def mixer_none(l, ti):
        c0, n, nseq, T = TILES[ti]
        for k in range(8):
            S.dve(lambda e, k=k: e.tensor_scalar(out=ZT[:, k, 0:n], in0=X[:, k, 0:n], scalar1=ALPHA, scalar2=None,
                                                 op0=ALU.mult), r=[bX[ti]], w=[bZT])

    def ln1(l, ti):
        c0, n, nseq, T = TILES[ti]

        def outf(k, tmp, tb, g, b):
            S.act(lambda e: e.activation(out=X[:, k, 0:n], in_=tmp, func=AF.Identity, scale=g, bias=b),
                  r=[tb, bPAR], w=[bX[ti]])
        layer_norm(lambda k: ZT[:, k, 0:n], bZT, 8, n, 104, 112, outf)

    tiles = cfg.get("tiles", [0, 1, 2, 3, 4])
    for ti in tiles:
        load_x(ti)
        for l in range(depth):
            ffn_params(l)
            mix = cfg.get("mix", "full")
            if l % 2 == 1 and mix in ("full", "c"):
                s5_setup(l // 2)
                if ti == 4:
                    s5_sample_state(l // 2)
                if cfg.get("s5dbg", 0) == 1:
                    mixer_none(l, ti)
                else:
                    mixer_c(l, ti)
            elif l % 2 == 0 and mix in ("full", "ab"):
                ab_params(l // 2)
                mixer_ab(l, ti)
            else:
                mixer_none(l, ti)
            ln1(l, ti)
            if ti == 4:
                ffn_sample_state(l)
            ffn_tile(l, ti)
        store_y(ti)
    S.finish()
    print("instructions:", S.ninst, {k: v for k, v in S.cnt.items()}, "sbuf_left", nc.sbuf_bytes_remaining)


WSHAPES = [
    ("w_in_ab", (2, 1024, 2816)), ("conv_a_w", (2, 31, 512)), ("conv_a_b", (2, 512)), ("ln_a_g", (2, 512)),
    ("ln_a_b", (2, 512)), ("mu_b", (2, 1792)), ("w0_b", (2, 512)), ("w2_b", (2, 64, 512)), ("a0_b", (2, 512)),
    ("a2_b", (2, 64, 512)), ("g2_b", (2, 128, 512)), ("k_k_b", (2, 512)), ("k_a_b", (2, 512)), ("r_k_b", (2, 8, 64)),
    ("ln_x_g", (2, 512)), ("ln_x_b", (2, 512)), ("w_out_ab", (2, 1024, 1024)), ("w_in_c", (2, 1024, 1024)),
    ("lam_re", (2, 64, 64)), ("lam_im", (2, 64, 64)), ("log_dt", (2, 64)), ("b_re", (2, 64, 64, 16)),
    ("b_im", (2, 64, 64, 16)), ("c_re", (2, 64, 16, 64)), ("c_im", (2, 64, 16, 64)), ("d_skip", (2, 1024)),
    ("w_out_c", (2, 1024, 2048)), ("w_up", (4, 1024, 5632)), ("conv_f_w", (4, 3, 2816)), ("conv_f_b", (4, 2816)),
    ("w_down", (4, 2816, 1024)), ("ln1_g", (4, 1024)), ("ln1_b", (4, 1024)), ("ln2_g", (4, 1024)), ("ln2_b", (4, 1024)),
]

OUT_NAMES = ["o_yp", "o_ys", "o_conv_p", "o_conv_s", "o_shift_p", "o_shift_s", "o_wkv_p", "o_wkv_s",
             "o_re_p", "o_re_s", "o_im_p", "o_im_s", "o_ffn_p", "o_ffn_s"]


def run(inputs, cfg=None, cores=None):
    cfg = cfg or {}
    cores = list(range(NCORES)) if cores is None else cores
    f = lambda a: np.ascontiguousarray(np.asarray(a, dtype=np.float32))
    consts = make_consts()
    wmap = {name: f(inputs[name]) for name, _ in WSHAPES}
    in_maps = []
    for c in cores:
        sl = slice(c * NSB, (c + 1) * NSB)
        m = dict(wmap)
        m["xp"] = f(inputs["x_prompt"][c])
        m["xs"] = f(inputs["x_sample"][sl]).reshape(NSB * TS, D)
        m["st_conv_a"] = f(inputs["state_conv_a"][sl])
        m["st_shift"] = f(inputs["state_shift_b"][sl])
        m["st_wkv"] = f(inputs["state_wkv_b"][sl])
        m["st_re"] = f(inputs["state_ssm_re"][sl])
        m["st_im"] = f(inputs["state_ssm_im"][sl])
        m["st_ffn"] = f(inputs["state_conv_ffn"][sl])
        m["cst"] = consts
        in_maps.append(m)
    nc = build(cfg)
    res = run_bass_kernel_spmd(nc, in_maps, core_ids=cores)
    return res.results


def kernel(**inputs):
    rs = run(inputs)
    outs = []
    for name in OUT_NAMES:
        if name.endswith("p"):
            outs.append(np.stack([np.asarray(r[name]) for r in rs], 0))
        else:
            outs.append(np.concatenate([np.asarray(r[name]) for r in rs], 0))
    outs[0] = outs[0].reshape(8, SEQ, D)
    outs[1] = outs[1].reshape(128, TS, D)
    return tuple(o.astype(np.float32) for o in outs)
```

```python
import numpy as np
from contextlib import ExitStack
import concourse.bass as bass
import concourse.mybir as mybir
from concourse.bass_utils import run_bass_kernel_spmd

F32 = mybir.dt.float32
BF16 = mybir.dt.bfloat16
AF = mybir.ActivationFunctionType
ALU = mybir.AluOpType
AX = mybir.AxisListType

D = 1024
SEQ = 2048
NSB = 16
TS = 4
NTOK = SEQ + NSB * TS
DEPTH = 4
DFF = 2816
NFF = 22
DBP = 1792
ALPHA = (2 * DEPTH) ** 0.25
LN_EPS = 1e-5
GN_EPS = 64 * 1e-5
NCORES = 8

C_ID = 0
C_MSU = 128
C_MUI = 192
C_MSL = 256
C_BONE = 320
C_BIND = 448
C_CM = 450
C_ONE = 962
C_W = 1090


def make_consts():
    c = np.zeros((128, C_W), np.float32)
    c[:, C_ID:C_ID + 128] = np.eye(128)
    i = np.arange(64)
    c[:64, C_MSU:C_MSU + 64] = (i[:, None] < i[None, :])
    c[:64, C_MUI:C_MUI + 64] = (i[:, None] <= i[None, :])
    c[:64, C_MSL:C_MSL + 64] = (i[:, None] > i[None, :])
    c[:64, C_BONE:C_BONE + 64] = 1.0
    c[64:, C_BONE + 64:C_BONE + 128] = 1.0
    c[:64, C_BIND] = 1.0
    c[64:, C_BIND + 1] = 1.0
    for q in range(4):
        m = np.zeros((128, 128), np.float32)
        m[32 * q:32 * q + 16, 0:64] = 1.0
        m[32 * q + 16:32 * q + 32, 64:128] = 1.0
        c[:, C_CM + 128 * q:C_CM + 128 * (q + 1)] = m
    c[:, C_ONE:C_ONE + 128] = 1.0
    return c


class Buf:
    __slots__ = ("name", "w", "r", "excl")

    def __init__(self, name, excl=False):
        self.name = name
        self.w = None
        self.r = {}
        self.excl = excl


class Sched:
    EPOCH = 30000

    def __init__(self, nc, es):
        self.nc = nc
        self.es = es
        self.engs = {"pe": nc.tensor, "dve": nc.vector, "act": nc.scalar, "pool": nc.gpsimd, "sp": nc.sync}
        self.cnt = {k: 0 for k in self.engs}
        self.esem = {}
        self.waited = {k: {} for k in self.engs}
        self.dq = {"sp": [es.enter_context(nc.semaphore(f"dsp{i}")) for i in range(24)],
                   "pool": [es.enter_context(nc.semaphore(f"dpl{i}")) for i in range(8)]}
        self.dcnt = {q: [0] * len(v) for q, v in self.dq.items()}
        self.dnext = {q: 0 for q in self.dq}
        self.out_tokens = []
        self.skip_same = {"pe": True, "dve": False, "act": False, "pool": False, "sp": True}
        self.ninst = 0
        self.rec = None

    def begin_rec(self):
        self.rec = []

    def end_rec(self):
        r, self.rec = self.rec, None
        return r

    @staticmethod
    def merge(a, b):
        i = j = 0
        while i < len(a) or j < len(b):
            if j >= len(b) or (i < len(a) and i * max(len(b), 1) <= j * max(len(a), 1)):
                f, args = a[i]
                i += 1
            else:
                f, args = b[j]
                j += 1
            f(*args)

    def _semh(self, key):
        if key[0] == "e":
            if key not in self.esem:
                self.esem[key] = self.es.enter_context(self.nc.semaphore(f"e_{key[1]}_{key[2]}"))
            return self.esem[key]
        return self.dq[key[1]][key[2]]

    def _deps(self, k, r, w):
        deps = {}

        def add(t):
            if t is None:
                return
            key, val = t
            if deps.get(key, 0) < val:
                deps[key] = val
        for b in r:
            add(b.w)
            if b.excl:
                for t in b.r.values():
                    add(t)
        for b in w:
            add(b.w)
            for t in b.r.values():
                add(t)
        return deps

    def _wait(self, k, deps):
        eng = self.engs[k]
        for key, val in deps.items():
            if key[0] == "e" and key[1] == k and self.skip_same[k]:
                continue
            if self.waited[k].get(key, 0) >= val:
                continue
            eng.wait_ge(self._semh(key), val)
            self.waited[k][key] = val

    def op(self, k, fn, r=(), w=(), inc=True):
        if self.rec is not None:
            self.rec.append((self._op, (k, fn, list(r), list(w), inc)))
            return None
        return self._op(k, fn, r, w, inc)

    def _op(self, k, fn, r=(), w=(), inc=True):
        self._wait(k, self._deps(k, r, w))
        ins = fn(self.engs[k])
        c = self.cnt[k] + 1
        key = ("e", k, (c - 1) // self.EPOCH)
        tok = (key, (c - 1) % self.EPOCH + 1)
        if inc:
            ins.then_inc(self._semh(key), 1)
            self.cnt[k] = c
        for b in r:
            b.r[k] = tok
        for b in w:
            b.w = tok
            b.r = {}
        self.ninst += 1
        return ins

    def fence(self, k):
        if self.rec is not None:
            self.rec.append((self._fence, (k,)))
            return
        self._fence(k)

    def _fence(self, k):
        c = self.cnt[k]
        if c == 0:
            return
        key = ("e", k, (c - 1) // self.EPOCH)
        self.engs[k].wait_ge(self._semh(key), (c - 1) % self.EPOCH + 1)

    def dve(self, fn, r=(), w=()):
        return self.op("dve", fn, r, w)

    def act(self, fn, r=(), w=()):
        return self.op("act", fn, r, w)

    def pool(self, fn, r=(), w=()):
        return self.op("pool", fn, r, w)

    def pe(self, fn, r=(), w=(), inc=True):
        return self.op("pe", fn, r, w, inc)

    def dma(self, q, out, in_, r=(), w=(), is_out=False):
        if self.rec is not None:
            self.rec.append((self._dma, (q, out, in_, list(r), list(w), is_out)))
            return
        self._dma(q, out, in_, r, w, is_out)

    def _dma(self, q, out, in_, r=(), w=(), is_out=False):
        deps = self._deps(q, r, w)
        idx = self.dnext[q]
        self.dnext[q] = (idx + 1) % len(self.dq[q])
        key = ("d", q, idx)
        if self.dcnt[q][idx] > 0:
            v = 16 * self.dcnt[q][idx]
            if deps.get(key, 0) < v:
                deps[key] = v
        self._wait(q, deps)
        self.engs[q].dma_start(out=out, in_=in_).then_inc(self.dq[q][idx], 16)
        self.dcnt[q][idx] += 1
        tok = (key, 16 * self.dcnt[q][idx])
        for b in r:
            b.r[key] = tok
        for b in w:
            b.w = tok
            b.r = {}
        if is_out:
            self.out_tokens.append(tok)
        self.ninst += 1

    def finish(self):
        deps = {}
        for key, val in self.out_tokens:
            if deps.get(key, 0) < val:
                deps[key] = val
        self._wait("sp", deps)


def build(cfg):
    depth = cfg.get("depth", DEPTH)
    nc = bass.Bass("TRN2", target_bir_lowering=False)
    es = ExitStack()
    with es:
        _build(nc, es, cfg, depth)
    return nc


def _build(nc, es, cfg, depth):
    S = Sched(nc, es)
    if cfg.get("unsafe_skip"):
        S.skip_same["dve"] = True
        S.skip_same["act"] = True

    def din(name, shape):
        return nc.dram_tensor(name, list(shape), F32, kind="ExternalInput").ap()

    def dout(name, shape):
        return nc.dram_tensor(name, list(shape), F32, kind="ExternalOutput").ap()

    def sb(name, shape, dt=F32):
        return es.enter_context(nc.sbuf_tensor(name, list(shape), dt))

    xp = din("xp", [SEQ, D])
    xs = din("xs", [NSB * TS, D])
    st_conv_a = din("st_conv_a", [NSB, 2, 30, 512])
    st_shift = din("st_shift", [NSB, 2, DBP])
    st_wkv = din("st_wkv", [NSB, 2, 8, 64, 64])
    st_re = din("st_re", [NSB, 2, 64, 64])
    st_im = din("st_im", [NSB, 2, 64, 64])
    st_ffn = din("st_ffn", [NSB, 4, 2, DFF])
    cst = din("cst", [128, C_W])
    W = {}
    for name, shape in WSHAPES:
        W[name] = din(name, shape)
    o_yp = dout("o_yp", [SEQ, D])
    o_ys = dout("o_ys", [NSB * TS, D])
    o_conv_p = dout("o_conv_p", [2, 30, 512])
    o_conv_s = dout("o_conv_s", [NSB, 2, 30, 512])
    o_shift_p = dout("o_shift_p", [2, DBP])
    o_shift_s = dout("o_shift_s", [NSB, 2, DBP])
    o_wkv_p = dout("o_wkv_p", [2, 8, 64, 64])
    o_wkv_s = dout("o_wkv_s", [NSB, 2, 8, 64, 64])
    o_re_p = dout("o_re_p", [2, 64, 64])
    o_re_s = dout("o_re_s", [NSB, 2, 64, 64])
    o_im_p = dout("o_im_p", [2, 64, 64])
    o_im_s = dout("o_im_s", [NSB, 2, 64, 64])
    o_ffn_p = dout("o_ffn_p", [4, 2, DFF])
    o_ffn_s = dout("o_ffn_s", [NSB, 4, 2, DFF])
    o_dbg = dout("o_dbg", [128, 2048]) if cfg.get("dbg") else None
    o_dbgB = dout("o_dbgB", [128, 9216]) if cfg.get("dbgr") else None
    o_dbgZ = dout("o_dbgZ", [128, 4096]) if cfg.get("dbgr") else None
    o_dbgS = dout("o_dbgS", [128, 256]) if cfg.get("dbgr") else None

    X = sb("X", [128, 8, 512])
    _bx = Buf("X")
    bX = [_bx] * 5
    XB = sb("XB", [128, 8, 512], BF16)
    bXB = Buf("XB")
    CST = sb("CST", [128, C_W])
    bCST = Buf("CST")
    IDB = None
    PAR = sb("PAR", [128, 512])
    bPAR = Buf("PAR")
    NWB = 6
    WB = [sb(f"WB{i}", [128, 11 * 128], BF16) for i in range(NWB)]
    bWB = [Buf(f"WB{i}") for i in range(NWB)]
    wb_i = [0]
    wb_cnt = [0]
    ZT = sb("ZT", [128, 8, 512])
    bZT = Buf("ZT")
    LNS = sb("LNS", [128, 4, 512])
    bLNS = [Buf(f"LNS{i}") for i in range(4)]
    BIG = sb("BIG", [128, 9, 1024])
    bBIG = [Buf(f"BIG{i}") for i in range(9)]
    QA = sb("QA", [128, 14 * 512])
    bQA = Buf("QA")
    QT = QA[:, :].rearrange("p (k n) -> p k n", k=14)
    bQT = bQA
    AT = QA[:, :].bitcast(BF16)[:, 0:NFF * 512].rearrange("p (k n) -> p k n", k=NFF)
    bAT = bQA
    ROWS = sb("ROWS", [128, 1024])
    bROWS = [Buf("ROWS0"), Buf("ROWS1")]
    ROWS2 = sb("ROWS2", [128, 1024])
    rows_t = [ROWS, ROWS2]
    rows_i = [0]
    FH = sb("FH", [128, DEPTH, NFF, 2])
    bFH = Buf("FH")
    HEXT = BIG[:, 0:2, 0:520]
    bHEXT = [bBIG[0], bBIG[1]]
    CV = BIG[:, 2:4, 0:512]
    bCV = [bBIG[2], bBIG[3]]
    SFH = BIG[:, 4, 0:NFF * 32].rearrange("p (c t) -> p c t", c=NFF)
    bSFH = bBIG[4]
    SEL = BIG[:, 5, 0:NFF * 32].rearrange("p (c t) -> p c t", c=NFF)
    bSEL = bBIG[5]

    PS = [es.enter_context(nc.psum_tensor(f"ps{i}", [128, 512], F32)) for i in range(8)]
    bPS = [Buf(f"ps{i}", excl=True) for i in range(8)]
    ps_i = [0]

    ps_pool = [None]
    ps_pi = [0, 0]

    def psum():
        if ps_pool[0] is not None:
            q = ps_pool[0]
            i = 4 * q + ps_pi[q]
            ps_pi[q] = (ps_pi[q] + 1) % 4
            return PS[i], bPS[i]
        i = ps_i[0]
        ps_i[0] = (i + 1) % 8
        return PS[i], bPS[i]

    def ident(n=128):
        return CST[0:n, C_ID:C_ID + n]

    S.dma("sp", CST[:], cst[:, :], w=[bCST])

    TILES = [(i * 512, 512, 1, 512) for i in range(4)] + [(2048, 64, NSB, TS)]

    def transpose_to(out_ps, in_ap, nrows, r, w):
        S.pe(lambda e: e.transpose(out_ps, in_ap, ident(nrows)), r=r + [bCST], w=w)

    def load_cols(rows_list, col0, dest=None, dbuf=None):
        i = rows_i[0]
        rows_i[0] ^= 1
        rt, rb = rows_t[i], bROWS[i]
        r0 = 0
        for ap in rows_list:
            n = ap.shape[0]
            S.dma("sp", rt[r0:r0 + n, 0:128], ap, w=[rb])
            r0 += n
        pt, pb = psum()
        transpose_to(pt[:, 0:r0], rt[0:r0, 0:128], r0, [rb], [pb])
        if dest is None:
            dest, dbuf = PAR, bPAR
        S.act(lambda e: e.copy(out=dest[:, col0:col0 + r0], in_=pt[:, 0:r0]), r=[pb], w=[dbuf])
        return r0

    wcache = {}

    def wslab(wd, k0, kt, f0, nf, key=None):
        i = wb_i[0]
        wb_i[0] = (i + 1) % NWB
        t, b = WB[i], bWB[i]
        v = t[:, 0:kt * nf].rearrange("p (k f) -> p k f", k=kt)
        if cfg.get("nodma") and wb_cnt[0] >= NWB:
            return v, b
        wb_cnt[0] += 1
        ck = (key, k0, kt, f0, nf)
        if key is not None and cfg.get("wcache", True) and ck in wcache:
            sc, scb = wcache[ck]
            S.dma("sp", t[:, 0:kt * nf], sc[:, :], r=[scb], w=[b])
            return v, b
        S.dma("pool", v, wd[k0:k0 + kt * 128, f0:f0 + nf].rearrange("(k p) f -> p k f", p=128), w=[b])
        if key is not None and cfg.get("wcache", True) and len(cfg.get("tiles", [0, 1, 2, 3, 4])) > 1:
            sc = nc.dram_tensor(f"wc{len(wcache)}", [128, kt * nf], BF16, kind="Internal").ap()
            scb = Buf(f"wc{len(wcache)}")
            wcache[ck] = (sc, scb)
            S.dma("sp", sc[:, :], t[:, 0:kt * nf], r=[b], w=[scb])
        return v, b

    def cast_xb(ti):
        c0, n, _, _ = TILES[ti]
        S.act(lambda e: e.copy(out=XB[:, :, 0:n], in_=X[:, :, 0:n]), r=[bX[ti]], w=[bXB])

    def layer_norm(zv, zb, nk, n, gcol, bcol, out_fn, eps=LN_EPS, ptile=None):
        inv = 1.0 / (nk * 128)
        if ptile is None:
            ptile = PAR
        zb = zb if isinstance(zb, list) else [zb]
        ones = CST[:, C_ONE:C_ONE + 128]
        pm, pmb = psum()
        for k in range(nk):
            S.pe(lambda e, k=k: e.matmul(pm[:, 0:n], ones, zv(k), start=(k == 0), stop=(k == nk - 1)),
                 r=zb + [bCST], w=[pmb], inc=(k == nk - 1))
        pq, pqb = psum()
        for k in range(nk):
            sq = LNS[:, 0, 0:n]
            S.act(lambda e, k=k: e.activation(out=sq, in_=zv(k), func=AF.Square), r=zb, w=[bLNS[0]])
            S.pe(lambda e, k=k: e.matmul(pq[:, 0:n], ones, sq, start=(k == 0), stop=(k == nk - 1)),
                 r=[bLNS[0], bCST], w=[pqb])
        mean = LNS[:, 1, 0:n]
        rstd = LNS[:, 2, 0:n]
        S.act(lambda e: e.mul(out=mean, in_=pm[:, 0:n], mul=inv), r=[pmb], w=[bLNS[1]])
        S.dve(lambda e: e.tensor_tensor(out=rstd, in0=mean, in1=mean, op=ALU.mult), r=[bLNS[1]], w=[bLNS[2]])
        S.dve(lambda e: e.scalar_tensor_tensor(out=rstd, in0=pq[:, 0:n], scalar=inv, in1=rstd,
                                               op0=ALU.mult, op1=ALU.subtract), r=[pqb, bLNS[2]], w=[bLNS[2]])
        S.dve(lambda e: e.tensor_scalar(out=rstd, in0=rstd, scalar1=eps, scalar2=None, op0=ALU.add),
              r=[bLNS[2]], w=[bLNS[2]])
        S.act(lambda e: e.activation(out=rstd, in_=rstd, func=AF.Sqrt), r=[bLNS[2]], w=[bLNS[2]])
        S.dve(lambda e: e.reciprocal(out=rstd, in_=rstd), r=[bLNS[2]], w=[bLNS[2]])
        for k in range(nk):
            tmp = LNS[:, 3, 0:n]
            S.dve(lambda e, k=k: e.tensor_tensor(out=tmp, in0=zv(k), in1=mean, op=ALU.subtract),
                  r=zb + [bLNS[1]], w=[bLNS[3]])
            S.dve(lambda e: e.tensor_tensor(out=tmp, in0=tmp, in1=rstd, op=ALU.mult),
                  r=[bLNS[3], bLNS[2]], w=[bLNS[3]])
            out_fn(k, tmp, bLNS[3], ptile[:, gcol + k:gcol + k + 1], ptile[:, bcol + k:bcol + k + 1])

    def load_x(ti):
        c0, n, _, _ = TILES[ti]
        for tb in range((n + 127) // 128):
            src = xp[c0 + tb * 128:c0 + (tb + 1) * 128, :] if ti < 4 else xs[:, :]
            nr = min(128, n)
            i = rows_i[0]
            rows_i[0] ^= 1
            rt, rb = rows_t[i], bROWS[i]
            S.dma("sp", rt[0:nr, :], src, w=[rb])
            for half in range(2):
                pt, pb = psum()
                for kk in range(4):
                    k = half * 4 + kk
                    S.pe(lambda e, k=k, kk=kk: e.transpose(pt[:, kk * 128:kk * 128 + nr],
                                                           rt[0:nr, k * 128:(k + 1) * 128], ident(nr)),
                         r=[rb, bCST], w=[pb])
                S.act(lambda e, half=half: e.copy(
                    out=X[:, half * 4:half * 4 + 4, tb * 128:tb * 128 + nr],
                    in_=pt[:, :].rearrange("p (k t) -> p k t", k=4)[:, :, 0:nr]), r=[pb], w=[bX[ti]])

    def store_y(ti):
        c0, n, _, _ = TILES[ti]
        for tb in range((n + 127) // 128):
            dst = o_yp[c0 + tb * 128:c0 + (tb + 1) * 128, :] if ti < 4 else o_ys[:, :]
            nr = min(128, n)
            i = rows_i[0]
            rows_i[0] ^= 1
            rt, rb = rows_t[i], bROWS[i]
            for half in range(2):
                pt, pb = psum()
                for kk in range(4):
                    k = half * 4 + kk
                    S.pe(lambda e, k=k, kk=kk: e.transpose(pt[0:nr, kk * 128:(kk + 1) * 128],
                                                           X[:, k, tb * 128:tb * 128 + nr], ident(128)),
                         r=[bX[ti], bCST], w=[pb])
                S.act(lambda e, half=half: e.copy(out=rt[0:nr, half * 512:(half + 1) * 512], in_=pt[0:nr, :]),
                      r=[pb], w=[rb])
            S.dma("sp", dst, rt[0:nr, :], r=[rb], is_out=True)

    def emit_rows(src_fn, nrows, nchunks, dst_fn, r):
        done = 0
        while done < nchunks:
            g = min(8, nchunks - done)
            i = rows_i[0]
            rows_i[0] ^= 1
            rt, rb = rows_t[i], bROWS[i]
            for h0 in range(0, g, 4):
                pt, pb = psum()
                hn = min(4, g - h0)
                for kk in range(hn):
                    c = done + h0 + kk
                    S.pe(lambda e, c=c, kk=kk: e.transpose(pt[0:nrows, kk * 128:(kk + 1) * 128],
                                                           src_fn(c), ident(128)), r=r + [bCST], w=[pb])
                S.act(lambda e, h0=h0, hn=hn: e.copy(out=rt[0:nrows, h0 * 128:(h0 + hn) * 128],
                                                      in_=pt[0:nrows, 0:hn * 128]), r=[pb], w=[rb])
            dst_fn(rt, rb, done, g)
            done += g

    def ffn_params(l):
        load_cols([W["conv_f_w"][l].rearrange("j (c p) -> (j c) p", p=128)], 0)
        load_cols([W["conv_f_b"][l].rearrange("(c p) -> c p", p=128),
                   W["ln2_g"][l].rearrange("(c p) -> c p", p=128),
                   W["ln2_b"][l].rearrange("(c p) -> c p", p=128),
                   W["ln1_g"][l].rearrange("(c p) -> c p", p=128),
                   W["ln1_b"][l].rearrange("(c p) -> c p", p=128)], 66)

    def ffn_sample_state(l):
        i = rows_i[0]
        for c0 in range(0, NFF, 8):
            g = min(8, NFF - c0)
            i = rows_i[0]
            rows_i[0] ^= 1
            rt, rb = rows_t[i], bROWS[i]
            for b in range(NSB):
                S.dma("sp", rt[2 * b:2 * b + 2, 0:g * 128], st_ffn[b, l, :, c0 * 128:(c0 + g) * 128], w=[rb])
            for h0 in range(0, g, 4):
                hn = min(4, g - h0)
                pt, pb = psum()
                for kk in range(hn):
                    S.pe(lambda e, kk=kk, h0=h0: e.transpose(pt[:, kk * 32:(kk + 1) * 32],
                                                             rt[0:32, (h0 + kk) * 128:(h0 + kk + 1) * 128], ident(32)),
                         r=[rb, bCST], w=[pb])
                S.act(lambda e, h0=h0, hn=hn, c0=c0: e.copy(
                    out=SFH[:, c0 + h0:c0 + h0 + hn, :],
                    in_=pt[:, 0:hn * 32].rearrange("p (k t) -> p k t", k=hn)), r=[pb], w=[bSFH])

    def ffn_tile(l, ti):
        c0, n, nseq, T = TILES[ti]
        E = T + 2
        cast_xb(ti)
        wup = W["w_up"][l]
        for g0 in range(0, NFF, 1):
            gn = 1
            wv, wvb = wslab(wup, 0, 8, g0 * 128, gn * 128, key=("up", l))
            wg, wgb = wslab(wup, 0, 8, DFF + g0 * 128, gn * 128, key=("up", l))
            for j in range(gn):
                hc = g0 + j
                ph, phb = psum()
                for k in range(8):
                    S.pe(lambda e, k=k, j=j: e.matmul(ph[:, 0:n], wv[:, k, j * 128:(j + 1) * 128], XB[:, k, 0:n],
                                                      start=(k == 0), stop=(k == 7)),
                         r=[wvb, bXB], w=[phb], inc=(k == 7))
                pg, pgb = psum()
                for k in range(8):
                    S.pe(lambda e, k=k, j=j: e.matmul(pg[:, 0:n], wg[:, k, j * 128:(j + 1) * 128], XB[:, k, 0:n],
                                                      start=(k == 0), stop=(k == 7)),
                         r=[wgb, bXB], w=[pgb], inc=(k == 7))
                hi = hc % 2
                hx = HEXT[:, hi, 0:nseq * E].rearrange("p (s e) -> p s e", s=nseq)
                hb = bHEXT[hi]
                if nseq == 1:
                    if ti == 0:
                        S.dve(lambda e: e.memset(hx[:, :, 0:2], 0.0), w=[hb])
                    else:
                        S.dve(lambda e, hc=hc: e.tensor_copy(out=hx[:, 0, 0:2], in_=FH[:, l, hc, :]), r=[bFH], w=[hb])
                else:
                    S.dve(lambda e, hc=hc: e.tensor_copy(
                        out=hx[:, :, 0:2], in_=SFH[:, hc, :].rearrange("p (b r) -> p b r", r=2)), r=[bSFH], w=[hb])
                S.act(lambda e: e.copy(out=hx[:, :, 2:E], in_=ph[:, 0:n].rearrange("p (s t) -> p s t", s=nseq)),
                      r=[phb], w=[hb])
                cv = CV[:, hi, 0:n].rearrange("p (s t) -> p s t", s=nseq)
                cb = bCV[hi]
                S.act(lambda e, hc=hc: e.activation(out=cv, in_=hx[:, :, 0:T], func=AF.Identity, scale=PAR[:, hc:hc + 1],
                                                    bias=PAR[:, 66 + hc:67 + hc]), r=[hb, bPAR], w=[cb])
                for j2 in (1, 2):
                    S.dve(lambda e, hc=hc, j2=j2: e.scalar_tensor_tensor(
                        out=cv, in0=hx[:, :, j2:j2 + T], scalar=PAR[:, 22 * j2 + hc:22 * j2 + hc + 1], in1=cv,
                        op0=ALU.mult, op1=ALU.add), r=[hb, bPAR, cb], w=[cb])
                if nseq == 1:
                    S.pool(lambda e, hc=hc: e.tensor_copy(out=FH[:, l, hc, :], in_=hx[:, 0, T:T + 2]), r=[hb], w=[bFH])
                else:
                    S.pool(lambda e, hc=hc: e.tensor_copy(
                        out=SEL[:, hc, :].rearrange("p (r b) -> p r b", r=2),
                        in_=hx[:, :, T:T + 2].rearrange("p b r -> p r b")), r=[hb], w=[bSEL])
                S.act(lambda e: e.activation(out=CV[:, hi, 0:n], in_=CV[:, hi, 0:n], func=AF.Silu), r=[cb], w=[cb])
                S.dve(lambda e, hc=hc: e.tensor_tensor(out=AT[:, hc, 0:n], in0=CV[:, hi, 0:n], in1=pg[:, 0:n],
                                                       op=ALU.mult), r=[cb, pgb], w=[bAT])
        if ti == 3:
            def dst(rt, rb, done, g):
                S.dma("sp", o_ffn_p[l, :, done * 128:(done + g) * 128], rt[0:2, 0:g * 128], r=[rb], is_out=True)
            emit_rows(lambda c: FH[:, l, c, :], 2, NFF, dst, [bFH])
        if ti == 4:
            def dst(rt, rb, done, g):
                for r_ in range(2):
                    S.dma("sp", o_ffn_s[:, l, r_, done * 128:(done + g) * 128],
                          rt[16 * r_:16 * r_ + 16, 0:g * 128], r=[rb], is_out=True)
            emit_rows(lambda c: SEL[:, c, :], 32, NFF, dst, [bSEL])
        wdn = W["w_down"][l]
        for o0 in range(0, 8):
            wd0, wdb0 = wslab(wdn, 0, 11, o0 * 128, 128, key=("dn", l))
            wd1, wdb1 = wslab(wdn, 11 * 128, 11, o0 * 128, 128, key=("dn", l))
            for j in range(1):
                oc = o0 + j
                pf, pfb = psum()
                for k in range(NFF):
                    wd_, wdb = (wd0, wdb0) if k < 11 else (wd1, wdb1)
                    S.pe(lambda e, k=k, wd_=wd_: e.matmul(pf[:, 0:n], wd_[:, k % 11, :], AT[:, k, 0:n],
                                                          start=(k == 0), stop=(k == NFF - 1)),
                         r=[wdb, bAT], w=[pfb], inc=(k == NFF - 1))
                S.dve(lambda e, oc=oc: e.scalar_tensor_tensor(out=ZT[:, oc, 0:n], in0=X[:, oc, 0:n], scalar=ALPHA,
                                                              in1=pf[:, 0:n], op0=ALU.mult, op1=ALU.add),
                      r=[bX[ti], pfb], w=[bZT])

        def outf(k, tmp, tb, g, b):
            S.act(lambda e: e.activation(out=X[:, k, 0:n], in_=tmp, func=AF.Identity, scale=g, bias=b),
                  r=[tb, bPAR], w=[bX[ti]])
        layer_norm(lambda k: ZT[:, k, 0:n], bZT, 8, n, 88, 96, outf)

    S5P = sb("S5P", [128, 24, 32])
    bS5P = Buf("S5P")
    BL = sb("BL", [128, 32, 2, 128], BF16)
    bBL = Buf("BL")
    CL = sb("CL", [128, 32, 2, 128], BF16)
    bCL = Buf("CL")
    HCS = sb("HCS", [128, 2, 2, 32])
    bHCS = Buf("HCS")
    BBX = sb("BBX", [128, 8, 128])
    bBBX = [Buf(f"BBX{i}") for i in range(8)]
    S.dve(lambda e: e.memset(HCS[:, :, :, :], 0.0), w=[bHCS])
    HCS2 = sb("HCS2", [128, 2, 2, 32])
    bHCS2 = Buf("HCS2")
    S.dve(lambda e: e.memset(BBX[:, :, :], 0.0), w=bBBX)
    P_LR, P_LI, P_DT, P_MAG, P_ANG, P_C, P_S, P_T0, P_T1, P_T2, P_FR, P_FI, P_AR0, P_AR1, P_AI, P_NAI, P_DEN = range(17)
    P_DSK = 17

    def s5_setup(j):
        alias_claim(0)
        sp = lambda i: S5P[:, i, :]
        dbg = cfg.get("s5stop", 99)
        load_cols([W["lam_re"][j].rearrange("(s g) p -> s (g p)", g=2)], 0, S5P[:, P_LR, :], bS5P)
        load_cols([W["lam_im"][j].rearrange("(s g) p -> s (g p)", g=2)], 0, S5P[:, P_LI, :], bS5P)
        load_cols([W["d_skip"][j].rearrange("(c p) -> c p", p=128)], 0, S5P[:, P_DSK, :], bS5P)
        i = rows_i[0]
        rows_i[0] ^= 1
        rt, rb = rows_t[i], bROWS[i]
        S.dma("sp", rt[0:32, 512:514], W["log_dt"][j].rearrange("(s g) -> s g", g=2), w=[rb])
        S.dve(lambda e: e.tensor_copy(out=rt[0:32, 0:128].rearrange("s (g p) -> s g p", g=2),
                                      in_=rt[0:32, 512:514].unsqueeze(2).to_broadcast([32, 2, 64])), r=[rb], w=[rb])
        pt, pb = psum()
        transpose_to(pt[:, 0:32], rt[0:32, 0:128], 32, [rb], [pb])
        S.act(lambda e: e.activation(out=sp(P_DT), in_=pt[:, 0:32], func=AF.Exp), r=[pb], w=[bS5P])
        R = [bS5P]
        tt = lambda o, a, b, op: S.dve(lambda e: e.tensor_tensor(out=sp(o), in0=sp(a), in1=sp(b), op=op), r=R, w=R)
        def ts(o, a, s1, s2, op0, op1=None):
            if s2 is None:
                S.dve(lambda e: e.tensor_scalar(out=sp(o), in0=sp(a), scalar1=s1, scalar2=None, op0=op0), r=R, w=R)
            else:
                S.dve(lambda e: e.tensor_scalar(out=sp(o), in0=sp(a), scalar1=s1, scalar2=s2, op0=op0, op1=op1), r=R, w=R)
        tt(P_MAG, P_LR, P_DT, ALU.mult)
        S.act(lambda e: e.activation(out=sp(P_MAG), in_=sp(P_MAG), func=AF.Exp), r=R, w=R)
        tt(P_ANG, P_LI, P_DT, ALU.mult)
        S.act(lambda e: e.activation(out=sp(P_T0), in_=sp(P_ANG), func=AF.Sin, scale=1.0 / 64), r=R, w=R)
        S.act(lambda e: e.activation(out=sp(P_S), in_=sp(P_ANG), func=AF.Sin, scale=1.0 / 32), r=R, w=R)
        tt(P_T0, P_T0, P_T0, ALU.mult)
        ts(P_C, P_T0, -2.0, 1.0, ALU.mult, ALU.add)
        for _ in range(5):
            tt(P_T0, P_C, P_C, ALU.mult)
            tt(P_T1, P_S, P_S, ALU.mult)
            tt(P_T2, P_C, P_S, ALU.mult)
            tt(P_C, P_T0, P_T1, ALU.subtract)
            ts(P_S, P_T2, 2.0, None, ALU.mult)
        tt(P_AR0, P_MAG, P_C, ALU.mult)
        S.dve(lambda e: e.tensor_copy(out=sp(P_AR1), in_=sp(P_AR0)), r=R, w=R)
        tt(P_AI, P_MAG, P_S, ALU.mult)
        ts(P_NAI, P_AI, -1.0, None, ALU.mult)
        tt(P_T0, P_LR, P_LR, ALU.mult)
        tt(P_T1, P_LI, P_LI, ALU.mult)
        tt(P_DEN, P_T0, P_T1, ALU.add)
        S.dve(lambda e: e.reciprocal(out=sp(P_DEN), in_=sp(P_DEN)), r=R, w=R)
        ts(P_T2, P_AR0, -1.0, None, ALU.add)
        tt(P_T0, P_T2, P_LR, ALU.mult)
        tt(P_T1, P_AI, P_LI, ALU.mult)
        tt(P_T0, P_T0, P_T1, ALU.add)
        tt(P_FR, P_T0, P_DEN, ALU.mult)
        tt(P_T0, P_AI, P_LR, ALU.mult)
        tt(P_T1, P_T2, P_LI, ALU.mult)
        tt(P_T0, P_T0, P_T1, ALU.subtract)
        tt(P_FI, P_T0, P_DEN, ALU.mult)
        if o_dbg is not None:
            S.dma("sp", o_dbg[:, 0:768], S5P[:, :, :].rearrange("p a b -> p (a b)"), r=[bS5P], is_out=True)
        BR = BIG[:, 0, 0:512].rearrange("p (s c) -> p s c", c=16)
        BI = BIG[:, 1, 0:512].rearrange("p (s c) -> p s c", c=16)
        BBR = BIG[:, 2, 0:512].rearrange("p (s c) -> p s c", c=16)
        BBI = BIG[:, 3, 0:512].rearrange("p (s c) -> p s c", c=16)
        TMP = BIG[:, 4, 0:512].rearrange("p (s c) -> p s c", c=16)
        for g2 in range(2):
            S.dma("sp", BR[64 * g2:64 * g2 + 64, :, :],
                  W["b_re"][j].rearrange("(s g) p c -> g p s c", g=2)[g2], w=[bBIG[0]])
            S.dma("sp", BI[64 * g2:64 * g2 + 64, :, :],
                  W["b_im"][j].rearrange("(s g) p c -> g p s c", g=2)[g2], w=[bBIG[1]])
        fr = S5P[:, P_FR, :].unsqueeze(2).to_broadcast([128, 32, 16])
        fi = S5P[:, P_FI, :].unsqueeze(2).to_broadcast([128, 32, 16])
        S.dve(lambda e: e.tensor_tensor(out=BBR, in0=BR, in1=fr, op=ALU.mult), r=[bBIG[0], bS5P], w=[bBIG[2]])
        S.dve(lambda e: e.tensor_tensor(out=TMP, in0=BI, in1=fi, op=ALU.mult), r=[bBIG[1], bS5P], w=[bBIG[4]])
        S.dve(lambda e: e.tensor_tensor(out=BBR, in0=BBR, in1=TMP, op=ALU.subtract), r=[bBIG[2], bBIG[4]], w=[bBIG[2]])
        S.dve(lambda e: e.tensor_tensor(out=BBI, in0=BI, in1=fr, op=ALU.mult), r=[bBIG[1], bS5P], w=[bBIG[3]])
        S.dve(lambda e: e.tensor_tensor(out=TMP, in0=BR, in1=fi, op=ALU.mult), r=[bBIG[0], bS5P], w=[bBIG[4]])
        S.dve(lambda e: e.tensor_tensor(out=BBI, in0=BBI, in1=TMP, op=ALU.add), r=[bBIG[3], bBIG[4]], w=[bBIG[3]])
        for s_ in range(32):
            q = s_ % 4
            for ri, src, sbuf_ in ((0, BBR, bBIG[2]), (1, BBI, bBIG[3])):
                bx = BBX[:, ri * 4 + q, :]
                bb = bBBX[ri * 4 + q]
                for g2 in range(2):
                    S.dve(lambda e, g2=g2, src=src, bx=bx: e.tensor_copy(
                        out=bx[64 * g2:64 * g2 + 64, 32 * q + 16 * g2:32 * q + 16 * g2 + 16],
                        in_=src[64 * g2:64 * g2 + 64, s_, :]), r=[sbuf_], w=[bb])
                pt, pb = psum()
                transpose_to(pt[:, 0:128], bx, 128, [bb], [pb])
                S.act(lambda e, ri=ri: e.copy(out=BL[:, s_, ri, :], in_=pt[:, 0:128]), r=[pb], w=[bBL])
        CR = BIG[:, 5, 0:512].rearrange("p (t q) -> p t q", q=64)
        CI = BIG[:, 6, 0:512].rearrange("p (t q) -> p t q", q=64)
        S.dma("sp", CR, W["c_re"][j].rearrange("(t g) c p -> (g c) t p", g=8), w=[bBIG[5]])
        S.dma("sp", CI, W["c_im"][j].rearrange("(t g) c p -> (g c) t p", g=8), w=[bBIG[6]])
        CX = BIG[:, 7, 0:256].rearrange("p (r x) -> p r x", r=2)
        for s_ in range(32):
            q, ct = s_ % 4, s_ // 4
            mk = CST[:, C_CM + 128 * q:C_CM + 128 * (q + 1)]
            for ri, src, sbuf_ in ((0, CR, bBIG[5]), (1, CI, bBIG[6])):
                cx = CX[:, ri, :]
                S.dve(lambda e, src=src, cx=cx: e.tensor_tensor(
                    out=cx.rearrange("p (g q) -> p g q", g=2),
                    in0=src[:, ct, :].unsqueeze(1).to_broadcast([128, 2, 64]),
                    in1=mk.rearrange("p (g q) -> p g q", g=2), op=ALU.mult), r=[sbuf_, bCST], w=[bBIG[7]])
                pt, pb = psum()
                transpose_to(pt[:, 0:128], cx, 128, [bBIG[7]], [pb])
                S.act(lambda e, ri=ri: e.mul(out=CL[:, s_, ri, :], in_=pt[:, 0:128], mul=(1.0 if ri == 0 else -1.0)),
                      r=[pb], w=[bCL])

    def s5_sample_state(j):
        S0 = BIG[:, 8, :].rearrange("p (r s b) -> p r s b", r=2, s=32)
        for ri, st in ((0, st_re), (1, st_im)):
            for q in range(4):
                i = rows_i[0]
                rows_i[0] ^= 1
                rt, rb = rows_t[i], bROWS[i]
                for b4 in range(4):
                    S.dma("sp", rt[32 * b4:32 * b4 + 32, 0:128],
                          st[4 * q + b4, j].rearrange("(s g) p -> s (g p)", g=2), w=[rb])
                pt, pb = psum()
                transpose_to(pt[:, 0:128], rt[:, 0:128], 128, [rb], [pb])
                S.act(lambda e, ri=ri, q=q: e.copy(
                    out=S0[:, ri, :, 4 * q:4 * q + 4].rearrange("p s b -> p b s"),
                    in_=pt[:, 0:128].rearrange("p (b s) -> p b s", b=4)), r=[pb], w=[bBIG[8]])

    tabD = [nc.dram_tensor(f"s5tab{j_}", [32, 128, 1024], F32, kind="Internal").ap() for j_ in range(2)]
    btabD = [Buf("tab0"), Buf("tab1")]
    bQR = [Buf(f"QR{i}") for i in range(14)]
    QR = lambda i: QA[:, i * 512:(i + 1) * 512]

    def qa_claim(take):
        if take:
            S.dve(lambda e: e.memset(QA[:, 0:2], 0.0), r=[bQA], w=bQR)
        else:
            S.dve(lambda e: e.memset(QA[:, 0:2], 0.0), r=bQR, w=[bQA])

    def s5_tables(j):
        for s_ in range(32):
            i = s_ % 2
            tc, tsn, tmp = QR(2 * i), QR(2 * i + 1), QR(4 + i)
            bb = [bQR[2 * i], bQR[2 * i + 1], bQR[4 + i]]
            S.dve(lambda e, tc=tc: e.tensor_copy(out=tc[:, 0:1], in_=S5P[:, P_C, s_:s_ + 1]), r=[bS5P], w=bb)
            S.dve(lambda e, tsn=tsn: e.tensor_copy(out=tsn[:, 0:1], in_=S5P[:, P_S, s_:s_ + 1]), r=[bS5P], w=bb)
            n_ = 1
            while n_ < 512:
                cn, sn = tc[:, n_ - 1:n_], tsn[:, n_ - 1:n_]
                S.dve(lambda e, n_=n_, sn=sn, tsn=tsn, tmp=tmp: e.tensor_scalar(
                    out=tmp[:, 0:n_], in0=tsn[:, 0:n_], scalar1=sn, scalar2=None, op0=ALU.mult), r=bb, w=bb)
                S.dve(lambda e, n_=n_, sn=sn, tc=tc, tmp=tmp: e.tensor_scalar(
                    out=tmp[:, 256:256 + n_], in0=tc[:, 0:n_], scalar1=sn, scalar2=None, op0=ALU.mult), r=bb, w=bb)
                S.dve(lambda e, n_=n_, cn=cn, tc=tc, tmp=tmp: e.scalar_tensor_tensor(
                    out=tc[:, n_:2 * n_], in0=tc[:, 0:n_], scalar=cn, in1=tmp[:, 0:n_], op0=ALU.mult, op1=ALU.subtract),
                    r=bb, w=bb)
                S.dve(lambda e, n_=n_, cn=cn, tsn=tsn, tmp=tmp: e.scalar_tensor_tensor(
                    out=tsn[:, n_:2 * n_], in0=tsn[:, 0:n_], scalar=cn, in1=tmp[:, 256:256 + n_], op0=ALU.mult, op1=ALU.add),
                    r=bb, w=bb)
                n_ *= 2
            S.dma("sp", tabD[j][s_, :, 0:512], tc, r=bb, w=[btabD[j]])
            S.dma("sp", tabD[j][s_, :, 512:1024], tsn, r=bb, w=[btabD[j]])

    def s5_scan_tile(j, ti, U, bU, UB, bUB, GY, bGY):
        qa_claim(True)
        if ti == 0:
            s5_tables(j)
        HBq = [QA[:, 6144 + 512 * i:6144 + 512 * (i + 1)].bitcast(BF16) for i in range(2)]
        HBv = lambda i: HBq[i // 2][:, (i % 2) * 512:(i % 2) * 512 + 512]
        bHB = [bQR[12], bQR[13]]
        py = pyb = None
        HIN, bHIN = (HCS, bHCS) if ti % 2 == 0 else (HCS2, bHCS2)
        HOUT, bHOUT = (HCS2, bHCS2) if ti % 2 == 0 else (HCS, bHCS)

        def bu_mm(sx):
            pr_, prb_ = psum()
            S.pe(lambda e: e.matmul(pr_[:, 0:512], BL[:, sx, 0, :], UB(sx // 4)[:, 0:512], start=True, stop=True),
                 r=[bBL] + bUB, w=[prb_])
            pi_, pib_ = psum()
            S.pe(lambda e: e.matmul(pi_[:, 0:512], BL[:, sx, 1, :], UB(sx // 4)[:, 0:512], start=True, stop=True),
                 r=[bBL] + bUB, w=[pib_])
            return pr_, prb_, pi_, pib_
        bu_q = [bu_mm(0), bu_mm(1)]
        pyst = [None, None]

        def stage_a(s_):
                ct, q = s_ // 4, s_ % 4
                i = s_ % 2
                tcs, tsn = QR(2 * i), QR(2 * i + 1)
                btab = [bQR[2 * i], bQR[2 * i + 1]]
                S.dma("sp", tcs, tabD[j][s_, :, 0:512], r=[btabD[j]], w=[btab[0]])
                S.dma("sp", tsn, tabD[j][s_, :, 512:1024], r=[btabD[j]], w=[btab[1]])
                pr, prb, pi, pib = bu_q.pop(0)
                T1, T2, T4 = QR(4 + 3 * i), QR(5 + 3 * i), QR(6 + 3 * i)
                bT1, bT2, bT4 = bQR[4 + 3 * i], bQR[5 + 3 * i], bQR[6 + 3 * i]
                U1, U2, U3, U4 = (LNS[:, k, :] for k in range(4))
                bU1, bU2, bU3, bU4 = bLNS
                mag = S5P[:, P_MAG, s_:s_ + 1].to_broadcast([128, 512])
                prv, piv = pr[:, 0:512], pi[:, 0:512]
                S.dve(lambda e: e.tensor_tensor(out=T1, in0=prv, in1=tcs, op=ALU.mult), r=[prb, btab[0]], w=[bT1])
                S.dve(lambda e: e.tensor_tensor(out=T2, in0=piv, in1=tsn, op=ALU.mult), r=[pib, btab[1]], w=[bT2])
                S.dve(lambda e: e.tensor_tensor(out=T1, in0=T1, in1=T2, op=ALU.add), r=[bT1, bT2], w=[bT1])
                S.dve(lambda e: e.tensor_tensor(out=T2, in0=piv, in1=tcs, op=ALU.mult), r=[pib, btab[0], bT1], w=[bT2])
                S.dve(lambda e: e.tensor_tensor(out=T4, in0=prv, in1=tsn, op=ALU.mult), r=[prb, btab[1]], w=[bT4])
                S.dve(lambda e: e.tensor_tensor(out=T2, in0=T2, in1=T4, op=ALU.subtract), r=[bT2, bT4], w=[bT2])
                S.dve(lambda e: e.tensor_tensor_scan(out=T1, data0=mag, data1=T1, initial=HIN[:, j, 0, s_:s_ + 1],
                                                     op0=ALU.mult, op1=ALU.add), r=[bT1, bS5P, bHIN], w=[bT1])
                S.dve(lambda e: e.tensor_tensor_scan(out=T2, data0=mag, data1=T2, initial=HIN[:, j, 1, s_:s_ + 1],
                                                     op0=ALU.mult, op1=ALU.add), r=[bT2, bS5P, bHIN], w=[bT2])
                return (tcs, tsn, btab, T1, T2, bT1, bT2, i, ct, q)

        def stage_b(s_, ctx):
                tcs, tsn, btab, T1, T2, bT1, bT2, i, ct, q = ctx
                U1, U2, U3, U4 = (LNS[:, k, :] for k in range(4))
                bU1, bU2, bU3, bU4 = bLNS
                py, pyb = pyst
                S.pool(lambda e: e.tensor_tensor(out=U2, in0=T2, in1=tsn, op=ALU.mult), r=[bT2, btab[1]], w=[bU2])
                S.pool(lambda e: e.tensor_tensor(out=U3, in0=T1, in1=tsn, op=ALU.mult), r=[bT1, btab[1]], w=[bU3])
                S.pool(lambda e: e.tensor_tensor(out=U4, in0=T2, in1=tcs, op=ALU.mult), r=[bT2, btab[0]], w=[bU4])
                S.dve(lambda e: e.tensor_tensor(out=U1, in0=T1, in1=tcs, op=ALU.mult), r=[bT1, btab[0]], w=[bU1])
                S.pool(lambda e: e.tensor_tensor(out=U3, in0=U3, in1=U4, op=ALU.add), r=[bU3, bU4], w=[bU3])
                S.dve(lambda e: e.tensor_tensor(out=U1, in0=U1, in1=U2, op=ALU.subtract), r=[bU1, bU2], w=[bU1])
                S.dve(lambda e: e.tensor_copy(out=HOUT[:, j, 0, s_:s_ + 1], in_=U1[:, 511:512]), r=[bU1], w=[bHOUT])
                S.dve(lambda e: e.tensor_copy(out=HOUT[:, j, 1, s_:s_ + 1], in_=U3[:, 511:512]), r=[bU3], w=[bHOUT])
                hbr, hbi = HBv(2 * i), HBv(2 * i + 1)
                S.act(lambda e, hbr=hbr: e.copy(out=hbr, in_=U1), r=[bU1], w=[bHB[i]])
                S.act(lambda e, hbi=hbi: e.copy(out=hbi, in_=U3), r=[bU3], w=[bHB[i]])
                if q == 0:
                    py, pyb = psum()
                pyst[0], pyst[1] = py, pyb
                S.pe(lambda e, hbr=hbr, py=py, q=q: e.matmul(py[:, 0:512], CL[:, s_, 0, :], hbr, start=(q == 0), stop=False),
                     r=[bCL, bHB[i]], w=[pyb], inc=False)
                S.pe(lambda e, hbi=hbi, py=py, q=q: e.matmul(py[:, 0:512], CL[:, s_, 1, :], hbi, start=False, stop=(q == 3)),
                     r=[bCL, bHB[i]], w=[pyb], inc=True)
                if q == 3:
                    Y, Y2 = QR(10), QR(11)
                    bY, bY2 = bQR[10], bQR[11]
                    S.dve(lambda e, ct=ct, py=py: e.scalar_tensor_tensor(out=Y, in0=U[:, ct, 0:512], scalar=S5P[:, P_DSK, ct:ct + 1],
                                                                         in1=py[:, 0:512], op0=ALU.mult, op1=ALU.add),
                          r=bU + [bS5P, pyb], w=[bY])
                    S.pool(lambda e: e.tensor_tensor(out=Y2, in0=Y, in1=Y, op=ALU.mult), r=[bY], w=[bY2])
                    S.pool(lambda e: e.tensor_scalar(out=Y2, in0=Y2, scalar1=0.044715, scalar2=1.0, op0=ALU.mult, op1=ALU.add),
                           r=[bY2], w=[bY2])
                    S.pool(lambda e: e.tensor_tensor(out=Y2, in0=Y2, in1=Y, op=ALU.mult), r=[bY2, bY], w=[bY2])
                    S.act(lambda e: e.activation(out=Y2, in_=Y2, func=AF.Sigmoid, scale=1.5957691216), r=[bY2], w=[bY2])
                    S.dve(lambda e, ct=ct: e.tensor_tensor(out=GY(ct)[:, 0:512], in0=Y, in1=Y2, op=ALU.mult),
                          r=[bY, bY2], w=bGY)

        ctx_next = stage_a(0)
        for s_ in range(32):
            ctx_cur = ctx_next
            if s_ + 2 < 32:
                bu_q.append(bu_mm(s_ + 2))
            if s_ + 1 < 32:
                ctx_next = stage_a(s_ + 1)
            stage_b(s_, ctx_cur)
        qa_claim(False)


    def mixer_c(l, ti):
        j = l // 2
        c0, n, nseq, T = TILES[ti]
        U = BIG[:, 0:4, :].rearrange("p a (b n) -> p (a b) n", b=2)
        bU = bBIG[0:4]
        UBf = [BIG[:, 4 + i, :].bitcast(BF16) for i in range(2)]
        GYf = [BIG[:, 6 + i, :].bitcast(BF16) for i in range(2)]
        UB = lambda k: UBf[k // 4][:, (k % 4) * 512:(k % 4) * 512 + 512]
        GY = lambda k: GYf[k // 4][:, (k % 4) * 512:(k % 4) * 512 + 512]
        bUB = bBIG[4:6]
        bGY = bBIG[6:8]
        S0 = BIG[:, 8, :].rearrange("p (r s b) -> p r s b", r=2, s=32)
        cast_xb(ti)
        for oc in range(8):
            wv, wvb = wslab(W["w_in_c"][j], 0, 8, oc * 128, 128, key=("inc", j))
            pu, pub = psum()
            for k in range(8):
                S.pe(lambda e, k=k: e.matmul(pu[:, 0:n], wv[:, k, :], XB[:, k, 0:n], start=(k == 0), stop=(k == 7)),
                     r=[wvb, bXB], w=[pub], inc=(k == 7))
            S.act(lambda e, oc=oc: e.copy(out=U[:, oc, 0:n], in_=pu[:, 0:n]), r=[pub], w=bU)
            S.dve(lambda e, oc=oc: e.tensor_copy(out=UB(oc)[:, 0:n], in_=pu[:, 0:n]), r=[pub], w=bUB)
        cstop = cfg.get("cstop", 99)
        if cstop <= 1:
            return mixer_none(l, ti)
        if nseq == 1 and cfg.get("s5scan", True):
            s5_scan_tile(j, ti, U, bU, UB, bUB, GY, bGY)
            nsub_skip = True
        else:
            nsub_skip = False
        qv = QA[:, :]
        BU = qv[:, 0:2048].rearrange("p (r s t) -> p r s t", r=2, s=32)
        HALL = qv[:, 2048:4096].rearrange("p (r s t) -> p r s t", r=2, s=32)
        HB = qv[:, 4096:5120].bitcast(BF16).rearrange("p (r s t) -> p r s t", r=2, s=32)
        T1f = qv[:, 5120:6144]
        T2f = qv[:, 6144:7168]
        bQ = [bQA]
        nsub = 0 if nsub_skip else n // 32
        for st_i in range(nsub):
            cs = st_i * 32
            for ri in range(2):
                for h in range(2):
                    pt, pb = psum()
                    for s16 in range(16):
                        s_ = h * 16 + s16
                        S.pe(lambda e, s_=s_, s16=s16, ri=ri: e.matmul(
                            pt[:, s16 * 32:(s16 + 1) * 32], BL[:, s_, ri, :], UB(s_ // 4)[:, cs:cs + 32],
                            start=True, stop=True), r=[bBL] + bUB, w=[pb], inc=(s16 == 15))
                    S.act(lambda e, ri=ri, h=h: e.copy(
                        out=BU[:, ri, 16 * h:16 * h + 16, :], in_=pt[:, :].rearrange("p (s t) -> p s t", s=16)),
                        r=[pb], w=bQ)
            if cstop <= 2:
                continue
            if nseq == 1:
                nb, nt = 1, 32
                shp = [128, 2, 32]
                bu_t = lambda t: BU[:, :, :, t]
                h_t = lambda t: HALL[:, :, :, t]
                prev0 = HCS[:, j, :, :]
                prev0b = [bHCS]
                T1 = T1f[:, 0:64].rearrange("p (r s) -> p r s", r=2)
                T2 = T2f[:, 0:64].rearrange("p (r s) -> p r s", r=2)
                arar = S5P[:, P_AR0:P_AR0 + 2, :]
                ai = S5P[:, P_AI, :]
                nai = S5P[:, P_NAI, :]
            else:
                nb, nt = 8, 4
                BU5 = BU.rearrange("p r s (b t) -> p r s b t", b=8)
                H5 = HALL.rearrange("p r s (b t) -> p r s b t", b=8)
                bu_t = lambda t: BU5[:, :, :, :, t]
                h_t = lambda t: H5[:, :, :, :, t]
                prev0 = S0[:, :, :, 8 * st_i:8 * st_i + 8]
                prev0b = [bBIG[8]]
                T1 = T1f[:, 0:512].rearrange("p (r s b) -> p r s b", r=2, s=32)
                T2 = T2f[:, 0:512].rearrange("p (r s b) -> p r s b", r=2, s=32)
                arar = S5P[:, P_AR0:P_AR0 + 2, :].unsqueeze(3).to_broadcast([128, 2, 32, 8])
                ai = S5P[:, P_AI, :].unsqueeze(2).to_broadcast([128, 32, 8])
                nai = S5P[:, P_NAI, :].unsqueeze(2).to_broadcast([128, 32, 8])
            for t in range(nt):
                prev = prev0 if t == 0 else h_t(t - 1)
                pbf = prev0b if t == 0 else bQ
                S.dve(lambda e, prev=prev: e.tensor_tensor(out=T1, in0=prev, in1=arar, op=ALU.mult),
                      r=pbf + [bS5P], w=bQ)
                S.dve(lambda e, prev=prev: e.tensor_tensor(out=T2[:, 0], in0=prev[:, 1], in1=nai, op=ALU.mult),
                      r=pbf + [bS5P], w=bQ)
                S.dve(lambda e, prev=prev: e.tensor_tensor(out=T2[:, 1], in0=prev[:, 0], in1=ai, op=ALU.mult),
                      r=pbf + [bS5P], w=bQ)
                S.dve(lambda e: e.tensor_tensor(out=T1, in0=T1, in1=T2, op=ALU.add), r=bQ, w=bQ)
                S.dve(lambda e, t=t: e.tensor_tensor(out=h_t(t), in0=T1, in1=bu_t(t), op=ALU.add), r=bQ, w=bQ)
            if nseq == 1:
                S.dve(lambda e: e.tensor_copy(out=HCS[:, j, :, :], in_=HALL[:, :, :, 31]), r=bQ, w=[bHCS])
            else:
                S.dve(lambda e: e.tensor_copy(out=S0[:, :, :, 8 * st_i:8 * st_i + 8], in_=h_t(3)), r=bQ, w=[bBIG[8]])
            if cstop <= 3:
                continue
            S.act(lambda e: e.copy(out=HB, in_=HALL), r=bQ, w=bQ)
            py, pyb = psum()
            for ct in range(8):
                i_ = 0
                for s4 in range(4):
                    for ri in range(2):
                        s_ = ct * 4 + s4
                        S.pe(lambda e, s_=s_, ri=ri, ct=ct, i_=i_: e.matmul(
                            py[:, ct * 32:(ct + 1) * 32], CL[:, s_, ri, :], HB[:, ri, s_, :],
                            start=(i_ == 0), stop=(i_ == 7)), r=[bCL] + bQ, w=[pyb], inc=(i_ == 7 and ct == 7))
                        i_ += 1
            Y = T1f[:, 0:256].rearrange("p (c t) -> p c t", c=8)
            Y2 = T2f[:, 0:256].rearrange("p (c t) -> p c t", c=8)
            dsk = S5P[:, P_DSK, 0:8].unsqueeze(2).to_broadcast([128, 8, 32])
            S.dve(lambda e: e.tensor_tensor(out=Y, in0=U[:, :, cs:cs + 32], in1=dsk, op=ALU.mult), r=bU + [bS5P], w=bQ)
            S.dve(lambda e: e.tensor_tensor(out=Y, in0=Y, in1=py[:, 0:256].rearrange("p (c t) -> p c t", c=8),
                                            op=ALU.add), r=bQ + [pyb], w=bQ)
            S.dve(lambda e: e.tensor_tensor(out=Y2, in0=Y, in1=Y, op=ALU.mult), r=bQ, w=bQ)
            S.dve(lambda e: e.tensor_scalar(out=Y2, in0=Y2, scalar1=0.044715, scalar2=1.0, op0=ALU.mult, op1=ALU.add),
                  r=bQ, w=bQ)
            S.dve(lambda e: e.tensor_tensor(out=Y2, in0=Y2, in1=Y, op=ALU.mult), r=bQ, w=bQ)
            S.act(lambda e: e.activation(out=Y2, in_=Y2, func=AF.Sigmoid, scale=1.5957691216), r=bQ, w=bQ)
            for k in range(8):
                S.dve(lambda e, k=k: e.tensor_tensor(out=GY(k)[:, cs:cs + 32], in0=Y[:, k, :], in1=Y2[:, k, :],
                                                     op=ALU.mult), r=bQ, w=bGY)
        if cstop <= 4:
            return mixer_none(l, ti)
        if ti == 3:
            for ri, od in ((0, o_re_p), (1, o_im_p)):
                i = rows_i[0]
                rows_i[0] ^= 1
                rt, rb = rows_t[i], bROWS[i]
                pt, pb = psum()
                S.pe(lambda e, ri=ri: e.transpose(pt[0:32, 0:128], HCS[:, j, ri, :], ident(128)), r=[bHCS, bCST], w=[pb])
                S.act(lambda e: e.copy(out=rt[0:32, 0:128], in_=pt[0:32, 0:128]), r=[pb], w=[rb])
                S.dma("sp", od[j].rearrange("(s g) p -> s (g p)", g=2), rt[0:32, 0:128], r=[rb], is_out=True)
        if ti == 4:
            STG = T1f[:, 0:128]
            for ri, od in ((0, o_re_s), (1, o_im_s)):
                for q in range(4):
                    i = rows_i[0]
                    rows_i[0] ^= 1
                    rt, rb = rows_t[i], bROWS[i]
                    S.dve(lambda e, ri=ri, q=q: e.tensor_copy(
                        out=STG.rearrange("p (b s) -> p b s", b=4),
                        in_=S0[:, ri, :, 4 * q:4 * q + 4].rearrange("p s b -> p b s")), r=[bBIG[8]], w=bQ)
                    pt, pb = psum()
                    S.pe(lambda e: e.transpose(pt[:, 0:128], STG, ident(128)), r=bQ + [bCST], w=[pb])
                    S.act(lambda e: e.copy(out=rt[:, 0:128], in_=pt[:, 0:128]), r=[pb], w=[rb])
                    for b4 in range(4):
                        S.dma("sp", od[4 * q + b4, j].rearrange("(s g) p -> s (g p)", g=2),
                              rt[32 * b4:32 * b4 + 32, 0:128], r=[rb], is_out=True)
        for oc in range(8):
            w1, w1b = wslab(W["w_out_c"][j], 0, 8, oc * 128, 128, key=("outc", j))
            w2, w2b = wslab(W["w_out_c"][j], 0, 8, 1024 + oc * 128, 128, key=("outc", j))
            p1, p1b = psum()
            for k in range(8):
                S.pe(lambda e, k=k: e.matmul(p1[:, 0:n], w1[:, k, :], GY(k)[:, 0:n], start=(k == 0), stop=(k == 7)),
                     r=[w1b] + bGY, w=[p1b], inc=(k == 7))
            p2, p2b = psum()
            for k in range(8):
                S.pe(lambda e, k=k: e.matmul(p2[:, 0:n], w2[:, k, :], GY(k)[:, 0:n], start=(k == 0), stop=(k == 7)),
                     r=[w2b] + bGY, w=[p2b], inc=(k == 7))
            sg = LNS[:, 0, 0:n]
            S.act(lambda e: e.activation(out=sg, in_=p2[:, 0:n], func=AF.Sigmoid), r=[p2b], w=[bLNS[0]])
            S.dve(lambda e: e.tensor_tensor(out=sg, in0=sg, in1=p1[:, 0:n], op=ALU.mult), r=[bLNS[0], p1b], w=[bLNS[0]])
            S.dve(lambda e, oc=oc: e.scalar_tensor_tensor(out=ZT[:, oc, 0:n], in0=X[:, oc, 0:n], scalar=ALPHA, in1=sg,
                                                          op0=ALU.mult, op1=ALU.add), r=[bX[ti], bLNS[0]], w=[bZT])


    YAB = sb("YAB", [128, 8, 512], BF16)
    bYAB = Buf("YAB")
    PA = sb("PA", [128, 192])
    bPA = Buf("PA")
    LOR = sb("LOR", [128, 3, 512])
    bLOR = Buf("LOR")
    LNX = sb("LNX", [64, 2, 512])
    bLNX = Buf("LNX")
    CAH = sb("CAH", [128, 2, 4, 30])
    bCAH = Buf("CAH")
    CSH = sb("CSH", [128, 2, 14])
    bCSH = Buf("CSH")
    SHS = sb("SHS", [128, 14, 16])
    bSHS = Buf("SHS")
    STS = sb("STS", [128, 2, 4, 64])
    bSTS = Buf("STS")
    S.dve(lambda e: e.memset(CAH[:, :, :, :], 0.0), w=[bCAH])
    S.dve(lambda e: e.memset(CSH[:, :, :], 0.0), w=[bCSH])
    S.dve(lambda e: e.memset(STS[:, :, :, :], 0.0), w=[bSTS])
    A_MU, A_W0, A_A0, A_KK, A_KA, A_RK, A_CB, A_LG, A_LB, A_CW = 0, 14, 18, 22, 26, 30, 34, 38, 42, 64

    def ab_params(j):
        v4 = lambda name: W[name][j].rearrange("(c p) -> c p", p=128)
        load_cols([W["mu_b"][j].rearrange("(c p) -> c p", p=128), v4("w0_b"), v4("a0_b"), v4("k_k_b"), v4("k_a_b"),
                   W["r_k_b"][j].rearrange("(c h) k -> c (h k)", h=2), v4("conv_a_b"), v4("ln_a_g"), v4("ln_a_b")],
                  0, PA, bPA)
        load_cols([W["conv_a_w"][j].rearrange("t (c p) -> (t c) p", p=128)], A_CW, PA, bPA)
        S.dma("sp", LOR[0:64, 0, :], W["w2_b"][j], w=[bLOR])
        S.dma("sp", LOR[64:128, 1, :], W["a2_b"][j], w=[bLOR])
        S.dma("sp", LOR[:, 2, :], W["g2_b"][j], w=[bLOR])
        S.dma("sp", LNX[:, 0, :], W["ln_x_g"][j].partition_broadcast(64), w=[bLNX])
        S.dma("sp", LNX[:, 1, :], W["ln_x_b"][j].partition_broadcast(64), w=[bLNX])

    def wkv_store(j, dst):
        i = rows_i[0]
        rows_i[0] ^= 1
        rt, rb = rows_t[i], bROWS[i]
        pt, pb = psum()
        for hp in range(4):
            S.pe(lambda e, hp=hp: e.transpose(pt[0:64, hp * 128:(hp + 1) * 128], STS[:, j, hp, :], ident(128)),
                 r=[bSTS, bCST], w=[pb])
        S.act(lambda e: e.copy(out=rt[0:64, 0:512], in_=pt[0:64, 0:512]), r=[pb], w=[rb])
        S.dma("sp", dst.rearrange("(hp h2) v k -> v hp h2 k", h2=2),
              rt[0:64, 0:512].rearrange("v (hp h2 k) -> v hp h2 k", hp=4, h2=2), r=[rb], is_out=True)

    def wkv_load(j, src):
        i = rows_i[0]
        rows_i[0] ^= 1
        rt, rb = rows_t[i], bROWS[i]
        S.dma("sp", rt[0:64, 0:512].rearrange("v (hp h2 k) -> v hp h2 k", hp=4, h2=2),
              src.rearrange("(hp h2) v k -> v hp h2 k", h2=2), w=[rb])
        pt, pb = psum()
        for hp in range(4):
            S.pe(lambda e, hp=hp: e.transpose(pt[:, hp * 64:(hp + 1) * 64], rt[0:64, hp * 128:(hp + 1) * 128], ident(64)),
                 r=[rb, bCST], w=[pb])
        S.act(lambda e: e.copy(out=STS[:, j, :, :], in_=pt[:, 0:256].rearrange("p (h v) -> p h v", h=4)),
              r=[pb], w=[bSTS])

    dbg_done = [False]
    BLf = BL[:, :, :, :].rearrange("p a b c -> p (a b c)").bitcast(F32)
    CLf = CL[:, :, :, :].rearrange("p a b c -> p (a b c)").bitcast(F32)
    bART = [Buf("ART0"), Buf("ART1")]
    bTM = [Buf("TM0"), Buf("TM1")]
    bZN = [Buf("ZN0"), Buf("ZN1")]
    bBK = bBIG[2]
    bZPQ = Buf("ZPQ")
    alias_bufs = bART + bTM + bZN

    def alias_claim(for_rwkv):
        if for_rwkv == 1:
            S.dve(lambda e: e.memset(CLf[:, 4090:4092], 0.0), r=[bBL, bCL, bZT], w=alias_bufs + [bZPQ])
        elif for_rwkv == 2:
            S.dve(lambda e: e.memset(CLf[:, 4090:4092], 0.0), r=alias_bufs + [bZPQ], w=[bZT])
        else:
            S.dve(lambda e: e.memset(CLf[:, 4090:4092], 0.0), r=alias_bufs, w=[bBL, bCL])

    def rw_views(j, col, L, p):
        V = {}
        fm = lambda slot, idx: BIG[:, slot, idx * 256:idx * 256 + 4 * L].rearrange("p (h l) -> p h l", h=4)
        V["LD"], V["LP"], V["EE"], V["KKn"] = fm(0, 0), fm(0, 1), fm(0, 2), fm(0, 3)
        V["KP"], V["AAv"], V["Bv"], V["RKR"] = fm(1, 0), fm(1, 1), fm(1, 2), fm(1, 3)
        V["BT"] = BIG[:, 2, 512:512 + 4 * L].rearrange("p (h l) -> p h l", h=4)
        V["KT"] = BIG[:, 2, 768:768 + 4 * L].rearrange("p (h l) -> p h l", h=4)
        V["BHf"], V["KHf"], V["TMPf"] = fm(3, 0), fm(3, 1), fm(3, 2)
        V["SGL"] = BIG[:, 3, 768:768 + L]
        V["TW"] = BIG[0:64, 3, 832:832 + L]
        ar0 = p * 520
        V["ART"] = CLf[:, ar0:ar0 + 8 * L].rearrange("p (h a l) -> p h a l", h=4, a=2)
        V["PL"] = CLf[:, ar0 + 512:ar0 + 516]
        tm0 = p * 2048
        V["Vt"], V["BHt"] = BLf[0:L, tm0:tm0 + 512], BLf[0:L, tm0 + 512:tm0 + 1024]
        V["KHt"], V["Gt"] = BLf[0:L, tm0 + 1024:tm0 + 1536], BLf[0:L, tm0 + 1536:tm0 + 2048]
        V["RKt"] = CLf[0:L, 1040 + 8 * p:1048 + 8 * p]
        zr = lambda i: ZT[0:L, i, 0:8 * L].rearrange("p (h l) -> p h l", h=8)
        V["Pq"] = [zr(0), zr(1)]
        V["Qq"] = [zr(2), zr(3)]
        if p == 0:
            V["Mv"], V["NAK"], V["NRB"], V["NRK"] = zr(4), zr(5), zr(6), zr(7)
        else:
            zc = lambda i: CLf[0:L, 2048 + 512 * i:2048 + 512 * i + 8 * L].rearrange("p (h l) -> p h l", h=8)
            V["Mv"], V["NAK"], V["NRB"], V["NRK"] = zc(0), zc(1), zc(2), zc(3)
        V["Zs"], V["Ws"] = BIG[0:L, 6, 0:512], BIG[0:L, 6, 512:1024]
        V["Ys"], V["Y2"] = BIG[0:L, 7, 0:512], BIG[0:L, 7, 512:1024]
        V["MU"], V["RS"] = BIG[0:L, 8, 0:8], BIG[0:L, 8, 8:16]
        return V

    hd = lambda h: (h // 2, 64 * (h % 2))
    HORD = [0, 2, 4, 6, 1, 3, 5, 7]

    def rwkv_part1(j, col, L, nlev, p):
        B0, B1, B2, B3 = bBIG[0], bBIG[1], bBIG[2], bBIG[3]
        V = rw_views(j, col, L, p)
        LD, LP, EE, KKn, KP, AAv, Bv, RKR = (V[k] for k in ("LD", "LP", "EE", "KKn", "KP", "AAv", "Bv", "RKR"))
        BT, KT, BHf, KHf, TMPf, SGL, TW = (V[k] for k in ("BT", "KT", "BHf", "KHf", "TMPf", "SGL", "TW"))
        ART, PL, Vt, BHt, KHt, Gt, RKt = (V[k] for k in ("ART", "PL", "Vt", "BHt", "KHt", "Gt", "RKt"))
        Pq, Qq, Mv, NAK, NRB, NRK = (V[k] for k in ("Pq", "Qq", "Mv", "NAK", "NRB", "NRK"))
        bA, bT, bZ = bART[p], bTM[p], bZN[p]
        R_ = QT[:, 0:4, col:col + L]
        K_ = QT[:, 4:8, col:col + L]
        V_ = QT[:, 8:12, col:col + L]
        WLv = QT[0:64, 12, col:col + L]
        ALv = QT[64:128, 12, col:col + L]
        GLv = QT[:, 13, col:col + L]
        bc4 = lambda c: PA[:, c:c + 4].unsqueeze(2).to_broadcast([128, 4, L])
        ones = CST[:, C_ONE:C_ONE + L]
        TT = lambda o, a, b, op, r, w: S.dve(lambda e: e.tensor_tensor(out=o, in0=a, in1=b, op=op), r=r, w=w)
        S.act(lambda e: e.activation(out=TW, in_=WLv, func=AF.Tanh), r=[bQA], w=[B3])
        pw, pwb = psum()
        for hp in range(4):
            S.pe(lambda e, hp=hp: e.matmul(pw[:, hp * L:(hp + 1) * L], LOR[0:64, 0, hp * 128:(hp + 1) * 128], TW,
                                           start=True, stop=True), r=[bLOR, B3], w=[pwb], inc=(hp == 3))
        pa_, pab = psum()
        for hp in range(4):
            S.pe(lambda e, hp=hp: e.matmul(pa_[:, hp * L:(hp + 1) * L], LOR[64:128, 1, hp * 128:(hp + 1) * 128], ALv,
                                           start=True, stop=True), r=[bLOR, bQA], w=[pab], inc=(hp == 3))
        for hp in range(4):
            S.act(lambda e, hp=hp: e.activation(out=LD[:, hp, :], in_=pw[:, hp * L:(hp + 1) * L], func=AF.Sigmoid,
                                                bias=PA[:, A_W0 + hp:A_W0 + hp + 1]), r=[pwb, bPA], w=[B0])
            S.act(lambda e, hp=hp: e.activation(out=AAv[:, hp, :], in_=pa_[:, hp * L:(hp + 1) * L], func=AF.Sigmoid,
                                                bias=PA[:, A_A0 + hp:A_A0 + hp + 1]), r=[pab, bPA], w=[B1])
        S.dve(lambda e: e.tensor_scalar(out=LD, in0=LD, scalar1=-0.6065306597126334, scalar2=None, op0=ALU.mult),
              r=[B0], w=[B0])
        for hp in range(4):
            S.dve(lambda e, hp=hp: e.tensor_tensor_scan(out=LP[:, hp, :], data0=ones, data1=LD[:, hp, :], initial=0.0,
                                                        op0=ALU.mult, op1=ALU.add), r=[B0, bCST], w=[B0])
        TT(KKn, K_, bc4(A_KK), ALU.mult, [bQA, bPA], [B0])
        TT(TMPf, KKn, KKn, ALU.mult, [B0], [B3])
        pss, pssb = psum()
        for hp in range(4):
            S.pe(lambda e, hp=hp: e.matmul(pss[:, hp * L:(hp + 1) * L], CST[:, C_BONE:C_BONE + 128], TMPf[:, hp, :],
                                           start=True, stop=True), r=[bCST, B3], w=[pssb], inc=(hp == 3))
        S.dve(lambda e: e.tensor_scalar(out=TMPf, in0=pss[:, 0:4 * L].rearrange("p (h l) -> p h l", h=4),
                                        scalar1=1e-24, scalar2=None, op0=ALU.max), r=[pssb], w=[B3])
        S.act(lambda e: e.activation(out=TMPf, in_=TMPf, func=AF.Sqrt), r=[B3], w=[B3])
        S.dve(lambda e: e.reciprocal(out=TMPf, in_=TMPf), r=[B3], w=[B3])
        TT(KKn, KKn, TMPf, ALU.mult, [B0, B3], [B0])
        S.dve(lambda e: e.tensor_scalar(out=TMPf, in0=AAv, scalar1=-1.0, scalar2=None, op0=ALU.add), r=[B1], w=[B3])
        TT(TMPf, TMPf, bc4(A_KA), ALU.mult, [B3, bPA], [B3])
        S.dve(lambda e: e.scalar_tensor_tensor(out=KP, in0=TMPf, scalar=1.0, in1=K_, op0=ALU.add, op1=ALU.mult),
              r=[B3, bQA], w=[B1])
        TT(Bv, KKn, AAv, ALU.mult, [B0, B1], [B1])
        S.act(lambda e: e.activation(out=EE, in_=LP, func=AF.Exp, scale=-1.0), r=[B0], w=[B0])
        TT(BT, Bv, EE, ALU.mult, [B1, B0], [bBK])
        TT(KT, KP, EE, ALU.mult, [B1, B0], [bBK])
        S.act(lambda e: e.activation(out=EE, in_=LP, func=AF.Exp), r=[B0], w=[B0])
        TT(ART[:, :, 1, :], R_, EE, ALU.mult, [bQA, B0], [bA])
        TT(TMPf, LP, LD, ALU.subtract, [B0], [B3])
        S.act(lambda e: e.activation(out=EE, in_=TMPf, func=AF.Exp), r=[B3], w=[B0])
        S.dve(lambda e: e.scalar_tensor_tensor(out=ART[:, :, 0, :], in0=KKn, scalar=-1.0, in1=EE,
                                               op0=ALU.mult, op1=ALU.mult), r=[B0], w=[bA])
        TT(TMPf, LP[:, :, L - 1:L].to_broadcast([128, 4, L]), LP, ALU.subtract, [B0], [B3])
        S.act(lambda e: e.activation(out=EE, in_=TMPf, func=AF.Exp), r=[B3], w=[B0])
        TT(BHf, Bv, EE, ALU.mult, [B1, B0], [B3])
        TT(KHf, KP, EE, ALU.mult, [B1, B0], [B3])
        S.act(lambda e: e.activation(out=PL, in_=LP[:, :, L - 1], func=AF.Exp), r=[B0], w=[bA])
        TT(RKR, R_, KP, ALU.mult, [bQA, B1], [B1])
        TT(RKR, RKR, bc4(A_RK), ALU.mult, [B1, bPA], [B1])
        S.act(lambda e: e.activation(out=SGL, in_=GLv, func=AF.Sigmoid), r=[bQA], w=[B3])
        for src, sbf, dst in ((V_, bQA, Vt), (BHf, B3, BHt), (KHf, B3, KHt)):
            pt, pb = psum()
            for hp in range(4):
                S.pe(lambda e, hp=hp, src=src, pt=pt: e.transpose(pt[0:L, hp * 128:(hp + 1) * 128], src[:, hp, :], ident(128)),
                     r=[sbf, bCST], w=[pb])
            S.act(lambda e, dst=dst, pt=pt: e.copy(out=dst, in_=pt[0:L, 0:512]), r=[pb], w=[bT])
        prk, prkb = psum()
        for hp in range(4):
            S.pe(lambda e, hp=hp: e.matmul(prk[0:L, 2 * hp:2 * hp + 2], RKR[:, hp, :], CST[:, C_BIND:C_BIND + 2],
                                           start=True, stop=True), r=[B1, bCST], w=[prkb], inc=(hp == 3))
        S.act(lambda e: e.copy(out=RKt, in_=prk[0:L, 0:8]), r=[prkb], w=[bT])
        pg, pgb = psum()
        S.pe(lambda e: e.matmul(pg[0:L, 0:512], SGL, LOR[:, 2, :], start=True, stop=True), r=[B3, bLOR], w=[pgb])
        S.act(lambda e: e.copy(out=Gt, in_=pg[0:L, 0:512]), r=[pgb], w=[bT])
        specs = [("bt", 0, Qq[0], C_MSU, bZPQ), ("bt", 1, NRB, C_MUI, bZ), ("kt", 0, NAK, C_MSU, bZ), ("kt", 1, NRK, C_MUI, bZ)]
        for lname, a_idx, dst, mcol, dbf in specs:
            pt, pb = psum()
            for h in HORD:
                hp, ho = hd(h)
                lh = (BT if lname == "bt" else KT)[ho:ho + 64, hp, :]
                if h == 1:
                    S.fence("pe")
                S.pe(lambda e, h=h, lh=lh, hp=hp, ho=ho, a_idx=a_idx, pt=pt: e.matmul(
                    pt[0:L, h * L:(h + 1) * L], lh, ART[ho:ho + 64, hp, a_idx, :], start=True, stop=True),
                    r=[bBK, bA], w=[pb], inc=(h in (6, 7)))
            S.dve(lambda e, dst=dst, mcol=mcol, pt=pt: e.tensor_tensor(
                out=dst, in0=pt[0:L, 0:8 * L].rearrange("p (h l) -> p h l", h=8),
                in1=CST[0:L, mcol:mcol + L].unsqueeze(1).to_broadcast([L, 8, L]), op=ALU.mult),
                r=[pb, bCST], w=[dbf])
        pt, pb = psum()
        for h in HORD:
            hp, ho = hd(h)
            if h == 1:
                S.fence("pe")
            S.pe(lambda e, h=h, hp=hp, ho=ho, pt=pt: e.matmul(pt[0:L, h * L:(h + 1) * L], ART[ho:ho + 64, hp, 0, :],
                                                              BT[ho:ho + 64, hp, :], start=True, stop=True),
                 r=[bBK, bA], w=[pb], inc=(h in (6, 7)))
        S.dve(lambda e, pt=pt: e.tensor_tensor(
            out=Pq[0], in0=pt[0:L, 0:8 * L].rearrange("p (h l) -> p h l", h=8),
            in1=CST[0:L, C_MSL:C_MSL + L].unsqueeze(1).to_broadcast([L, 8, L]), op=ALU.mult), r=[pb, bCST], w=[bZPQ])
        S.dve(lambda e: e.tensor_tensor(out=Mv, in0=Qq[0], in1=CST[0:L, C_ID:C_ID + L].unsqueeze(1).to_broadcast([L, 8, L]),
                                        op=ALU.add), r=[bZPQ, bCST], w=[bZ])
        cur = 0
        for lev in range(1, nlev):
            nxt = 1 - cur
            pP, pPb = psum()
            for h in range(8):
                S.pe(lambda e, h=h, cur=cur, pP=pP: e.matmul(pP[0:L, h * L:(h + 1) * L], Qq[cur][:, h, :], Pq[cur][:, h, :],
                                                             start=True, stop=True), r=[bZPQ], w=[pPb], inc=(h == 7))
            need_q = lev < nlev - 1
            if need_q:
                pQ, pQb = psum()
                for h in range(8):
                    S.pe(lambda e, h=h, cur=cur, pQ=pQ: e.matmul(pQ[0:L, h * L:(h + 1) * L], Pq[cur][:, h, :], Qq[cur][:, h, :],
                                                                 start=True, stop=True), r=[bZPQ], w=[pQb], inc=(h == 7))
            S.act(lambda e, nxt=nxt, pP=pP: e.copy(out=Pq[nxt], in_=pP[0:L, 0:8 * L].rearrange("p (h l) -> p h l", h=8)),
                  r=[pPb], w=[bZPQ])
            if need_q:
                S.dve(lambda e, nxt=nxt, pQ=pQ: e.tensor_copy(out=Qq[nxt], in_=pQ[0:L, 0:8 * L].rearrange("p (h l) -> p h l", h=8)),
                      r=[pQb], w=[bZPQ])
            pM, pMb = psum()
            for h in range(8):
                S.pe(lambda e, h=h, nxt=nxt, pM=pM: e.matmul(pM[0:L, h * L:(h + 1) * L], Pq[nxt][:, h, :], Mv[:, h, :],
                                                             start=True, stop=True), r=[bZPQ, bZ], w=[pMb], inc=(h == 7))
            S.dve(lambda e, pM=pM: e.tensor_tensor(out=Mv, in0=Mv, in1=pM[0:L, 0:8 * L].rearrange("p (h l) -> p h l", h=8),
                                                   op=ALU.add), r=[bZ, pMb], w=[bZ])
            cur = nxt

    def rwkv_part2(j, col, L, p):
        B6, B7, B8 = bBIG[6], bBIG[7], bBIG[8]
        V = rw_views(j, col, L, p)
        ART, PL, Vt, BHt, KHt, Gt, RKt = (V[k] for k in ("ART", "PL", "Vt", "BHt", "KHt", "Gt", "RKt"))
        Mv, NAK, NRB, NRK = (V[k] for k in ("Mv", "NAK", "NRB", "NRK"))
        Zs, Ws, Ys, Y2, MU, RS = (V[k] for k in ("Zs", "Ws", "Ys", "Y2", "MU", "RS"))
        bA, bT, bZ = bART[p], bTM[p], bZN[p]
        ST = STS[:, j, :, :]
        TT = lambda o, a, b, op, r, w: S.dve(lambda e: e.tensor_tensor(out=o, in0=a, in1=b, op=op), r=r, w=w)
        pz, pzb = psum()
        S.fence("pe")
        for i_, h in enumerate((1, 3, 5, 7)):
            hp, ho = hd(h)
            S.pe(lambda e, h=h, hp=hp, ho=ho, i_=i_: e.matmul(pz[0:L, h * 64:(h + 1) * 64], ART[ho:ho + 64, hp, 0, :],
                                                              ST[ho:ho + 64, hp, :], start=(i_ == 0), stop=False,
                                                              skip_group_check=True),
                 r=[bA, bSTS], w=[pzb], inc=True)
        S.fence("pe")
        for h in (0, 2, 4, 6):
            hp, ho = hd(h)
            S.pe(lambda e, h=h, hp=hp, ho=ho: e.matmul(pz[0:L, h * 64:(h + 1) * 64], ART[ho:ho + 64, hp, 0, :],
                                                       ST[ho:ho + 64, hp, :], start=False, stop=False,
                                                       skip_group_check=True),
                 r=[bA, bSTS], w=[pzb], inc=False)
        for h in range(8):
            S.pe(lambda e, h=h: e.matmul(pz[0:L, h * 64:(h + 1) * 64], NAK[:, h, :], Vt[:, h * 64:(h + 1) * 64],
                                         start=False, stop=True, skip_group_check=True),
                 r=[bZ, bT], w=[pzb], inc=(h == 7))
        S.act(lambda e: e.copy(out=Zs, in_=pz[0:L, 0:512]), r=[pzb], w=[B6])
        pw2, pw2b = psum()
        for h in range(8):
            S.pe(lambda e, h=h: e.matmul(pw2[0:L, h * 64:(h + 1) * 64], Mv[:, h, :], Zs[:, h * 64:(h + 1) * 64],
                                         start=True, stop=True), r=[bZ, B6], w=[pw2b], inc=(h == 7))
        S.act(lambda e: e.copy(out=Ws, in_=pw2[0:L, 0:512]), r=[pw2b], w=[B6])
        py, pyb = psum()
        S.fence("pe")
        for i_, h in enumerate((1, 3, 5, 7)):
            hp, ho = hd(h)
            S.pe(lambda e, h=h, hp=hp, ho=ho, i_=i_: e.matmul(py[0:L, h * 64:(h + 1) * 64], ART[ho:ho + 64, hp, 1, :],
                                                              ST[ho:ho + 64, hp, :], start=(i_ == 0), stop=False,
                                                              skip_group_check=True),
                 r=[bA, bSTS], w=[pyb], inc=True)
        S.fence("pe")
        for h in (0, 2, 4, 6):
            hp, ho = hd(h)
            S.pe(lambda e, h=h, hp=hp, ho=ho: e.matmul(py[0:L, h * 64:(h + 1) * 64], ART[ho:ho + 64, hp, 1, :],
                                                       ST[ho:ho + 64, hp, :], start=False, stop=False,
                                                       skip_group_check=True),
                 r=[bA, bSTS], w=[pyb], inc=False)
        for h in range(8):
            S.pe(lambda e, h=h: e.matmul(py[0:L, h * 64:(h + 1) * 64], NRB[:, h, :], Ws[:, h * 64:(h + 1) * 64],
                                         start=False, stop=False, skip_group_check=True),
                 r=[bZ, B6], w=[pyb], inc=False)
            S.pe(lambda e, h=h: e.matmul(py[0:L, h * 64:(h + 1) * 64], NRK[:, h, :], Vt[:, h * 64:(h + 1) * 64],
                                         start=False, stop=True, skip_group_check=True),
                 r=[bZ, bT], w=[pyb], inc=(h == 7))
        psn, psnb = psum()
        for h in range(8):
            hp, ho = hd(h)
            S.pe(lambda e, h=h, hp=hp, ho=ho: e.matmul(psn[ho:ho + 64, hp * 64:(hp + 1) * 64], BHt[:, h * 64:(h + 1) * 64],
                                                       Ws[:, h * 64:(h + 1) * 64], start=True, stop=False),
                 r=[bT, B6], w=[psnb], inc=False)
            S.pe(lambda e, h=h, hp=hp, ho=ho: e.matmul(psn[ho:ho + 64, hp * 64:(hp + 1) * 64], KHt[:, h * 64:(h + 1) * 64],
                                                       Vt[:, h * 64:(h + 1) * 64], start=False, stop=True),
                 r=[bT], w=[psnb], inc=(h == 7))
        S.act(lambda e: e.copy(out=Ys, in_=py[0:L, 0:512]), r=[pyb], w=[B7])
        TT(ST, ST, PL.unsqueeze(2).to_broadcast([128, 4, 64]), ALU.mult, [bSTS, bA], [bSTS])
        TT(ST, ST, psn[:, 0:256].rearrange("p (h v) -> p h v", h=4), ALU.add, [bSTS, psnb], [bSTS])
        Y3 = Ys.rearrange("p (h v) -> p h v", h=8)
        Y23 = Y2.rearrange("p (h v) -> p h v", h=8)
        V3 = Vt.rearrange("p (h v) -> p h v", h=8)
        b8 = lambda ap: ap.unsqueeze(2).to_broadcast([L, 8, 64])
        S.dve(lambda e: e.reduce_sum(out=MU, in_=Y3, axis=AX.X), r=[B7], w=[B8])
        S.dve(lambda e: e.tensor_scalar(out=MU, in0=MU, scalar1=1.0 / 64, scalar2=None, op0=ALU.mult), r=[B8], w=[B8])
        TT(Y3, Y3, b8(MU), ALU.subtract, [B7, B8], [B7])
        TT(Y23, Y3, Y3, ALU.mult, [B7], [B7])
        S.dve(lambda e: e.reduce_sum(out=RS, in_=Y23, axis=AX.X), r=[B7], w=[B8])
        S.dve(lambda e: e.tensor_scalar(out=RS, in0=RS, scalar1=1.0 / 64, scalar2=GN_EPS, op0=ALU.mult, op1=ALU.add),
              r=[B8], w=[B8])
        S.act(lambda e: e.activation(out=RS, in_=RS, func=AF.Sqrt), r=[B8], w=[B8])
        S.dve(lambda e: e.reciprocal(out=RS, in_=RS), r=[B8], w=[B8])
        TT(Y3, Y3, b8(RS), ALU.mult, [B7, B8], [B7])
        TT(Ys, Ys, LNX[0:L, 0, :], ALU.mult, [B7, bLNX], [B7])
        TT(Ys, Ys, LNX[0:L, 1, :], ALU.add, [B7, bLNX], [B7])
        TT(Y23, V3, b8(RKt), ALU.mult, [bT], [B7])
        TT(Ys, Ys, Y2, ALU.add, [B7], [B7])
        TT(Ys, Ys, Gt, ALU.mult, [B7, bT], [B7])
        pT, pTb = psum()
        for hp in range(4):
            S.pe(lambda e, hp=hp: e.transpose(pT[:, hp * L:(hp + 1) * L], Ys[:, hp * 128:(hp + 1) * 128], ident(L)),
                 r=[B7, bCST], w=[pTb])
        S.act(lambda e: e.copy(out=YAB[:, 4:8, col:col + L], in_=pT[:, 0:4 * L].rearrange("p (h l) -> p h l", h=4)),
              r=[pTb], w=[bYAB])

    def mixer_ab(l, ti):
        j = l // 2
        c0, n, nseq, T = TILES[ti]
        E = 30 + T
        w_in = W["w_in_ab"][j]
        cast_xb(ti)
        UEXT = lambda c: BIG[:, c, 0:nseq * E].rearrange("p (s e) -> p s e", s=nseq)
        CVA = lambda c: BIG[:, 4 + c, 0:n]
        UALL = BIG[:, 0:4, 0:nseq * E].rearrange("p c (s e) -> p c s e", s=nseq)
        if nseq == 1:
            if ti == 0:
                S.dve(lambda e: e.memset(UALL[:, :, 0, 0:30], 0.0), w=bBIG[0:4])
            else:
                S.dve(lambda e: e.tensor_copy(out=UALL[:, :, 0, 0:30], in_=CAH[:, j, :, :]), r=[bCAH], w=bBIG[0:4])
        else:
            for q in range(4):
                i = rows_i[0]
                rows_i[0] ^= 1
                rt, rb = rows_t[i], bROWS[i]
                for b4 in range(4):
                    S.dma("sp", rt[30 * b4:30 * b4 + 30, 0:512], st_conv_a[4 * q + b4, j], w=[rb])
                pt, pb = psum()
                for c in range(4):
                    S.pe(lambda e, c=c: e.transpose(pt[:, c * 120:(c + 1) * 120], rt[0:120, c * 128:(c + 1) * 128], ident(120)),
                         r=[rb, bCST], w=[pb])
                for c in range(4):
                    S.act(lambda e, c=c, q=q: e.copy(out=UEXT(c)[:, 4 * q:4 * q + 4, 0:30],
                                                     in_=pt[:, c * 120:(c + 1) * 120].rearrange("p (b r) -> p b r", b=4)),
                          r=[pb], w=[bBIG[c]])
        for c in range(4):
            wv, wvb = wslab(w_in, 0, 8, c * 128, 128, key=("inab", j))
            wg, wgb = wslab(w_in, 0, 8, 512 + c * 128, 128, key=("inab", j))
            pv, pvb = psum()
            for k in range(8):
                S.pe(lambda e, k=k: e.matmul(pv[:, 0:n], wv[:, k, :], XB[:, k, 0:n], start=(k == 0), stop=(k == 7)),
                     r=[wvb, bXB], w=[pvb], inc=(k == 7))
            pg, pgb = psum()
            for k in range(8):
                S.pe(lambda e, k=k: e.matmul(pg[:, 0:n], wg[:, k, :], XB[:, k, 0:n], start=(k == 0), stop=(k == 7)),
                     r=[wgb, bXB], w=[pgb], inc=(k == 7))
            sg = LNS[:, 0, 0:n]
            S.act(lambda e: e.activation(out=sg, in_=pg[:, 0:n], func=AF.Sigmoid), r=[pgb], w=[bLNS[0]])
            S.dve(lambda e, c=c: e.tensor_tensor(out=UEXT(c)[:, :, 30:E], in0=sg.rearrange("p (s t) -> p s t", s=nseq),
                                                 in1=pv[:, 0:n].rearrange("p (s t) -> p s t", s=nseq), op=ALU.mult),
                  r=[bLNS[0], pvb], w=[bBIG[c]])
        for c in range(4):
            acc = CVA(c).rearrange("p (s t) -> p s t", s=nseq)
            S.dve(lambda e, c=c, acc=acc: e.tensor_scalar(out=acc, in0=UEXT(c)[:, :, 0:T], scalar1=PA[:, A_CW + c:A_CW + c + 1],
                                                          scalar2=PA[:, A_CB + c:A_CB + c + 1], op0=ALU.mult, op1=ALU.add),
                  r=[bBIG[c], bPA], w=[bBIG[4 + c]])
            for t_ in range(1, 31):
                S.dve(lambda e, c=c, acc=acc, t_=t_: e.scalar_tensor_tensor(
                    out=acc, in0=UEXT(c)[:, :, t_:t_ + T], scalar=PA[:, A_CW + 4 * t_ + c:A_CW + 4 * t_ + c + 1], in1=acc,
                    op0=ALU.mult, op1=ALU.add), r=[bBIG[c], bPA, bBIG[4 + c]], w=[bBIG[4 + c]])
        if nseq == 1:
            S.dve(lambda e: e.tensor_copy(out=CAH[:, j, :, :], in_=UALL[:, :, 0, T:T + 30]), r=bBIG[0:4], w=[bCAH])
            if ti == 3:
                def dst(rt, rb, done, g):
                    S.dma("sp", o_conv_p[j], rt[0:30, 0:512], r=[rb], is_out=True)
                emit_rows(lambda c: CAH[:, j, c, :], 30, 4, dst, [bCAH])
        else:
            for q in range(4):
                i = rows_i[0]
                rows_i[0] ^= 1
                rt, rb = rows_t[i], bROWS[i]
                pt, pb = psum()
                for c in range(4):
                    stg = LNS[:, c, 0:120]
                    S.dve(lambda e, c=c, q=q, stg=stg: e.tensor_copy(out=stg.rearrange("p (b r) -> p b r", b=4),
                                                                     in_=UEXT(c)[:, 4 * q:4 * q + 4, T:T + 30]),
                          r=[bBIG[c]], w=[bLNS[c]])
                    S.pe(lambda e, c=c, stg=stg: e.transpose(pt[0:120, c * 128:(c + 1) * 128], stg, ident(128)),
                         r=[bLNS[c], bCST], w=[pb])
                S.act(lambda e: e.copy(out=rt[0:120, 0:512], in_=pt[0:120, 0:512]), r=[pb], w=[rb])
                for b4 in range(4):
                    S.dma("sp", o_conv_s[4 * q + b4, j], rt[30 * b4:30 * b4 + 30, 0:512], r=[rb], is_out=True)
        def outf(k, tmp, tb, g, b):
            S.act(lambda e: e.activation(out=YAB[:, k, 0:n], in_=tmp, func=AF.Silu, scale=g, bias=b),
                  r=[tb, bPA], w=[bYAB])
        layer_norm(lambda k: CVA(k), bBIG[4:8], 4, n, A_LG, A_LB, outf, ptile=PA)
        ZTf = ZT[:, :, :].rearrange("p a b -> p (a b)")
        PBX = lambda i: ZTf[:, i * 1024:i * 1024 + nseq * (1 + T)].rearrange("p (s e) -> p s e", s=nseq)
        DTMP = ZTf[:, 2048:2048 + n].rearrange("p (s t) -> p s t", s=nseq)
        if nseq > 1:
            for half in range(2):
                i = rows_i[0]
                rows_i[0] ^= 1
                rt, rb = rows_t[i], bROWS[i]
                S.dma("sp", rt[0:16, 0:896], st_shift[:, j, half * 896:(half + 1) * 896], w=[rb])
                pt, pb = psum()
                for m in range(7):
                    S.pe(lambda e, m=m: e.transpose(pt[:, m * 16:(m + 1) * 16], rt[0:16, m * 128:(m + 1) * 128], ident(16)),
                         r=[rb, bCST], w=[pb])
                S.act(lambda e, half=half: e.copy(out=SHS[:, 7 * half:7 * half + 7, :],
                                                  in_=pt[:, 0:112].rearrange("p (m b) -> p m b", m=7)), r=[pb], w=[bSHS])
        for m in range(14):
            wp, wpb = wslab(w_in, 0, 8, 1024 + m * 128, 128, key=("inab", j))
            pp, ppb = psum()
            for k in range(8):
                S.pe(lambda e, k=k, m=m: e.matmul(pp[:, 0:n], wp[:, k, :], XB[:, k, 0:n],
                                                  start=(k == 0), stop=(k == 7)), r=[wpb, bXB], w=[ppb], inc=(k == 7))
            pbx = PBX(m % 2)
            S.act(lambda e, pbx=pbx: e.copy(out=pbx[:, :, 1:1 + T], in_=pp[:, 0:n].rearrange("p (s t) -> p s t", s=nseq)),
                  r=[ppb], w=[bZT])
            if nseq == 1:
                if ti == 0:
                    S.dve(lambda e, pbx=pbx: e.memset(pbx[:, :, 0:1], 0.0), w=[bZT])
                else:
                    S.dve(lambda e, pbx=pbx, m=m: e.tensor_copy(out=pbx[:, 0, 0:1], in_=CSH[:, j, m:m + 1]), r=[bCSH], w=[bZT])
            else:
                S.dve(lambda e, pbx=pbx, m=m: e.tensor_copy(out=pbx[:, :, 0], in_=SHS[:, m, :]), r=[bSHS], w=[bZT])
            S.dve(lambda e, pbx=pbx: e.tensor_tensor(out=DTMP, in0=pbx[:, :, 0:T], in1=pbx[:, :, 1:1 + T], op=ALU.subtract),
                  r=[bZT], w=[bZT])
            S.dve(lambda e, pbx=pbx, m=m: e.scalar_tensor_tensor(
                out=QT[:, m, 0:n].rearrange("p (s t) -> p s t", s=nseq), in0=DTMP, scalar=PA[:, A_MU + m:A_MU + m + 1],
                in1=pbx[:, :, 1:1 + T], op0=ALU.mult, op1=ALU.add), r=[bZT, bPA], w=[bQA])
            if nseq == 1:
                S.dve(lambda e, pbx=pbx, m=m: e.tensor_copy(out=CSH[:, j, m:m + 1], in_=pbx[:, 0, T:T + 1]), r=[bZT], w=[bCSH])
            else:
                S.dve(lambda e, pbx=pbx, m=m: e.tensor_copy(out=SHS[:, m, :], in_=pbx[:, :, T]), r=[bZT], w=[bSHS])
        if ti == 3:
            i = rows_i[0]
            rows_i[0] ^= 1
            rt, rb = rows_t[i], bROWS[i]
            pt, pb = psum()
            S.pe(lambda e: e.transpose(pt[0:14, 0:128], CSH[:, j, :], ident(128)), r=[bCSH, bCST], w=[pb])
            S.act(lambda e: e.copy(out=rt[0:14, 0:128], in_=pt[0:14, 0:128]), r=[pb], w=[rb])
            S.dma("sp", o_shift_p[j].rearrange("(c p) -> c p", p=128), rt[0:14, 0:128], r=[rb], is_out=True)
        if ti == 4:
            for half in range(2):
                i = rows_i[0]
                rows_i[0] ^= 1
                rt, rb = rows_t[i], bROWS[i]
                for m0, mn in ((0, 4), (4, 3)):
                    pt, pb = psum()
                    for mm in range(mn):
                        m = m0 + mm
                        S.pe(lambda e, m=m, mm=mm, half=half: e.transpose(pt[0:16, mm * 128:(mm + 1) * 128],
                                                                          SHS[:, 7 * half + m, :], ident(128)),
                             r=[bSHS, bCST], w=[pb])
                    S.act(lambda e, m0=m0, mn=mn, pt=pt: e.copy(out=rt[0:16, m0 * 128:(m0 + mn) * 128], in_=pt[0:16, 0:mn * 128]),
                          r=[pb], w=[rb])
                S.dma("sp", o_shift_s[:, j, half * 896:(half + 1) * 896], rt[0:16, 0:896], r=[rb], is_out=True)
        if cfg.get("norwkv"):
            S.dve(lambda e: e.memset(YAB[:, 4:8, 0:n], 0.0), w=[bYAB])
        elif nseq == 1:
            alias_claim(1)
            rwkv_part1(j, 0, 64, 6, 0)
            for ch in range(8):
                S.begin_rec()
                ps_pool[0] = 0
                if ch + 1 < 8:
                    rwkv_part1(j, (ch + 1) * 64, 64, 6, (ch + 1) % 2)
                ra = S.end_rec()
                S.begin_rec()
                ps_pool[0] = 1
                rwkv_part2(j, ch * 64, 64, ch % 2)
                rb_ = S.end_rec()
                ps_pool[0] = None
                S.merge(ra, rb_)
            if ti == 3:
                wkv_store(j, o_wkv_p[j])
        else:
            alias_claim(1)
            rwkv_part1(j, 0, 4, 2, 0)
            for b in range(NSB):
                S.begin_rec()
                ps_pool[0] = 0
                if b + 1 < NSB:
                    rwkv_part1(j, (b + 1) * 4, 4, 2, (b + 1) % 2)
                ra = S.end_rec()
                S.begin_rec()
                ps_pool[0] = 1
                wkv_load(j, st_wkv[b, j])
                rwkv_part2(j, b * 4, 4, b % 2)
                wkv_store(j, o_wkv_s[b, j])
                rb_ = S.end_rec()
                ps_pool[0] = None
                S.merge(ra, rb_)
        if not cfg.get("norwkv"):
            alias_claim(2)
        for oc in range(8):
            wo, wob = wslab(W["w_out_ab"][j], 0, 8, oc * 128, 128, key=("outab", j))
            po, pob = psum()
            for k in range(8):
                S.pe(lambda e, k=k: e.matmul(po[:, 0:n], wo[:, k, :], YAB[:, k, 0:n], start=(k == 0), stop=(k == 7)),
                     r=[wob, bYAB], w=[pob], inc=(k == 7))
            S.dve(lambda e, oc=oc: e.scalar_tensor_tensor(out=ZT[:, oc, 0:n], in0=X[:, oc, 0:n], scalar=ALPHA, in1=po[:, 0:n],
                                                          op0=ALU.mult, op1=ALU.add), r=[bX[ti], pob], w=[bZT])


    def mixer_none(l, ti):
        c0, n, nseq, T = TILES[ti]
        for k in range(8):
            S.dve(lambda e, k=k: e.tensor_scalar(out=ZT[:, k, 0:n], in0=X[:, k, 0:n], scalar1=ALPHA, scalar2=None,
                                                 op0=ALU.mult), r=[bX[ti]], w=[bZT])

    def ln1(l, ti):
        c0, n, nseq, T = TILES[ti]

        def outf(k, tmp, tb, g, b):
            S.act(lambda e: e.activation(out=X[:, k, 0:n], in_=tmp, func=AF.Identity, scale=g, bias=b),
                  r=[tb, bPAR], w=[bX[ti]])
        layer_norm(lambda k: ZT[:, k, 0:n], bZT, 8, n, 104, 112, outf)

    tiles = cfg.get("tiles", [0, 1, 2, 3, 4])
    for ti in tiles:
        load_x(ti)
        for l in range(depth):
            ffn_params(l)
            mix = cfg.get("mix", "full")
            if l % 2 == 1 and mix in ("full", "c"):
                s5_setup(l // 2)
                if ti == 4:
                    s5_sample_state(l // 2)
                if cfg.get("s5dbg", 0) == 1:
                    mixer_none(l, ti)
                else:
                    mixer_c(l, ti)
            elif l % 2 == 0 and mix in ("full", "ab"):
                ab_params(l // 2)
                mixer_ab(l, ti)
            else:
                mixer_none(l, ti)
            ln1(l, ti)
            if ti == 4:
                ffn_sample_state(l)
            ffn_tile(l, ti)
        store_y(ti)
    S.finish()
    print("instructions:", S.ninst, {k: v for k, v in S.cnt.items()}, "sbuf_left", nc.sbuf_bytes_remaining)


WSHAPES = [
    ("w_in_ab", (2, 1024, 2816)), ("conv_a_w", (2, 31, 512)), ("conv_a_b", (2, 512)), ("ln_a_g", (2, 512)),
    ("ln_a_b", (2, 512)), ("mu_b", (2, 1792)), ("w0_b", (2, 512)), ("w2_b", (2, 64, 512)), ("a0_b", (2, 512)),
    ("a2_b", (2, 64, 512)), ("g2_b", (2, 128, 512)), ("k_k_b", (2, 512)), ("k_a_b", (2, 512)), ("r_k_b", (2, 8, 64)),
    ("ln_x_g", (2, 512)), ("ln_x_b", (2, 512)), ("w_out_ab", (2, 1024, 1024)), ("w_in_c", (2, 1024, 1024)),
    ("lam_re", (2, 64, 64)), ("lam_im", (2, 64, 64)), ("log_dt", (2, 64)), ("b_re", (2, 64, 64, 16)),
    ("b_im", (2, 64, 64, 16)), ("c_re", (2, 64, 16, 64)), ("c_im", (2, 64, 16, 64)), ("d_skip", (2, 1024)),
    ("w_out_c", (2, 1024, 2048)), ("w_up", (4, 1024, 5632)), ("conv_f_w", (4, 3, 2816)), ("conv_f_b", (4, 2816)),
    ("w_down", (4, 2816, 1024)), ("ln1_g", (4, 1024)), ("ln1_b", (4, 1024)), ("ln2_g", (4, 1024)), ("ln2_b", (4, 1024)),
]

OUT_NAMES = ["o_yp", "o_ys", "o_conv_p", "o_conv_s", "o_shift_p", "o_shift_s", "o_wkv_p", "o_wkv_s",
             "o_re_p", "o_re_s", "o_im_p", "o_im_s", "o_ffn_p", "o_ffn_s"]


def run(inputs, cfg=None, cores=None):
    cfg = cfg or {}
    cores = list(range(NCORES)) if cores is None else cores
    f = lambda a: np.ascontiguousarray(np.asarray(a, dtype=np.float32))
    consts = make_consts()
    wmap = {name: f(inputs[name]) for name, _ in WSHAPES}
    in_maps = []
    for c in cores:
        sl = slice(c * NSB, (c + 1) * NSB)
        m = dict(wmap)
        m["xp"] = f(inputs["x_prompt"][c])
        m["xs"] = f(inputs["x_sample"][sl]).reshape(NSB * TS, D)
        m["st_conv_a"] = f(inputs["state_conv_a"][sl])
        m["st_shift"] = f(inputs["state_shift_b"][sl])
        m["st_wkv"] = f(inputs["state_wkv_b"][sl])
        m["st_re"] = f(inputs["state_ssm_re"][sl])
        m["st_im"] = f(inputs["state_ssm_im"][sl])
        m["st_ffn"] = f(inputs["state_conv_ffn"][sl])
        m["cst"] = consts
        in_maps.append(m)
    nc = build(cfg)
    res = run_bass_kernel_spmd(nc, in_maps, core_ids=cores)
    return res.results


def kernel(**inputs):
    rs = run(inputs)
    outs = []
    for name in OUT_NAMES:
        if name.endswith("p"):
            outs.append(np.stack([np.asarray(r[name]) for r in rs], 0))
        else:
            outs.append(np.concatenate([np.asarray(r[name]) for r in rs], 0))
    outs[0] = outs[0].reshape(8, SEQ, D)
    outs[1] = outs[1].reshape(128, TS, D)
    return tuple(o.astype(np.float32) for o in outs)
```

```python
import numpy as np
from contextlib import ExitStack
import concourse.bass as bass
import concourse.mybir as mybir
from concourse.bass_utils import run_bass_kernel_spmd

F32 = mybir.dt.float32
BF16 = mybir.dt.bfloat16
AF = mybir.ActivationFunctionType
ALU = mybir.AluOpType
AX = mybir.AxisListType

D = 1024
SEQ = 2048
NSB = 16
TS = 4
NTOK = SEQ + NSB * TS
DEPTH = 4
DFF = 2816
NFF = 22
DBP = 1792
ALPHA = (2 * DEPTH) ** 0.25
LN_EPS = 1e-5
GN_EPS = 64 * 1e-5
NCORES = 8

C_ID = 0
C_MSU = 128
C_MUI = 192
C_MSL = 256
C_BONE = 320
C_BIND = 448
C_CM = 450
C_ONE = 962
C_W = 1090


def make_consts():
    c = np.zeros((128, C_W), np.float32)
    c[:, C_ID:C_ID + 128] = np.eye(128)
    i = np.arange(64)
    c[:64, C_MSU:C_MSU + 64] = (i[:, None] < i[None, :])
    c[:64, C_MUI:C_MUI + 64] = (i[:, None] <= i[None, :])
    c[:64, C_MSL:C_MSL + 64] = (i[:, None] > i[None, :])
    c[:64, C_BONE:C_BONE + 64] = 1.0
    c[64:, C_BONE + 64:C_BONE + 128] = 1.0
    c[:64, C_BIND] = 1.0
    c[64:, C_BIND + 1] = 1.0
    for q in range(4):
        m = np.zeros((128, 128), np.float32)
        m[32 * q:32 * q + 16, 0:64] = 1.0
        m[32 * q + 16:32 * q + 32, 64:128] = 1.0
        c[:, C_CM + 128 * q:C_CM + 128 * (q + 1)] = m
    c[:, C_ONE:C_ONE + 128] = 1.0
    return c


class Buf:
    __slots__ = ("name", "w", "r", "excl")

    def __init__(self, name, excl=False):
        self.name = name
        self.w = None
        self.r = {}
        self.excl = excl


class Sched:
    EPOCH = 30000

    def __init__(self, nc, es):
        self.nc = nc
        self.es = es
        self.engs = {"pe": nc.tensor, "dve": nc.vector, "act": nc.scalar, "pool": nc.gpsimd, "sp": nc.sync}
        self.cnt = {k: 0 for k in self.engs}
        self.esem = {}
        self.waited = {k: {} for k in self.engs}
        self.dq = {"sp": [es.enter_context(nc.semaphore(f"dsp{i}")) for i in range(24)],
                   "pool": [es.enter_context(nc.semaphore(f"dpl{i}")) for i in range(8)]}
        self.dcnt = {q: [0] * len(v) for q, v in self.dq.items()}
        self.dnext = {q: 0 for q in self.dq}
        self.out_tokens = []
        self.skip_same = {"pe": True, "dve": False, "act": False, "pool": False, "sp": True}
        self.ninst = 0
        self.rec = None

    def begin_rec(self):
        self.rec = []

    def end_rec(self):
        r, self.rec = self.rec, None
        return r

    @staticmethod
    def merge(a, b):
        i = j = 0
        while i < len(a) or j < len(b):
            if j >= len(b) or (i < len(a) and i * max(len(b), 1) <= j * max(len(a), 1)):
                f, args = a[i]
                i += 1
            else:
                f, args = b[j]
                j += 1
            f(*args)

    def _semh(self, key):
        if key[0] == "e":
            if key not in self.esem:
                self.esem[key] = self.es.enter_context(self.nc.semaphore(f"e_{key[1]}_{key[2]}"))
            return self.esem[key]
        return self.dq[key[1]][key[2]]

    def _deps(self, k, r, w):
        deps = {}

        def add(t):
            if t is None:
                return
            key, val = t
            if deps.get(key, 0) < val:
                deps[key] = val
        for b in r:
            add(b.w)
            if b.excl:
                for t in b.r.values():
                    add(t)
        for b in w:
            add(b.w)
            for t in b.r.values():
                add(t)
        return deps

    def _wait(self, k, deps):
        eng = self.engs[k]
        for key, val in deps.items():
            if key[0] == "e" and key[1] == k and self.skip_same[k]:
                continue
            if self.waited[k].get(key, 0) >= val:
                continue
            eng.wait_ge(self._semh(key), val)
            self.waited[k][key] = val

    def op(self, k, fn, r=(), w=(), inc=True):
        if self.rec is not None:
            self.rec.append((self._op, (k, fn, list(r), list(w), inc)))
            return None
        return self._op(k, fn, r, w, inc)

    def _op(self, k, fn, r=(), w=(), inc=True):
        self._wait(k, self._deps(k, r, w))
        ins = fn(self.engs[k])
        c = self.cnt[k] + 1
        key = ("e", k, (c - 1) // self.EPOCH)
        tok = (key, (c - 1) % self.EPOCH + 1)
        if inc:
            ins.then_inc(self._semh(key), 1)
            self.cnt[k] = c
        for b in r:
            b.r[k] = tok
        for b in w:
            b.w = tok
            b.r = {}
        self.ninst += 1
        return ins

    def fence(self, k):
        if self.rec is not None:
            self.rec.append((self._fence, (k,)))
            return
        self._fence(k)

    def _fence(self, k):
        c = self.cnt[k]
        if c == 0:
            return
        key = ("e", k, (c - 1) // self.EPOCH)
        self.engs[k].wait_ge(self._semh(key), (c - 1) % self.EPOCH + 1)

    def dve(self, fn, r=(), w=()):
        return self.op("dve", fn, r, w)

    def act(self, fn, r=(), w=()):
        return self.op("act", fn, r, w)

    def pool(self, fn, r=(), w=()):
        return self.op("pool", fn, r, w)

    def pe(self, fn, r=(), w=(), inc=True):
        return self.op("pe", fn, r, w, inc)

    def dma(self, q, out, in_, r=(), w=(), is_out=False):
        if self.rec is not None:
            self.rec.append((self._dma, (q, out, in_, list(r), list(w), is_out)))
            return
        self._dma(q, out, in_, r, w, is_out)

    def _dma(self, q, out, in_, r=(), w=(), is_out=False):
        deps = self._deps(q, r, w)
        idx = self.dnext[q]
        self.dnext[q] = (idx + 1) % len(self.dq[q])
        key = ("d", q, idx)
        if self.dcnt[q][idx] > 0:
            v = 16 * self.dcnt[q][idx]
            if deps.get(key, 0) < v:
                deps[key] = v
        self._wait(q, deps)
        self.engs[q].dma_start(out=out, in_=in_).then_inc(self.dq[q][idx], 16)
        self.dcnt[q][idx] += 1
        tok = (key, 16 * self.dcnt[q][idx])
        for b in r:
            b.r[key] = tok
        for b in w:
            b.w = tok
            b.r = {}
        if is_out:
            self.out_tokens.append(tok)
        self.ninst += 1

    def finish(self):
        deps = {}
        for key, val in self.out_tokens:
            if deps.get(key, 0) < val:
                deps[key] = val
        self._wait("sp", deps)


def build(cfg):
    depth = cfg.get("depth", DEPTH)
    nc = bass.Bass("TRN2", target_bir_lowering=False)
    es = ExitStack()
    with es:
        _build(nc, es, cfg, depth)
    return nc


def _build(nc, es, cfg, depth):
    S = Sched(nc, es)
    if cfg.get("unsafe_skip"):
        S.skip_same["dve"] = True
        S.skip_same["act"] = True

    def din(name, shape):
        return nc.dram_tensor(name, list(shape), F32, kind="ExternalInput").ap()

    def dout(name, shape):
        return nc.dram_tensor(name, list(shape), F32, kind="ExternalOutput").ap()

    def sb(name, shape, dt=F32):
        return es.enter_context(nc.sbuf_tensor(name, list(shape), dt))

    xp = din("xp", [SEQ, D])
    xs = din("xs", [NSB * TS, D])
    st_conv_a = din("st_conv_a", [NSB, 2, 30, 512])
    st_shift = din("st_shift", [NSB, 2, DBP])
    st_wkv = din("st_wkv", [NSB, 2, 8, 64, 64])
    st_re = din("st_re", [NSB, 2, 64, 64])
    st_im = din("st_im", [NSB, 2, 64, 64])
    st_ffn = din("st_ffn", [NSB, 4, 2, DFF])
    cst = din("cst", [128, C_W])
    W = {}
    for name, shape in WSHAPES:
        W[name] = din(name, shape)
    o_yp = dout("o_yp", [SEQ, D])
    o_ys = dout("o_ys", [NSB * TS, D])
    o_conv_p = dout("o_conv_p", [2, 30, 512])
    o_conv_s = dout("o_conv_s", [NSB, 2, 30, 512])
    o_shift_p = dout("o_shift_p", [2, DBP])
    o_shift_s = dout("o_shift_s", [NSB, 2, DBP])
    o_wkv_p = dout("o_wkv_p", [2, 8, 64, 64])
    o_wkv_s = dout("o_wkv_s", [NSB, 2, 8, 64, 64])
    o_re_p = dout("o_re_p", [2, 64, 64])
    o_re_s = dout("o_re_s", [NSB, 2, 64, 64])
    o_im_p = dout("o_im_p", [2, 64, 64])
    o_im_s = dout("o_im_s", [NSB, 2, 64, 64])
    o_ffn_p = dout("o_ffn_p", [4, 2, DFF])
    o_ffn_s = dout("o_ffn_s", [NSB, 4, 2, DFF])
    o_dbg = dout("o_dbg", [128, 2048]) if cfg.get("dbg") else None
    o_dbgB = dout("o_dbgB", [128, 9216]) if cfg.get("dbgr") else None
    o_dbgZ = dout("o_dbgZ", [128, 4096]) if cfg.get("dbgr") else None
    o_dbgS = dout("o_dbgS", [128, 256]) if cfg.get("dbgr") else None

    X = sb("X", [128, 8, 512])
    _bx = Buf("X")
    bX = [_bx] * 5
    XB = sb("XB", [128, 8, 512], BF16)
    bXB = Buf("XB")
    CST = sb("CST", [128, C_W])
    bCST = Buf("CST")
    IDB = None
    PAR = sb("PAR", [128, 512])
    bPAR = Buf("PAR")
    NWB = 6
    WB = [sb(f"WB{i}", [128, 11 * 128], BF16) for i in range(NWB)]
    bWB = [Buf(f"WB{i}") for i in range(NWB)]
    wb_i = [0]
    wb_cnt = [0]
    ZT = sb("ZT", [128, 8, 512])
    bZT = Buf("ZT")
    LNS = sb("LNS", [128, 4, 512])
    bLNS = [Buf(f"LNS{i}") for i in range(4)]
    BIG = sb("BIG", [128, 9, 1024])
    bBIG = [Buf(f"BIG{i}") for i in range(9)]
    QA = sb("QA", [128, 14 * 512])
    bQA = Buf("QA")
    QT = QA[:, :].rearrange("p (k n) -> p k n", k=14)
    bQT = bQA
    AT = QA[:, :].bitcast(BF16)[:, 0:NFF * 512].rearrange("p (k n) -> p k n", k=NFF)
    bAT = bQA
    ROWS = sb("ROWS", [128, 1024])
    bROWS = [Buf("ROWS0"), Buf("ROWS1")]
    ROWS2 = sb("ROWS2", [128, 1024])
    rows_t = [ROWS, ROWS2]
    rows_i = [0]
    FH = sb("FH", [128, DEPTH, NFF, 2])
    bFH = Buf("FH")
    HEXT = BIG[:, 0:2, 0:520]
    bHEXT = [bBIG[0], bBIG[1]]
    CV = BIG[:, 2:4, 0:512]
    bCV = [bBIG[2], bBIG[3]]
    SFH = BIG[:, 4, 0:NFF * 32].rearrange("p (c t) -> p c t", c=NFF)
    bSFH = bBIG[4]
    SEL = BIG[:, 5, 0:NFF * 32].rearrange("p (c t) -> p c t", c=NFF)
    bSEL = bBIG[5]

    PS = [es.enter_context(nc.psum_tensor(f"ps{i}", [128, 512], F32)) for i in range(8)]
    bPS = [Buf(f"ps{i}", excl=True) for i in range(8)]
    ps_i = [0]

    ps_pool = [None]
    ps_pi = [0, 0]

    def psum():
        if ps_pool[0] is not None:
            q = ps_pool[0]
            i = 4 * q + ps_pi[q]
            ps_pi[q] = (ps_pi[q] + 1) % 4
            return PS[i], bPS[i]
        i = ps_i[0]
        ps_i[0] = (i + 1) % 8
        return PS[i], bPS[i]

    def ident(n=128):
        return CST[0:n, C_ID:C_ID + n]

    S.dma("sp", CST[:], cst[:, :], w=[bCST])

    TILES = [(i * 512, 512, 1, 512) for i in range(4)] + [(2048, 64, NSB, TS)]

    def transpose_to(out_ps, in_ap, nrows, r, w):
        S.pe(lambda e: e.transpose(out_ps, in_ap, ident(nrows)), r=r + [bCST], w=w)

    def load_cols(rows_list, col0, dest=None, dbuf=None):
        i = rows_i[0]
        rows_i[0] ^= 1
        rt, rb = rows_t[i], bROWS[i]
        r0 = 0
        for ap in rows_list:
            n = ap.shape[0]
            S.dma("sp", rt[r0:r0 + n, 0:128], ap, w=[rb])
            r0 += n
        pt, pb = psum()
        transpose_to(pt[:, 0:r0], rt[0:r0, 0:128], r0, [rb], [pb])
        if dest is None:
            dest, dbuf = PAR, bPAR
        S.act(lambda e: e.copy(out=dest[:, col0:col0 + r0], in_=pt[:, 0:r0]), r=[pb], w=[dbuf])
        return r0

    wcache = {}

    def wslab(wd, k0, kt, f0, nf, key=None):
        i = wb_i[0]
        wb_i[0] = (i + 1) % NWB
        t, b = WB[i], bWB[i]
        v = t[:, 0:kt * nf].rearrange("p (k f) -> p k f", k=kt)
        if cfg.get("nodma") and wb_cnt[0] >= NWB:
            return v, b
        wb_cnt[0] += 1
        ck = (key, k0, kt, f0, nf)
        if key is not None and cfg.get("wcache", True) and ck in wcache:
            sc, scb = wcache[ck]
            S.dma("sp", t[:, 0:kt * nf], sc[:, :], r=[scb], w=[b])
            return v, b
        S.dma("pool", v, wd[k0:k0 + kt * 128, f0:f0 + nf].rearrange("(k p) f -> p k f", p=128), w=[b])
        if key is not None and cfg.get("wcache", True) and len(cfg.get("tiles", [0, 1, 2, 3, 4])) > 1:
            sc = nc.dram_tensor(f"wc{len(wcache)}", [128, kt * nf], BF16, kind="Internal").ap()
            scb = Buf(f"wc{len(wcache)}")
            wcache[ck] = (sc, scb)
            S.dma("sp", sc[:, :], t[:, 0:kt * nf], r=[b], w=[scb])
        return v, b

    def cast_xb(ti):
        c0, n, _, _ = TILES[ti]
        S.act(lambda e: e.copy(out=XB[:, :, 0:n], in_=X[:, :, 0:n]), r=[bX[ti]], w=[bXB])

    def layer_norm(zv, zb, nk, n, gcol, bcol, out_fn, eps=LN_EPS, ptile=None):
        inv = 1.0 / (nk * 128)
        if ptile is None:
            ptile = PAR
        zb = zb if isinstance(zb, list) else [zb]
        ones = CST[:, C_ONE:C_ONE + 128]
        pm, pmb = psum()
        for k in range(nk):
            S.pe(lambda e, k=k: e.matmul(pm[:, 0:n], ones, zv(k), start=(k == 0), stop=(k == nk - 1)),
                 r=zb + [bCST], w=[pmb], inc=(k == nk - 1))
        pq, pqb = psum()
        for k in range(nk):
            sq = LNS[:, 0, 0:n]
            S.act(lambda e, k=k: e.activation(out=sq, in_=zv(k), func=AF.Square), r=zb, w=[bLNS[0]])
            S.pe(lambda e, k=k: e.matmul(pq[:, 0:n], ones, sq, start=(k == 0), stop=(k == nk - 1)),
                 r=[bLNS[0], bCST], w=[pqb])
        mean = LNS[:, 1, 0:n]
        rstd = LNS[:, 2, 0:n]
        S.act(lambda e: e.mul(out=mean, in_=pm[:, 0:n], mul=inv), r=[pmb], w=[bLNS[1]])
        S.dve(lambda e: e.tensor_tensor(out=rstd, in0=mean, in1=mean, op=ALU.mult), r=[bLNS[1]], w=[bLNS[2]])
        S.dve(lambda e: e.scalar_tensor_tensor(out=rstd, in0=pq[:, 0:n], scalar=inv, in1=rstd,
                                               op0=ALU.mult, op1=ALU.subtract), r=[pqb, bLNS[2]], w=[bLNS[2]])
        S.dve(lambda e: e.tensor_scalar(out=rstd, in0=rstd, scalar1=eps, scalar2=None, op0=ALU.add),
              r=[bLNS[2]], w=[bLNS[2]])
        S.act(lambda e: e.activation(out=rstd, in_=rstd, func=AF.Sqrt), r=[bLNS[2]], w=[bLNS[2]])
        S.dve(lambda e: e.reciprocal(out=rstd, in_=rstd), r=[bLNS[2]], w=[bLNS[2]])
        for k in range(nk):
            tmp = LNS[:, 3, 0:n]
            S.dve(lambda e, k=k: e.tensor_tensor(out=tmp, in0=zv(k), in1=mean, op=ALU.subtract),
                  r=zb + [bLNS[1]], w=[bLNS[3]])
            S.dve(lambda e: e.tensor_tensor(out=tmp, in0=tmp, in1=rstd, op=ALU.mult),
                  r=[bLNS[3], bLNS[2]], w=[bLNS[3]])
            out_fn(k, tmp, bLNS[3], ptile[:, gcol + k:gcol + k + 1], ptile[:, bcol + k:bcol + k + 1])

    def load_x(ti):
        c0, n, _, _ = TILES[ti]
        for tb in range((n + 127) // 128):
            src = xp[c0 + tb * 128:c0 + (tb + 1) * 128, :] if ti < 4 else xs[:, :]
            nr = min(128, n)
            i = rows_i[0]
            rows_i[0] ^= 1
            rt, rb = rows_t[i], bROWS[i]
            S.dma("sp", rt[0:nr, :], src, w=[rb])
            for half in range(2):
                pt, pb = psum()
                for kk in range(4):
                    k = half * 4 + kk
                    S.pe(lambda e, k=k, kk=kk: e.transpose(pt[:, kk * 128:kk * 128 + nr],
                                                           rt[0:nr, k * 128:(k + 1) * 128], ident(nr)),
                         r=[rb, bCST], w=[pb])
                S.act(lambda e, half=half: e.copy(
                    out=X[:, half * 4:half * 4 + 4, tb * 128:tb * 128 + nr],
                    in_=pt[:, :].rearrange("p (k t) -> p k t", k=4)[:, :, 0:nr]), r=[pb], w=[bX[ti]])

    def store_y(ti):
        c0, n, _, _ = TILES[ti]
        for tb in range((n + 127) // 128):
            dst = o_yp[c0 + tb * 128:c0 + (tb + 1) * 128, :] if ti < 4 else o_ys[:, :]
            nr = min(128, n)
            i = rows_i[0]
            rows_i[0] ^= 1
            rt, rb = rows_t[i], bROWS[i]
            for half in range(2):
                pt, pb = psum()
                for kk in range(4):
                    k = half * 4 + kk
                    S.pe(lambda e, k=k, kk=kk: e.transpose(pt[0:nr, kk * 128:(kk + 1) * 128],
                                                           X[:, k, tb * 128:tb * 128 + nr], ident(128)),
                         r=[bX[ti], bCST], w=[pb])
                S.act(lambda e, half=half: e.copy(out=rt[0:nr, half * 512:(half + 1) * 512], in_=pt[0:nr, :]),
                      r=[pb], w=[rb])
            S.dma("sp", dst, rt[0:nr, :], r=[rb], is_out=True)

    def emit_rows(src_fn, nrows, nchunks, dst_fn, r):
        done = 0
        while done < nchunks:
            g = min(8, nchunks - done)
            i = rows_i[0]
            rows_i[0] ^= 1
            rt, rb = rows_t[i], bROWS[i]
            for h0 in range(0, g, 4):
                pt, pb = psum()
                hn = min(4, g - h0)
                for kk in range(hn):
                    c = done + h0 + kk
                    S.pe(lambda e, c=c, kk=kk: e.transpose(pt[0:nrows, kk * 128:(kk + 1) * 128],
                                                           src_fn(c), ident(128)), r=r + [bCST], w=[pb])
                S.act(lambda e, h0=h0, hn=hn: e.copy(out=rt[0:nrows, h0 * 128:(h0 + hn) * 128],
                                                      in_=pt[0:nrows, 0:hn * 128]), r=[pb], w=[rb])
            dst_fn(rt, rb, done, g)
            done += g

    def ffn_params(l):
        load_cols([W["conv_f_w"][l].rearrange("j (c p) -> (j c) p", p=128)], 0)
        load_cols([W["conv_f_b"][l].rearrange("(c p) -> c p", p=128),
                   W["ln2_g"][l].rearrange("(c p) -> c p", p=128),
                   W["ln2_b"][l].rearrange("(c p) -> c p", p=128),
                   W["ln1_g"][l].rearrange("(c p) -> c p", p=128),
                   W["ln1_b"][l].rearrange("(c p) -> c p", p=128)], 66)

    def ffn_sample_state(l):
        i = rows_i[0]
        for c0 in range(0, NFF, 8):
            g = min(8, NFF - c0)
            i = rows_i[0]
            rows_i[0] ^= 1
            rt, rb = rows_t[i], bROWS[i]
            for b in range(NSB):
                S.dma("sp", rt[2 * b:2 * b + 2, 0:g * 128], st_ffn[b, l, :, c0 * 128:(c0 + g) * 128], w=[rb])
            for h0 in range(0, g, 4):
                hn = min(4, g - h0)
                pt, pb = psum()
                for kk in range(hn):
                    S.pe(lambda e, kk=kk, h0=h0: e.transpose(pt[:, kk * 32:(kk + 1) * 32],
                                                             rt[0:32, (h0 + kk) * 128:(h0 + kk + 1) * 128], ident(32)),
                         r=[rb, bCST], w=[pb])
                S.act(lambda e, h0=h0, hn=hn, c0=c0: e.copy(
                    out=SFH[:, c0 + h0:c0 + h0 + hn, :],
                    in_=pt[:, 0:hn * 32].rearrange("p (k t) -> p k t", k=hn)), r=[pb], w=[bSFH])

    def ffn_tile(l, ti):
        c0, n, nseq, T = TILES[ti]
        E = T + 2
        cast_xb(ti)
        wup = W["w_up"][l]
        for g0 in range(0, NFF, 1):
            gn = 1
            wv, wvb = wslab(wup, 0, 8, g0 * 128, gn * 128, key=("up", l))
            wg, wgb = wslab(wup, 0, 8, DFF + g0 * 128, gn * 128, key=("up", l))
            for j in range(gn):
                hc = g0 + j
                ph, phb = psum()
                for k in range(8):
                    S.pe(lambda e, k=k, j=j: e.matmul(ph[:, 0:n], wv[:, k, j * 128:(j + 1) * 128], XB[:, k, 0:n],
                                                      start=(k == 0), stop=(k == 7)),
                         r=[wvb, bXB], w=[phb], inc=(k == 7))
                pg, pgb = psum()
                for k in range(8):
                    S.pe(lambda e, k=k, j=j: e.matmul(pg[:, 0:n], wg[:, k, j * 128:(j + 1) * 128], XB[:, k, 0:n],
                                                      start=(k == 0), stop=(k == 7)),
                         r=[wgb, bXB], w=[pgb], inc=(k == 7))
                hi = hc % 2
                hx = HEXT[:, hi, 0:nseq * E].rearrange("p (s e) -> p s e", s=nseq)
                hb = bHEXT[hi]
                if nseq == 1:
                    if ti == 0:
                        S.dve(lambda e: e.memset(hx[:, :, 0:2], 0.0), w=[hb])
                    else:
                        S.dve(lambda e, hc=hc: e.tensor_copy(out=hx[:, 0, 0:2], in_=FH[:, l, hc, :]), r=[bFH], w=[hb])
                else:
                    S.dve(lambda e, hc=hc: e.tensor_copy(
                        out=hx[:, :, 0:2], in_=SFH[:, hc, :].rearrange("p (b r) -> p b r", r=2)), r=[bSFH], w=[hb])
                S.act(lambda e: e.copy(out=hx[:, :, 2:E], in_=ph[:, 0:n].rearrange("p (s t) -> p s t", s=nseq)),
                      r=[phb], w=[hb])
                cv = CV[:, hi, 0:n].rearrange("p (s t) -> p s t", s=nseq)
                cb = bCV[hi]
                S.act(lambda e, hc=hc: e.activation(out=cv, in_=hx[:, :, 0:T], func=AF.Identity, scale=PAR[:, hc:hc + 1],
                                                    bias=PAR[:, 66 + hc:67 + hc]), r=[hb, bPAR], w=[cb])
                for j2 in (1, 2):
                    S.dve(lambda e, hc=hc, j2=j2: e.scalar_tensor_tensor(
                        out=cv, in0=hx[:, :, j2:j2 + T], scalar=PAR[:, 22 * j2 + hc:22 * j2 + hc + 1], in1=cv,
                        op0=ALU.mult, op1=ALU.add), r=[hb, bPAR, cb], w=[cb])
                if nseq == 1:
                    S.pool(lambda e, hc=hc: e.tensor_copy(out=FH[:, l, hc, :], in_=hx[:, 0, T:T + 2]), r=[hb], w=[bFH])
                else:
                    S.pool(lambda e, hc=hc: e.tensor_copy(
                        out=SEL[:, hc, :].rearrange("p (r b) -> p r b", r=2),
                        in_=hx[:, :, T:T + 2].rearrange("p b r -> p r b")), r=[hb], w=[bSEL])
                S.act(lambda e: e.activation(out=CV[:, hi, 0:n], in_=CV[:, hi, 0:n], func=AF.Silu), r=[cb], w=[cb])
                S.dve(lambda e, hc=hc: e.tensor_tensor(out=AT[:, hc, 0:n], in0=CV[:, hi, 0:n], in1=pg[:, 0:n],
                                                       op=ALU.mult), r=[cb, pgb], w=[bAT])
        if ti == 3:
            def dst(rt, rb, done, g):
                S.dma("sp", o_ffn_p[l, :, done * 128:(done + g) * 128], rt[0:2, 0:g * 128], r=[rb], is_out=True)
            emit_rows(lambda c: FH[:, l, c, :], 2, NFF, dst, [bFH])
        if ti == 4:
            def dst(rt, rb, done, g):
                for r_ in range(2):
                    S.dma("sp", o_ffn_s[:, l, r_, done * 128:(done + g) * 128],
                          rt[16 * r_:16 * r_ + 16, 0:g * 128], r=[rb], is_out=True)
            emit_rows(lambda c: SEL[:, c, :], 32, NFF, dst, [bSEL])
        wdn = W["w_down"][l]
        for o0 in range(0, 8):
            wd0, wdb0 = wslab(wdn, 0, 11, o0 * 128, 128, key=("dn", l))
            wd1, wdb1 = wslab(wdn, 11 * 128, 11, o0 * 128, 128, key=("dn", l))
            for j in range(1):
                oc = o0 + j
                pf, pfb = psum()
                for k in range(NFF):
                    wd_, wdb = (wd0, wdb0) if k < 11 else (wd1, wdb1)
                    S.pe(lambda e, k=k, wd_=wd_: e.matmul(pf[:, 0:n], wd_[:, k % 11, :], AT[:, k, 0:n],
                                                          start=(k == 0), stop=(k == NFF - 1)),
                         r=[wdb, bAT], w=[pfb], inc=(k == NFF - 1))
                S.dve(lambda e, oc=oc: e.scalar_tensor_tensor(out=ZT[:, oc, 0:n], in0=X[:, oc, 0:n], scalar=ALPHA,
                                                              in1=pf[:, 0:n], op0=ALU.mult, op1=ALU.add),
                      r=[bX[ti], pfb], w=[bZT])

        def outf(k, tmp, tb, g, b):
            S.act(lambda e: e.activation(out=X[:, k, 0:n], in_=tmp, func=AF.Identity, scale=g, bias=b),
                  r=[tb, bPAR], w=[bX[ti]])
        layer_norm(lambda k: ZT[:, k, 0:n], bZT, 8, n, 88, 96, outf)

    S5P = sb("S5P", [128, 24, 32])
    bS5P = Buf("S5P")
    BL = sb("BL", [128, 32, 2, 128], BF16)
    bBL = Buf("BL")
    CL = sb("CL", [128, 32, 2, 128], BF16)
    bCL = Buf("CL")
    HCS = sb("HCS", [128, 2, 2, 32])
    bHCS = Buf("HCS")
    BBX = sb("BBX", [128, 8, 128])
    bBBX = [Buf(f"BBX{i}") for i in range(8)]
    S.dve(lambda e: e.memset(HCS[:, :, :, :], 0.0), w=[bHCS])
    HCS2 = sb("HCS2", [128, 2, 2, 32])
    bHCS2 = Buf("HCS2")
    S.dve(lambda e: e.memset(BBX[:, :, :], 0.0), w=bBBX)
    P_LR, P_LI, P_DT, P_MAG, P_ANG, P_C, P_S, P_T0, P_T1, P_T2, P_FR, P_FI, P_AR0, P_AR1, P_AI, P_NAI, P_DEN = range(17)
    P_DSK = 17

    def s5_setup(j):
        alias_claim(0)
        sp = lambda i: S5P[:, i, :]
        dbg = cfg.get("s5stop", 99)
        load_cols([W["lam_re"][j].rearrange("(s g) p -> s (g p)", g=2)], 0, S5P[:, P_LR, :], bS5P)
        load_cols([W["lam_im"][j].rearrange("(s g) p -> s (g p)", g=2)], 0, S5P[:, P_LI, :], bS5P)
        load_cols([W["d_skip"][j].rearrange("(c p) -> c p", p=128)], 0, S5P[:, P_DSK, :], bS5P)
        i = rows_i[0]
        rows_i[0] ^= 1
        rt, rb = rows_t[i], bROWS[i]
        S.dma("sp", rt[0:32, 512:514], W["log_dt"][j].rearrange("(s g) -> s g", g=2), w=[rb])
        S.dve(lambda e: e.tensor_copy(out=rt[0:32, 0:128].rearrange("s (g p) -> s g p", g=2),
                                      in_=rt[0:32, 512:514].unsqueeze(2).to_broadcast([32, 2, 64])), r=[rb], w=[rb])
        pt, pb = psum()
        transpose_to(pt[:, 0:32], rt[0:32, 0:128], 32, [rb], [pb])
        S.act(lambda e: e.activation(out=sp(P_DT), in_=pt[:, 0:32], func=AF.Exp), r=[pb], w=[bS5P])
        R = [bS5P]
        tt = lambda o, a, b, op: S.dve(lambda e: e.tensor_tensor(out=sp(o), in0=sp(a), in1=sp(b), op=op), r=R, w=R)
        def ts(o, a, s1, s2, op0, op1=None):
            if s2 is None:
                S.dve(lambda e: e.tensor_scalar(out=sp(o), in0=sp(a), scalar1=s1, scalar2=None, op0=op0), r=R, w=R)
            else:
                S.dve(lambda e: e.tensor_scalar(out=sp(o), in0=sp(a), scalar1=s1, scalar2=s2, op0=op0, op1=op1), r=R, w=R)
        tt(P_MAG, P_LR, P_DT, ALU.mult)
        S.act(lambda e: e.activation(out=sp(P_MAG), in_=sp(P_MAG), func=AF.Exp), r=R, w=R)
        tt(P_ANG, P_LI, P_DT, ALU.mult)
        S.act(lambda e: e.activation(out=sp(P_T0), in_=sp(P_ANG), func=AF.Sin, scale=1.0 / 64), r=R, w=R)
        S.act(lambda e: e.activation(out=sp(P_S), in_=sp(P_ANG), func=AF.Sin, scale=1.0 / 32), r=R, w=R)
        tt(P_T0, P_T0, P_T0, ALU.mult)
        ts(P_C, P_T0, -2.0, 1.0, ALU.mult, ALU.add)
        for _ in range(5):
            tt(P_T0, P_C, P_C, ALU.mult)
            tt(P_T1, P_S, P_S, ALU.mult)
            tt(P_T2, P_C, P_S, ALU.mult)
            tt(P_C, P_T0, P_T1, ALU.subtract)
            ts(P_S, P_T2, 2.0, None, ALU.mult)
        tt(P_AR0, P_MAG, P_C, ALU.mult)
        S.dve(lambda e: e.tensor_copy(out=sp(P_AR1), in_=sp(P_AR0)), r=R, w=R)
        tt(P_AI, P_MAG, P_S, ALU.mult)
        ts(P_NAI, P_AI, -1.0, None, ALU.mult)
        tt(P_T0, P_LR, P_LR, ALU.mult)
        tt(P_T1, P_LI, P_LI, ALU.mult)
        tt(P_DEN, P_T0, P_T1, ALU.add)
        S.dve(lambda e: e.reciprocal(out=sp(P_DEN), in_=sp(P_DEN)), r=R, w=R)
        ts(P_T2, P_AR0, -1.0, None, ALU.add)
        tt(P_T0, P_T2, P_LR, ALU.mult)
        tt(P_T1, P_AI, P_LI, ALU.mult)
        tt(P_T0, P_T0, P_T1, ALU.add)
        tt(P_FR, P_T0, P_DEN, ALU.mult)
        tt(P_T0, P_AI, P_LR, ALU.mult)
        tt(P_T1, P_T2, P_LI, ALU.mult)
        tt(P_T0, P_T0, P_T1, ALU.subtract)
        tt(P_FI, P_T0, P_DEN, ALU.mult)
        if o_dbg is not None:
            S.dma("sp", o_dbg[:, 0:768], S5P[:, :, :].rearrange("p a b -> p (a b)"), r=[bS5P], is_out=True)
        BR = BIG[:, 0, 0:512].rearrange("p (s c) -> p s c", c=16)
        BI = BIG[:, 1, 0:512].rearrange("p (s c) -> p s c", c=16)
        BBR = BIG[:, 2, 0:512].rearrange("p (s c) -> p s c", c=16)
        BBI = BIG[:, 3, 0:512].rearrange("p (s c) -> p s c", c=16)
        TMP = BIG[:, 4, 0:512].rearrange("p (s c) -> p s c", c=16)
        for g2 in range(2):
            S.dma("sp", BR[64 * g2:64 * g2 + 64, :, :],
                  W["b_re"][j].rearrange("(s g) p c -> g p s c", g=2)[g2], w=[bBIG[0]])
            S.dma("sp", BI[64 * g2:64 * g2 + 64, :, :],
                  W["b_im"][j].rearrange("(s g) p c -> g p s c", g=2)[g2], w=[bBIG[1]])
        fr = S5P[:, P_FR, :].unsqueeze(2).to_broadcast([128, 32, 16])
        fi = S5P[:, P_FI, :].unsqueeze(2).to_broadcast([128, 32, 16])
        S.dve(lambda e: e.tensor_tensor(out=BBR, in0=BR, in1=fr, op=ALU.mult), r=[bBIG[0], bS5P], w=[bBIG[2]])
        S.dve(lambda e: e.tensor_tensor(out=TMP, in0=BI, in1=fi, op=ALU.mult), r=[bBIG[1], bS5P], w=[bBIG[4]])
        S.dve(lambda e: e.tensor_tensor(out=BBR, in0=BBR, in1=TMP, op=ALU.subtract), r=[bBIG[2], bBIG[4]], w=[bBIG[2]])
        S.dve(lambda e: e.tensor_tensor(out=BBI, in0=BI, in1=fr, op=ALU.mult), r=[bBIG[1], bS5P], w=[bBIG[3]])
        S.dve(lambda e: e.tensor_tensor(out=TMP, in0=BR, in1=fi, op=ALU.mult), r=[bBIG[0], bS5P], w=[bBIG[4]])
        S.dve(lambda e: e.tensor_tensor(out=BBI, in0=BBI, in1=TMP, op=ALU.add), r=[bBIG[3], bBIG[4]], w=[bBIG[3]])
        for s_ in range(32):
            q = s_ % 4
            for ri, src, sbuf_ in ((0, BBR, bBIG[2]), (1, BBI, bBIG[3])):
                bx = BBX[:, ri * 4 + q, :]
                bb = bBBX[ri * 4 + q]
                for g2 in range(2):
                    S.dve(lambda e, g2=g2, src=src, bx=bx: e.tensor_copy(
                        out=bx[64 * g2:64 * g2 + 64, 32 * q + 16 * g2:32 * q + 16 * g2 + 16],
                        in_=src[64 * g2:64 * g2 + 64, s_, :]), r=[sbuf_], w=[bb])
                pt, pb = psum()
                transpose_to(pt[:, 0:128], bx, 128, [bb], [pb])
                S.act(lambda e, ri=ri: e.copy(out=BL[:, s_, ri, :], in_=pt[:, 0:128]), r=[pb], w=[bBL])
        CR = BIG[:, 5, 0:512].rearrange("p (t q) -> p t q", q=64)
        CI = BIG[:, 6, 0:512].rearrange("p (t q) -> p t q", q=64)
        S.dma("sp", CR, W["c_re"][j].rearrange("(t g) c p -> (g c) t p", g=8), w=[bBIG[5]])
        S.dma("sp", CI, W["c_im"][j].rearrange("(t g) c p -> (g c) t p", g=8), w=[bBIG[6]])
        CX = BIG[:, 7, 0:256].rearrange("p (r x) -> p r x", r=2)
        for s_ in range(32):
            q, ct = s_ % 4, s_ // 4
            mk = CST[:, C_CM + 128 * q:C_CM + 128 * (q + 1)]
            for ri, src, sbuf_ in ((0, CR, bBIG[5]), (1, CI, bBIG[6])):
                cx = CX[:, ri, :]
                S.dve(lambda e, src=src, cx=cx: e.tensor_tensor(
                    out=cx.rearrange("p (g q) -> p g q", g=2),
                    in0=src[:, ct, :].unsqueeze(1).to_broadcast([128, 2, 64]),
                    in1=mk.rearrange("p (g q) -> p g q", g=2), op=ALU.mult), r=[sbuf_, bCST], w=[bBIG[7]])
                pt, pb = psum()
                transpose_to(pt[:, 0:128], cx, 128, [bBIG[7]], [pb])
                S.act(lambda e, ri=ri: e.mul(out=CL[:, s_, ri, :], in_=pt[:, 0:128], mul=(1.0 if ri == 0 else -1.0)),
                      r=[pb], w=[bCL])

    s5c = {}

    def s5_setup_cached(j, first):
        if not cfg.get("s5cache", True):
            return s5_setup(j)
        if first:
            s5_setup(j)
            sc = [nc.dram_tensor(f"s5c{j}_{k}", shp, dt_, kind="Internal").ap()
                  for k, (shp, dt_) in enumerate((([128, 8192], BF16), ([128, 8192], BF16), ([128, 768], F32)))]
            bsc = Buf(f"s5c{j}")
            s5c[j] = (sc, bsc)
            S.dma("sp", sc[0][:, :], BL[:, :, :, :].rearrange("p a b c -> p (a b c)"), r=[bBL], w=[bsc])
            S.dma("sp", sc[1][:, :], CL[:, :, :, :].rearrange("p a b c -> p (a b c)"), r=[bCL], w=[bsc])
            S.dma("sp", sc[2][:, :], S5P[:, :, :].rearrange("p a b -> p (a b)"), r=[bS5P], w=[bsc])
        else:
            sc, bsc = s5c[j]
            alias_claim(0)
            S.dma("sp", BL[:, :, :, :].rearrange("p a b c -> p (a b c)"), sc[0][:, :], r=[bsc], w=[bBL])
            S.dma("sp", CL[:, :, :, :].rearrange("p a b c -> p (a b c)"), sc[1][:, :], r=[bsc], w=[bCL])
            S.dma("sp", S5P[:, :, :].rearrange("p a b -> p (a b)"), sc[2][:, :], r=[bsc], w=[bS5P])


    def s5_sample_state(j):
        S0 = BIG[:, 8, :].rearrange("p (r s b) -> p r s b", r=2, s=32)
        for ri, st in ((0, st_re), (1, st_im)):
            for q in range(4):
                i = rows_i[0]
                rows_i[0] ^= 1
                rt, rb = rows_t[i], bROWS[i]
                for b4 in range(4):
                    S.dma("sp", rt[32 * b4:32 * b4 + 32, 0:128],
                          st[4 * q + b4, j].rearrange("(s g) p -> s (g p)", g=2), w=[rb])
                pt, pb = psum()
                transpose_to(pt[:, 0:128], rt[:, 0:128], 128, [rb], [pb])
                S.act(lambda e, ri=ri, q=q: e.copy(
                    out=S0[:, ri, :, 4 * q:4 * q + 4].rearrange("p s b -> p b s"),
                    in_=pt[:, 0:128].rearrange("p (b s) -> p b s", b=4)), r=[pb], w=[bBIG[8]])

    tabD = [nc.dram_tensor(f"s5tab{j_}", [32, 128, 1024], F32, kind="Internal").ap() for j_ in range(2)]
    btabD = [Buf("tab0"), Buf("tab1")]
    bQR = [Buf(f"QR{i}") for i in range(14)]
    QR = lambda i: QA[:, i * 512:(i + 1) * 512]

    def qa_claim(take):
        if take:
            S.dve(lambda e: e.memset(QA[:, 0:2], 0.0), r=[bQA], w=bQR)
        else:
            S.dve(lambda e: e.memset(QA[:, 0:2], 0.0), r=bQR, w=[bQA])

    def s5_tables(j):
        TC = QA[:, 0:2048].rearrange("p (s n) -> p s n", s=4)
        TS = QA[:, 2048:4096].rearrange("p (s n) -> p s n", s=4)
        TA = QA[:, 4096:5120].rearrange("p (s n) -> p s n", s=4)
        TB_ = QA[:, 5120:6144].rearrange("p (s n) -> p s n", s=4)
        bb = bQR[0:12]
        for g in range(8):
            S.dve(lambda e, g=g: e.tensor_copy(out=TC[:, :, 0:1], in_=S5P[:, P_C, 4 * g:4 * g + 4].unsqueeze(2)), r=[bS5P], w=bb)
            S.dve(lambda e, g=g: e.tensor_copy(out=TS[:, :, 0:1], in_=S5P[:, P_S, 4 * g:4 * g + 4].unsqueeze(2)), r=[bS5P], w=bb)
            n_ = 1
            while n_ < 512:
                cn = TC[:, :, n_ - 1:n_].to_broadcast([128, 4, n_])
                sn = TS[:, :, n_ - 1:n_].to_broadcast([128, 4, n_])
                TT_ = lambda o, a_, b_, op: S.dve(lambda e: e.tensor_tensor(out=o, in0=a_, in1=b_, op=op), r=bb, w=bb)
                TT_(TA[:, :, 0:n_], TS[:, :, 0:n_], sn, ALU.mult)
                TT_(TB_[:, :, 0:n_], TC[:, :, 0:n_], sn, ALU.mult)
                TT_(TC[:, :, n_:2 * n_], TC[:, :, 0:n_], cn, ALU.mult)
                TT_(TC[:, :, n_:2 * n_], TC[:, :, n_:2 * n_], TA[:, :, 0:n_], ALU.subtract)
                TT_(TS[:, :, n_:2 * n_], TS[:, :, 0:n_], cn, ALU.mult)
                TT_(TS[:, :, n_:2 * n_], TS[:, :, n_:2 * n_], TB_[:, :, 0:n_], ALU.add)
                n_ *= 2
            for q in range(4):
                S.dma("sp", tabD[j][4 * g + q, :, 0:512], TC[:, q, :], r=bb, w=[btabD[j]])
                S.dma("sp", tabD[j][4 * g + q, :, 512:1024], TS[:, q, :], r=bb, w=[btabD[j]])

    def s5_scan_tile(j, ti, U, bU, UB, bUB, GY, bGY):
        qa_claim(True)
        if ti == 0:
            s5_tables(j)
        HBq = [QA[:, 6144 + 512 * i:6144 + 512 * (i + 1)].bitcast(BF16) for i in range(2)]
        HBv = lambda i: HBq[i // 2][:, (i % 2) * 512:(i % 2) * 512 + 512]
        bHB = [bQR[12], bQR[13]]
        py = pyb = None
        HIN, bHIN = (HCS, bHCS) if ti % 2 == 0 else (HCS2, bHCS2)
        HOUT, bHOUT = (HCS2, bHCS2) if ti % 2 == 0 else (HCS, bHCS)

        def bu_mm(sx):
            pr_, prb_ = psum()
            S.pe(lambda e: e.matmul(pr_[:, 0:512], BL[:, sx, 0, :], UB(sx // 4)[:, 0:512], start=True, stop=True),
                 r=[bBL] + bUB, w=[prb_])
            pi_, pib_ = psum()
            S.pe(lambda e: e.matmul(pi_[:, 0:512], BL[:, sx, 1, :], UB(sx // 4)[:, 0:512], start=True, stop=True),
                 r=[bBL] + bUB, w=[pib_])
            return pr_, prb_, pi_, pib_
        bu_q = [bu_mm(0), bu_mm(1)]
        pyst = [None, None]

        def stage_a(s_):
                ct, q = s_ // 4, s_ % 4
                i = s_ % 2
                tcs, tsn = QR(2 * i), QR(2 * i + 1)
                btab = [bQR[2 * i], bQR[2 * i + 1]]
                S.dma("sp", tcs, tabD[j][s_, :, 0:512], r=[btabD[j]], w=[btab[0]])
                S.dma("sp", tsn, tabD[j][s_, :, 512:1024], r=[btabD[j]], w=[btab[1]])
                pr, prb, pi, pib = bu_q.pop(0)
                T1, T2, T4 = QR(4 + 3 * i), QR(5 + 3 * i), QR(6 + 3 * i)
                bT1, bT2, bT4 = bQR[4 + 3 * i], bQR[5 + 3 * i], bQR[6 + 3 * i]
                U1, U2, U3, U4 = (LNS[:, k, :] for k in range(4))
                bU1, bU2, bU3, bU4 = bLNS
                mag = S5P[:, P_MAG, s_:s_ + 1].to_broadcast([128, 512])
                prv, piv = pr[:, 0:512], pi[:, 0:512]
                S.dve(lambda e: e.tensor_tensor(out=T1, in0=prv, in1=tcs, op=ALU.mult), r=[prb, btab[0]], w=[bT1])
                S.dve(lambda e: e.tensor_tensor(out=T2, in0=piv, in1=tsn, op=ALU.mult), r=[pib, btab[1]], w=[bT2])
                S.dve(lambda e: e.tensor_tensor(out=T1, in0=T1, in1=T2, op=ALU.add), r=[bT1, bT2], w=[bT1])
                S.dve(lambda e: e.tensor_tensor(out=T2, in0=piv, in1=tcs, op=ALU.mult), r=[pib, btab[0], bT1], w=[bT2])
                S.dve(lambda e: e.tensor_tensor(out=T4, in0=prv, in1=tsn, op=ALU.mult), r=[prb, btab[1]], w=[bT4])
                S.dve(lambda e: e.tensor_tensor(out=T2, in0=T2, in1=T4, op=ALU.subtract), r=[bT2, bT4], w=[bT2])
                S.dve(lambda e: e.tensor_tensor_scan(out=T1, data0=mag, data1=T1, initial=HIN[:, j, 0, s_:s_ + 1],
                                                     op0=ALU.mult, op1=ALU.add), r=[bT1, bS5P, bHIN], w=[bT1])
                S.dve(lambda e: e.tensor_tensor_scan(out=T2, data0=mag, data1=T2, initial=HIN[:, j, 1, s_:s_ + 1],
                                                     op0=ALU.mult, op1=ALU.add), r=[bT2, bS5P, bHIN], w=[bT2])
                return (tcs, tsn, btab, T1, T2, bT1, bT2, i, ct, q)

        def stage_b(s_, ctx):
                tcs, tsn, btab, T1, T2, bT1, bT2, i, ct, q = ctx
                U1, U2, U3, U4 = (LNS[:, k, :] for k in range(4))
                bU1, bU2, bU3, bU4 = bLNS
                py, pyb = pyst
                S.pool(lambda e: e.tensor_tensor(out=U2, in0=T2, in1=tsn, op=ALU.mult), r=[bT2, btab[1]], w=[bU2])
                S.pool(lambda e: e.tensor_tensor(out=U3, in0=T1, in1=tsn, op=ALU.mult), r=[bT1, btab[1]], w=[bU3])
                S.pool(lambda e: e.tensor_tensor(out=U4, in0=T2, in1=tcs, op=ALU.mult), r=[bT2, btab[0]], w=[bU4])
                S.dve(lambda e: e.tensor_tensor(out=U1, in0=T1, in1=tcs, op=ALU.mult), r=[bT1, btab[0]], w=[bU1])
                S.pool(lambda e: e.tensor_tensor(out=U3, in0=U3, in1=U4, op=ALU.add), r=[bU3, bU4], w=[bU3])
                S.dve(lambda e: e.tensor_tensor(out=U1, in0=U1, in1=U2, op=ALU.subtract), r=[bU1, bU2], w=[bU1])
                S.dve(lambda e: e.tensor_copy(out=HOUT[:, j, 0, s_:s_ + 1], in_=U1[:, 511:512]), r=[bU1], w=[bHOUT])
                S.dve(lambda e: e.tensor_copy(out=HOUT[:, j, 1, s_:s_ + 1], in_=U3[:, 511:512]), r=[bU3], w=[bHOUT])
                hbr, hbi = HBv(2 * i), HBv(2 * i + 1)
                S.act(lambda e, hbr=hbr: e.copy(out=hbr, in_=U1), r=[bU1], w=[bHB[i]])
                S.act(lambda e, hbi=hbi: e.copy(out=hbi, in_=U3), r=[bU3], w=[bHB[i]])
                if q == 0:
                    py, pyb = psum()
                pyst[0], pyst[1] = py, pyb
                S.pe(lambda e, hbr=hbr, py=py, q=q: e.matmul(py[:, 0:512], CL[:, s_, 0, :], hbr, start=(q == 0), stop=False),
                     r=[bCL, bHB[i]], w=[pyb], inc=False)
                S.pe(lambda e, hbi=hbi, py=py, q=q: e.matmul(py[:, 0:512], CL[:, s_, 1, :], hbi, start=False, stop=(q == 3)),
                     r=[bCL, bHB[i]], w=[pyb], inc=True)
                if q == 3:
                    Y, Y2 = QR(10), QR(11)
                    bY, bY2 = bQR[10], bQR[11]
                    S.dve(lambda e, ct=ct, py=py: e.scalar_tensor_tensor(out=Y, in0=U[:, ct, 0:512], scalar=S5P[:, P_DSK, ct:ct + 1],
                                                                         in1=py[:, 0:512], op0=ALU.mult, op1=ALU.add),
                          r=bU + [bS5P, pyb], w=[bY])
                    S.pool(lambda e: e.tensor_tensor(out=Y2, in0=Y, in1=Y, op=ALU.mult), r=[bY], w=[bY2])
                    S.pool(lambda e: e.tensor_scalar(out=Y2, in0=Y2, scalar1=0.044715, scalar2=1.0, op0=ALU.mult, op1=ALU.add),
                           r=[bY2], w=[bY2])
                    S.pool(lambda e: e.tensor_tensor(out=Y2, in0=Y2, in1=Y, op=ALU.mult), r=[bY2, bY], w=[bY2])
                    S.act(lambda e: e.activation(out=Y2, in_=Y2, func=AF.Sigmoid, scale=1.5957691216), r=[bY2], w=[bY2])
                    S.dve(lambda e, ct=ct: e.tensor_tensor(out=GY(ct)[:, 0:512], in0=Y, in1=Y2, op=ALU.mult),
                          r=[bY, bY2], w=bGY)

        ctx_next = stage_a(0)
        for s_ in range(32):
            ctx_cur = ctx_next
            if s_ + 2 < 32:
                bu_q.append(bu_mm(s_ + 2))
            if s_ + 1 < 32:
                ctx_next = stage_a(s_ + 1)
            stage_b(s_, ctx_cur)
        qa_claim(False)


    def mixer_c(l, ti):
        j = l // 2
        c0, n, nseq, T = TILES[ti]
        U = BIG[:, 0:4, :].rearrange("p a (b n) -> p (a b) n", b=2)
        bU = bBIG[0:4]
        UBf = [BIG[:, 4 + i, :].bitcast(BF16) for i in range(2)]
        GYf = [BIG[:, 6 + i, :].bitcast(BF16) for i in range(2)]
        UB = lambda k: UBf[k // 4][:, (k % 4) * 512:(k % 4) * 512 + 512]
        GY = lambda k: GYf[k // 4][:, (k % 4) * 512:(k % 4) * 512 + 512]
        bUB = bBIG[4:6]
        bGY = bBIG[6:8]
        S0 = BIG[:, 8, :].rearrange("p (r s b) -> p r s b", r=2, s=32)
        cast_xb(ti)
        for oc in range(8):
            wv, wvb = wslab(W["w_in_c"][j], 0, 8, oc * 128, 128, key=("inc", j))
            pu, pub = psum()
            for k in range(8):
                S.pe(lambda e, k=k: e.matmul(pu[:, 0:n], wv[:, k, :], XB[:, k, 0:n], start=(k == 0), stop=(k == 7)),
                     r=[wvb, bXB], w=[pub], inc=(k == 7))
            S.act(lambda e, oc=oc: e.copy(out=U[:, oc, 0:n], in_=pu[:, 0:n]), r=[pub], w=bU)
            S.dve(lambda e, oc=oc: e.tensor_copy(out=UB(oc)[:, 0:n], in_=pu[:, 0:n]), r=[pub], w=bUB)
        cstop = cfg.get("cstop", 99)
        if cstop <= 1:
            return mixer_none(l, ti)
        if nseq == 1 and cfg.get("s5scan", True):
            s5_scan_tile(j, ti, U, bU, UB, bUB, GY, bGY)
            nsub_skip = True
        else:
            nsub_skip = False
        qv = QA[:, :]
        BU = qv[:, 0:2048].rearrange("p (r s t) -> p r s t", r=2, s=32)
        HALL = qv[:, 2048:4096].rearrange("p (r s t) -> p r s t", r=2, s=32)
        HB = qv[:, 4096:5120].bitcast(BF16).rearrange("p (r s t) -> p r s t", r=2, s=32)
        T1f = qv[:, 5120:6144]
        T2f = qv[:, 6144:7168]
        bQ = [bQA]
        nsub = 0 if nsub_skip else n // 32
        for st_i in range(nsub):
            cs = st_i * 32
            for ri in range(2):
                for h in range(2):
                    pt, pb = psum()
                    for s16 in range(16):
                        s_ = h * 16 + s16
                        S.pe(lambda e, s_=s_, s16=s16, ri=ri: e.matmul(
                            pt[:, s16 * 32:(s16 + 1) * 32], BL[:, s_, ri, :], UB(s_ // 4)[:, cs:cs + 32],
                            start=True, stop=True), r=[bBL] + bUB, w=[pb], inc=(s16 == 15))
                    S.act(lambda e, ri=ri, h=h: e.copy(
                        out=BU[:, ri, 16 * h:16 * h + 16, :], in_=pt[:, :].rearrange("p (s t) -> p s t", s=16)),
                        r=[pb], w=bQ)
            if cstop <= 2:
                continue
            if nseq == 1:
                nb, nt = 1, 32
                shp = [128, 2, 32]
                bu_t = lambda t: BU[:, :, :, t]
                h_t = lambda t: HALL[:, :, :, t]
                prev0 = HCS[:, j, :, :]
                prev0b = [bHCS]
                T1 = T1f[:, 0:64].rearrange("p (r s) -> p r s", r=2)
                T2 = T2f[:, 0:64].rearrange("p (r s) -> p r s", r=2)
                arar = S5P[:, P_AR0:P_AR0 + 2, :]
                ai = S5P[:, P_AI, :]
                nai = S5P[:, P_NAI, :]
            else:
                nb, nt = 8, 4
                BU5 = BU.rearrange("p r s (b t) -> p r s b t", b=8)
                H5 = HALL.rearrange("p r s (b t) -> p r s b t", b=8)
                bu_t = lambda t: BU5[:, :, :, :, t]
                h_t = lambda t: H5[:, :, :, :, t]
                prev0 = S0[:, :, :, 8 * st_i:8 * st_i + 8]
                prev0b = [bBIG[8]]
                T1 = T1f[:, 0:512].rearrange("p (r s b) -> p r s b", r=2, s=32)
                T2 = T2f[:, 0:512].rearrange("p (r s b) -> p r s b", r=2, s=32)
                arar = S5P[:, P_AR0:P_AR0 + 2, :].unsqueeze(3).to_broadcast([128, 2, 32, 8])
                ai = S5P[:, P_AI, :].unsqueeze(2).to_broadcast([128, 32, 8])
                nai = S5P[:, P_NAI, :].unsqueeze(2).to_broadcast([128, 32, 8])
            for t in range(nt):
                prev = prev0 if t == 0 else h_t(t - 1)
                pbf = prev0b if t == 0 else bQ
                S.dve(lambda e, prev=prev: e.tensor_tensor(out=T1, in0=prev, in1=arar, op=ALU.mult),
                      r=pbf + [bS5P], w=bQ)
                S.dve(lambda e, prev=prev: e.tensor_tensor(out=T2[:, 0], in0=prev[:, 1], in1=nai, op=ALU.mult),
                      r=pbf + [bS5P], w=bQ)
                S.dve(lambda e, prev=prev: e.tensor_tensor(out=T2[:, 1], in0=prev[:, 0], in1=ai, op=ALU.mult),
                      r=pbf + [bS5P], w=bQ)
                S.dve(lambda e: e.tensor_tensor(out=T1, in0=T1, in1=T2, op=ALU.add), r=bQ, w=bQ)
                S.dve(lambda e, t=t: e.tensor_tensor(out=h_t(t), in0=T1, in1=bu_t(t), op=ALU.add), r=bQ, w=bQ)
            if nseq == 1:
                S.dve(lambda e: e.tensor_copy(out=HCS[:, j, :, :], in_=HALL[:, :, :, 31]), r=bQ, w=[bHCS])
            else:
                S.dve(lambda e: e.tensor_copy(out=S0[:, :, :, 8 * st_i:8 * st_i + 8], in_=h_t(3)), r=bQ, w=[bBIG[8]])
            if cstop <= 3:
                continue
            S.act(lambda e: e.copy(out=HB, in_=HALL), r=bQ, w=bQ)
            py, pyb = psum()
            for ct in range(8):
                i_ = 0
                for s4 in range(4):
                    for ri in range(2):
                        s_ = ct * 4 + s4
                        S.pe(lambda e, s_=s_, ri=ri, ct=ct, i_=i_: e.matmul(
                            py[:, ct * 32:(ct + 1) * 32], CL[:, s_, ri, :], HB[:, ri, s_, :],
                            start=(i_ == 0), stop=(i_ == 7)), r=[bCL] + bQ, w=[pyb], inc=(i_ == 7 and ct == 7))
                        i_ += 1
            Y = T1f[:, 0:256].rearrange("p (c t) -> p c t", c=8)
            Y2 = T2f[:, 0:256].rearrange("p (c t) -> p c t", c=8)
            dsk = S5P[:, P_DSK, 0:8].unsqueeze(2).to_broadcast([128, 8, 32])
            S.dve(lambda e: e.tensor_tensor(out=Y, in0=U[:, :, cs:cs + 32], in1=dsk, op=ALU.mult), r=bU + [bS5P], w=bQ)
            S.dve(lambda e: e.tensor_tensor(out=Y, in0=Y, in1=py[:, 0:256].rearrange("p (c t) -> p c t", c=8),
                                            op=ALU.add), r=bQ + [pyb], w=bQ)
            S.dve(lambda e: e.tensor_tensor(out=Y2, in0=Y, in1=Y, op=ALU.mult), r=bQ, w=bQ)
            S.dve(lambda e: e.tensor_scalar(out=Y2, in0=Y2, scalar1=0.044715, scalar2=1.0, op0=ALU.mult, op1=ALU.add),
                  r=bQ, w=bQ)
            S.dve(lambda e: e.tensor_tensor(out=Y2, in0=Y2, in1=Y, op=ALU.mult), r=bQ, w=bQ)
            S.act(lambda e: e.activation(out=Y2, in_=Y2, func=AF.Sigmoid, scale=1.5957691216), r=bQ, w=bQ)
            for k in range(8):
                S.dve(lambda e, k=k: e.tensor_tensor(out=GY(k)[:, cs:cs + 32], in0=Y[:, k, :], in1=Y2[:, k, :],
                                                     op=ALU.mult), r=bQ, w=bGY)
        if cstop <= 4:
            return mixer_none(l, ti)
        if ti == 3:
            for ri, od in ((0, o_re_p), (1, o_im_p)):
                i = rows_i[0]
                rows_i[0] ^= 1
                rt, rb = rows_t[i], bROWS[i]
                pt, pb = psum()
                S.pe(lambda e, ri=ri: e.transpose(pt[0:32, 0:128], HCS[:, j, ri, :], ident(128)), r=[bHCS, bCST], w=[pb])
                S.act(lambda e: e.copy(out=rt[0:32, 0:128], in_=pt[0:32, 0:128]), r=[pb], w=[rb])
                S.dma("sp", od[j].rearrange("(s g) p -> s (g p)", g=2), rt[0:32, 0:128], r=[rb], is_out=True)
        if ti == 4:
            STG = T1f[:, 0:128]
            for ri, od in ((0, o_re_s), (1, o_im_s)):
                for q in range(4):
                    i = rows_i[0]
                    rows_i[0] ^= 1
                    rt, rb = rows_t[i], bROWS[i]
                    S.dve(lambda e, ri=ri, q=q: e.tensor_copy(
                        out=STG.rearrange("p (b s) -> p b s", b=4),
                        in_=S0[:, ri, :, 4 * q:4 * q + 4].rearrange("p s b -> p b s")), r=[bBIG[8]], w=bQ)
                    pt, pb = psum()
                    S.pe(lambda e: e.transpose(pt[:, 0:128], STG, ident(128)), r=bQ + [bCST], w=[pb])
                    S.act(lambda e: e.copy(out=rt[:, 0:128], in_=pt[:, 0:128]), r=[pb], w=[rb])
                    for b4 in range(4):
                        S.dma("sp", od[4 * q + b4, j].rearrange("(s g) p -> s (g p)", g=2),
                              rt[32 * b4:32 * b4 + 32, 0:128], r=[rb], is_out=True)
        for oc in range(8):
            w1, w1b = wslab(W["w_out_c"][j], 0, 8, oc * 128, 128, key=("outc", j))
            w2, w2b = wslab(W["w_out_c"][j], 0, 8, 1024 + oc * 128, 128, key=("outc", j))
            p1, p1b = psum()
            for k in range(8):
                S.pe(lambda e, k=k: e.matmul(p1[:, 0:n], w1[:, k, :], GY(k)[:, 0:n], start=(k == 0), stop=(k == 7)),
                     r=[w1b] + bGY, w=[p1b], inc=(k == 7))
            p2, p2b = psum()
            for k in range(8):
                S.pe(lambda e, k=k: e.matmul(p2[:, 0:n], w2[:, k, :], GY(k)[:, 0:n], start=(k == 0), stop=(k == 7)),
                     r=[w2b] + bGY, w=[p2b], inc=(k == 7))
            sg = LNS[:, 0, 0:n]
            S.act(lambda e: e.activation(out=sg, in_=p2[:, 0:n], func=AF.Sigmoid), r=[p2b], w=[bLNS[0]])
            S.dve(lambda e: e.tensor_tensor(out=sg, in0=sg, in1=p1[:, 0:n], op=ALU.mult), r=[bLNS[0], p1b], w=[bLNS[0]])
            S.dve(lambda e, oc=oc: e.scalar_tensor_tensor(out=ZT[:, oc, 0:n], in0=X[:, oc, 0:n], scalar=ALPHA, in1=sg,
                                                          op0=ALU.mult, op1=ALU.add), r=[bX[ti], bLNS[0]], w=[bZT])


    YAB = sb("YAB", [128, 8, 512], BF16)
    bYAB = Buf("YAB")
    PA = sb("PA", [128, 192])
    bPA = Buf("PA")
    LOR = sb("LOR", [128, 3, 512])
    bLOR = Buf("LOR")
    LNX = sb("LNX", [64, 2, 512])
    bLNX = Buf("LNX")
    CAH = sb("CAH", [128, 2, 4, 30])
    bCAH = Buf("CAH")
    CSH = sb("CSH", [128, 2, 14])
    bCSH = Buf("CSH")
    SHS = sb("SHS", [128, 14, 16])
    bSHS = Buf("SHS")
    STS = sb("STS", [128, 2, 4, 64])
    bSTS = Buf("STS")
    S.dve(lambda e: e.memset(CAH[:, :, :, :], 0.0), w=[bCAH])
    S.dve(lambda e: e.memset(CSH[:, :, :], 0.0), w=[bCSH])
    S.dve(lambda e: e.memset(STS[:, :, :, :], 0.0), w=[bSTS])
    A_MU, A_W0, A_A0, A_KK, A_KA, A_RK, A_CB, A_LG, A_LB, A_CW = 0, 14, 18, 22, 26, 30, 34, 38, 42, 64

    def ab_params(j):
        v4 = lambda name: W[name][j].rearrange("(c p) -> c p", p=128)
        load_cols([W["mu_b"][j].rearrange("(c p) -> c p", p=128), v4("w0_b"), v4("a0_b"), v4("k_k_b"), v4("k_a_b"),
                   W["r_k_b"][j].rearrange("(c h) k -> c (h k)", h=2), v4("conv_a_b"), v4("ln_a_g"), v4("ln_a_b")],
                  0, PA, bPA)
        load_cols([W["conv_a_w"][j].rearrange("t (c p) -> (t c) p", p=128)], A_CW, PA, bPA)
        S.dma("sp", LOR[0:64, 0, :], W["w2_b"][j], w=[bLOR])
        S.dma("sp", LOR[64:128, 1, :], W["a2_b"][j], w=[bLOR])
        S.dma("sp", LOR[:, 2, :], W["g2_b"][j], w=[bLOR])
        S.dma("sp", LNX[:, 0, :], W["ln_x_g"][j].partition_broadcast(64), w=[bLNX])
        S.dma("sp", LNX[:, 1, :], W["ln_x_b"][j].partition_broadcast(64), w=[bLNX])

    def wkv_store(j, dst):
        i = rows_i[0]
        rows_i[0] ^= 1
        rt, rb = rows_t[i], bROWS[i]
        pt, pb = psum()
        for hp in range(4):
            S.pe(lambda e, hp=hp: e.transpose(pt[0:64, hp * 128:(hp + 1) * 128], STS[:, j, hp, :], ident(128)),
                 r=[bSTS, bCST], w=[pb])
        S.act(lambda e: e.copy(out=rt[0:64, 0:512], in_=pt[0:64, 0:512]), r=[pb], w=[rb])
        S.dma("sp", dst.rearrange("(hp h2) v k -> v hp h2 k", h2=2),
              rt[0:64, 0:512].rearrange("v (hp h2 k) -> v hp h2 k", hp=4, h2=2), r=[rb], is_out=True)

    def wkv_load(j, src):
        i = rows_i[0]
        rows_i[0] ^= 1
        rt, rb = rows_t[i], bROWS[i]
        S.dma("sp", rt[0:64, 0:512].rearrange("v (hp h2 k) -> v hp h2 k", hp=4, h2=2),
              src.rearrange("(hp h2) v k -> v hp h2 k", h2=2), w=[rb])
        pt, pb = psum()
        for hp in range(4):
            S.pe(lambda e, hp=hp: e.transpose(pt[:, hp * 64:(hp + 1) * 64], rt[0:64, hp * 128:(hp + 1) * 128], ident(64)),
                 r=[rb, bCST], w=[pb])
        S.act(lambda e: e.copy(out=STS[:, j, :, :], in_=pt[:, 0:256].rearrange("p (h v) -> p h v", h=4)),
              r=[pb], w=[bSTS])

    dbg_done = [False]
    BLf = BL[:, :, :, :].rearrange("p a b c -> p (a b c)").bitcast(F32)
    CLf = CL[:, :, :, :].rearrange("p a b c -> p (a b c)").bitcast(F32)
    bART = [Buf("ART0"), Buf("ART1")]
    bTM = [Buf("TM0"), Buf("TM1")]
    bZN = [Buf("ZN0"), Buf("ZN1")]
    bBK = bBIG[2]
    bZPQ = Buf("ZPQ")
    alias_bufs = bART + bTM + bZN

    def alias_claim(for_rwkv):
        if for_rwkv == 1:
            S.dve(lambda e: e.memset(CLf[:, 4090:4092], 0.0), r=[bBL, bCL, bZT], w=alias_bufs + [bZPQ])
        elif for_rwkv == 2:
            S.dve(lambda e: e.memset(CLf[:, 4090:4092], 0.0), r=alias_bufs + [bZPQ], w=[bZT])
        else:
            S.dve(lambda e: e.memset(CLf[:, 4090:4092], 0.0), r=alias_bufs, w=[bBL, bCL])

    def rw_views(j, col, L, p):
        V = {}
        fm = lambda slot, idx: BIG[:, slot, idx * 256:idx * 256 + 4 * L].rearrange("p (h l) -> p h l", h=4)
        V["LD"], V["LP"], V["EE"], V["KKn"] = fm(0, 0), fm(0, 1), fm(0, 2), fm(0, 3)
        V["KP"], V["AAv"], V["Bv"], V["RKR"] = fm(1, 0), fm(1, 1), fm(1, 2), fm(1, 3)
        V["BT"] = BIG[:, 2, 512:512 + 4 * L].rearrange("p (h l) -> p h l", h=4)
        V["KT"] = BIG[:, 2, 768:768 + 4 * L].rearrange("p (h l) -> p h l", h=4)
        V["BHf"], V["KHf"], V["TMPf"] = fm(3, 0), fm(3, 1), fm(3, 2)
        V["SGL"] = BIG[:, 3, 768:768 + L]
        V["TW"] = BIG[0:64, 3, 832:832 + L]
        ar0 = p * 520
        V["ART"] = CLf[:, ar0:ar0 + 8 * L].rearrange("p (h a l) -> p h a l", h=4, a=2)
        V["PL"] = CLf[:, ar0 + 512:ar0 + 516]
        tm0 = p * 2048
        V["Vt"], V["BHt"] = BLf[0:L, tm0:tm0 + 512], BLf[0:L, tm0 + 512:tm0 + 1024]
        V["KHt"], V["Gt"] = BLf[0:L, tm0 + 1024:tm0 + 1536], BLf[0:L, tm0 + 1536:tm0 + 2048]
        V["RKt"] = CLf[0:L, 1040 + 8 * p:1048 + 8 * p]
        zr = lambda i: ZT[0:L, i, 0:8 * L].rearrange("p (h l) -> p h l", h=8)
        V["Pq"] = [zr(0), zr(1)]
        V["Qq"] = [zr(2), zr(3)]
        if p == 0:
            V["Mv"], V["NAK"], V["NRB"], V["NRK"] = zr(4), zr(5), zr(6), zr(7)
        else:
            zc = lambda i: CLf[0:L, 2048 + 512 * i:2048 + 512 * i + 8 * L].rearrange("p (h l) -> p h l", h=8)
            V["Mv"], V["NAK"], V["NRB"], V["NRK"] = zc(0), zc(1), zc(2), zc(3)
        V["Zs"], V["Ws"] = BIG[0:L, 6, 0:512], BIG[0:L, 6, 512:1024]
        V["Ys"], V["Y2"] = BIG[0:L, 7, 0:512], BIG[0:L, 7, 512:1024]
        V["MU"], V["RS"] = BIG[0:L, 8, 0:8], BIG[0:L, 8, 8:16]
        return V

    hd = lambda h: (h // 2, 64 * (h % 2))
    HORD = [0, 2, 4, 6, 1, 3, 5, 7]

    def rwkv_part1(j, col, L, nlev, p):
        B0, B1, B2, B3 = bBIG[0], bBIG[1], bBIG[2], bBIG[3]
        V = rw_views(j, col, L, p)
        LD, LP, EE, KKn, KP, AAv, Bv, RKR = (V[k] for k in ("LD", "LP", "EE", "KKn", "KP", "AAv", "Bv", "RKR"))
        BT, KT, BHf, KHf, TMPf, SGL, TW = (V[k] for k in ("BT", "KT", "BHf", "KHf", "TMPf", "SGL", "TW"))
        ART, PL, Vt, BHt, KHt, Gt, RKt = (V[k] for k in ("ART", "PL", "Vt", "BHt", "KHt", "Gt", "RKt"))
        Pq, Qq, Mv, NAK, NRB, NRK = (V[k] for k in ("Pq", "Qq", "Mv", "NAK", "NRB", "NRK"))
        bA, bT, bZ = bART[p], bTM[p], bZN[p]
        R_ = QT[:, 0:4, col:col + L]
        K_ = QT[:, 4:8, col:col + L]
        V_ = QT[:, 8:12, col:col + L]
        WLv = QT[0:64, 12, col:col + L]
        ALv = QT[64:128, 12, col:col + L]
        GLv = QT[:, 13, col:col + L]
        bc4 = lambda c: PA[:, c:c + 4].unsqueeze(2).to_broadcast([128, 4, L])
        ones = CST[:, C_ONE:C_ONE + L]
        TT = lambda o, a, b, op, r, w: S.dve(lambda e: e.tensor_tensor(out=o, in0=a, in1=b, op=op), r=r, w=w)
        S.act(lambda e: e.activation(out=TW, in_=WLv, func=AF.Tanh), r=[bQA], w=[B3])
        pw, pwb = psum()
        for hp in range(4):
            S.pe(lambda e, hp=hp: e.matmul(pw[:, hp * L:(hp + 1) * L], LOR[0:64, 0, hp * 128:(hp + 1) * 128], TW,
                                           start=True, stop=True), r=[bLOR, B3], w=[pwb], inc=(hp == 3))
        pa_, pab = psum()
        for hp in range(4):
            S.pe(lambda e, hp=hp: e.matmul(pa_[:, hp * L:(hp + 1) * L], LOR[64:128, 1, hp * 128:(hp + 1) * 128], ALv,
                                           start=True, stop=True), r=[bLOR, bQA], w=[pab], inc=(hp == 3))
        for hp in range(4):
            S.act(lambda e, hp=hp: e.activation(out=LD[:, hp, :], in_=pw[:, hp * L:(hp + 1) * L], func=AF.Sigmoid,
                                                bias=PA[:, A_W0 + hp:A_W0 + hp + 1]), r=[pwb, bPA], w=[B0])
            S.act(lambda e, hp=hp: e.activation(out=AAv[:, hp, :], in_=pa_[:, hp * L:(hp + 1) * L], func=AF.Sigmoid,
                                                bias=PA[:, A_A0 + hp:A_A0 + hp + 1]), r=[pab, bPA], w=[B1])
        S.dve(lambda e: e.tensor_scalar(out=LD, in0=LD, scalar1=-0.6065306597126334, scalar2=None, op0=ALU.mult),
              r=[B0], w=[B0])
        for hp in range(4):
            S.dve(lambda e, hp=hp: e.tensor_tensor_scan(out=LP[:, hp, :], data0=ones, data1=LD[:, hp, :], initial=0.0,
                                                        op0=ALU.mult, op1=ALU.add), r=[B0, bCST], w=[B0])
        TT(KKn, K_, bc4(A_KK), ALU.mult, [bQA, bPA], [B0])
        TT(TMPf, KKn, KKn, ALU.mult, [B0], [B3])
        pss, pssb = psum()
        for hp in range(4):
            S.pe(lambda e, hp=hp: e.matmul(pss[:, hp * L:(hp + 1) * L], CST[:, C_BONE:C_BONE + 128], TMPf[:, hp, :],
                                           start=True, stop=True), r=[bCST, B3], w=[pssb], inc=(hp == 3))
        S.dve(lambda e: e.tensor_scalar(out=TMPf, in0=pss[:, 0:4 * L].rearrange("p (h l) -> p h l", h=4),
                                        scalar1=1e-24, scalar2=None, op0=ALU.max), r=[pssb], w=[B3])
        S.act(lambda e: e.activation(out=TMPf, in_=TMPf, func=AF.Sqrt), r=[B3], w=[B3])
        S.dve(lambda e: e.reciprocal(out=TMPf, in_=TMPf), r=[B3], w=[B3])
        TT(KKn, KKn, TMPf, ALU.mult, [B0, B3], [B0])
        S.dve(lambda e: e.tensor_scalar(out=TMPf, in0=AAv, scalar1=-1.0, scalar2=None, op0=ALU.add), r=[B1], w=[B3])
        TT(TMPf, TMPf, bc4(A_KA), ALU.mult, [B3, bPA], [B3])
        S.dve(lambda e: e.scalar_tensor_tensor(out=KP, in0=TMPf, scalar=1.0, in1=K_, op0=ALU.add, op1=ALU.mult),
              r=[B3, bQA], w=[B1])
        TT(Bv, KKn, AAv, ALU.mult, [B0, B1], [B1])
        S.act(lambda e: e.activation(out=EE, in_=LP, func=AF.Exp, scale=-1.0), r=[B0], w=[B0])
        TT(BT, Bv, EE, ALU.mult, [B1, B0], [bBK])
        TT(KT, KP, EE, ALU.mult, [B1, B0], [bBK])
        S.act(lambda e: e.activation(out=EE, in_=LP, func=AF.Exp), r=[B0], w=[B0])
        TT(ART[:, :, 1, :], R_, EE, ALU.mult, [bQA, B0], [bA])
        TT(TMPf, LP, LD, ALU.subtract, [B0], [B3])
        S.act(lambda e: e.activation(out=EE, in_=TMPf, func=AF.Exp), r=[B3], w=[B0])
        S.dve(lambda e: e.scalar_tensor_tensor(out=ART[:, :, 0, :], in0=KKn, scalar=-1.0, in1=EE,
                                               op0=ALU.mult, op1=ALU.mult), r=[B0], w=[bA])
        TT(TMPf, LP[:, :, L - 1:L].to_broadcast([128, 4, L]), LP, ALU.subtract, [B0], [B3])
        S.act(lambda e: e.activation(out=EE, in_=TMPf, func=AF.Exp), r=[B3], w=[B0])
        TT(BHf, Bv, EE, ALU.mult, [B1, B0], [B3])
        TT(KHf, KP, EE, ALU.mult, [B1, B0], [B3])
        S.act(lambda e: e.activation(out=PL, in_=LP[:, :, L - 1], func=AF.Exp), r=[B0], w=[bA])
        TT(RKR, R_, KP, ALU.mult, [bQA, B1], [B1])
        TT(RKR, RKR, bc4(A_RK), ALU.mult, [B1, bPA], [B1])
        S.act(lambda e: e.activation(out=SGL, in_=GLv, func=AF.Sigmoid), r=[bQA], w=[B3])
        for src, sbf, dst in ((V_, bQA, Vt), (BHf, B3, BHt), (KHf, B3, KHt)):
            pt, pb = psum()
            for hp in range(4):
                S.pe(lambda e, hp=hp, src=src, pt=pt: e.transpose(pt[0:L, hp * 128:(hp + 1) * 128], src[:, hp, :], ident(128)),
                     r=[sbf, bCST], w=[pb])
            S.act(lambda e, dst=dst, pt=pt: e.copy(out=dst, in_=pt[0:L, 0:512]), r=[pb], w=[bT])
        prk, prkb = psum()
        for hp in range(4):
            S.pe(lambda e, hp=hp: e.matmul(prk[0:L, 2 * hp:2 * hp + 2], RKR[:, hp, :], CST[:, C_BIND:C_BIND + 2],
                                           start=True, stop=True), r=[B1, bCST], w=[prkb], inc=(hp == 3))
        S.act(lambda e: e.copy(out=RKt, in_=prk[0:L, 0:8]), r=[prkb], w=[bT])
        pg, pgb = psum()
        S.pe(lambda e: e.matmul(pg[0:L, 0:512], SGL, LOR[:, 2, :], start=True, stop=True), r=[B3, bLOR], w=[pgb])
        S.act(lambda e: e.copy(out=Gt, in_=pg[0:L, 0:512]), r=[pgb], w=[bT])
        specs = [("bt", 0, Qq[0], C_MSU, bZPQ), ("bt", 1, NRB, C_MUI, bZ), ("kt", 0, NAK, C_MSU, bZ), ("kt", 1, NRK, C_MUI, bZ)]
        for lname, a_idx, dst, mcol, dbf in specs:
            pt, pb = psum()
            for h in HORD:
                hp, ho = hd(h)
                lh = (BT if lname == "bt" else KT)[ho:ho + 64, hp, :]
                if h == 1:
                    S.fence("pe")
                S.pe(lambda e, h=h, lh=lh, hp=hp, ho=ho, a_idx=a_idx, pt=pt: e.matmul(
                    pt[0:L, h * L:(h + 1) * L], lh, ART[ho:ho + 64, hp, a_idx, :], start=True, stop=True),
                    r=[bBK, bA], w=[pb], inc=(h in (6, 7)))
            S.dve(lambda e, dst=dst, mcol=mcol, pt=pt: e.tensor_tensor(
                out=dst, in0=pt[0:L, 0:8 * L].rearrange("p (h l) -> p h l", h=8),
                in1=CST[0:L, mcol:mcol + L].unsqueeze(1).to_broadcast([L, 8, L]), op=ALU.mult),
                r=[pb, bCST], w=[dbf])
        pt, pb = psum()
        for h in HORD:
            hp, ho = hd(h)
            if h == 1:
                S.fence("pe")
            S.pe(lambda e, h=h, hp=hp, ho=ho, pt=pt: e.matmul(pt[0:L, h * L:(h + 1) * L], ART[ho:ho + 64, hp, 0, :],
                                                              BT[ho:ho + 64, hp, :], start=True, stop=True),
                 r=[bBK, bA], w=[pb], inc=(h in (6, 7)))
        S.dve(lambda e, pt=pt: e.tensor_tensor(
            out=Pq[0], in0=pt[0:L, 0:8 * L].rearrange("p (h l) -> p h l", h=8),
            in1=CST[0:L, C_MSL:C_MSL + L].unsqueeze(1).to_broadcast([L, 8, L]), op=ALU.mult), r=[pb, bCST], w=[bZPQ])
        S.dve(lambda e: e.tensor_tensor(out=Mv, in0=Qq[0], in1=CST[0:L, C_ID:C_ID + L].unsqueeze(1).to_broadcast([L, 8, L]),
                                        op=ALU.add), r=[bZPQ, bCST], w=[bZ])
        cur = 0
        for lev in range(1, nlev):
            nxt = 1 - cur
            pP, pPb = psum()
            for h in range(8):
                S.pe(lambda e, h=h, cur=cur, pP=pP: e.matmul(pP[0:L, h * L:(h + 1) * L], Qq[cur][:, h, :], Pq[cur][:, h, :],
                                                             start=True, stop=True), r=[bZPQ], w=[pPb], inc=(h == 7))
            need_q = lev < nlev - 1
            if need_q:
                pQ, pQb = psum()
                for h in range(8):
                    S.pe(lambda e, h=h, cur=cur, pQ=pQ: e.matmul(pQ[0:L, h * L:(h + 1) * L], Pq[cur][:, h, :], Qq[cur][:, h, :],
                                                                 start=True, stop=True), r=[bZPQ], w=[pQb], inc=(h == 7))
            S.act(lambda e, nxt=nxt, pP=pP: e.copy(out=Pq[nxt], in_=pP[0:L, 0:8 * L].rearrange("p (h l) -> p h l", h=8)),
                  r=[pPb], w=[bZPQ])
            if need_q:
                S.dve(lambda e, nxt=nxt, pQ=pQ: e.tensor_copy(out=Qq[nxt], in_=pQ[0:L, 0:8 * L].rearrange("p (h l) -> p h l", h=8)),
                      r=[pQb], w=[bZPQ])
            pM, pMb = psum()
            for h in range(8):
                S.pe(lambda e, h=h, nxt=nxt, pM=pM: e.matmul(pM[0:L, h * L:(h + 1) * L], Pq[nxt][:, h, :], Mv[:, h, :],
                                                             start=True, stop=True), r=[bZPQ, bZ], w=[pMb], inc=(h == 7))
            S.dve(lambda e, pM=pM: e.tensor_tensor(out=Mv, in0=Mv, in1=pM[0:L, 0:8 * L].rearrange("p (h l) -> p h l", h=8),
                                                   op=ALU.add), r=[bZ, pMb], w=[bZ])
            cur = nxt

    def rwkv_part2(j, col, L, p):
        B6, B7, B8 = bBIG[6], bBIG[7], bBIG[8]
        V = rw_views(j, col, L, p)
        ART, PL, Vt, BHt, KHt, Gt, RKt = (V[k] for k in ("ART", "PL", "Vt", "BHt", "KHt", "Gt", "RKt"))
        Mv, NAK, NRB, NRK = (V[k] for k in ("Mv", "NAK", "NRB", "NRK"))
        Zs, Ws, Ys, Y2, MU, RS = (V[k] for k in ("Zs", "Ws", "Ys", "Y2", "MU", "RS"))
        bA, bT, bZ = bART[p], bTM[p], bZN[p]
        ST = STS[:, j, :, :]
        TT = lambda o, a, b, op, r, w: S.dve(lambda e: e.tensor_tensor(out=o, in0=a, in1=b, op=op), r=r, w=w)
        pz, pzb = psum()
        S.fence("pe")
        for i_, h in enumerate((1, 3, 5, 7)):
            hp, ho = hd(h)
            S.pe(lambda e, h=h, hp=hp, ho=ho, i_=i_: e.matmul(pz[0:L, h * 64:(h + 1) * 64], ART[ho:ho + 64, hp, 0, :],
                                                              ST[ho:ho + 64, hp, :], start=(i_ == 0), stop=False,
                                                              skip_group_check=True),
                 r=[bA, bSTS], w=[pzb], inc=True)
        S.fence("pe")
        for h in (0, 2, 4, 6):
            hp, ho = hd(h)
            S.pe(lambda e, h=h, hp=hp, ho=ho: e.matmul(pz[0:L, h * 64:(h + 1) * 64], ART[ho:ho + 64, hp, 0, :],
                                                       ST[ho:ho + 64, hp, :], start=False, stop=False,
                                                       skip_group_check=True),
                 r=[bA, bSTS], w=[pzb], inc=False)
        for h in range(8):
            S.pe(lambda e, h=h: e.matmul(pz[0:L, h * 64:(h + 1) * 64], NAK[:, h, :], Vt[:, h * 64:(h + 1) * 64],
                                         start=False, stop=True, skip_group_check=True),
                 r=[bZ, bT], w=[pzb], inc=(h == 7))
        S.act(lambda e: e.copy(out=Zs, in_=pz[0:L, 0:512]), r=[pzb], w=[B6])
        pw2, pw2b = psum()
        for h in range(8):
            S.pe(lambda e, h=h: e.matmul(pw2[0:L, h * 64:(h + 1) * 64], Mv[:, h, :], Zs[:, h * 64:(h + 1) * 64],
                                         start=True, stop=True), r=[bZ, B6], w=[pw2b], inc=(h == 7))
        S.act(lambda e: e.copy(out=Ws, in_=pw2[0:L, 0:512]), r=[pw2b], w=[B6])
        py, pyb = psum()
        S.fence("pe")
        for i_, h in enumerate((1, 3, 5, 7)):
            hp, ho = hd(h)
            S.pe(lambda e, h=h, hp=hp, ho=ho, i_=i_: e.matmul(py[0:L, h * 64:(h + 1) * 64], ART[ho:ho + 64, hp, 1, :],
                                                              ST[ho:ho + 64, hp, :], start=(i_ == 0), stop=False,
                                                              skip_group_check=True),
                 r=[bA, bSTS], w=[pyb], inc=True)
        S.fence("pe")
        for h in (0, 2, 4, 6):
            hp, ho = hd(h)
            S.pe(lambda e, h=h, hp=hp, ho=ho: e.matmul(py[0:L, h * 64:(h + 1) * 64], ART[ho:ho + 64, hp, 1, :],
                                                       ST[ho:ho + 64, hp, :], start=False, stop=False,
                                                       skip_group_check=True),
                 r=[bA, bSTS], w=[pyb], inc=False)
        for h in range(8):
            S.pe(lambda e, h=h: e.matmul(py[0:L, h * 64:(h + 1) * 64], NRB[:, h, :], Ws[:, h * 64:(h + 1) * 64],
                                         start=False, stop=False, skip_group_check=True),
                 r=[bZ, B6], w=[pyb], inc=False)
            S.pe(lambda e, h=h: e.matmul(py[0:L, h * 64:(h + 1) * 64], NRK[:, h, :], Vt[:, h * 64:(h + 1) * 64],
                                         start=False, stop=True, skip_group_check=True),
                 r=[bZ, bT], w=[pyb], inc=(h == 7))
        psn, psnb = psum()
        for h in range(8):
            hp, ho = hd(h)
            S.pe(lambda e, h=h, hp=hp, ho=ho: e.matmul(psn[ho:ho + 64, hp * 64:(hp + 1) * 64], BHt[:, h * 64:(h + 1) * 64],
                                                       Ws[:, h * 64:(h + 1) * 64], start=True, stop=False),
                 r=[bT, B6], w=[psnb], inc=False)
            S.pe(lambda e, h=h, hp=hp, ho=ho: e.matmul(psn[ho:ho + 64, hp * 64:(hp + 1) * 64], KHt[:, h * 64:(h + 1) * 64],
                                                       Vt[:, h * 64:(h + 1) * 64], start=False, stop=True),
                 r=[bT], w=[psnb], inc=(h == 7))
        S.act(lambda e: e.copy(out=Ys, in_=py[0:L, 0:512]), r=[pyb], w=[B7])
        TT(ST, ST, PL.unsqueeze(2).to_broadcast([128, 4, 64]), ALU.mult, [bSTS, bA], [bSTS])
        TT(ST, ST, psn[:, 0:256].rearrange("p (h v) -> p h v", h=4), ALU.add, [bSTS, psnb], [bSTS])
        Y3 = Ys.rearrange("p (h v) -> p h v", h=8)
        Y23 = Y2.rearrange("p (h v) -> p h v", h=8)
        V3 = Vt.rearrange("p (h v) -> p h v", h=8)
        b8 = lambda ap: ap.unsqueeze(2).to_broadcast([L, 8, 64])
        S.dve(lambda e: e.reduce_sum(out=MU, in_=Y3, axis=AX.X), r=[B7], w=[B8])
        S.dve(lambda e: e.tensor_scalar(out=MU, in0=MU, scalar1=1.0 / 64, scalar2=None, op0=ALU.mult), r=[B8], w=[B8])
        TT(Y3, Y3, b8(MU), ALU.subtract, [B7, B8], [B7])
        TT(Y23, Y3, Y3, ALU.mult, [B7], [B7])
        S.dve(lambda e: e.reduce_sum(out=RS, in_=Y23, axis=AX.X), r=[B7], w=[B8])
        S.dve(lambda e: e.tensor_scalar(out=RS, in0=RS, scalar1=1.0 / 64, scalar2=GN_EPS, op0=ALU.mult, op1=ALU.add),
              r=[B8], w=[B8])
        S.act(lambda e: e.activation(out=RS, in_=RS, func=AF.Sqrt), r=[B8], w=[B8])
        S.dve(lambda e: e.reciprocal(out=RS, in_=RS), r=[B8], w=[B8])
        TT(Y3, Y3, b8(RS), ALU.mult, [B7, B8], [B7])
        TT(Ys, Ys, LNX[0:L, 0, :], ALU.mult, [B7, bLNX], [B7])
        TT(Ys, Ys, LNX[0:L, 1, :], ALU.add, [B7, bLNX], [B7])
        TT(Y23, V3, b8(RKt), ALU.mult, [bT], [B7])
        TT(Ys, Ys, Y2, ALU.add, [B7], [B7])
        TT(Ys, Ys, Gt, ALU.mult, [B7, bT], [B7])
        pT, pTb = psum()
        for hp in range(4):
            S.pe(lambda e, hp=hp: e.transpose(pT[:, hp * L:(hp + 1) * L], Ys[:, hp * 128:(hp + 1) * 128], ident(L)),
                 r=[B7, bCST], w=[pTb])
        S.act(lambda e: e.copy(out=YAB[:, 4:8, col:col + L], in_=pT[:, 0:4 * L].rearrange("p (h l) -> p h l", h=4)),
              r=[pTb], w=[bYAB])

    def mixer_ab(l, ti):
        j = l // 2
        c0, n, nseq, T = TILES[ti]
        E = 30 + T
        w_in = W["w_in_ab"][j]
        cast_xb(ti)
        UEXT = lambda c: BIG[:, c, 0:nseq * E].rearrange("p (s e) -> p s e", s=nseq)
        CVA = lambda c: BIG[:, 4 + c, 0:n]
        UALL = BIG[:, 0:4, 0:nseq * E].rearrange("p c (s e) -> p c s e", s=nseq)
        if nseq == 1:
            if ti == 0:
                S.dve(lambda e: e.memset(UALL[:, :, 0, 0:30], 0.0), w=bBIG[0:4])
            else:
                S.dve(lambda e: e.tensor_copy(out=UALL[:, :, 0, 0:30], in_=CAH[:, j, :, :]), r=[bCAH], w=bBIG[0:4])
        else:
            for q in range(4):
                i = rows_i[0]
                rows_i[0] ^= 1
                rt, rb = rows_t[i], bROWS[i]
                for b4 in range(4):
                    S.dma("sp", rt[30 * b4:30 * b4 + 30, 0:512], st_conv_a[4 * q + b4, j], w=[rb])
                pt, pb = psum()
                for c in range(4):
                    S.pe(lambda e, c=c: e.transpose(pt[:, c * 120:(c + 1) * 120], rt[0:120, c * 128:(c + 1) * 128], ident(120)),
                         r=[rb, bCST], w=[pb])
                for c in range(4):
                    S.act(lambda e, c=c, q=q: e.copy(out=UEXT(c)[:, 4 * q:4 * q + 4, 0:30],
                                                     in_=pt[:, c * 120:(c + 1) * 120].rearrange("p (b r) -> p b r", b=4)),
                          r=[pb], w=[bBIG[c]])
        for c in range(4):
            wv, wvb = wslab(w_in, 0, 8, c * 128, 128, key=("inab", j))
            wg, wgb = wslab(w_in, 0, 8, 512 + c * 128, 128, key=("inab", j))
            pv, pvb = psum()
            for k in range(8):
                S.pe(lambda e, k=k: e.matmul(pv[:, 0:n], wv[:, k, :], XB[:, k, 0:n], start=(k == 0), stop=(k == 7)),
                     r=[wvb, bXB], w=[pvb], inc=(k == 7))
            pg, pgb = psum()
            for k in range(8):
                S.pe(lambda e, k=k: e.matmul(pg[:, 0:n], wg[:, k, :], XB[:, k, 0:n], start=(k == 0), stop=(k == 7)),
                     r=[wgb, bXB], w=[pgb], inc=(k == 7))
            sg = LNS[:, 0, 0:n]
            S.act(lambda e: e.activation(out=sg, in_=pg[:, 0:n], func=AF.Sigmoid), r=[pgb], w=[bLNS[0]])
            S.dve(lambda e, c=c: e.tensor_tensor(out=UEXT(c)[:, :, 30:E], in0=sg.rearrange("p (s t) -> p s t", s=nseq),
                                                 in1=pv[:, 0:n].rearrange("p (s t) -> p s t", s=nseq), op=ALU.mult),
                  r=[bLNS[0], pvb], w=[bBIG[c]])
        for c in range(4):
            acc = CVA(c).rearrange("p (s t) -> p s t", s=nseq)
            S.dve(lambda e, c=c, acc=acc: e.tensor_scalar(out=acc, in0=UEXT(c)[:, :, 0:T], scalar1=PA[:, A_CW + c:A_CW + c + 1],
                                                          scalar2=PA[:, A_CB + c:A_CB + c + 1], op0=ALU.mult, op1=ALU.add),
                  r=[bBIG[c], bPA], w=[bBIG[4 + c]])
            for t_ in range(1, 31):
                S.dve(lambda e, c=c, acc=acc, t_=t_: e.scalar_tensor_tensor(
                    out=acc, in0=UEXT(c)[:, :, t_:t_ + T], scalar=PA[:, A_CW + 4 * t_ + c:A_CW + 4 * t_ + c + 1], in1=acc,
                    op0=ALU.mult, op1=ALU.add), r=[bBIG[c], bPA, bBIG[4 + c]], w=[bBIG[4 + c]])
        if nseq == 1:
            S.dve(lambda e: e.tensor_copy(out=CAH[:, j, :, :], in_=UALL[:, :, 0, T:T + 30]), r=bBIG[0:4], w=[bCAH])
            if ti == 3:
                def dst(rt, rb, done, g):
                    S.dma("sp", o_conv_p[j], rt[0:30, 0:512], r=[rb], is_out=True)
                emit_rows(lambda c: CAH[:, j, c, :], 30, 4, dst, [bCAH])
        else:
            for q in range(4):
                i = rows_i[0]
                rows_i[0] ^= 1
                rt, rb = rows_t[i], bROWS[i]
                pt, pb = psum()
                for c in range(4):
                    stg = LNS[:, c, 0:120]
                    S.dve(lambda e, c=c, q=q, stg=stg: e.tensor_copy(out=stg.rearrange("p (b r) -> p b r", b=4),
                                                                     in_=UEXT(c)[:, 4 * q:4 * q + 4, T:T + 30]),
                          r=[bBIG[c]], w=[bLNS[c]])
                    S.pe(lambda e, c=c, stg=stg: e.transpose(pt[0:120, c * 128:(c + 1) * 128], stg, ident(128)),
                         r=[bLNS[c], bCST], w=[pb])
                S.act(lambda e: e.copy(out=rt[0:120, 0:512], in_=pt[0:120, 0:512]), r=[pb], w=[rb])
                for b4 in range(4):
                    S.dma("sp", o_conv_s[4 * q + b4, j], rt[30 * b4:30 * b4 + 30, 0:512], r=[rb], is_out=True)
        def outf(k, tmp, tb, g, b):
            S.act(lambda e: e.activation(out=YAB[:, k, 0:n], in_=tmp, func=AF.Silu, scale=g, bias=b),
                  r=[tb, bPA], w=[bYAB])
        layer_norm(lambda k: CVA(k), bBIG[4:8], 4, n, A_LG, A_LB, outf, ptile=PA)
        ZTf = ZT[:, :, :].rearrange("p a b -> p (a b)")
        PBX = lambda i: ZTf[:, i * 1024:i * 1024 + nseq * (1 + T)].rearrange("p (s e) -> p s e", s=nseq)
        DTMP = ZTf[:, 2048:2048 + n].rearrange("p (s t) -> p s t", s=nseq)
        if nseq > 1:
            for half in range(2):
                i = rows_i[0]
                rows_i[0] ^= 1
                rt, rb = rows_t[i], bROWS[i]
                S.dma("sp", rt[0:16, 0:896], st_shift[:, j, half * 896:(half + 1) * 896], w=[rb])
                pt, pb = psum()
                for m in range(7):
                    S.pe(lambda e, m=m: e.transpose(pt[:, m * 16:(m + 1) * 16], rt[0:16, m * 128:(m + 1) * 128], ident(16)),
                         r=[rb, bCST], w=[pb])
                S.act(lambda e, half=half: e.copy(out=SHS[:, 7 * half:7 * half + 7, :],
                                                  in_=pt[:, 0:112].rearrange("p (m b) -> p m b", m=7)), r=[pb], w=[bSHS])
        for m in range(14):
            wp, wpb = wslab(w_in, 0, 8, 1024 + m * 128, 128, key=("inab", j))
            pp, ppb = psum()
            for k in range(8):
                S.pe(lambda e, k=k, m=m: e.matmul(pp[:, 0:n], wp[:, k, :], XB[:, k, 0:n],
                                                  start=(k == 0), stop=(k == 7)), r=[wpb, bXB], w=[ppb], inc=(k == 7))
            pbx = PBX(m % 2)
            S.act(lambda e, pbx=pbx: e.copy(out=pbx[:, :, 1:1 + T], in_=pp[:, 0:n].rearrange("p (s t) -> p s t", s=nseq)),
                  r=[ppb], w=[bZT])
            if nseq == 1:
                if ti == 0:
                    S.dve(lambda e, pbx=pbx: e.memset(pbx[:, :, 0:1], 0.0), w=[bZT])
                else:
                    S.dve(lambda e, pbx=pbx, m=m: e.tensor_copy(out=pbx[:, 0, 0:1], in_=CSH[:, j, m:m + 1]), r=[bCSH], w=[bZT])
            else:
                S.dve(lambda e, pbx=pbx, m=m: e.tensor_copy(out=pbx[:, :, 0], in_=SHS[:, m, :]), r=[bSHS], w=[bZT])
            S.dve(lambda e, pbx=pbx: e.tensor_tensor(out=DTMP, in0=pbx[:, :, 0:T], in1=pbx[:, :, 1:1 + T], op=ALU.subtract),
                  r=[bZT], w=[bZT])
            S.dve(lambda e, pbx=pbx, m=m: e.scalar_tensor_tensor(
                out=QT[:, m, 0:n].rearrange("p (s t) -> p s t", s=nseq), in0=DTMP, scalar=PA[:, A_MU + m:A_MU + m + 1],
                in1=pbx[:, :, 1:1 + T], op0=ALU.mult, op1=ALU.add), r=[bZT, bPA], w=[bQA])
            if nseq == 1:
                S.dve(lambda e, pbx=pbx, m=m: e.tensor_copy(out=CSH[:, j, m:m + 1], in_=pbx[:, 0, T:T + 1]), r=[bZT], w=[bCSH])
            else:
                S.dve(lambda e, pbx=pbx, m=m: e.tensor_copy(out=SHS[:, m, :], in_=pbx[:, :, T]), r=[bZT], w=[bSHS])
        if ti == 3:
            i = rows_i[0]
            rows_i[0] ^= 1
            rt, rb = rows_t[i], bROWS[i]
            pt, pb = psum()
            S.pe(lambda e: e.transpose(pt[0:14, 0:128], CSH[:, j, :], ident(128)), r=[bCSH, bCST], w=[pb])
            S.act(lambda e: e.copy(out=rt[0:14, 0:128], in_=pt[0:14, 0:128]), r=[pb], w=[rb])
            S.dma("sp", o_shift_p[j].rearrange("(c p) -> c p", p=128), rt[0:14, 0:128], r=[rb], is_out=True)
        if ti == 4:
            for half in range(2):
                i = rows_i[0]
                rows_i[0] ^= 1
                rt, rb = rows_t[i], bROWS[i]
                for m0, mn in ((0, 4), (4, 3)):
                    pt, pb = psum()
                    for mm in range(mn):
                        m = m0 + mm
                        S.pe(lambda e, m=m, mm=mm, half=half: e.transpose(pt[0:16, mm * 128:(mm + 1) * 128],
                                                                          SHS[:, 7 * half + m, :], ident(128)),
                             r=[bSHS, bCST], w=[pb])
                    S.act(lambda e, m0=m0, mn=mn, pt=pt: e.copy(out=rt[0:16, m0 * 128:(m0 + mn) * 128], in_=pt[0:16, 0:mn * 128]),
                          r=[pb], w=[rb])
                S.dma("sp", o_shift_s[:, j, half * 896:(half + 1) * 896], rt[0:16, 0:896], r=[rb], is_out=True)
        if cfg.get("norwkv"):
            S.dve(lambda e: e.memset(YAB[:, 4:8, 0:n], 0.0), w=[bYAB])
        elif nseq == 1:
            alias_claim(1)
            rwkv_part1(j, 0, 64, 6, 0)
            for ch in range(8):
                S.begin_rec()
                ps_pool[0] = 0
                if ch + 1 < 8:
                    rwkv_part1(j, (ch + 1) * 64, 64, 6, (ch + 1) % 2)
                ra = S.end_rec()
                S.begin_rec()
                ps_pool[0] = 1
                rwkv_part2(j, ch * 64, 64, ch % 2)
                rb_ = S.end_rec()
                ps_pool[0] = None
                S.merge(ra, rb_)
            if ti == 3:
                wkv_store(j, o_wkv_p[j])
        else:
            alias_claim(1)
            rwkv_part1(j, 0, 4, 2, 0)
            for b in range(NSB):
                S.begin_rec()
                ps_pool[0] = 0
                if b + 1 < NSB:
                    rwkv_part1(j, (b + 1) * 4, 4, 2, (b + 1) % 2)
                ra = S.end_rec()
                S.begin_rec()
                ps_pool[0] = 1
                wkv_load(j, st_wkv[b, j])
                rwkv_part2(j, b * 4, 4, b % 2)
                wkv_store(j, o_wkv_s[b, j])
                rb_ = S.end_rec()
                ps_pool[0] = None
                S.merge(ra, rb_)
        if not cfg.get("norwkv"):
            alias_claim(2)
        for oc in range(8):
            wo, wob = wslab(W["w_out_ab"][j], 0, 8, oc * 128, 128, key=("outab", j))
            po, pob = psum()
            for k in range(8):
                S.pe(lambda e, k=k: e.matmul(po[:, 0:n], wo[:, k, :], YAB[:, k, 0:n], start=(k == 0), stop=(k == 7)),
                     r=[wob, bYAB], w=[pob], inc=(k == 7))
            S.dve(lambda e, oc=oc: e.scalar_tensor_tensor(out=ZT[:, oc, 0:n], in0=X[:, oc, 0:n], scalar=ALPHA, in1=po[:, 0:n],
                                                          op0=ALU.mult, op1=ALU.add), r=[bX[ti], pob], w=[bZT])


    def mixer_none(l, ti):
        c0, n, nseq, T = TILES[ti]
        for k in range(8):
            S.dve(lambda e, k=k: e.tensor_scalar(out=ZT[:, k, 0:n], in0=X[:, k, 0:n], scalar1=ALPHA, scalar2=None,
                                                 op0=ALU.mult), r=[bX[ti]], w=[bZT])

    def ln1(l, ti):
        c0, n, nseq, T = TILES[ti]

        def outf(k, tmp, tb, g, b):
            S.act(lambda e: e.activation(out=X[:, k, 0:n], in_=tmp, func=AF.Identity, scale=g, bias=b),
                  r=[tb, bPAR], w=[bX[ti]])
        layer_norm(lambda k: ZT[:, k, 0:n], bZT, 8, n, 104, 112, outf)

    tiles = cfg.get("tiles", [0, 1, 2, 3, 4])
    for ti in tiles:
        load_x(ti)
        for l in range(depth):
            ffn_params(l)
            mix = cfg.get("mix", "full")
            if l % 2 == 1 and mix in ("full", "c"):
                s5_setup_cached(l // 2, ti == tiles[0])
                if ti == 4:
                    s5_sample_state(l // 2)
                if cfg.get("s5dbg", 0) == 1:
                    mixer_none(l, ti)
                else:
                    mixer_c(l, ti)
            elif l % 2 == 0 and mix in ("full", "ab"):
                ab_params(l // 2)
                mixer_ab(l, ti)
            else:
                mixer_none(l, ti)
            ln1(l, ti)
            if ti == 4:
                ffn_sample_state(l)
            ffn_tile(l, ti)
        store_y(ti)
    S.finish()
    print("instructions:", S.ninst, {k: v for k, v in S.cnt.items()}, "sbuf_left", nc.sbuf_bytes_remaining)


WSHAPES = [
    ("w_in_ab", (2, 1024, 2816)), ("conv_a_w", (2, 31, 512)), ("conv_a_b", (2, 512)), ("ln_a_g", (2, 512)),
    ("ln_a_b", (2, 512)), ("mu_b", (2, 1792)), ("w0_b", (2, 512)), ("w2_b", (2, 64, 512)), ("a0_b", (2, 512)),
    ("a2_b", (2, 64, 512)), ("g2_b", (2, 128, 512)), ("k_k_b", (2, 512)), ("k_a_b", (2, 512)), ("r_k_b", (2, 8, 64)),
    ("ln_x_g", (2, 512)), ("ln_x_b", (2, 512)), ("w_out_ab", (2, 1024, 1024)), ("w_in_c", (2, 1024, 1024)),
    ("lam_re", (2, 64, 64)), ("lam_im", (2, 64, 64)), ("log_dt", (2, 64)), ("b_re", (2, 64, 64, 16)),
    ("b_im", (2, 64, 64, 16)), ("c_re", (2, 64, 16, 64)), ("c_im", (2, 64, 16, 64)), ("d_skip", (2, 1024)),
    ("w_out_c", (2, 1024, 2048)), ("w_up", (4, 1024, 5632)), ("conv_f_w", (4, 3, 2816)), ("conv_f_b", (4, 2816)),
    ("w_down", (4, 2816, 1024)), ("ln1_g", (4, 1024)), ("ln1_b", (4, 1024)), ("ln2_g", (4, 1024)), ("ln2_b", (4, 1024)),
]

OUT_NAMES = ["o_yp", "o_ys", "o_conv_p", "o_conv_s", "o_shift_p", "o_shift_s", "o_wkv_p", "o_wkv_s",
             "o_re_p", "o_re_s", "o_im_p", "o_im_s", "o_ffn_p", "o_ffn_s"]


def run(inputs, cfg=None, cores=None):
    cfg = cfg or {}
    cores = list(range(NCORES)) if cores is None else cores
    f = lambda a: np.ascontiguousarray(np.asarray(a, dtype=np.float32))
    consts = make_consts()
    wmap = {name: f(inputs[name]) for name, _ in WSHAPES}
    in_maps = []
    for c in cores:
        sl = slice(c * NSB, (c + 1) * NSB)
        m = dict(wmap)
        m["xp"] = f(inputs["x_prompt"][c])
        m["xs"] = f(inputs["x_sample"][sl]).reshape(NSB * TS, D)
        m["st_conv_a"] = f(inputs["state_conv_a"][sl])
        m["st_shift"] = f(inputs["state_shift_b"][sl])
        m["st_wkv"] = f(inputs["state_wkv_b"][sl])
        m["st_re"] = f(inputs["state_ssm_re"][sl])
        m["st_im"] = f(inputs["state_ssm_im"][sl])
        m["st_ffn"] = f(inputs["state_conv_ffn"][sl])
        m["cst"] = consts
        in_maps.append(m)
    nc = build(cfg)
    res = run_bass_kernel_spmd(nc, in_maps, core_ids=cores)
    return res.results


def kernel(**inputs):
    rs = run(inputs)
    outs = []
    for name in OUT_NAMES:
        if name.endswith("p"):
            outs.append(np.stack([np.asarray(r[name]) for r in rs], 0))
        else:
            outs.append(np.concatenate([np.asarray(r[name]) for r in rs], 0))
    outs[0] = outs[0].reshape(8, SEQ, D)
    outs[1] = outs[1].reshape(128, TS, D)
    return tuple(o.astype(np.float32) for o in outs)
```

```python
import numpy as np
from contextlib import ExitStack
import concourse.bass as bass
import concourse.mybir as mybir
from concourse.bass_utils import run_bass_kernel_spmd

F32 = mybir.dt.float32
BF16 = mybir.dt.bfloat16
AF = mybir.ActivationFunctionType
ALU = mybir.AluOpType
AX = mybir.AxisListType

D = 1024
SEQ = 2048
NSB = 16
TS = 4
NTOK = SEQ + NSB * TS
DEPTH = 4
DFF = 2816
NFF = 22
DBP = 1792
ALPHA = (2 * DEPTH) ** 0.25
LN_EPS = 1e-5
GN_EPS = 64 * 1e-5
NCORES = 8

C_ID = 0
C_MSU = 128
C_MUI = 192
C_MSL = 256
C_BONE = 320
C_BIND = 448
C_CM = 450
C_ONE = 962
C_W = 1090


def make_consts():
    c = np.zeros((128, C_W), np.float32)
    c[:, C_ID:C_ID + 128] = np.eye(128)
    i = np.arange(64)
    c[:64, C_MSU:C_MSU + 64] = (i[:, None] < i[None, :])
    c[:64, C_MUI:C_MUI + 64] = (i[:, None] <= i[None, :])
    c[:64, C_MSL:C_MSL + 64] = (i[:, None] > i[None, :])
    c[:64, C_BONE:C_BONE + 64] = 1.0
    c[64:, C_BONE + 64:C_BONE + 128] = 1.0
    c[:64, C_BIND] = 1.0
    c[64:, C_BIND + 1] = 1.0
    for q in range(4):
        m = np.zeros((128, 128), np.float32)
        m[32 * q:32 * q + 16, 0:64] = 1.0
        m[32 * q + 16:32 * q + 32, 64:128] = 1.0
        c[:, C_CM + 128 * q:C_CM + 128 * (q + 1)] = m
    c[:, C_ONE:C_ONE + 128] = 1.0
    return c


class Buf:
    __slots__ = ("name", "w", "r", "excl")

    def __init__(self, name, excl=False):
        self.name = name
        self.w = None
        self.r = {}
        self.excl = excl


class Sched:
    EPOCH = 30000

    def __init__(self, nc, es):
        self.nc = nc
        self.es = es
        self.engs = {"pe": nc.tensor, "dve": nc.vector, "act": nc.scalar, "pool": nc.gpsimd, "sp": nc.sync}
        self.cnt = {k: 0 for k in self.engs}
        self.esem = {}
        self.waited = {k: {} for k in self.engs}
        self.dq = {"sp": [es.enter_context(nc.semaphore(f"dsp{i}")) for i in range(24)],
                   "pool": [es.enter_context(nc.semaphore(f"dpl{i}")) for i in range(8)]}
        self.dcnt = {q: [0] * len(v) for q, v in self.dq.items()}
        self.dnext = {q: 0 for q in self.dq}
        self.out_tokens = []
        self.skip_same = {"pe": True, "dve": False, "act": False, "pool": False, "sp": True}
        self.ninst = 0
        self.rec = None

    def begin_rec(self):
        self.rec = []

    def end_rec(self):
        r, self.rec = self.rec, None
        return r

    @staticmethod
    def merge(a, b):
        i = j = 0
        while i < len(a) or j < len(b):
            if j >= len(b) or (i < len(a) and i * max(len(b), 1) <= j * max(len(a), 1)):
                f, args = a[i]
                i += 1
            else:
                f, args = b[j]
                j += 1
            f(*args)

    def _semh(self, key):
        if key[0] == "e":
            if key not in self.esem:
                self.esem[key] = self.es.enter_context(self.nc.semaphore(f"e_{key[1]}_{key[2]}"))
            return self.esem[key]
        return self.dq[key[1]][key[2]]

    def _deps(self, k, r, w):
        deps = {}

        def add(t):
            if t is None:
                return
            key, val = t
            if deps.get(key, 0) < val:
                deps[key] = val
        for b in r:
            add(b.w)
            if b.excl:
                for t in b.r.values():
                    add(t)
        for b in w:
            add(b.w)
            for t in b.r.values():
                add(t)
        return deps

    def _wait(self, k, deps):
        eng = self.engs[k]
        for key, val in deps.items():
            if key[0] == "e" and key[1] == k and self.skip_same[k]:
                continue
            if self.waited[k].get(key, 0) >= val:
                continue
            eng.wait_ge(self._semh(key), val)
            self.waited[k][key] = val

    def op(self, k, fn, r=(), w=(), inc=True):
        if self.rec is not None:
            self.rec.append((self._op, (k, fn, list(r), list(w), inc)))
            return None
        return self._op(k, fn, r, w, inc)

    def _op(self, k, fn, r=(), w=(), inc=True):
        self._wait(k, self._deps(k, r, w))
        ins = fn(self.engs[k])
        c = self.cnt[k] + 1
        key = ("e", k, (c - 1) // self.EPOCH)
        tok = (key, (c - 1) % self.EPOCH + 1)
        if inc:
            ins.then_inc(self._semh(key), 1)
            self.cnt[k] = c
        for b in r:
            b.r[k] = tok
        for b in w:
            b.w = tok
            b.r = {}
        self.ninst += 1
        return ins

    def fence(self, k):
        if self.rec is not None:
            self.rec.append((self._fence, (k,)))
            return
        self._fence(k)

    def _fence(self, k):
        c = self.cnt[k]
        if c == 0:
            return
        key = ("e", k, (c - 1) // self.EPOCH)
        self.engs[k].wait_ge(self._semh(key), (c - 1) % self.EPOCH + 1)

    def dve(self, fn, r=(), w=()):
        return self.op("dve", fn, r, w)

    def act(self, fn, r=(), w=()):
        return self.op("act", fn, r, w)

    def pool(self, fn, r=(), w=()):
        return self.op("pool", fn, r, w)

    def pe(self, fn, r=(), w=(), inc=True):
        return self.op("pe", fn, r, w, inc)

    def dma(self, q, out, in_, r=(), w=(), is_out=False):
        if self.rec is not None:
            self.rec.append((self._dma, (q, out, in_, list(r), list(w), is_out)))
            return
        self._dma(q, out, in_, r, w, is_out)

    def _dma(self, q, out, in_, r=(), w=(), is_out=False):
        deps = self._deps(q, r, w)
        idx = self.dnext[q]
        self.dnext[q] = (idx + 1) % len(self.dq[q])
        key = ("d", q, idx)
        if self.dcnt[q][idx] > 0:
            v = 16 * self.dcnt[q][idx]
            if deps.get(key, 0) < v:
                deps[key] = v
        self._wait(q, deps)
        self.engs[q].dma_start(out=out, in_=in_).then_inc(self.dq[q][idx], 16)
        self.dcnt[q][idx] += 1
        tok = (key, 16 * self.dcnt[q][idx])
        for b in r:
            b.r[key] = tok
        for b in w:
            b.w = tok
            b.r = {}
        if is_out:
            self.out_tokens.append(tok)
        self.ninst += 1

    def finish(self):
        deps = {}
        for key, val in self.out_tokens:
            if deps.get(key, 0) < val:
                deps[key] = val
        self._wait("sp", deps)


def build(cfg):
    depth = cfg.get("depth", DEPTH)
    nc = bass.Bass("TRN2", target_bir_lowering=False)
    es = ExitStack()
    with es:
        _build(nc, es, cfg, depth)
    return nc


def _build(nc, es, cfg, depth):
    S = Sched(nc, es)
    if cfg.get("unsafe_skip"):
        S.skip_same["dve"] = True
        S.skip_same["act"] = True

    def din(name, shape):
        return nc.dram_tensor(name, list(shape), F32, kind="ExternalInput").ap()

    def dout(name, shape):
        return nc.dram_tensor(name, list(shape), F32, kind="ExternalOutput").ap()

    def sb(name, shape, dt=F32):
        return es.enter_context(nc.sbuf_tensor(name, list(shape), dt))

    xp = din("xp", [SEQ, D])
    xs = din("xs", [NSB * TS, D])
    st_conv_a = din("st_conv_a", [NSB, 2, 30, 512])
    st_shift = din("st_shift", [NSB, 2, DBP])
    st_wkv = din("st_wkv", [NSB, 2, 8, 64, 64])
    st_re = din("st_re", [NSB, 2, 64, 64])
    st_im = din("st_im", [NSB, 2, 64, 64])
    st_ffn = din("st_ffn", [NSB, 4, 2, DFF])
    cst = din("cst", [128, C_W])
    W = {}
    for name, shape in WSHAPES:
        W[name] = din(name, shape)
    o_yp = dout("o_yp", [SEQ, D])
    o_ys = dout("o_ys", [NSB * TS, D])
    o_conv_p = dout("o_conv_p", [2, 30, 512])
    o_conv_s = dout("o_conv_s", [NSB, 2, 30, 512])
    o_shift_p = dout("o_shift_p", [2, DBP])
    o_shift_s = dout("o_shift_s", [NSB, 2, DBP])
    o_wkv_p = dout("o_wkv_p", [2, 8, 64, 64])
    o_wkv_s = dout("o_wkv_s", [NSB, 2, 8, 64, 64])
    o_re_p = dout("o_re_p", [2, 64, 64])
    o_re_s = dout("o_re_s", [NSB, 2, 64, 64])
    o_im_p = dout("o_im_p", [2, 64, 64])
    o_im_s = dout("o_im_s", [NSB, 2, 64, 64])
    o_ffn_p = dout("o_ffn_p", [4, 2, DFF])
    o_ffn_s = dout("o_ffn_s", [NSB, 4, 2, DFF])
    o_dbg = dout("o_dbg", [128, 2048]) if cfg.get("dbg") else None
    o_dbgB = dout("o_dbgB", [128, 9216]) if cfg.get("dbgr") else None
    o_dbgZ = dout("o_dbgZ", [128, 4096]) if cfg.get("dbgr") else None
    o_dbgS = dout("o_dbgS", [128, 256]) if cfg.get("dbgr") else None

    X = sb("X", [128, 8, 512])
    _bx = Buf("X")
    bX = [_bx] * 5
    XB = sb("XB", [128, 8, 512], BF16)
    bXB = Buf("XB")
    CST = sb("CST", [128, C_W])
    bCST = Buf("CST")
    IDB = None
    PAR = sb("PAR", [128, 512])
    bPAR = Buf("PAR")
    NWB = 6
    WB = [sb(f"WB{i}", [128, 11 * 128], BF16) for i in range(NWB)]
    bWB = [Buf(f"WB{i}") for i in range(NWB)]
    wb_i = [0]
    wb_cnt = [0]
    ZT = sb("ZT", [128, 8, 512])
    bZT = Buf("ZT")
    LNS = sb("LNS", [128, 4, 512])
    bLNS = [Buf(f"LNS{i}") for i in range(4)]
    BIG = sb("BIG", [128, 9, 1024])
    bBIG = [Buf(f"BIG{i}") for i in range(9)]
    QA = sb("QA", [128, 14 * 512])
    bQA = Buf("QA")
    QT = QA[:, :].rearrange("p (k n) -> p k n", k=14)
    bQT = bQA
    AT = QA[:, :].bitcast(BF16)[:, 0:NFF * 512].rearrange("p (k n) -> p k n", k=NFF)
    bAT = bQA
    ROWS = sb("ROWS", [128, 1024])
    bROWS = [Buf("ROWS0"), Buf("ROWS1")]
    ROWS2 = sb("ROWS2", [128, 1024])
    rows_t = [ROWS, ROWS2]
    rows_i = [0]
    FH = sb("FH", [128, DEPTH, NFF, 2])
    bFH = Buf("FH")
    HEXT = BIG[:, 0:2, 0:520]
    bHEXT = [bBIG[0], bBIG[1]]
    CV = BIG[:, 2:4, 0:512]
    bCV = [bBIG[2], bBIG[3]]
    SFH = BIG[:, 4, 0:NFF * 32].rearrange("p (c t) -> p c t", c=NFF)
    bSFH = bBIG[4]
    SEL = BIG[:, 5, 0:NFF * 32].rearrange("p (c t) -> p c t", c=NFF)
    bSEL = bBIG[5]

    PS = [es.enter_context(nc.psum_tensor(f"ps{i}", [128, 512], F32)) for i in range(8)]
    bPS = [Buf(f"ps{i}", excl=True) for i in range(8)]
    ps_i = [0]

    ps_pool = [None]
    ps_pi = [0, 0]

    def psum():
        if ps_pool[0] is not None:
            q = ps_pool[0]
            i = 4 * q + ps_pi[q]
            ps_pi[q] = (ps_pi[q] + 1) % 4
            return PS[i], bPS[i]
        i = ps_i[0]
        ps_i[0] = (i + 1) % 8
        return PS[i], bPS[i]

    def ident(n=128):
        return CST[0:n, C_ID:C_ID + n]

    S.dma("sp", CST[:], cst[:, :], w=[bCST])

    TILES = [(i * 512, 512, 1, 512) for i in range(4)] + [(2048, 64, NSB, TS)]

    def transpose_to(out_ps, in_ap, nrows, r, w):
        S.pe(lambda e: e.transpose(out_ps, in_ap, ident(nrows)), r=r + [bCST], w=w)

    def load_cols(rows_list, col0, dest=None, dbuf=None):
        i = rows_i[0]
        rows_i[0] ^= 1
        rt, rb = rows_t[i], bROWS[i]
        r0 = 0
        for ap in rows_list:
            n = ap.shape[0]
            S.dma("sp", rt[r0:r0 + n, 0:128], ap, w=[rb])
            r0 += n
        pt, pb = psum()
        transpose_to(pt[:, 0:r0], rt[0:r0, 0:128], r0, [rb], [pb])
        if dest is None:
            dest, dbuf = PAR, bPAR
        S.act(lambda e: e.copy(out=dest[:, col0:col0 + r0], in_=pt[:, 0:r0]), r=[pb], w=[dbuf])
        return r0

    wcache = {}

    def wslab(wd, k0, kt, f0, nf, key=None):
        i = wb_i[0]
        wb_i[0] = (i + 1) % NWB
        t, b = WB[i], bWB[i]
        v = t[:, 0:kt * nf].rearrange("p (k f) -> p k f", k=kt)
        if cfg.get("nodma") and wb_cnt[0] >= NWB:
            return v, b
        wb_cnt[0] += 1
        ck = (key, k0, kt, f0, nf)
        if key is not None and cfg.get("wcache", True) and ck in wcache:
            sc, scb = wcache[ck]
            S.dma("sp", t[:, 0:kt * nf], sc[:, :], r=[scb], w=[b])
            return v, b
        S.dma("pool", v, wd[k0:k0 + kt * 128, f0:f0 + nf].rearrange("(k p) f -> p k f", p=128), w=[b])
        if key is not None and cfg.get("wcache", True) and len(cfg.get("tiles", [0, 1, 2, 3, 4])) > 1:
            sc = nc.dram_tensor(f"wc{len(wcache)}", [128, kt * nf], BF16, kind="Internal").ap()
            scb = Buf(f"wc{len(wcache)}")
            wcache[ck] = (sc, scb)
            S.dma("sp", sc[:, :], t[:, 0:kt * nf], r=[b], w=[scb])
        return v, b

    def cast_xb(ti):
        c0, n, _, _ = TILES[ti]
        S.act(lambda e: e.copy(out=XB[:, :, 0:n], in_=X[:, :, 0:n]), r=[bX[ti]], w=[bXB])

    def layer_norm(zv, zb, nk, n, gcol, bcol, out_fn, eps=LN_EPS, ptile=None):
        inv = 1.0 / (nk * 128)
        if ptile is None:
            ptile = PAR
        zb = zb if isinstance(zb, list) else [zb]
        ones = CST[:, C_ONE:C_ONE + 128]
        pm, pmb = psum()
        for k in range(nk):
            S.pe(lambda e, k=k: e.matmul(pm[:, 0:n], ones, zv(k), start=(k == 0), stop=(k == nk - 1)),
                 r=zb + [bCST], w=[pmb], inc=(k == nk - 1))
        pq, pqb = psum()
        for k in range(nk):
            sq = LNS[:, k % 4, 0:n]
            S.act(lambda e, k=k, sq=sq: e.activation(out=sq, in_=zv(k), func=AF.Square), r=zb, w=[bLNS[k % 4]])
            S.pe(lambda e, k=k, sq=sq: e.matmul(pq[:, 0:n], ones, sq, start=(k == 0), stop=(k == nk - 1)),
                 r=[bLNS[k % 4], bCST], w=[pqb])
        mean = LNS[:, 1, 0:n]
        rstd = LNS[:, 2, 0:n]
        S.act(lambda e: e.mul(out=mean, in_=pm[:, 0:n], mul=inv), r=[pmb], w=[bLNS[1]])
        S.dve(lambda e: e.tensor_tensor(out=rstd, in0=mean, in1=mean, op=ALU.mult), r=[bLNS[1]], w=[bLNS[2]])
        S.dve(lambda e: e.scalar_tensor_tensor(out=rstd, in0=pq[:, 0:n], scalar=inv, in1=rstd,
                                               op0=ALU.mult, op1=ALU.subtract), r=[pqb, bLNS[2]], w=[bLNS[2]])
        S.dve(lambda e: e.tensor_scalar(out=rstd, in0=rstd, scalar1=eps, scalar2=None, op0=ALU.add),
              r=[bLNS[2]], w=[bLNS[2]])
        S.act(lambda e: e.activation(out=rstd, in_=rstd, func=AF.Sqrt), r=[bLNS[2]], w=[bLNS[2]])
        S.dve(lambda e: e.reciprocal(out=rstd, in_=rstd), r=[bLNS[2]], w=[bLNS[2]])
        for k in range(nk):
            tmp = LNS[:, 3, 0:n]
            S.dve(lambda e, k=k: e.tensor_tensor(out=tmp, in0=zv(k), in1=mean, op=ALU.subtract),
                  r=zb + [bLNS[1]], w=[bLNS[3]])
            S.dve(lambda e: e.tensor_tensor(out=tmp, in0=tmp, in1=rstd, op=ALU.mult),
                  r=[bLNS[3], bLNS[2]], w=[bLNS[3]])
            out_fn(k, tmp, bLNS[3], ptile[:, gcol + k:gcol + k + 1], ptile[:, bcol + k:bcol + k + 1])

    def load_x(ti):
        c0, n, _, _ = TILES[ti]
        for tb in range((n + 127) // 128):
            src = xp[c0 + tb * 128:c0 + (tb + 1) * 128, :] if ti < 4 else xs[:, :]
            nr = min(128, n)
            i = rows_i[0]
            rows_i[0] ^= 1
            rt, rb = rows_t[i], bROWS[i]
            S.dma("sp", rt[0:nr, :], src, w=[rb])
            for half in range(2):
                pt, pb = psum()
                for kk in range(4):
                    k = half * 4 + kk
                    S.pe(lambda e, k=k, kk=kk: e.transpose(pt[:, kk * 128:kk * 128 + nr],
                                                           rt[0:nr, k * 128:(k + 1) * 128], ident(nr)),
                         r=[rb, bCST], w=[pb])
                S.act(lambda e, half=half: e.copy(
                    out=X[:, half * 4:half * 4 + 4, tb * 128:tb * 128 + nr],
                    in_=pt[:, :].rearrange("p (k t) -> p k t", k=4)[:, :, 0:nr]), r=[pb], w=[bX[ti]])

    def store_y(ti):
        c0, n, _, _ = TILES[ti]
        for tb in range((n + 127) // 128):
            dst = o_yp[c0 + tb * 128:c0 + (tb + 1) * 128, :] if ti < 4 else o_ys[:, :]
            nr = min(128, n)
            i = rows_i[0]
            rows_i[0] ^= 1
            rt, rb = rows_t[i], bROWS[i]
            for half in range(2):
                pt, pb = psum()
                for kk in range(4):
                    k = half * 4 + kk
                    S.pe(lambda e, k=k, kk=kk: e.transpose(pt[0:nr, kk * 128:(kk + 1) * 128],
                                                           X[:, k, tb * 128:tb * 128 + nr], ident(128)),
                         r=[bX[ti], bCST], w=[pb])
                S.act(lambda e, half=half: e.copy(out=rt[0:nr, half * 512:(half + 1) * 512], in_=pt[0:nr, :]),
                      r=[pb], w=[rb])
            S.dma("sp", dst, rt[0:nr, :], r=[rb], is_out=True)

    def emit_rows(src_fn, nrows, nchunks, dst_fn, r):
        done = 0
        while done < nchunks:
            g = min(8, nchunks - done)
            i = rows_i[0]
            rows_i[0] ^= 1
            rt, rb = rows_t[i], bROWS[i]
            for h0 in range(0, g, 4):
                pt, pb = psum()
                hn = min(4, g - h0)
                for kk in range(hn):
                    c = done + h0 + kk
                    S.pe(lambda e, c=c, kk=kk: e.transpose(pt[0:nrows, kk * 128:(kk + 1) * 128],
                                                           src_fn(c), ident(128)), r=r + [bCST], w=[pb])
                S.act(lambda e, h0=h0, hn=hn: e.copy(out=rt[0:nrows, h0 * 128:(h0 + hn) * 128],
                                                      in_=pt[0:nrows, 0:hn * 128]), r=[pb], w=[rb])
            dst_fn(rt, rb, done, g)
            done += g

    def ffn_params(l):
        load_cols([W["conv_f_w"][l].rearrange("j (c p) -> (j c) p", p=128)], 0)
        load_cols([W["conv_f_b"][l].rearrange("(c p) -> c p", p=128),
                   W["ln2_g"][l].rearrange("(c p) -> c p", p=128),
                   W["ln2_b"][l].rearrange("(c p) -> c p", p=128),
                   W["ln1_g"][l].rearrange("(c p) -> c p", p=128),
                   W["ln1_b"][l].rearrange("(c p) -> c p", p=128)], 66)

    def ffn_sample_state(l):
        i = rows_i[0]
        for c0 in range(0, NFF, 8):
            g = min(8, NFF - c0)
            i = rows_i[0]
            rows_i[0] ^= 1
            rt, rb = rows_t[i], bROWS[i]
            for b in range(NSB):
                S.dma("sp", rt[2 * b:2 * b + 2, 0:g * 128], st_ffn[b, l, :, c0 * 128:(c0 + g) * 128], w=[rb])
            for h0 in range(0, g, 4):
                hn = min(4, g - h0)
                pt, pb = psum()
                for kk in range(hn):
                    S.pe(lambda e, kk=kk, h0=h0: e.transpose(pt[:, kk * 32:(kk + 1) * 32],
                                                             rt[0:32, (h0 + kk) * 128:(h0 + kk + 1) * 128], ident(32)),
                         r=[rb, bCST], w=[pb])
                S.act(lambda e, h0=h0, hn=hn, c0=c0: e.copy(
                    out=SFH[:, c0 + h0:c0 + h0 + hn, :],
                    in_=pt[:, 0:hn * 32].rearrange("p (k t) -> p k t", k=hn)), r=[pb], w=[bSFH])

    def ffn_tile(l, ti):
        c0, n, nseq, T = TILES[ti]
        E = T + 2
        cast_xb(ti)
        wup = W["w_up"][l]
        for g0 in range(0, NFF, 1):
            gn = 1
            wv, wvb = wslab(wup, 0, 8, g0 * 128, gn * 128, key=("up", l))
            wg, wgb = wslab(wup, 0, 8, DFF + g0 * 128, gn * 128, key=("up", l))
            for j in range(gn):
                hc = g0 + j
                ph, phb = psum()
                for k in range(8):
                    S.pe(lambda e, k=k, j=j: e.matmul(ph[:, 0:n], wv[:, k, j * 128:(j + 1) * 128], XB[:, k, 0:n],
                                                      start=(k == 0), stop=(k == 7)),
                         r=[wvb, bXB], w=[phb], inc=(k == 7))
                pg, pgb = psum()
                for k in range(8):
                    S.pe(lambda e, k=k, j=j: e.matmul(pg[:, 0:n], wg[:, k, j * 128:(j + 1) * 128], XB[:, k, 0:n],
                                                      start=(k == 0), stop=(k == 7)),
                         r=[wgb, bXB], w=[pgb], inc=(k == 7))
                hi = hc % 2
                hx = HEXT[:, hi, 0:nseq * E].rearrange("p (s e) -> p s e", s=nseq)
                hb = bHEXT[hi]
                if nseq == 1:
                    if ti == 0:
                        S.dve(lambda e: e.memset(hx[:, :, 0:2], 0.0), w=[hb])
                    else:
                        S.dve(lambda e, hc=hc: e.tensor_copy(out=hx[:, 0, 0:2], in_=FH[:, l, hc, :]), r=[bFH], w=[hb])
                else:
                    S.dve(lambda e, hc=hc: e.tensor_copy(
                        out=hx[:, :, 0:2], in_=SFH[:, hc, :].rearrange("p (b r) -> p b r", r=2)), r=[bSFH], w=[hb])
                S.act(lambda e: e.copy(out=hx[:, :, 2:E], in_=ph[:, 0:n].rearrange("p (s t) -> p s t", s=nseq)),
                      r=[phb], w=[hb])
                cv = CV[:, hi, 0:n].rearrange("p (s t) -> p s t", s=nseq)
                cb = bCV[hi]
                S.act(lambda e, hc=hc: e.activation(out=cv, in_=hx[:, :, 0:T], func=AF.Identity, scale=PAR[:, hc:hc + 1],
                                                    bias=PAR[:, 66 + hc:67 + hc]), r=[hb, bPAR], w=[cb])
                for j2 in (1, 2):
                    S.dve(lambda e, hc=hc, j2=j2: e.scalar_tensor_tensor(
                        out=cv, in0=hx[:, :, j2:j2 + T], scalar=PAR[:, 22 * j2 + hc:22 * j2 + hc + 1], in1=cv,
                        op0=ALU.mult, op1=ALU.add), r=[hb, bPAR, cb], w=[cb])
                if nseq == 1:
                    S.pool(lambda e, hc=hc: e.tensor_copy(out=FH[:, l, hc, :], in_=hx[:, 0, T:T + 2]), r=[hb], w=[bFH])
                else:
                    S.pool(lambda e, hc=hc: e.tensor_copy(
                        out=SEL[:, hc, :].rearrange("p (r b) -> p r b", r=2),
                        in_=hx[:, :, T:T + 2].rearrange("p b r -> p r b")), r=[hb], w=[bSEL])
                S.act(lambda e: e.activation(out=CV[:, hi, 0:n], in_=CV[:, hi, 0:n], func=AF.Silu), r=[cb], w=[cb])
                S.dve(lambda e, hc=hc: e.tensor_tensor(out=AT[:, hc, 0:n], in0=CV[:, hi, 0:n], in1=pg[:, 0:n],
                                                       op=ALU.mult), r=[cb, pgb], w=[bAT])
        if ti == 3:
            def dst(rt, rb, done, g):
                S.dma("sp", o_ffn_p[l, :, done * 128:(done + g) * 128], rt[0:2, 0:g * 128], r=[rb], is_out=True)
            emit_rows(lambda c: FH[:, l, c, :], 2, NFF, dst, [bFH])
        if ti == 4:
            def dst(rt, rb, done, g):
                for r_ in range(2):
                    S.dma("sp", o_ffn_s[:, l, r_, done * 128:(done + g) * 128],
                          rt[16 * r_:16 * r_ + 16, 0:g * 128], r=[rb], is_out=True)
            emit_rows(lambda c: SEL[:, c, :], 32, NFF, dst, [bSEL])
        wdn = W["w_down"][l]
        for o0 in range(0, 8):
            wd0, wdb0 = wslab(wdn, 0, 11, o0 * 128, 128, key=("dn", l))
            wd1, wdb1 = wslab(wdn, 11 * 128, 11, o0 * 128, 128, key=("dn", l))
            for j in range(1):
                oc = o0 + j
                pf, pfb = psum()
                for k in range(NFF):
                    wd_, wdb = (wd0, wdb0) if k < 11 else (wd1, wdb1)
                    S.pe(lambda e, k=k, wd_=wd_: e.matmul(pf[:, 0:n], wd_[:, k % 11, :], AT[:, k, 0:n],
                                                          start=(k == 0), stop=(k == NFF - 1)),
                         r=[wdb, bAT], w=[pfb], inc=(k == NFF - 1))
                S.dve(lambda e, oc=oc: e.scalar_tensor_tensor(out=ZT[:, oc, 0:n], in0=X[:, oc, 0:n], scalar=ALPHA,
                                                              in1=pf[:, 0:n], op0=ALU.mult, op1=ALU.add),
                      r=[bX[ti], pfb], w=[bZT])

        def outf(k, tmp, tb, g, b):
            S.act(lambda e: e.activation(out=X[:, k, 0:n], in_=tmp, func=AF.Identity, scale=g, bias=b),
                  r=[tb, bPAR], w=[bX[ti]])
        layer_norm(lambda k: ZT[:, k, 0:n], bZT, 8, n, 88, 96, outf)

    S5P = sb("S5P", [128, 24, 32])
    bS5P = Buf("S5P")
    BL = sb("BL", [128, 32, 2, 128], BF16)
    bBL = Buf("BL")
    CL = sb("CL", [128, 32, 2, 128], BF16)
    bCL = Buf("CL")
    HCS = sb("HCS", [128, 2, 2, 32])
    bHCS = Buf("HCS")
    BBX = sb("BBX", [128, 8, 128])
    bBBX = [Buf(f"BBX{i}") for i in range(8)]
    S.dve(lambda e: e.memset(HCS[:, :, :, :], 0.0), w=[bHCS])
    HCS2 = sb("HCS2", [128, 2, 2, 32])
    bHCS2 = Buf("HCS2")
    S.dve(lambda e: e.memset(BBX[:, :, :], 0.0), w=bBBX)
    P_LR, P_LI, P_DT, P_MAG, P_ANG, P_C, P_S, P_T0, P_T1, P_T2, P_FR, P_FI, P_AR0, P_AR1, P_AI, P_NAI, P_DEN = range(17)
    P_DSK = 17

    def s5_setup(j):
        alias_claim(0)
        sp = lambda i: S5P[:, i, :]
        dbg = cfg.get("s5stop", 99)
        load_cols([W["lam_re"][j].rearrange("(s g) p -> s (g p)", g=2)], 0, S5P[:, P_LR, :], bS5P)
        load_cols([W["lam_im"][j].rearrange("(s g) p -> s (g p)", g=2)], 0, S5P[:, P_LI, :], bS5P)
        load_cols([W["d_skip"][j].rearrange("(c p) -> c p", p=128)], 0, S5P[:, P_DSK, :], bS5P)
        i = rows_i[0]
        rows_i[0] ^= 1
        rt, rb = rows_t[i], bROWS[i]
        S.dma("sp", rt[0:32, 512:514], W["log_dt"][j].rearrange("(s g) -> s g", g=2), w=[rb])
        S.dve(lambda e: e.tensor_copy(out=rt[0:32, 0:128].rearrange("s (g p) -> s g p", g=2),
                                      in_=rt[0:32, 512:514].unsqueeze(2).to_broadcast([32, 2, 64])), r=[rb], w=[rb])
        pt, pb = psum()
        transpose_to(pt[:, 0:32], rt[0:32, 0:128], 32, [rb], [pb])
        S.act(lambda e: e.activation(out=sp(P_DT), in_=pt[:, 0:32], func=AF.Exp), r=[pb], w=[bS5P])
        R = [bS5P]
        tt = lambda o, a, b, op: S.dve(lambda e: e.tensor_tensor(out=sp(o), in0=sp(a), in1=sp(b), op=op), r=R, w=R)
        def ts(o, a, s1, s2, op0, op1=None):
            if s2 is None:
                S.dve(lambda e: e.tensor_scalar(out=sp(o), in0=sp(a), scalar1=s1, scalar2=None, op0=op0), r=R, w=R)
            else:
                S.dve(lambda e: e.tensor_scalar(out=sp(o), in0=sp(a), scalar1=s1, scalar2=s2, op0=op0, op1=op1), r=R, w=R)
        tt(P_MAG, P_LR, P_DT, ALU.mult)
        S.act(lambda e: e.activation(out=sp(P_MAG), in_=sp(P_MAG), func=AF.Exp), r=R, w=R)
        tt(P_ANG, P_LI, P_DT, ALU.mult)
        S.act(lambda e: e.activation(out=sp(P_T0), in_=sp(P_ANG), func=AF.Sin, scale=1.0 / 64), r=R, w=R)
        S.act(lambda e: e.activation(out=sp(P_S), in_=sp(P_ANG), func=AF.Sin, scale=1.0 / 32), r=R, w=R)
        tt(P_T0, P_T0, P_T0, ALU.mult)
        ts(P_C, P_T0, -2.0, 1.0, ALU.mult, ALU.add)
        for _ in range(5):
            tt(P_T0, P_C, P_C, ALU.mult)
            tt(P_T1, P_S, P_S, ALU.mult)
            tt(P_T2, P_C, P_S, ALU.mult)
            tt(P_C, P_T0, P_T1, ALU.subtract)
            ts(P_S, P_T2, 2.0, None, ALU.mult)
        tt(P_AR0, P_MAG, P_C, ALU.mult)
        S.dve(lambda e: e.tensor_copy(out=sp(P_AR1), in_=sp(P_AR0)), r=R, w=R)
        tt(P_AI, P_MAG, P_S, ALU.mult)
        ts(P_NAI, P_AI, -1.0, None, ALU.mult)
        tt(P_T0, P_LR, P_LR, ALU.mult)
        tt(P_T1, P_LI, P_LI, ALU.mult)
        tt(P_DEN, P_T0, P_T1, ALU.add)
        S.dve(lambda e: e.reciprocal(out=sp(P_DEN), in_=sp(P_DEN)), r=R, w=R)
        ts(P_T2, P_AR0, -1.0, None, ALU.add)
        tt(P_T0, P_T2, P_LR, ALU.mult)
        tt(P_T1, P_AI, P_LI, ALU.mult)
        tt(P_T0, P_T0, P_T1, ALU.add)
        tt(P_FR, P_T0, P_DEN, ALU.mult)
        tt(P_T0, P_AI, P_LR, ALU.mult)
        tt(P_T1, P_T2, P_LI, ALU.mult)
        tt(P_T0, P_T0, P_T1, ALU.subtract)
        tt(P_FI, P_T0, P_DEN, ALU.mult)
        if o_dbg is not None:
            S.dma("sp", o_dbg[:, 0:768], S5P[:, :, :].rearrange("p a b -> p (a b)"), r=[bS5P], is_out=True)
        BR = BIG[:, 0, 0:512].rearrange("p (s c) -> p s c", c=16)
        BI = BIG[:, 1, 0:512].rearrange("p (s c) -> p s c", c=16)
        BBR = BIG[:, 2, 0:512].rearrange("p (s c) -> p s c", c=16)
        BBI = BIG[:, 3, 0:512].rearrange("p (s c) -> p s c", c=16)
        TMP = BIG[:, 4, 0:512].rearrange("p (s c) -> p s c", c=16)
        for g2 in range(2):
            S.dma("sp", BR[64 * g2:64 * g2 + 64, :, :],
                  W["b_re"][j].rearrange("(s g) p c -> g p s c", g=2)[g2], w=[bBIG[0]])
            S.dma("sp", BI[64 * g2:64 * g2 + 64, :, :],
                  W["b_im"][j].rearrange("(s g) p c -> g p s c", g=2)[g2], w=[bBIG[1]])
        fr = S5P[:, P_FR, :].unsqueeze(2).to_broadcast([128, 32, 16])
        fi = S5P[:, P_FI, :].unsqueeze(2).to_broadcast([128, 32, 16])
        S.dve(lambda e: e.tensor_tensor(out=BBR, in0=BR, in1=fr, op=ALU.mult), r=[bBIG[0], bS5P], w=[bBIG[2]])
        S.dve(lambda e: e.tensor_tensor(out=TMP, in0=BI, in1=fi, op=ALU.mult), r=[bBIG[1], bS5P], w=[bBIG[4]])
        S.dve(lambda e: e.tensor_tensor(out=BBR, in0=BBR, in1=TMP, op=ALU.subtract), r=[bBIG[2], bBIG[4]], w=[bBIG[2]])
        S.dve(lambda e: e.tensor_tensor(out=BBI, in0=BI, in1=fr, op=ALU.mult), r=[bBIG[1], bS5P], w=[bBIG[3]])
        S.dve(lambda e: e.tensor_tensor(out=TMP, in0=BR, in1=fi, op=ALU.mult), r=[bBIG[0], bS5P], w=[bBIG[4]])
        S.dve(lambda e: e.tensor_tensor(out=BBI, in0=BBI, in1=TMP, op=ALU.add), r=[bBIG[3], bBIG[4]], w=[bBIG[3]])
        for s_ in range(32):
            q = s_ % 4
            for ri, src, sbuf_ in ((0, BBR, bBIG[2]), (1, BBI, bBIG[3])):
                bx = BBX[:, ri * 4 + q, :]
                bb = bBBX[ri * 4 + q]
                for g2 in range(2):
                    S.dve(lambda e, g2=g2, src=src, bx=bx: e.tensor_copy(
                        out=bx[64 * g2:64 * g2 + 64, 32 * q + 16 * g2:32 * q + 16 * g2 + 16],
                        in_=src[64 * g2:64 * g2 + 64, s_, :]), r=[sbuf_], w=[bb])
                pt, pb = psum()
                transpose_to(pt[:, 0:128], bx, 128, [bb], [pb])
                S.act(lambda e, ri=ri: e.copy(out=BL[:, s_, ri, :], in_=pt[:, 0:128]), r=[pb], w=[bBL])
        CR = BIG[:, 5, 0:512].rearrange("p (t q) -> p t q", q=64)
        CI = BIG[:, 6, 0:512].rearrange("p (t q) -> p t q", q=64)
        S.dma("sp", CR, W["c_re"][j].rearrange("(t g) c p -> (g c) t p", g=8), w=[bBIG[5]])
        S.dma("sp", CI, W["c_im"][j].rearrange("(t g) c p -> (g c) t p", g=8), w=[bBIG[6]])
        CX = BIG[:, 7, 0:256].rearrange("p (r x) -> p r x", r=2)
        for s_ in range(32):
            q, ct = s_ % 4, s_ // 4
            mk = CST[:, C_CM + 128 * q:C_CM + 128 * (q + 1)]
            for ri, src, sbuf_ in ((0, CR, bBIG[5]), (1, CI, bBIG[6])):
                cx = CX[:, ri, :]
                S.dve(lambda e, src=src, cx=cx: e.tensor_tensor(
                    out=cx.rearrange("p (g q) -> p g q", g=2),
                    in0=src[:, ct, :].unsqueeze(1).to_broadcast([128, 2, 64]),
                    in1=mk.rearrange("p (g q) -> p g q", g=2), op=ALU.mult), r=[sbuf_, bCST], w=[bBIG[7]])
                pt, pb = psum()
                transpose_to(pt[:, 0:128], cx, 128, [bBIG[7]], [pb])
                S.act(lambda e, ri=ri: e.mul(out=CL[:, s_, ri, :], in_=pt[:, 0:128], mul=(1.0 if ri == 0 else -1.0)),
                      r=[pb], w=[bCL])

    s5c = {}

    def s5_setup_cached(j, first):
        if not cfg.get("s5cache", True):
            return s5_setup(j)
        if first:
            s5_setup(j)
            sc = [nc.dram_tensor(f"s5c{j}_{k}", shp, dt_, kind="Internal").ap()
                  for k, (shp, dt_) in enumerate((([128, 8192], BF16), ([128, 8192], BF16), ([128, 768], F32)))]
            bsc = Buf(f"s5c{j}")
            s5c[j] = (sc, bsc)
            S.dma("sp", sc[0][:, :], BL[:, :, :, :].rearrange("p a b c -> p (a b c)"), r=[bBL], w=[bsc])
            S.dma("sp", sc[1][:, :], CL[:, :, :, :].rearrange("p a b c -> p (a b c)"), r=[bCL], w=[bsc])
            S.dma("sp", sc[2][:, :], S5P[:, :, :].rearrange("p a b -> p (a b)"), r=[bS5P], w=[bsc])
        else:
            sc, bsc = s5c[j]
            alias_claim(0)
            S.dma("sp", BL[:, :, :, :].rearrange("p a b c -> p (a b c)"), sc[0][:, :], r=[bsc], w=[bBL])
            S.dma("sp", CL[:, :, :, :].rearrange("p a b c -> p (a b c)"), sc[1][:, :], r=[bsc], w=[bCL])
            S.dma("sp", S5P[:, :, :].rearrange("p a b -> p (a b)"), sc[2][:, :], r=[bsc], w=[bS5P])


    def s5_sample_state(j):
        S0 = BIG[:, 8, :].rearrange("p (r s b) -> p r s b", r=2, s=32)
        for ri, st in ((0, st_re), (1, st_im)):
            for q in range(4):
                i = rows_i[0]
                rows_i[0] ^= 1
                rt, rb = rows_t[i], bROWS[i]
                for b4 in range(4):
                    S.dma("sp", rt[32 * b4:32 * b4 + 32, 0:128],
                          st[4 * q + b4, j].rearrange("(s g) p -> s (g p)", g=2), w=[rb])
                pt, pb = psum()
                transpose_to(pt[:, 0:128], rt[:, 0:128], 128, [rb], [pb])
                S.act(lambda e, ri=ri, q=q: e.copy(
                    out=S0[:, ri, :, 4 * q:4 * q + 4].rearrange("p s b -> p b s"),
                    in_=pt[:, 0:128].rearrange("p (b s) -> p b s", b=4)), r=[pb], w=[bBIG[8]])

    tabD = [nc.dram_tensor(f"s5tab{j_}", [32, 128, 1024], F32, kind="Internal").ap() for j_ in range(2)]
    btabD = [Buf("tab0"), Buf("tab1")]
    bQR = [Buf(f"QR{i}") for i in range(14)]
    QR = lambda i: QA[:, i * 512:(i + 1) * 512]

    def qa_claim(take):
        if take:
            S.dve(lambda e: e.memset(QA[:, 0:2], 0.0), r=[bQA], w=bQR)
        else:
            S.dve(lambda e: e.memset(QA[:, 0:2], 0.0), r=bQR, w=[bQA])

    def s5_tables(j):
        TC = QA[:, 0:2048].rearrange("p (s n) -> p s n", s=4)
        TS = QA[:, 2048:4096].rearrange("p (s n) -> p s n", s=4)
        TA = QA[:, 4096:5120].rearrange("p (s n) -> p s n", s=4)
        TB_ = QA[:, 5120:6144].rearrange("p (s n) -> p s n", s=4)
        bb = bQR[0:12]
        for g in range(8):
            S.dve(lambda e, g=g: e.tensor_copy(out=TC[:, :, 0:1], in_=S5P[:, P_C, 4 * g:4 * g + 4].unsqueeze(2)), r=[bS5P], w=bb)
            S.dve(lambda e, g=g: e.tensor_copy(out=TS[:, :, 0:1], in_=S5P[:, P_S, 4 * g:4 * g + 4].unsqueeze(2)), r=[bS5P], w=bb)
            n_ = 1
            while n_ < 512:
                cn = TC[:, :, n_ - 1:n_].to_broadcast([128, 4, n_])
                sn = TS[:, :, n_ - 1:n_].to_broadcast([128, 4, n_])
                TT_ = lambda o, a_, b_, op: S.dve(lambda e: e.tensor_tensor(out=o, in0=a_, in1=b_, op=op), r=bb, w=bb)
                TT_(TA[:, :, 0:n_], TS[:, :, 0:n_], sn, ALU.mult)
                TT_(TB_[:, :, 0:n_], TC[:, :, 0:n_], sn, ALU.mult)
                TT_(TC[:, :, n_:2 * n_], TC[:, :, 0:n_], cn, ALU.mult)
                TT_(TC[:, :, n_:2 * n_], TC[:, :, n_:2 * n_], TA[:, :, 0:n_], ALU.subtract)
                TT_(TS[:, :, n_:2 * n_], TS[:, :, 0:n_], cn, ALU.mult)
                TT_(TS[:, :, n_:2 * n_], TS[:, :, n_:2 * n_], TB_[:, :, 0:n_], ALU.add)
                n_ *= 2
            for q in range(4):
                S.dma("sp", tabD[j][4 * g + q, :, 0:512], TC[:, q, :], r=bb, w=[btabD[j]])
                S.dma("sp", tabD[j][4 * g + q, :, 512:1024], TS[:, q, :], r=bb, w=[btabD[j]])

    def s5_scan_tile(j, ti, U, bU, UB, bUB, GY, bGY):
        qa_claim(True)
        if ti == 0:
            s5_tables(j)
        HBq = [QA[:, 6144 + 512 * i:6144 + 512 * (i + 1)].bitcast(BF16) for i in range(2)]
        HBv = lambda i: HBq[i // 2][:, (i % 2) * 512:(i % 2) * 512 + 512]
        bHB = [bQR[12], bQR[13]]
        py = pyb = None
        HIN, bHIN = (HCS, bHCS) if ti % 2 == 0 else (HCS2, bHCS2)
        HOUT, bHOUT = (HCS2, bHCS2) if ti % 2 == 0 else (HCS, bHCS)

        def bu_mm(sx):
            pr_, prb_ = psum()
            S.pe(lambda e: e.matmul(pr_[:, 0:512], BL[:, sx, 0, :], UB(sx // 4)[:, 0:512], start=True, stop=True),
                 r=[bBL] + bUB, w=[prb_])
            pi_, pib_ = psum()
            S.pe(lambda e: e.matmul(pi_[:, 0:512], BL[:, sx, 1, :], UB(sx // 4)[:, 0:512], start=True, stop=True),
                 r=[bBL] + bUB, w=[pib_])
            return pr_, prb_, pi_, pib_
        bu_q = [bu_mm(0), bu_mm(1)]
        pyst = [None, None]

        def stage_a(s_):
                ct, q = s_ // 4, s_ % 4
                i = s_ % 2
                tcs, tsn = QR(2 * i), QR(2 * i + 1)
                btab = [bQR[2 * i], bQR[2 * i + 1]]
                S.dma("sp", tcs, tabD[j][s_, :, 0:512], r=[btabD[j]], w=[btab[0]])
                S.dma("sp", tsn, tabD[j][s_, :, 512:1024], r=[btabD[j]], w=[btab[1]])
                pr, prb, pi, pib = bu_q.pop(0)
                T1, T2, T4 = QR(4 + 3 * i), QR(5 + 3 * i), QR(6 + 3 * i)
                bT1, bT2, bT4 = bQR[4 + 3 * i], bQR[5 + 3 * i], bQR[6 + 3 * i]
                U1, U2, U3, U4 = (LNS[:, k, :] for k in range(4))
                bU1, bU2, bU3, bU4 = bLNS
                mag = S5P[:, P_MAG, s_:s_ + 1].to_broadcast([128, 512])
                prv, piv = pr[:, 0:512], pi[:, 0:512]
                S.dve(lambda e: e.tensor_tensor(out=T1, in0=prv, in1=tcs, op=ALU.mult), r=[prb, btab[0]], w=[bT1])
                S.dve(lambda e: e.tensor_tensor(out=T2, in0=piv, in1=tsn, op=ALU.mult), r=[pib, btab[1]], w=[bT2])
                S.dve(lambda e: e.tensor_tensor(out=T1, in0=T1, in1=T2, op=ALU.add), r=[bT1, bT2], w=[bT1])
                S.dve(lambda e: e.tensor_tensor(out=T2, in0=piv, in1=tcs, op=ALU.mult), r=[pib, btab[0], bT1], w=[bT2])
                S.dve(lambda e: e.tensor_tensor(out=T4, in0=prv, in1=tsn, op=ALU.mult), r=[prb, btab[1]], w=[bT4])
                S.dve(lambda e: e.tensor_tensor(out=T2, in0=T2, in1=T4, op=ALU.subtract), r=[bT2, bT4], w=[bT2])
                S.dve(lambda e: e.tensor_tensor_scan(out=T1, data0=mag, data1=T1, initial=HIN[:, j, 0, s_:s_ + 1],
                                                     op0=ALU.mult, op1=ALU.add), r=[bT1, bS5P, bHIN], w=[bT1])
                S.dve(lambda e: e.tensor_tensor_scan(out=T2, data0=mag, data1=T2, initial=HIN[:, j, 1, s_:s_ + 1],
                                                     op0=ALU.mult, op1=ALU.add), r=[bT2, bS5P, bHIN], w=[bT2])
                return (tcs, tsn, btab, T1, T2, bT1, bT2, i, ct, q)

        def stage_b(s_, ctx):
                tcs, tsn, btab, T1, T2, bT1, bT2, i, ct, q = ctx
                U1, U2, U3, U4 = (LNS[:, k, :] for k in range(4))
                bU1, bU2, bU3, bU4 = bLNS
                py, pyb = pyst
                S.pool(lambda e: e.tensor_tensor(out=U2, in0=T2, in1=tsn, op=ALU.mult), r=[bT2, btab[1]], w=[bU2])
                S.pool(lambda e: e.tensor_tensor(out=U3, in0=T1, in1=tsn, op=ALU.mult), r=[bT1, btab[1]], w=[bU3])
                S.pool(lambda e: e.tensor_tensor(out=U4, in0=T2, in1=tcs, op=ALU.mult), r=[bT2, btab[0]], w=[bU4])
                S.dve(lambda e: e.tensor_tensor(out=U1, in0=T1, in1=tcs, op=ALU.mult), r=[bT1, btab[0]], w=[bU1])
                S.pool(lambda e: e.tensor_tensor(out=U3, in0=U3, in1=U4, op=ALU.add), r=[bU3, bU4], w=[bU3])
                S.dve(lambda e: e.tensor_tensor(out=U1, in0=U1, in1=U2, op=ALU.subtract), r=[bU1, bU2], w=[bU1])
                S.dve(lambda e: e.tensor_copy(out=HOUT[:, j, 0, s_:s_ + 1], in_=U1[:, 511:512]), r=[bU1], w=[bHOUT])
                S.dve(lambda e: e.tensor_copy(out=HOUT[:, j, 1, s_:s_ + 1], in_=U3[:, 511:512]), r=[bU3], w=[bHOUT])
                hbr, hbi = HBv(2 * i), HBv(2 * i + 1)
                S.act(lambda e, hbr=hbr: e.copy(out=hbr, in_=U1), r=[bU1], w=[bHB[i]])
                S.act(lambda e, hbi=hbi: e.copy(out=hbi, in_=U3), r=[bU3], w=[bHB[i]])
                if q == 0:
                    py, pyb = psum()
                pyst[0], pyst[1] = py, pyb
                S.pe(lambda e, hbr=hbr, py=py, q=q: e.matmul(py[:, 0:512], CL[:, s_, 0, :], hbr, start=(q == 0), stop=False),
                     r=[bCL, bHB[i]], w=[pyb], inc=False)
                S.pe(lambda e, hbi=hbi, py=py, q=q: e.matmul(py[:, 0:512], CL[:, s_, 1, :], hbi, start=False, stop=(q == 3)),
                     r=[bCL, bHB[i]], w=[pyb], inc=True)
                if q == 3:
                    Y, Y2 = QR(10), QR(11)
                    bY, bY2 = bQR[10], bQR[11]
                    S.dve(lambda e, ct=ct, py=py: e.scalar_tensor_tensor(out=Y, in0=U[:, ct, 0:512], scalar=S5P[:, P_DSK, ct:ct + 1],
                                                                         in1=py[:, 0:512], op0=ALU.mult, op1=ALU.add),
                          r=bU + [bS5P, pyb], w=[bY])
                    S.pool(lambda e: e.tensor_tensor(out=Y2, in0=Y, in1=Y, op=ALU.mult), r=[bY], w=[bY2])
                    S.pool(lambda e: e.tensor_scalar(out=Y2, in0=Y2, scalar1=0.044715, scalar2=1.0, op0=ALU.mult, op1=ALU.add),
                           r=[bY2], w=[bY2])
                    S.pool(lambda e: e.tensor_tensor(out=Y2, in0=Y2, in1=Y, op=ALU.mult), r=[bY2, bY], w=[bY2])
                    S.act(lambda e: e.activation(out=Y2, in_=Y2, func=AF.Sigmoid, scale=1.5957691216), r=[bY2], w=[bY2])
                    S.dve(lambda e, ct=ct: e.tensor_tensor(out=GY(ct)[:, 0:512], in0=Y, in1=Y2, op=ALU.mult),
                          r=[bY, bY2], w=bGY)

        ctx_next = stage_a(0)
        for s_ in range(32):
            ctx_cur = ctx_next
            if s_ + 2 < 32:
                bu_q.append(bu_mm(s_ + 2))
            if s_ + 1 < 32:
                ctx_next = stage_a(s_ + 1)
            stage_b(s_, ctx_cur)
        qa_claim(False)


    def mixer_c(l, ti):
        j = l // 2
        c0, n, nseq, T = TILES[ti]
        U = BIG[:, 0:4, :].rearrange("p a (b n) -> p (a b) n", b=2)
        bU = bBIG[0:4]
        UBf = [BIG[:, 4 + i, :].bitcast(BF16) for i in range(2)]
        GYf = [BIG[:, 6 + i, :].bitcast(BF16) for i in range(2)]
        UB = lambda k: UBf[k // 4][:, (k % 4) * 512:(k % 4) * 512 + 512]
        GY = lambda k: GYf[k // 4][:, (k % 4) * 512:(k % 4) * 512 + 512]
        bUB = bBIG[4:6]
        bGY = bBIG[6:8]
        S0 = BIG[:, 8, :].rearrange("p (r s b) -> p r s b", r=2, s=32)
        cast_xb(ti)
        for oc in range(8):
            wv, wvb = wslab(W["w_in_c"][j], 0, 8, oc * 128, 128, key=("inc", j))
            pu, pub = psum()
            for k in range(8):
                S.pe(lambda e, k=k: e.matmul(pu[:, 0:n], wv[:, k, :], XB[:, k, 0:n], start=(k == 0), stop=(k == 7)),
                     r=[wvb, bXB], w=[pub], inc=(k == 7))
            S.act(lambda e, oc=oc: e.copy(out=U[:, oc, 0:n], in_=pu[:, 0:n]), r=[pub], w=bU)
            S.dve(lambda e, oc=oc: e.tensor_copy(out=UB(oc)[:, 0:n], in_=pu[:, 0:n]), r=[pub], w=bUB)
        cstop = cfg.get("cstop", 99)
        if cstop <= 1:
            return mixer_none(l, ti)
        if nseq == 1 and cfg.get("s5scan", True):
            s5_scan_tile(j, ti, U, bU, UB, bUB, GY, bGY)
            nsub_skip = True
        else:
            nsub_skip = False
        qv = QA[:, :]
        BU = qv[:, 0:2048].rearrange("p (r s t) -> p r s t", r=2, s=32)
        HALL = qv[:, 2048:4096].rearrange("p (r s t) -> p r s t", r=2, s=32)
        HB = qv[:, 4096:5120].bitcast(BF16).rearrange("p (r s t) -> p r s t", r=2, s=32)
        T1f = qv[:, 5120:6144]
        T2f = qv[:, 6144:7168]
        bQ = [bQA]
        nsub = 0 if nsub_skip else n // 32
        for st_i in range(nsub):
            cs = st_i * 32
            for ri in range(2):
                for h in range(2):
                    pt, pb = psum()
                    for s16 in range(16):
                        s_ = h * 16 + s16
                        S.pe(lambda e, s_=s_, s16=s16, ri=ri: e.matmul(
                            pt[:, s16 * 32:(s16 + 1) * 32], BL[:, s_, ri, :], UB(s_ // 4)[:, cs:cs + 32],
                            start=True, stop=True), r=[bBL] + bUB, w=[pb], inc=(s16 == 15))
                    S.act(lambda e, ri=ri, h=h: e.copy(
                        out=BU[:, ri, 16 * h:16 * h + 16, :], in_=pt[:, :].rearrange("p (s t) -> p s t", s=16)),
                        r=[pb], w=bQ)
            if cstop <= 2:
                continue
            if nseq == 1:
                nb, nt = 1, 32
                shp = [128, 2, 32]
                bu_t = lambda t: BU[:, :, :, t]
                h_t = lambda t: HALL[:, :, :, t]
                prev0 = HCS[:, j, :, :]
                prev0b = [bHCS]
                T1 = T1f[:, 0:64].rearrange("p (r s) -> p r s", r=2)
                T2 = T2f[:, 0:64].rearrange("p (r s) -> p r s", r=2)
                arar = S5P[:, P_AR0:P_AR0 + 2, :]
                ai = S5P[:, P_AI, :]
                nai = S5P[:, P_NAI, :]
            else:
                nb, nt = 8, 4
                BU5 = BU.rearrange("p r s (b t) -> p r s b t", b=8)
                H5 = HALL.rearrange("p r s (b t) -> p r s b t", b=8)
                bu_t = lambda t: BU5[:, :, :, :, t]
                h_t = lambda t: H5[:, :, :, :, t]
                prev0 = S0[:, :, :, 8 * st_i:8 * st_i + 8]
                prev0b = [bBIG[8]]
                T1 = T1f[:, 0:512].rearrange("p (r s b) -> p r s b", r=2, s=32)
                T2 = T2f[:, 0:512].rearrange("p (r s b) -> p r s b", r=2, s=32)
                arar = S5P[:, P_AR0:P_AR0 + 2, :].unsqueeze(3).to_broadcast([128, 2, 32, 8])
                ai = S5P[:, P_AI, :].unsqueeze(2).to_broadcast([128, 32, 8])
                nai = S5P[:, P_NAI, :].unsqueeze(2).to_broadcast([128, 32, 8])
            for t in range(nt):
                prev = prev0 if t == 0 else h_t(t - 1)
                pbf = prev0b if t == 0 else bQ
                S.dve(lambda e, prev=prev: e.tensor_tensor(out=T1, in0=prev, in1=arar, op=ALU.mult),
                      r=pbf + [bS5P], w=bQ)
                S.dve(lambda e, prev=prev: e.tensor_tensor(out=T2[:, 0], in0=prev[:, 1], in1=nai, op=ALU.mult),
                      r=pbf + [bS5P], w=bQ)
                S.dve(lambda e, prev=prev: e.tensor_tensor(out=T2[:, 1], in0=prev[:, 0], in1=ai, op=ALU.mult),
                      r=pbf + [bS5P], w=bQ)
                S.dve(lambda e: e.tensor_tensor(out=T1, in0=T1, in1=T2, op=ALU.add), r=bQ, w=bQ)
                S.dve(lambda e, t=t: e.tensor_tensor(out=h_t(t), in0=T1, in1=bu_t(t), op=ALU.add), r=bQ, w=bQ)
            if nseq == 1:
                S.dve(lambda e: e.tensor_copy(out=HCS[:, j, :, :], in_=HALL[:, :, :, 31]), r=bQ, w=[bHCS])
            else:
                S.dve(lambda e: e.tensor_copy(out=S0[:, :, :, 8 * st_i:8 * st_i + 8], in_=h_t(3)), r=bQ, w=[bBIG[8]])
            if cstop <= 3:
                continue
            S.act(lambda e: e.copy(out=HB, in_=HALL), r=bQ, w=bQ)
            py, pyb = psum()
            for ct in range(8):
                i_ = 0
                for s4 in range(4):
                    for ri in range(2):
                        s_ = ct * 4 + s4
                        S.pe(lambda e, s_=s_, ri=ri, ct=ct, i_=i_: e.matmul(
                            py[:, ct * 32:(ct + 1) * 32], CL[:, s_, ri, :], HB[:, ri, s_, :],
                            start=(i_ == 0), stop=(i_ == 7)), r=[bCL] + bQ, w=[pyb], inc=(i_ == 7 and ct == 7))
                        i_ += 1
            Y = T1f[:, 0:256].rearrange("p (c t) -> p c t", c=8)
            Y2 = T2f[:, 0:256].rearrange("p (c t) -> p c t", c=8)
            dsk = S5P[:, P_DSK, 0:8].unsqueeze(2).to_broadcast([128, 8, 32])
            S.dve(lambda e: e.tensor_tensor(out=Y, in0=U[:, :, cs:cs + 32], in1=dsk, op=ALU.mult), r=bU + [bS5P], w=bQ)
            S.dve(lambda e: e.tensor_tensor(out=Y, in0=Y, in1=py[:, 0:256].rearrange("p (c t) -> p c t", c=8),
                                            op=ALU.add), r=bQ + [pyb], w=bQ)
            S.dve(lambda e: e.tensor_tensor(out=Y2, in0=Y, in1=Y, op=ALU.mult), r=bQ, w=bQ)
            S.dve(lambda e: e.tensor_scalar(out=Y2, in0=Y2, scalar1=0.044715, scalar2=1.0, op0=ALU.mult, op1=ALU.add),
                  r=bQ, w=bQ)
            S.dve(lambda e: e.tensor_tensor(out=Y2, in0=Y2, in1=Y, op=ALU.mult), r=bQ, w=bQ)
            S.act(lambda e: e.activation(out=Y2, in_=Y2, func=AF.Sigmoid, scale=1.5957691216), r=bQ, w=bQ)
            for k in range(8):
                S.dve(lambda e, k=k: e.tensor_tensor(out=GY(k)[:, cs:cs + 32], in0=Y[:, k, :], in1=Y2[:, k, :],
                                                     op=ALU.mult), r=bQ, w=bGY)
        if cstop <= 4:
            return mixer_none(l, ti)
        if ti == 3:
            for ri, od in ((0, o_re_p), (1, o_im_p)):
                i = rows_i[0]
                rows_i[0] ^= 1
                rt, rb = rows_t[i], bROWS[i]
                pt, pb = psum()
                S.pe(lambda e, ri=ri: e.transpose(pt[0:32, 0:128], HCS[:, j, ri, :], ident(128)), r=[bHCS, bCST], w=[pb])
                S.act(lambda e: e.copy(out=rt[0:32, 0:128], in_=pt[0:32, 0:128]), r=[pb], w=[rb])
                S.dma("sp", od[j].rearrange("(s g) p -> s (g p)", g=2), rt[0:32, 0:128], r=[rb], is_out=True)
        if ti == 4:
            STG = T1f[:, 0:128]
            for ri, od in ((0, o_re_s), (1, o_im_s)):
                for q in range(4):
                    i = rows_i[0]
                    rows_i[0] ^= 1
                    rt, rb = rows_t[i], bROWS[i]
                    S.dve(lambda e, ri=ri, q=q: e.tensor_copy(
                        out=STG.rearrange("p (b s) -> p b s", b=4),
                        in_=S0[:, ri, :, 4 * q:4 * q + 4].rearrange("p s b -> p b s")), r=[bBIG[8]], w=bQ)
                    pt, pb = psum()
                    S.pe(lambda e: e.transpose(pt[:, 0:128], STG, ident(128)), r=bQ + [bCST], w=[pb])
                    S.act(lambda e: e.copy(out=rt[:, 0:128], in_=pt[:, 0:128]), r=[pb], w=[rb])
                    for b4 in range(4):
                        S.dma("sp", od[4 * q + b4, j].rearrange("(s g) p -> s (g p)", g=2),
                              rt[32 * b4:32 * b4 + 32, 0:128], r=[rb], is_out=True)
        for oc in range(8):
            w1, w1b = wslab(W["w_out_c"][j], 0, 8, oc * 128, 128, key=("outc", j))
            w2, w2b = wslab(W["w_out_c"][j], 0, 8, 1024 + oc * 128, 128, key=("outc", j))
            p1, p1b = psum()
            for k in range(8):
                S.pe(lambda e, k=k: e.matmul(p1[:, 0:n], w1[:, k, :], GY(k)[:, 0:n], start=(k == 0), stop=(k == 7)),
                     r=[w1b] + bGY, w=[p1b], inc=(k == 7))
            p2, p2b = psum()
            for k in range(8):
                S.pe(lambda e, k=k: e.matmul(p2[:, 0:n], w2[:, k, :], GY(k)[:, 0:n], start=(k == 0), stop=(k == 7)),
                     r=[w2b] + bGY, w=[p2b], inc=(k == 7))
            sg = LNS[:, 0, 0:n]
            S.act(lambda e: e.activation(out=sg, in_=p2[:, 0:n], func=AF.Sigmoid), r=[p2b], w=[bLNS[0]])
            S.dve(lambda e: e.tensor_tensor(out=sg, in0=sg, in1=p1[:, 0:n], op=ALU.mult), r=[bLNS[0], p1b], w=[bLNS[0]])
            S.dve(lambda e, oc=oc: e.scalar_tensor_tensor(out=ZT[:, oc, 0:n], in0=X[:, oc, 0:n], scalar=ALPHA, in1=sg,
                                                          op0=ALU.mult, op1=ALU.add), r=[bX[ti], bLNS[0]], w=[bZT])


    YAB = sb("YAB", [128, 8, 512], BF16)
    bYAB = Buf("YAB")
    PA = sb("PA", [128, 192])
    bPA = Buf("PA")
    LOR = sb("LOR", [128, 3, 512])
    bLOR = Buf("LOR")
    LNX = sb("LNX", [64, 2, 512])
    bLNX = Buf("LNX")
    CAH = sb("CAH", [128, 2, 4, 30])
    bCAH = Buf("CAH")
    CSH = sb("CSH", [128, 2, 14])
    bCSH = Buf("CSH")
    SHS = sb("SHS", [128, 14, 16])
    bSHS = Buf("SHS")
    STS = sb("STS", [128, 2, 4, 64])
    bSTS = Buf("STS")
    S.dve(lambda e: e.memset(CAH[:, :, :, :], 0.0), w=[bCAH])
    S.dve(lambda e: e.memset(CSH[:, :, :], 0.0), w=[bCSH])
    S.dve(lambda e: e.memset(STS[:, :, :, :], 0.0), w=[bSTS])
    A_MU, A_W0, A_A0, A_KK, A_KA, A_RK, A_CB, A_LG, A_LB, A_CW = 0, 14, 18, 22, 26, 30, 34, 38, 42, 64

    def ab_params(j, first=True):
        v4 = lambda name: W[name][j].rearrange("(c p) -> c p", p=128)

        def build():
            load_cols([W["mu_b"][j].rearrange("(c p) -> c p", p=128), v4("w0_b"), v4("a0_b"), v4("k_k_b"), v4("k_a_b"),
                       W["r_k_b"][j].rearrange("(c h) k -> c (h k)", h=2), v4("conv_a_b"), v4("ln_a_g"), v4("ln_a_b")],
                      0, PA, bPA)
            load_cols([W["conv_a_w"][j].rearrange("t (c p) -> (t c) p", p=128)], A_CW, PA, bPA)
        params_cached(("ab", j), build, PA[:, :], bPA, first)
        S.dma("sp", LOR[0:64, 0, :], W["w2_b"][j], w=[bLOR])
        S.dma("sp", LOR[64:128, 1, :], W["a2_b"][j], w=[bLOR])
        S.dma("sp", LOR[:, 2, :], W["g2_b"][j], w=[bLOR])
        S.dma("sp", LNX[:, 0, :], W["ln_x_g"][j].partition_broadcast(64), w=[bLNX])
        S.dma("sp", LNX[:, 1, :], W["ln_x_b"][j].partition_broadcast(64), w=[bLNX])

    def wkv_store(j, dst):
        i = rows_i[0]
        rows_i[0] ^= 1
        rt, rb = rows_t[i], bROWS[i]
        pt, pb = psum()
        for hp in range(4):
            S.pe(lambda e, hp=hp: e.transpose(pt[0:64, hp * 128:(hp + 1) * 128], STS[:, j, hp, :], ident(128)),
                 r=[bSTS, bCST], w=[pb])
        S.act(lambda e: e.copy(out=rt[0:64, 0:512], in_=pt[0:64, 0:512]), r=[pb], w=[rb])
        S.dma("sp", dst.rearrange("(hp h2) v k -> v hp h2 k", h2=2),
              rt[0:64, 0:512].rearrange("v (hp h2 k) -> v hp h2 k", hp=4, h2=2), r=[rb], is_out=True)

    def wkv_load(j, src):
        i = rows_i[0]
        rows_i[0] ^= 1
        rt, rb = rows_t[i], bROWS[i]
        S.dma("sp", rt[0:64, 0:512].rearrange("v (hp h2 k) -> v hp h2 k", hp=4, h2=2),
              src.rearrange("(hp h2) v k -> v hp h2 k", h2=2), w=[rb])
        pt, pb = psum()
        for hp in range(4):
            S.pe(lambda e, hp=hp: e.transpose(pt[:, hp * 64:(hp + 1) * 64], rt[0:64, hp * 128:(hp + 1) * 128], ident(64)),
                 r=[rb, bCST], w=[pb])
        S.act(lambda e: e.copy(out=STS[:, j, :, :], in_=pt[:, 0:256].rearrange("p (h v) -> p h v", h=4)),
              r=[pb], w=[bSTS])

    dbg_done = [False]
    BLf = BL[:, :, :, :].rearrange("p a b c -> p (a b c)").bitcast(F32)
    CLf = CL[:, :, :, :].rearrange("p a b c -> p (a b c)").bitcast(F32)
    bART = [Buf("ART0"), Buf("ART1")]
    bTM = [Buf("TM0"), Buf("TM1")]
    bZN = [Buf("ZN0"), Buf("ZN1")]
    bBK = bBIG[2]
    bZPQ = Buf("ZPQ")
    alias_bufs = bART + bTM + bZN

    def alias_claim(for_rwkv):
        if for_rwkv == 1:
            S.dve(lambda e: e.memset(CLf[:, 4090:4092], 0.0), r=[bBL, bCL, bZT], w=alias_bufs + [bZPQ])
        elif for_rwkv == 2:
            S.dve(lambda e: e.memset(CLf[:, 4090:4092], 0.0), r=alias_bufs + [bZPQ], w=[bZT])
        else:
            S.dve(lambda e: e.memset(CLf[:, 4090:4092], 0.0), r=alias_bufs, w=[bBL, bCL])

    def rw_views(j, col, L, p):
        V = {}
        fm = lambda slot, idx: BIG[:, slot, idx * 256:idx * 256 + 4 * L].rearrange("p (h l) -> p h l", h=4)
        V["LD"], V["LP"], V["EE"], V["KKn"] = fm(0, 0), fm(0, 1), fm(0, 2), fm(0, 3)
        V["KP"], V["AAv"], V["Bv"], V["RKR"] = fm(1, 0), fm(1, 1), fm(1, 2), fm(1, 3)
        V["BT"] = BIG[:, 2, 512:512 + 4 * L].rearrange("p (h l) -> p h l", h=4)
        V["KT"] = BIG[:, 2, 768:768 + 4 * L].rearrange("p (h l) -> p h l", h=4)
        V["BHf"], V["KHf"], V["TMPf"] = fm(3, 0), fm(3, 1), fm(3, 2)
        V["SGL"] = BIG[:, 3, 768:768 + L]
        V["TW"] = BIG[0:64, 3, 832:832 + L]
        ar0 = p * 520
        V["ART"] = CLf[:, ar0:ar0 + 8 * L].rearrange("p (h a l) -> p h a l", h=4, a=2)
        V["PL"] = CLf[:, ar0 + 512:ar0 + 516]
        tm0 = p * 2048
        V["Vt"], V["BHt"] = BLf[0:L, tm0:tm0 + 512], BLf[0:L, tm0 + 512:tm0 + 1024]
        V["KHt"], V["Gt"] = BLf[0:L, tm0 + 1024:tm0 + 1536], BLf[0:L, tm0 + 1536:tm0 + 2048]
        V["RKt"] = CLf[0:L, 1040 + 8 * p:1048 + 8 * p]
        zr = lambda i: ZT[0:L, i, 0:8 * L].rearrange("p (h l) -> p h l", h=8)
        V["Pq"] = [zr(0), zr(1)]
        V["Qq"] = [zr(2), zr(3)]
        if p == 0:
            V["Mv"], V["NAK"], V["NRB"], V["NRK"] = zr(4), zr(5), zr(6), zr(7)
        else:
            zc = lambda i: CLf[0:L, 2048 + 512 * i:2048 + 512 * i + 8 * L].rearrange("p (h l) -> p h l", h=8)
            V["Mv"], V["NAK"], V["NRB"], V["NRK"] = zc(0), zc(1), zc(2), zc(3)
        V["Zs"], V["Ws"] = BIG[0:L, 6, 0:512], BIG[0:L, 6, 512:1024]
        V["Ys"], V["Y2"] = BIG[0:L, 7, 0:512], BIG[0:L, 7, 512:1024]
        V["MU"], V["RS"] = BIG[0:L, 8, 0:8], BIG[0:L, 8, 8:16]
        return V

    hd = lambda h: (h // 2, 64 * (h % 2))
    HORD = [0, 2, 4, 6, 1, 3, 5, 7]

    def rwkv_part1(j, col, L, nlev, p):
        B0, B1, B2, B3 = bBIG[0], bBIG[1], bBIG[2], bBIG[3]
        V = rw_views(j, col, L, p)
        LD, LP, EE, KKn, KP, AAv, Bv, RKR = (V[k] for k in ("LD", "LP", "EE", "KKn", "KP", "AAv", "Bv", "RKR"))
        BT, KT, BHf, KHf, TMPf, SGL, TW = (V[k] for k in ("BT", "KT", "BHf", "KHf", "TMPf", "SGL", "TW"))
        ART, PL, Vt, BHt, KHt, Gt, RKt = (V[k] for k in ("ART", "PL", "Vt", "BHt", "KHt", "Gt", "RKt"))
        Pq, Qq, Mv, NAK, NRB, NRK = (V[k] for k in ("Pq", "Qq", "Mv", "NAK", "NRB", "NRK"))
        bA, bT, bZ = bART[p], bTM[p], bZN[p]
        R_ = QT[:, 0:4, col:col + L]
        K_ = QT[:, 4:8, col:col + L]
        V_ = QT[:, 8:12, col:col + L]
        WLv = QT[0:64, 12, col:col + L]
        ALv = QT[64:128, 12, col:col + L]
        GLv = QT[:, 13, col:col + L]
        bc4 = lambda c: PA[:, c:c + 4].unsqueeze(2).to_broadcast([128, 4, L])
        ones = CST[:, C_ONE:C_ONE + L]
        TT = lambda o, a, b, op, r, w: S.dve(lambda e: e.tensor_tensor(out=o, in0=a, in1=b, op=op), r=r, w=w)
        S.act(lambda e: e.activation(out=TW, in_=WLv, func=AF.Tanh), r=[bQA], w=[B3])
        pw, pwb = psum()
        for hp in range(4):
            S.pe(lambda e, hp=hp: e.matmul(pw[:, hp * L:(hp + 1) * L], LOR[0:64, 0, hp * 128:(hp + 1) * 128], TW,
                                           start=True, stop=True), r=[bLOR, B3], w=[pwb], inc=(hp == 3))
        pa_, pab = psum()
        for hp in range(4):
            S.pe(lambda e, hp=hp: e.matmul(pa_[:, hp * L:(hp + 1) * L], LOR[64:128, 1, hp * 128:(hp + 1) * 128], ALv,
                                           start=True, stop=True), r=[bLOR, bQA], w=[pab], inc=(hp == 3))
        for hp in range(4):
            S.act(lambda e, hp=hp: e.activation(out=LD[:, hp, :], in_=pw[:, hp * L:(hp + 1) * L], func=AF.Sigmoid,
                                                bias=PA[:, A_W0 + hp:A_W0 + hp + 1]), r=[pwb, bPA], w=[B0])
            S.act(lambda e, hp=hp: e.activation(out=AAv[:, hp, :], in_=pa_[:, hp * L:(hp + 1) * L], func=AF.Sigmoid,
                                                bias=PA[:, A_A0 + hp:A_A0 + hp + 1]), r=[pab, bPA], w=[B1])
        S.dve(lambda e: e.tensor_scalar(out=LD, in0=LD, scalar1=-0.6065306597126334, scalar2=None, op0=ALU.mult),
              r=[B0], w=[B0])
        for hp in range(4):
            S.dve(lambda e, hp=hp: e.tensor_tensor_scan(out=LP[:, hp, :], data0=ones, data1=LD[:, hp, :], initial=0.0,
                                                        op0=ALU.mult, op1=ALU.add), r=[B0, bCST], w=[B0])
        TT(KKn, K_, bc4(A_KK), ALU.mult, [bQA, bPA], [B0])
        TT(TMPf, KKn, KKn, ALU.mult, [B0], [B3])
        pss, pssb = psum()
        for hp in range(4):
            S.pe(lambda e, hp=hp: e.matmul(pss[:, hp * L:(hp + 1) * L], CST[:, C_BONE:C_BONE + 128], TMPf[:, hp, :],
                                           start=True, stop=True), r=[bCST, B3], w=[pssb], inc=(hp == 3))
        S.dve(lambda e: e.tensor_scalar(out=TMPf, in0=pss[:, 0:4 * L].rearrange("p (h l) -> p h l", h=4),
                                        scalar1=1e-24, scalar2=None, op0=ALU.max), r=[pssb], w=[B3])
        S.act(lambda e: e.activation(out=TMPf, in_=TMPf, func=AF.Sqrt), r=[B3], w=[B3])
        S.dve(lambda e: e.reciprocal(out=TMPf, in_=TMPf), r=[B3], w=[B3])
        TT(KKn, KKn, TMPf, ALU.mult, [B0, B3], [B0])
        S.dve(lambda e: e.tensor_scalar(out=TMPf, in0=AAv, scalar1=-1.0, scalar2=None, op0=ALU.add), r=[B1], w=[B3])
        TT(TMPf, TMPf, bc4(A_KA), ALU.mult, [B3, bPA], [B3])
        S.dve(lambda e: e.scalar_tensor_tensor(out=KP, in0=TMPf, scalar=1.0, in1=K_, op0=ALU.add, op1=ALU.mult),
              r=[B3, bQA], w=[B1])
        TT(Bv, KKn, AAv, ALU.mult, [B0, B1], [B1])
        S.act(lambda e: e.activation(out=EE, in_=LP, func=AF.Exp, scale=-1.0), r=[B0], w=[B0])
        TT(BT, Bv, EE, ALU.mult, [B1, B0], [bBK])
        TT(KT, KP, EE, ALU.mult, [B1, B0], [bBK])
        S.act(lambda e: e.activation(out=EE, in_=LP, func=AF.Exp), r=[B0], w=[B0])
        TT(ART[:, :, 1, :], R_, EE, ALU.mult, [bQA, B0], [bA])
        TT(TMPf, LP, LD, ALU.subtract, [B0], [B3])
        S.act(lambda e: e.activation(out=EE, in_=TMPf, func=AF.Exp), r=[B3], w=[B0])
        S.dve(lambda e: e.scalar_tensor_tensor(out=ART[:, :, 0, :], in0=KKn, scalar=-1.0, in1=EE,
                                               op0=ALU.mult, op1=ALU.mult), r=[B0], w=[bA])
        TT(TMPf, LP[:, :, L - 1:L].to_broadcast([128, 4, L]), LP, ALU.subtract, [B0], [B3])
        S.act(lambda e: e.activation(out=EE, in_=TMPf, func=AF.Exp), r=[B3], w=[B0])
        TT(BHf, Bv, EE, ALU.mult, [B1, B0], [B3])
        TT(KHf, KP, EE, ALU.mult, [B1, B0], [B3])
        S.act(lambda e: e.activation(out=PL, in_=LP[:, :, L - 1], func=AF.Exp), r=[B0], w=[bA])
        TT(RKR, R_, KP, ALU.mult, [bQA, B1], [B1])
        TT(RKR, RKR, bc4(A_RK), ALU.mult, [B1, bPA], [B1])
        S.act(lambda e: e.activation(out=SGL, in_=GLv, func=AF.Sigmoid), r=[bQA], w=[B3])
        for src, sbf, dst in ((V_, bQA, Vt), (BHf, B3, BHt), (KHf, B3, KHt)):
            pt, pb = psum()
            for hp in range(4):
                S.pe(lambda e, hp=hp, src=src, pt=pt: e.transpose(pt[0:L, hp * 128:(hp + 1) * 128], src[:, hp, :], ident(128)),
                     r=[sbf, bCST], w=[pb])
            S.act(lambda e, dst=dst, pt=pt: e.copy(out=dst, in_=pt[0:L, 0:512]), r=[pb], w=[bT])
        prk, prkb = psum()
        for hp in range(4):
            S.pe(lambda e, hp=hp: e.matmul(prk[0:L, 2 * hp:2 * hp + 2], RKR[:, hp, :], CST[:, C_BIND:C_BIND + 2],
                                           start=True, stop=True), r=[B1, bCST], w=[prkb], inc=(hp == 3))
        S.act(lambda e: e.copy(out=RKt, in_=prk[0:L, 0:8]), r=[prkb], w=[bT])
        pg, pgb = psum()
        S.pe(lambda e: e.matmul(pg[0:L, 0:512], SGL, LOR[:, 2, :], start=True, stop=True), r=[B3, bLOR], w=[pgb])
        S.act(lambda e: e.copy(out=Gt, in_=pg[0:L, 0:512]), r=[pgb], w=[bT])
        specs = [("bt", 0, Qq[0], C_MSU, bZPQ), ("bt", 1, NRB, C_MUI, bZ), ("kt", 0, NAK, C_MSU, bZ), ("kt", 1, NRK, C_MUI, bZ)]
        for lname, a_idx, dst, mcol, dbf in specs:
            pt, pb = psum()
            for h in HORD:
                hp, ho = hd(h)
                lh = (BT if lname == "bt" else KT)[ho:ho + 64, hp, :]
                if h == 1:
                    S.fence("pe")
                S.pe(lambda e, h=h, lh=lh, hp=hp, ho=ho, a_idx=a_idx, pt=pt: e.matmul(
                    pt[0:L, h * L:(h + 1) * L], lh, ART[ho:ho + 64, hp, a_idx, :], start=True, stop=True),
                    r=[bBK, bA], w=[pb], inc=(h in (6, 7)))
            S.dve(lambda e, dst=dst, mcol=mcol, pt=pt: e.tensor_tensor(
                out=dst, in0=pt[0:L, 0:8 * L].rearrange("p (h l) -> p h l", h=8),
                in1=CST[0:L, mcol:mcol + L].unsqueeze(1).to_broadcast([L, 8, L]), op=ALU.mult),
                r=[pb, bCST], w=[dbf])
        pt, pb = psum()
        for h in HORD:
            hp, ho = hd(h)
            if h == 1:
                S.fence("pe")
            S.pe(lambda e, h=h, hp=hp, ho=ho, pt=pt: e.matmul(pt[0:L, h * L:(h + 1) * L], ART[ho:ho + 64, hp, 0, :],
                                                              BT[ho:ho + 64, hp, :], start=True, stop=True),
                 r=[bBK, bA], w=[pb], inc=(h in (6, 7)))
        S.dve(lambda e, pt=pt: e.tensor_tensor(
            out=Pq[0], in0=pt[0:L, 0:8 * L].rearrange("p (h l) -> p h l", h=8),
            in1=CST[0:L, C_MSL:C_MSL + L].unsqueeze(1).to_broadcast([L, 8, L]), op=ALU.mult), r=[pb, bCST], w=[bZPQ])
        S.dve(lambda e: e.tensor_tensor(out=Mv, in0=Qq[0], in1=CST[0:L, C_ID:C_ID + L].unsqueeze(1).to_broadcast([L, 8, L]),
                                        op=ALU.add), r=[bZPQ, bCST], w=[bZ])
        cur = 0
        for lev in range(1, nlev):
            nxt = 1 - cur
            pP, pPb = psum()
            for h in range(8):
                S.pe(lambda e, h=h, cur=cur, pP=pP: e.matmul(pP[0:L, h * L:(h + 1) * L], Qq[cur][:, h, :], Pq[cur][:, h, :],
                                                             start=True, stop=True), r=[bZPQ], w=[pPb], inc=(h == 7))
            need_q = lev < nlev - 1
            if need_q:
                pQ, pQb = psum()
                for h in range(8):
                    S.pe(lambda e, h=h, cur=cur, pQ=pQ: e.matmul(pQ[0:L, h * L:(h + 1) * L], Pq[cur][:, h, :], Qq[cur][:, h, :],
                                                                 start=True, stop=True), r=[bZPQ], w=[pQb], inc=(h == 7))
            S.act(lambda e, nxt=nxt, pP=pP: e.copy(out=Pq[nxt], in_=pP[0:L, 0:8 * L].rearrange("p (h l) -> p h l", h=8)),
                  r=[pPb], w=[bZPQ])
            if need_q:
                S.dve(lambda e, nxt=nxt, pQ=pQ: e.tensor_copy(out=Qq[nxt], in_=pQ[0:L, 0:8 * L].rearrange("p (h l) -> p h l", h=8)),
                      r=[pQb], w=[bZPQ])
            pM, pMb = psum()
            for h in range(8):
                S.pe(lambda e, h=h, nxt=nxt, pM=pM: e.matmul(pM[0:L, h * L:(h + 1) * L], Pq[nxt][:, h, :], Mv[:, h, :],
                                                             start=True, stop=True), r=[bZPQ, bZ], w=[pMb], inc=(h == 7))
            S.dve(lambda e, pM=pM: e.tensor_tensor(out=Mv, in0=Mv, in1=pM[0:L, 0:8 * L].rearrange("p (h l) -> p h l", h=8),
                                                   op=ALU.add), r=[bZ, pMb], w=[bZ])
            cur = nxt

    def rwkv_part2(j, col, L, p):
        B6, B7, B8 = bBIG[6], bBIG[7], bBIG[8]
        V = rw_views(j, col, L, p)
        ART, PL, Vt, BHt, KHt, Gt, RKt = (V[k] for k in ("ART", "PL", "Vt", "BHt", "KHt", "Gt", "RKt"))
        Mv, NAK, NRB, NRK = (V[k] for k in ("Mv", "NAK", "NRB", "NRK"))
        Zs, Ws, Ys, Y2, MU, RS = (V[k] for k in ("Zs", "Ws", "Ys", "Y2", "MU", "RS"))
        bA, bT, bZ = bART[p], bTM[p], bZN[p]
        ST = STS[:, j, :, :]
        TT = lambda o, a, b, op, r, w: S.dve(lambda e: e.tensor_tensor(out=o, in0=a, in1=b, op=op), r=r, w=w)
        pz, pzb = psum()
        S.fence("pe")
        for i_, h in enumerate((1, 3, 5, 7)):
            hp, ho = hd(h)
            S.pe(lambda e, h=h, hp=hp, ho=ho, i_=i_: e.matmul(pz[0:L, h * 64:(h + 1) * 64], ART[ho:ho + 64, hp, 0, :],
                                                              ST[ho:ho + 64, hp, :], start=(i_ == 0), stop=False,
                                                              skip_group_check=True),
                 r=[bA, bSTS], w=[pzb], inc=True)
        S.fence("pe")
        for h in (0, 2, 4, 6):
            hp, ho = hd(h)
            S.pe(lambda e, h=h, hp=hp, ho=ho: e.matmul(pz[0:L, h * 64:(h + 1) * 64], ART[ho:ho + 64, hp, 0, :],
                                                       ST[ho:ho + 64, hp, :], start=False, stop=False,
                                                       skip_group_check=True),
                 r=[bA, bSTS], w=[pzb], inc=False)
        for h in range(8):
            S.pe(lambda e, h=h: e.matmul(pz[0:L, h * 64:(h + 1) * 64], NAK[:, h, :], Vt[:, h * 64:(h + 1) * 64],
                                         start=False, stop=True, skip_group_check=True),
                 r=[bZ, bT], w=[pzb], inc=(h == 7))
        S.act(lambda e: e.copy(out=Zs, in_=pz[0:L, 0:512]), r=[pzb], w=[B6])
        pw2, pw2b = psum()
        for h in range(8):
            S.pe(lambda e, h=h: e.matmul(pw2[0:L, h * 64:(h + 1) * 64], Mv[:, h, :], Zs[:, h * 64:(h + 1) * 64],
                                         start=True, stop=True), r=[bZ, B6], w=[pw2b], inc=(h == 7))
        S.act(lambda e: e.copy(out=Ws, in_=pw2[0:L, 0:512]), r=[pw2b], w=[B6])
        py, pyb = psum()
        S.fence("pe")
        for i_, h in enumerate((1, 3, 5, 7)):
            hp, ho = hd(h)
            S.pe(lambda e, h=h, hp=hp, ho=ho, i_=i_: e.matmul(py[0:L, h * 64:(h + 1) * 64], ART[ho:ho + 64, hp, 1, :],
                                                              ST[ho:ho + 64, hp, :], start=(i_ == 0), stop=False,
                                                              skip_group_check=True),
                 r=[bA, bSTS], w=[pyb], inc=True)
        S.fence("pe")
        for h in (0, 2, 4, 6):
            hp, ho = hd(h)
            S.pe(lambda e, h=h, hp=hp, ho=ho: e.matmul(py[0:L, h * 64:(h + 1) * 64], ART[ho:ho + 64, hp, 1, :],
                                                       ST[ho:ho + 64, hp, :], start=False, stop=False,
                                                       skip_group_check=True),
                 r=[bA, bSTS], w=[pyb], inc=False)
        for h in range(8):
            S.pe(lambda e, h=h: e.matmul(py[0:L, h * 64:(h + 1) * 64], NRB[:, h, :], Ws[:, h * 64:(h + 1) * 64],
                                         start=False, stop=False, skip_group_check=True),
                 r=[bZ, B6], w=[pyb], inc=False)
            S.pe(lambda e, h=h: e.matmul(py[0:L, h * 64:(h + 1) * 64], NRK[:, h, :], Vt[:, h * 64:(h + 1) * 64],
                                         start=False, stop=True, skip_group_check=True),
                 r=[bZ, bT], w=[pyb], inc=(h == 7))
        psn, psnb = psum()
        for h in range(8):
            hp, ho = hd(h)
            S.pe(lambda e, h=h, hp=hp, ho=ho: e.matmul(psn[ho:ho + 64, hp * 64:(hp + 1) * 64], BHt[:, h * 64:(h + 1) * 64],
                                                       Ws[:, h * 64:(h + 1) * 64], start=True, stop=False),
                 r=[bT, B6], w=[psnb], inc=False)
            S.pe(lambda e, h=h, hp=hp, ho=ho: e.matmul(psn[ho:ho + 64, hp * 64:(hp + 1) * 64], KHt[:, h * 64:(h + 1) * 64],
                                                       Vt[:, h * 64:(h + 1) * 64], start=False, stop=True),
                 r=[bT], w=[psnb], inc=(h == 7))
        S.act(lambda e: e.copy(out=Ys, in_=py[0:L, 0:512]), r=[pyb], w=[B7])
        TT(ST, ST, PL.unsqueeze(2).to_broadcast([128, 4, 64]), ALU.mult, [bSTS, bA], [bSTS])
        TT(ST, ST, psn[:, 0:256].rearrange("p (h v) -> p h v", h=4), ALU.add, [bSTS, psnb], [bSTS])
        Y3 = Ys.rearrange("p (h v) -> p h v", h=8)
        Y23 = Y2.rearrange("p (h v) -> p h v", h=8)
        V3 = Vt.rearrange("p (h v) -> p h v", h=8)
        b8 = lambda ap: ap.unsqueeze(2).to_broadcast([L, 8, 64])
        S.dve(lambda e: e.reduce_sum(out=MU, in_=Y3, axis=AX.X), r=[B7], w=[B8])
        S.dve(lambda e: e.tensor_scalar(out=MU, in0=MU, scalar1=1.0 / 64, scalar2=None, op0=ALU.mult), r=[B8], w=[B8])
        TT(Y3, Y3, b8(MU), ALU.subtract, [B7, B8], [B7])
        TT(Y23, Y3, Y3, ALU.mult, [B7], [B7])
        S.dve(lambda e: e.reduce_sum(out=RS, in_=Y23, axis=AX.X), r=[B7], w=[B8])
        S.dve(lambda e: e.tensor_scalar(out=RS, in0=RS, scalar1=1.0 / 64, scalar2=GN_EPS, op0=ALU.mult, op1=ALU.add),
              r=[B8], w=[B8])
        S.act(lambda e: e.activation(out=RS, in_=RS, func=AF.Sqrt), r=[B8], w=[B8])
        S.dve(lambda e: e.reciprocal(out=RS, in_=RS), r=[B8], w=[B8])
        TT(Y3, Y3, b8(RS), ALU.mult, [B7, B8], [B7])
        TT(Ys, Ys, LNX[0:L, 0, :], ALU.mult, [B7, bLNX], [B7])
        TT(Ys, Ys, LNX[0:L, 1, :], ALU.add, [B7, bLNX], [B7])
        TT(Y23, V3, b8(RKt), ALU.mult, [bT], [B7])
        TT(Ys, Ys, Y2, ALU.add, [B7], [B7])
        TT(Ys, Ys, Gt, ALU.mult, [B7, bT], [B7])
        pT, pTb = psum()
        for hp in range(4):
            S.pe(lambda e, hp=hp: e.transpose(pT[:, hp * L:(hp + 1) * L], Ys[:, hp * 128:(hp + 1) * 128], ident(L)),
                 r=[B7, bCST], w=[pTb])
        S.act(lambda e: e.copy(out=YAB[:, 4:8, col:col + L], in_=pT[:, 0:4 * L].rearrange("p (h l) -> p h l", h=4)),
              r=[pTb], w=[bYAB])

    def mixer_ab(l, ti):
        j = l // 2
        c0, n, nseq, T = TILES[ti]
        E = 30 + T
        w_in = W["w_in_ab"][j]
        cast_xb(ti)
        UEXT = lambda c: BIG[:, c, 0:nseq * E].rearrange("p (s e) -> p s e", s=nseq)
        CVA = lambda c: BIG[:, 4 + c, 0:n]
        UALL = BIG[:, 0:4, 0:nseq * E].rearrange("p c (s e) -> p c s e", s=nseq)
        if nseq == 1:
            if ti == 0:
                S.dve(lambda e: e.memset(UALL[:, :, 0, 0:30], 0.0), w=bBIG[0:4])
            else:
                S.dve(lambda e: e.tensor_copy(out=UALL[:, :, 0, 0:30], in_=CAH[:, j, :, :]), r=[bCAH], w=bBIG[0:4])
        else:
            for q in range(4):
                i = rows_i[0]
                rows_i[0] ^= 1
                rt, rb = rows_t[i], bROWS[i]
                for b4 in range(4):
                    S.dma("sp", rt[30 * b4:30 * b4 + 30, 0:512], st_conv_a[4 * q + b4, j], w=[rb])
                pt, pb = psum()
                for c in range(4):
                    S.pe(lambda e, c=c: e.transpose(pt[:, c * 120:(c + 1) * 120], rt[0:120, c * 128:(c + 1) * 128], ident(120)),
                         r=[rb, bCST], w=[pb])
                for c in range(4):
                    S.act(lambda e, c=c, q=q: e.copy(out=UEXT(c)[:, 4 * q:4 * q + 4, 0:30],
                                                     in_=pt[:, c * 120:(c + 1) * 120].rearrange("p (b r) -> p b r", b=4)),
                          r=[pb], w=[bBIG[c]])
        for c in range(4):
            wv, wvb = wslab(w_in, 0, 8, c * 128, 128, key=("inab", j))
            wg, wgb = wslab(w_in, 0, 8, 512 + c * 128, 128, key=("inab", j))
            pv, pvb = psum()
            for k in range(8):
                S.pe(lambda e, k=k: e.matmul(pv[:, 0:n], wv[:, k, :], XB[:, k, 0:n], start=(k == 0), stop=(k == 7)),
                     r=[wvb, bXB], w=[pvb], inc=(k == 7))
            pg, pgb = psum()
            for k in range(8):
                S.pe(lambda e, k=k: e.matmul(pg[:, 0:n], wg[:, k, :], XB[:, k, 0:n], start=(k == 0), stop=(k == 7)),
                     r=[wgb, bXB], w=[pgb], inc=(k == 7))
            sg = LNS[:, 0, 0:n]
            S.act(lambda e: e.activation(out=sg, in_=pg[:, 0:n], func=AF.Sigmoid), r=[pgb], w=[bLNS[0]])
            S.dve(lambda e, c=c: e.tensor_tensor(out=UEXT(c)[:, :, 30:E], in0=sg.rearrange("p (s t) -> p s t", s=nseq),
                                                 in1=pv[:, 0:n].rearrange("p (s t) -> p s t", s=nseq), op=ALU.mult),
                  r=[bLNS[0], pvb], w=[bBIG[c]])
        for c in range(4):
            acc = CVA(c).rearrange("p (s t) -> p s t", s=nseq)
            S.dve(lambda e, c=c, acc=acc: e.tensor_scalar(out=acc, in0=UEXT(c)[:, :, 0:T], scalar1=PA[:, A_CW + c:A_CW + c + 1],
                                                          scalar2=PA[:, A_CB + c:A_CB + c + 1], op0=ALU.mult, op1=ALU.add),
                  r=[bBIG[c], bPA], w=[bBIG[4 + c]])
            for t_ in range(1, 31):
                S.dve(lambda e, c=c, acc=acc, t_=t_: e.scalar_tensor_tensor(
                    out=acc, in0=UEXT(c)[:, :, t_:t_ + T], scalar=PA[:, A_CW + 4 * t_ + c:A_CW + 4 * t_ + c + 1], in1=acc,
                    op0=ALU.mult, op1=ALU.add), r=[bBIG[c], bPA, bBIG[4 + c]], w=[bBIG[4 + c]])
        if nseq == 1:
            S.dve(lambda e: e.tensor_copy(out=CAH[:, j, :, :], in_=UALL[:, :, 0, T:T + 30]), r=bBIG[0:4], w=[bCAH])
            if ti == 3:
                def dst(rt, rb, done, g):
                    S.dma("sp", o_conv_p[j], rt[0:30, 0:512], r=[rb], is_out=True)
                emit_rows(lambda c: CAH[:, j, c, :], 30, 4, dst, [bCAH])
        else:
            for q in range(4):
                i = rows_i[0]
                rows_i[0] ^= 1
                rt, rb = rows_t[i], bROWS[i]
                pt, pb = psum()
                for c in range(4):
                    stg = LNS[:, c, 0:120]
                    S.dve(lambda e, c=c, q=q, stg=stg: e.tensor_copy(out=stg.rearrange("p (b r) -> p b r", b=4),
                                                                     in_=UEXT(c)[:, 4 * q:4 * q + 4, T:T + 30]),
                          r=[bBIG[c]], w=[bLNS[c]])
                    S.pe(lambda e, c=c, stg=stg: e.transpose(pt[0:120, c * 128:(c + 1) * 128], stg, ident(128)),
                         r=[bLNS[c], bCST], w=[pb])
                S.act(lambda e: e.copy(out=rt[0:120, 0:512], in_=pt[0:120, 0:512]), r=[pb], w=[rb])
                for b4 in range(4):
                    S.dma("sp", o_conv_s[4 * q + b4, j], rt[30 * b4:30 * b4 + 30, 0:512], r=[rb], is_out=True)
        def outf(k, tmp, tb, g, b):
            S.act(lambda e: e.activation(out=YAB[:, k, 0:n], in_=tmp, func=AF.Silu, scale=g, bias=b),
                  r=[tb, bPA], w=[bYAB])
        layer_norm(lambda k: CVA(k), bBIG[4:8], 4, n, A_LG, A_LB, outf, ptile=PA)
        ZTf = ZT[:, :, :].rearrange("p a b -> p (a b)")
        PBX = lambda i: ZTf[:, i * 1024:i * 1024 + nseq * (1 + T)].rearrange("p (s e) -> p s e", s=nseq)
        DTMP = ZTf[:, 2048:2048 + n].rearrange("p (s t) -> p s t", s=nseq)
        if nseq > 1:
            for half in range(2):
                i = rows_i[0]
                rows_i[0] ^= 1
                rt, rb = rows_t[i], bROWS[i]
                S.dma("sp", rt[0:16, 0:896], st_shift[:, j, half * 896:(half + 1) * 896], w=[rb])
                pt, pb = psum()
                for m in range(7):
                    S.pe(lambda e, m=m: e.transpose(pt[:, m * 16:(m + 1) * 16], rt[0:16, m * 128:(m + 1) * 128], ident(16)),
                         r=[rb, bCST], w=[pb])
                S.act(lambda e, half=half: e.copy(out=SHS[:, 7 * half:7 * half + 7, :],
                                                  in_=pt[:, 0:112].rearrange("p (m b) -> p m b", m=7)), r=[pb], w=[bSHS])
        for m in range(14):
            wp, wpb = wslab(w_in, 0, 8, 1024 + m * 128, 128, key=("inab", j))
            pp, ppb = psum()
            for k in range(8):
                S.pe(lambda e, k=k, m=m: e.matmul(pp[:, 0:n], wp[:, k, :], XB[:, k, 0:n],
                                                  start=(k == 0), stop=(k == 7)), r=[wpb, bXB], w=[ppb], inc=(k == 7))
            pbx = PBX(m % 2)
            S.act(lambda e, pbx=pbx: e.copy(out=pbx[:, :, 1:1 + T], in_=pp[:, 0:n].rearrange("p (s t) -> p s t", s=nseq)),
                  r=[ppb], w=[bZT])
            if nseq == 1:
                if ti == 0:
                    S.dve(lambda e, pbx=pbx: e.memset(pbx[:, :, 0:1], 0.0), w=[bZT])
                else:
                    S.dve(lambda e, pbx=pbx, m=m: e.tensor_copy(out=pbx[:, 0, 0:1], in_=CSH[:, j, m:m + 1]), r=[bCSH], w=[bZT])
            else:
                S.dve(lambda e, pbx=pbx, m=m: e.tensor_copy(out=pbx[:, :, 0], in_=SHS[:, m, :]), r=[bSHS], w=[bZT])
            S.dve(lambda e, pbx=pbx: e.tensor_tensor(out=DTMP, in0=pbx[:, :, 0:T], in1=pbx[:, :, 1:1 + T], op=ALU.subtract),
                  r=[bZT], w=[bZT])
            S.dve(lambda e, pbx=pbx, m=m: e.scalar_tensor_tensor(
                out=QT[:, m, 0:n].rearrange("p (s t) -> p s t", s=nseq), in0=DTMP, scalar=PA[:, A_MU + m:A_MU + m + 1],
                in1=pbx[:, :, 1:1 + T], op0=ALU.mult, op1=ALU.add), r=[bZT, bPA], w=[bQA])
            if nseq == 1:
                S.dve(lambda e, pbx=pbx, m=m: e.tensor_copy(out=CSH[:, j, m:m + 1], in_=pbx[:, 0, T:T + 1]), r=[bZT], w=[bCSH])
            else:
                S.dve(lambda e, pbx=pbx, m=m: e.tensor_copy(out=SHS[:, m, :], in_=pbx[:, :, T]), r=[bZT], w=[bSHS])
        if ti == 3:
            i = rows_i[0]
            rows_i[0] ^= 1
            rt, rb = rows_t[i], bROWS[i]
            pt, pb = psum()
            S.pe(lambda e: e.transpose(pt[0:14, 0:128], CSH[:, j, :], ident(128)), r=[bCSH, bCST], w=[pb])
            S.act(lambda e: e.copy(out=rt[0:14, 0:128], in_=pt[0:14, 0:128]), r=[pb], w=[rb])
            S.dma("sp", o_shift_p[j].rearrange("(c p) -> c p", p=128), rt[0:14, 0:128], r=[rb], is_out=True)
        if ti == 4:
            for half in range(2):
                i = rows_i[0]
                rows_i[0] ^= 1
                rt, rb = rows_t[i], bROWS[i]
                for m0, mn in ((0, 4), (4, 3)):
                    pt, pb = psum()
                    for mm in range(mn):
                        m = m0 + mm
                        S.pe(lambda e, m=m, mm=mm, half=half: e.transpose(pt[0:16, mm * 128:(mm + 1) * 128],
                                                                          SHS[:, 7 * half + m, :], ident(128)),
                             r=[bSHS, bCST], w=[pb])
                    S.act(lambda e, m0=m0, mn=mn, pt=pt: e.copy(out=rt[0:16, m0 * 128:(m0 + mn) * 128], in_=pt[0:16, 0:mn * 128]),
                          r=[pb], w=[rb])
                S.dma("sp", o_shift_s[:, j, half * 896:(half + 1) * 896], rt[0:16, 0:896], r=[rb], is_out=True)
        if cfg.get("norwkv"):
            S.dve(lambda e: e.memset(YAB[:, 4:8, 0:n], 0.0), w=[bYAB])
        elif nseq == 1:
            alias_claim(1)
            rwkv_part1(j, 0, 64, 6, 0)
            for ch in range(8):
                S.begin_rec()
                ps_pool[0] = 0
                if ch + 1 < 8:
                    rwkv_part1(j, (ch + 1) * 64, 64, 6, (ch + 1) % 2)
                ra = S.end_rec()
                S.begin_rec()
                ps_pool[0] = 1
                rwkv_part2(j, ch * 64, 64, ch % 2)
                rb_ = S.end_rec()
                ps_pool[0] = None
                S.merge(ra, rb_)
            if ti == 3:
                wkv_store(j, o_wkv_p[j])
        else:
            alias_claim(1)
            rwkv_part1(j, 0, 4, 2, 0)
            for b in range(NSB):
                S.begin_rec()
                ps_pool[0] = 0
                if b + 1 < NSB:
                    rwkv_part1(j, (b + 1) * 4, 4, 2, (b + 1) % 2)
                ra = S.end_rec()
                S.begin_rec()
                ps_pool[0] = 1
                wkv_load(j, st_wkv[b, j])
                rwkv_part2(j, b * 4, 4, b % 2)
                wkv_store(j, o_wkv_s[b, j])
                rb_ = S.end_rec()
                ps_pool[0] = None
                S.merge(ra, rb_)
        if not cfg.get("norwkv"):
            alias_claim(2)
        for oc in range(8):
            wo, wob = wslab(W["w_out_ab"][j], 0, 8, oc * 128, 128, key=("outab", j))
            po, pob = psum()
            for k in range(8):
                S.pe(lambda e, k=k: e.matmul(po[:, 0:n], wo[:, k, :], YAB[:, k, 0:n], start=(k == 0), stop=(k == 7)),
                     r=[wob, bYAB], w=[pob], inc=(k == 7))
            S.dve(lambda e, oc=oc: e.scalar_tensor_tensor(out=ZT[:, oc, 0:n], in0=X[:, oc, 0:n], scalar=ALPHA, in1=po[:, 0:n],
                                                          op0=ALU.mult, op1=ALU.add), r=[bX[ti], pob], w=[bZT])


    def mixer_none(l, ti):
        c0, n, nseq, T = TILES[ti]
        for k in range(8):
            S.dve(lambda e, k=k: e.tensor_scalar(out=ZT[:, k, 0:n], in0=X[:, k, 0:n], scalar1=ALPHA, scalar2=None,
                                                 op0=ALU.mult), r=[bX[ti]], w=[bZT])

    def ln1(l, ti):
        c0, n, nseq, T = TILES[ti]

        def outf(k, tmp, tb, g, b):
            S.act(lambda e: e.activation(out=X[:, k, 0:n], in_=tmp, func=AF.Identity, scale=g, bias=b),
                  r=[tb, bPAR], w=[bX[ti]])
        layer_norm(lambda k: ZT[:, k, 0:n], bZT, 8, n, 104, 112, outf)

    tiles = cfg.get("tiles", [0, 1, 2, 3, 4])
    pcache = {}

    def params_cached(key, build, tile2d, buf, first):
        if not cfg.get("pcache", True) or len(tiles) == 1:
            return build()
        if first:
            build()
            sc = nc.dram_tensor(f"pc{len(pcache)}", [128, tile2d.shape[1]], F32, kind="Internal").ap()
            scb = Buf(f"pc{len(pcache)}")
            pcache[key] = (sc, scb)
            S.dma("sp", sc[:, :], tile2d, r=[buf], w=[scb])
        else:
            sc, scb = pcache[key]
            S.dma("sp", tile2d, sc[:, :], r=[scb], w=[buf])


    for ti in tiles:
        load_x(ti)
        for l in range(depth):
            params_cached(("ffn", l), lambda: ffn_params(l), PAR[:, :], bPAR, ti == tiles[0])
            mix = cfg.get("mix", "full")
            if l % 2 == 1 and mix in ("full", "c"):
                s5_setup_cached(l // 2, ti == tiles[0])
                if ti == 4:
                    s5_sample_state(l // 2)
                if cfg.get("s5dbg", 0) == 1:
                    mixer_none(l, ti)
                else:
                    mixer_c(l, ti)
            elif l % 2 == 0 and mix in ("full", "ab"):
                ab_params(l // 2, ti == tiles[0])
                mixer_ab(l, ti)
            else:
                mixer_none(l, ti)
            ln1(l, ti)
            if ti == 4:
                ffn_sample_state(l)
            ffn_tile(l, ti)
        store_y(ti)
    S.finish()
    print("instructions:", S.ninst, {k: v for k, v in S.cnt.items()}, "sbuf_left", nc.sbuf_bytes_remaining)


WSHAPES = [
    ("w_in_ab", (2, 1024, 2816)), ("conv_a_w", (2, 31, 512)), ("conv_a_b", (2, 512)), ("ln_a_g", (2, 512)),
    ("ln_a_b", (2, 512)), ("mu_b", (2, 1792)), ("w0_b", (2, 512)), ("w2_b", (2, 64, 512)), ("a0_b", (2, 512)),
    ("a2_b", (2, 64, 512)), ("g2_b", (2, 128, 512)), ("k_k_b", (2, 512)), ("k_a_b", (2, 512)), ("r_k_b", (2, 8, 64)),
    ("ln_x_g", (2, 512)), ("ln_x_b", (2, 512)), ("w_out_ab", (2, 1024, 1024)), ("w_in_c", (2, 1024, 1024)),
    ("lam_re", (2, 64, 64)), ("lam_im", (2, 64, 64)), ("log_dt", (2, 64)), ("b_re", (2, 64, 64, 16)),
    ("b_im", (2, 64, 64, 16)), ("c_re", (2, 64, 16, 64)), ("c_im", (2, 64, 16, 64)), ("d_skip", (2, 1024)),
    ("w_out_c", (2, 1024, 2048)), ("w_up", (4, 1024, 5632)), ("conv_f_w", (4, 3, 2816)), ("conv_f_b", (4, 2816)),
    ("w_down", (4, 2816, 1024)), ("ln1_g", (4, 1024)), ("ln1_b", (4, 1024)), ("ln2_g", (4, 1024)), ("ln2_b", (4, 1024)),
]

OUT_NAMES = ["o_yp", "o_ys", "o_conv_p", "o_conv_s", "o_shift_p", "o_shift_s", "o_wkv_p", "o_wkv_s",
             "o_re_p", "o_re_s", "o_im_p", "o_im_s", "o_ffn_p", "o_ffn_s"]


def run(inputs, cfg=None, cores=None):
    cfg = cfg or {}
    cores = list(range(NCORES)) if cores is None else cores
    f = lambda a: np.ascontiguousarray(np.asarray(a, dtype=np.float32))
    consts = make_consts()
    wmap = {name: f(inputs[name]) for name, _ in WSHAPES}
    in_maps = []
    for c in cores:
        sl = slice(c * NSB, (c + 1) * NSB)
        m = dict(wmap)
        m["xp"] = f(inputs["x_prompt"][c])
        m["xs"] = f(inputs["x_sample"][sl]).reshape(NSB * TS, D)
        m["st_conv_a"] = f(inputs["state_conv_a"][sl])
        m["st_shift"] = f(inputs["state_shift_b"][sl])
        m["st_wkv"] = f(inputs["state_wkv_b"][sl])
        m["st_re"] = f(inputs["state_ssm_re"][sl])
        m["st_im"] = f(inputs["state_ssm_im"][sl])
        m["st_ffn"] = f(inputs["state_conv_ffn"][sl])
        m["cst"] = consts
        in_maps.append(m)
    nc = build(cfg)
    res = run_bass_kernel_spmd(nc, in_maps, core_ids=cores)
    return res.results


def kernel(**inputs):
    rs = run(inputs)
    outs = []
    for name in OUT_NAMES:
        if name.endswith("p"):
            outs.append(np.stack([np.asarray(r[name]) for r in rs], 0))
        else:
            outs.append(np.concatenate([np.asarray(r[name]) for r in rs], 0))
    outs[0] = outs[0].reshape(8, SEQ, D)
    outs[1] = outs[1].reshape(128, TS, D)
    return tuple(o.astype(np.float32) for o in outs)
```
